# Optimizing a Trainium2 kernel written in Bass

```python
import math
import jax, jax.numpy as jnp
from jax import lax
import numpy as np

D_MODEL = 2048
BATCH = 4
SEQ = 4096
DEPTH = 2

GRID_W = 64
CTX_LEN = 256
EPS = 1e-6
N_AB_LAYERS = (DEPTH + 1) // 2
N_C_LAYERS = DEPTH // 2

ATT_WIDTH = D_MODEL // 2
CONV_WIDTH = D_MODEL - ATT_WIDTH
DIFF_HEAD_DIM = 64
DIFF_HEADS = ATT_WIDTH // (2 * DIFF_HEAD_DIM)
DIFF_V_DIM = 2 * DIFF_HEAD_DIM
CONV_K = 31
Q_BLOCK = 128
ROPE_THETA = 10000.0
ROPE_AXIS_DIM = DIFF_HEAD_DIM // 2
AB_IN = 4 * ATT_WIDTH + 3 * CONV_WIDTH

HGRN_WIDTH = D_MODEL
HGRN_HEAD_DIM = 128
HGRN_HEADS = HGRN_WIDTH // HGRN_HEAD_DIM
HGRN_CHUNK = 64
C_IN = 5 * HGRN_WIDTH

kernel_name = 'hybrid_diffattn_conformer_hgrn2_dit'

F32 = jnp.float32


def rms_norm(x, g):
    xf = x.astype(F32)
    y = xf * lax.rsqrt(jnp.mean(xf * xf, axis=-1, keepdims=True) + EPS)
    return y.astype(x.dtype) * g


def layer_norm(x, g, b):
    xf = x.astype(F32)
    mu = jnp.mean(xf, axis=-1, keepdims=True)
    xc = xf - mu
    y = xc * lax.rsqrt(jnp.mean(xc * xc, axis=-1, keepdims=True) + EPS)
    return y.astype(x.dtype) * g + b


def axial_rope_tables(length):
    rows = length // GRID_W
    row = jnp.repeat(jnp.arange(rows), GRID_W).astype(F32)
    col = jnp.tile(jnp.arange(GRID_W), rows).astype(F32)
    inv = ROPE_THETA ** (-jnp.arange(0, ROPE_AXIS_DIM, 2, dtype=F32) / ROPE_AXIS_DIM)
    def axis_angles(pos):
        a = pos[:, None] * inv[None, :]
        return jnp.concatenate([a, a], axis=-1)
    ang = jnp.concatenate([axis_angles(row), axis_angles(col)], axis=-1)
    return jnp.cos(ang), jnp.sin(ang)


def apply_rope(x, cos, sin):
    xs = x.reshape(x.shape[:-1] + (2, 2, ROPE_AXIS_DIM // 2))
    rot = jnp.stack([-xs[..., 1, :], xs[..., 0, :]], axis=-2).reshape(x.shape)
    c = cos[None, :, None, None, :].astype(x.dtype)
    s = sin[None, :, None, None, :].astype(x.dtype)
    return x * c + rot * s


def diff_attend(q, k, v, lam):
    s = jnp.einsum('bqhmd,bkhmd->bhmqk', q, k, preferred_element_type=F32)
    p = jax.nn.softmax(s, axis=-1)
    a = p[:, :, 0] - lam * p[:, :, 1]
    return jnp.einsum('bhqk,bkhv->bqhv', a.astype(v.dtype), v)


def depthwise_conv(x, w, b):
    y = lax.conv_general_dilated(
        x, w[:, None, :].astype(x.dtype), (1,), [(CONV_K // 2, CONV_K // 2)],
        dimension_numbers=('NWC', 'WIO', 'NWC'), feature_group_count=x.shape[-1])
    return y + b


def chunk_scan(q, k, v, logf, s0):
    bsz, length, nh, _ = q.shape
    dv = v.shape[-1]
    n_chunks = length // HGRN_CHUNK
    def blk(t):
        return t.reshape(bsz, n_chunks, HGRN_CHUNK, nh, t.shape[-1]).transpose(1, 0, 3, 2, 4)
    q, k, v, logf = blk(q), blk(k), blk(v), blk(logf)
    b = jnp.cumsum(logf, axis=3)
    b_end = b[:, :, :, -1:, :]
    q_dec = q * jnp.exp(b)
    k_inv = k * jnp.exp(-b)
    k_end = k * jnp.exp(b_end - b)
    mask = jnp.tril(jnp.ones((HGRN_CHUNK, HGRN_CHUNK), dtype=bool))
    a = jnp.where(mask, jnp.einsum('nbhtd,nbhsd->nbhts', q_dec, k_inv), 0.0)
    o_intra = jnp.einsum('nbhts,nbhsv->nbhtv', a, v)
    decay_end = jnp.exp(b_end[:, :, :, 0, :])
    def step(s, xs):
        qd, ke, vv, de = xs
        o = jnp.einsum('bhtd,bhdv->bhtv', qd, s)
        s = de[..., None] * s + jnp.einsum('bhsd,bhsv->bhdv', ke, vv)
        return s, o
    s_fin, o_inter = lax.scan(step, s0, (q_dec, k_end, v, decay_end))
    o = (o_intra + o_inter).transpose(1, 0, 3, 2, 4).reshape(bsz, length, nh, dv)
    return o, s_fin


def attn_conv_layer(h_lat, h_ctx, cos, sin, w_in, w_out, qn_g, kn_g, lq1, lk1, lq2, lk2,
                    subln_g, conv_w, conv_b, cln_g, cln_b, layer_idx, need_ctx):
    bsz, seq, _ = h_lat.shape
    lam_init = 0.8 - 0.6 * math.exp(-0.3 * layer_idx)
    lam = (jnp.exp(jnp.sum(lq1.astype(F32) * lk1.astype(F32)))
           - jnp.exp(jnp.sum(lq2.astype(F32) * lk2.astype(F32))) + lam_init)
    cuts = [ATT_WIDTH, 2 * ATT_WIDTH, 3 * ATT_WIDTH, 4 * ATT_WIDTH,
            4 * ATT_WIDTH + CONV_WIDTH, 4 * ATT_WIDTH + 2 * CONV_WIDTH]
    q_scale = DIFF_HEAD_DIM ** -0.5

    def project(h):
        q, k, v, z_a, g_val, g_gate, z_b = jnp.split(h @ w_in, cuts, axis=-1)
        shp = h.shape[:2]
        q = rms_norm(q.reshape(shp + (DIFF_HEADS, 2, DIFF_HEAD_DIM)), qn_g)
        k = rms_norm(k.reshape(shp + (DIFF_HEADS, 2, DIFF_HEAD_DIM)), kn_g)
        v = v.reshape(shp + (DIFF_HEADS, DIFF_V_DIM))
        return q, k, v, z_a, g_val, g_gate, z_b

    def attn_out(o, z_a):
        o = rms_norm(o, subln_g) * (1.0 - lam_init)
        return o.reshape(o.shape[:2] + (ATT_WIDTH,)) * jax.nn.silu(z_a)

    def conv_out(g_val, g_gate, z_b):
        y = g_val * jax.nn.sigmoid(g_gate)
        y = depthwise_conv(y, conv_w, conv_b)
        y = layer_norm(y, cln_g, cln_b)
        return jax.nn.silu(y) * jax.nn.silu(z_b)

    qc, kc, vc, zac, gvc, ggc, zbc = project(h_ctx)
    ql, kl, vl, zal, gvl, ggl, zbl = project(h_lat)
    ql = apply_rope(ql, cos, sin) * q_scale
    kl = apply_rope(kl, cos, sin)
    k_all = jnp.concatenate([kc, kl], axis=1)
    v_all = jnp.concatenate([vc, vl], axis=1)
    nblk = seq // Q_BLOCK
    qb = ql.reshape((bsz, nblk, Q_BLOCK) + ql.shape[2:]).swapaxes(0, 1)
    ob = lax.map(lambda qq: diff_attend(qq, k_all, v_all, lam), qb)
    o_lat = ob.swapaxes(0, 1).reshape(bsz, seq, DIFF_HEADS, DIFF_V_DIM)
    y_lat = jnp.concatenate([attn_out(o_lat, zal), conv_out(gvl, ggl, zbl)], axis=-1) @ w_out
    y_ctx = None
    if need_ctx:
        o_ctx = diff_attend(qc * q_scale, kc, vc, lam)
        y_ctx = jnp.concatenate([attn_out(o_ctx, zac), conv_out(gvc, ggc, zbc)], axis=-1) @ w_out
    return y_lat, y_ctx


def hgrn_layer(h_lat, h_ctx, w_in, w_out, lb, onorm_g, need_ctx):
    bsz = h_lat.shape[0]

    def heads(t):
        return t.reshape(t.shape[:2] + (HGRN_HEADS, HGRN_HEAD_DIM))

    def project(h):
        q, i, u_fw, u_bw, z = jnp.split(h @ w_in, 5, axis=-1)
        q = heads(jax.nn.silu(q).astype(F32))
        v = heads(i.astype(F32))
        dirs = []
        for u, lbd in ((u_fw, lb[0]), (u_bw, lb[1])):
            uf = u.astype(F32)
            logf = jnp.log(lbd + (1.0 - lbd) * jax.nn.sigmoid(uf))
            k = (1.0 - lbd) * jax.nn.sigmoid(-uf)
            dirs.append((heads(logf), heads(k)))
        return q, v, dirs, z

    def flip(t):
        return jnp.flip(t, axis=1)

    def readout(o, z):
        o = rms_norm(o, onorm_g).astype(z.dtype)
        return (o.reshape(o.shape[:2] + (HGRN_WIDTH,)) * jax.nn.silu(z)) @ w_out

    qc, vc, (fwc, bwc), zc = project(h_ctx)
    ql, vl, (fwl, bwl), zl = project(h_lat)
    zero = jnp.zeros((bsz, HGRN_HEADS, HGRN_HEAD_DIM, HGRN_HEAD_DIM), F32)
    o_cf, s_cf = chunk_scan(qc, fwc[1], vc, fwc[0], zero)
    o_cb, s_cb = chunk_scan(flip(qc), flip(bwc[1]), flip(vc), flip(bwc[0]), zero)
    o_lf, _ = chunk_scan(ql, fwl[1], vl, fwl[0], s_cf)
    o_lb, _ = chunk_scan(flip(ql), flip(bwl[1]), flip(vl), flip(bwl[0]), s_cb)
    y_lat = readout(o_lf + flip(o_lb), zl)
    y_ctx = readout(o_cf + flip(o_cb), zc) if need_ctx else None
    return y_lat, y_ctx


def setup_inputs(seed: int = 0) -> dict:
    key = jax.random.key(seed)
    ks = jax.random.split(key, 24)
    D = D_MODEL
    def nrm(k, shape, scale):
        return jax.random.normal(k, shape, F32) * scale
    return {
        'x': nrm(ks[0], (BATCH, SEQ, D), 1.0),
        'c': nrm(ks[1], (BATCH, D), 1.0),
        'ctx': nrm(ks[2], (BATCH, CTX_LEN, D), 1.0),
        'c_ctx': nrm(ks[3], (D,), 1.0),
        'w_ada': nrm(ks[4], (DEPTH, D, 3 * D), 0.5 * D ** -0.5),
        'b_ada': nrm(ks[5], (DEPTH, 3 * D), 0.02),
        'norm_g': 1.0 + nrm(ks[6], (DEPTH, D), 0.02),
        'w_in_ab': nrm(ks[7], (N_AB_LAYERS, D, AB_IN), D ** -0.5),
        'w_out_ab': nrm(ks[8], (N_AB_LAYERS, D, D), D ** -0.5),
        'qn_g': 1.0 + nrm(ks[9], (N_AB_LAYERS, DIFF_HEAD_DIM), 0.02),
        'kn_g': 1.0 + nrm(ks[10], (N_AB_LAYERS, DIFF_HEAD_DIM), 0.02),
        'lam_q1': nrm(ks[11], (N_AB_LAYERS, DIFF_HEAD_DIM), 0.1),
        'lam_k1': nrm(ks[12], (N_AB_LAYERS, DIFF_HEAD_DIM), 0.1),
        'lam_q2': nrm(ks[13], (N_AB_LAYERS, DIFF_HEAD_DIM), 0.1),
        'lam_k2': nrm(ks[14], (N_AB_LAYERS, DIFF_HEAD_DIM), 0.1),
        'subln_g': 1.0 + nrm(ks[15], (N_AB_LAYERS, DIFF_V_DIM), 0.02),
        'conv_w': nrm(ks[16], (N_AB_LAYERS, CONV_K, CONV_WIDTH), CONV_K ** -0.5),
        'conv_b': nrm(ks[17], (N_AB_LAYERS, CONV_WIDTH), 0.02),
        'cln_g': 1.0 + nrm(ks[18], (N_AB_LAYERS, CONV_WIDTH), 0.02),
        'cln_b': nrm(ks[19], (N_AB_LAYERS, CONV_WIDTH), 0.02),
        'w_in_c': nrm(ks[20], (N_C_LAYERS, D, C_IN), D ** -0.5),
        'w_out_c': nrm(ks[21], (N_C_LAYERS, D, D), D ** -0.5),
        'lb_gamma': nrm(ks[22], (2, DEPTH, HGRN_WIDTH), 0.5),
        'onorm_g': 1.0 + nrm(ks[23], (N_C_LAYERS, HGRN_HEAD_DIM), 0.02),
    }


def reference(x, c, ctx, c_ctx, w_ada, b_ada, norm_g, w_in_ab, w_out_ab, qn_g, kn_g,
              lam_q1, lam_k1, lam_q2, lam_k2, subln_g, conv_w, conv_b, cln_g, cln_b,
              w_in_c, w_out_c, lb_gamma, onorm_g):
    seq = x.shape[1]
    cos, sin = axial_rope_tables(seq)
    p = jax.nn.softmax(lb_gamma.astype(F32), axis=1)
    lb_all = jnp.cumsum(p, axis=1) - p[:, :1]
    sc = jax.nn.silu(c)
    scc = jax.nn.silu(c_ctx)
    h_ctx_stream = ctx
    for l in range(DEPTH):
        need_ctx = l < DEPTH - 1
        shift, scale, gate = jnp.split(sc @ w_ada[l] + b_ada[l], 3, axis=-1)
        shift_c, scale_c, gate_c = jnp.split(scc @ w_ada[l] + b_ada[l], 3, axis=-1)
        h_lat = rms_norm(x, norm_g[l]) * (1.0 + scale[:, None, :]) + shift[:, None, :]
        h_ctx = rms_norm(h_ctx_stream, norm_g[l]) * (1.0 + scale_c) + shift_c
        j = l // 2
        if l % 2 == 0:
            y_lat, y_ctx = attn_conv_layer(
                h_lat, h_ctx, cos, sin, w_in_ab[j], w_out_ab[j], qn_g[j], kn_g[j],
                lam_q1[j], lam_k1[j], lam_q2[j], lam_k2[j], subln_g[j],
                conv_w[j], conv_b[j], cln_g[j], cln_b[j], l, need_ctx)
        else:
            y_lat, y_ctx = hgrn_layer(h_lat, h_ctx, w_in_c[j], w_out_c[j], lb_all[:, l],
                                      onorm_g[j], need_ctx)
        x = x + gate[:, None, :] * y_lat.astype(x.dtype)
        if need_ctx:
            h_ctx_stream = h_ctx_stream + gate_c * y_ctx.astype(h_ctx_stream.dtype)
    return x
```

```python
from contextlib import ExitStack
import math
import numpy as np
import ml_dtypes
import concourse.bass as bass
import concourse.mybir as mybir
from concourse.bass_utils import run_bass_kernel_spmd

F32 = mybir.dt.float32
BF16 = mybir.dt.bfloat16
AF = mybir.ActivationFunctionType
ALU = mybir.AluOpType
AX = mybir.AxisListType

D = 2048
SEQ = 4096
HALF = 2048
CTX = 256
EPS = 1e-6
NKEY = CTX + SEQ
STQ = "sp"


class Buf:
    def __init__(self, t, name="", psum=False):
        self.t = t
        self.name = name
        self.psum = psum
        self.w = {}
        self.r = {}
        self.pr = {}
        self.open = False

    def __getitem__(self, k):
        return self.t[k]


def _merge(d, s):
    for k, v in s.items():
        if d.get(k, 0) < v:
            d[k] = v


class Sched:
    GEN = 12000
    NSLOT = 8

    def __init__(self, nc):
        self.nc = nc
        self.eng = {"pe": nc.tensor, "dve": nc.vector, "act": nc.scalar,
                    "pool": nc.gpsimd, "sp": nc.sync}
        self.cnt = {e: 0 for e in self.eng}
        self.gen = {e: 0 for e in self.eng}
        self.sems = {}
        self.waited = {e: {} for e in self.eng}
        self.dma_i = {e: 0 for e in self.eng}
        self.nops = 0
        self.nwaits = 0

    def sem(self, key):
        if key not in self.sems:
            self.sems[key] = self.nc.alloc_semaphore("s_" + "_".join(str(k) for k in key))
        return self.sems[key]

    def _wait(self, e, deps):
        for key, val in deps.items():
            if key[0] == "E" and key[1] == e and e in ("pe", "sp"):
                continue
            if self.waited[e].get(key, 0) >= val:
                continue
            self.eng[e].wait_ge(self.sem(key), val)
            self.waited[e][key] = val
            self.nwaits += 1

    def _deps(self, e, r, w, wa):
        deps = {}
        for b in r:
            _merge(deps, b.w)
            if b.psum:
                _merge(deps, {k: v for k, v in b.r.items() if k[1] != e})
        for b in w:
            _merge(deps, b.w)
            _merge(deps, b.r)
            _merge(deps, b.pr)
        for b in wa:
            if not b.open:
                b.pr = dict(b.r)
                _merge(b.pr, b.w)
                b.r = {}
                b.w = {}
                b.open = True
            _merge(deps, b.pr)
        return deps

    def _record(self, key, val, r, w, wa):
        ev = {key: val}
        for b in r:
            _merge(b.r, ev)
            b.open = False
        for b in w:
            b.w = dict(ev)
            b.r = {}
            b.pr = {}
            b.open = False
        for b in wa:
            _merge(b.w, ev)

    def op(self, e, fn, r=(), w=(), wa=(), sig=True):
        deps = self._deps(e, r, w, wa)
        self._wait(e, deps)
        ins = fn()
        self.nops += 1
        key = ("E", e, self.gen[e])
        if sig:
            self.cnt[e] += 1
            ins.then_inc(self.sem(key), 1)
            self._record(key, self.cnt[e], r, w, wa)
            if self.cnt[e] >= self.GEN:
                self.gen[e] += 1
                self.cnt[e] = 0
        else:
            self._record(key, self.cnt[e] + 1, r, w, wa)
        return ins

    def dma(self, e, out, in_, r=(), w=(), wa=(), **kw):
        i = self.dma_i[e]
        slot = i % self.NSLOT
        key = ("D", e, slot)
        val = 16 * (i // self.NSLOT + 1)
        deps = self._deps(e, r, w, wa)
        if val > 16:
            _merge(deps, {key: val - 16})
        self._wait(e, deps)
        ins = self.eng[e].dma_start(out=out, in_=in_, **kw)
        ins.then_inc(self.sem(key), 16)
        self.dma_i[e] = i + 1
        self._record(key, val, r, w, wa)
        return ins

    def coll(self, kind, out, in_, groups, r=(), w=()):
        e = "pool"
        self.ncoll = getattr(self, "ncoll", 0) + 1
        key = ("C", e, self.ncoll)
        deps = self._deps(e, r, w, ())
        self._wait(e, deps)
        ins = self.nc.gpsimd.collective_compute(kind, ALU.bypass, replica_groups=groups, ins=[in_], outs=[out])
        ins.then_inc(self.sem(key), 1)
        self.colls = getattr(self, "colls", {})
        self.colls[key] = 1
        self._record(key, 1, r, w, ())
        return ins

    def barrier(self):
        allev = {}
        for e in self.eng:
            if self.cnt[e] > 0:
                allev[("E", e, self.gen[e])] = self.cnt[e]
            elif self.gen[e] > 0:
                allev[("E", e, self.gen[e] - 1)] = self.GEN
        for e in self.eng:
            n = self.dma_i[e]
            for slot in range(min(n, self.NSLOT)):
                last = ((n - 1 - slot) // self.NSLOT) * self.NSLOT + slot
                allev[("D", e, slot)] = 16 * (last // self.NSLOT + 1)
        allev.update(getattr(self, "colls", {}))
        for e in self.eng:
            for key, val in allev.items():
                if key[0] == "E" and key[1] == e:
                    continue
                if self.waited[e].get(key, 0) >= val:
                    continue
                self.eng[e].wait_ge(self.sem(key), val)
                self.waited[e][key] = val
                self.nwaits += 1


class Ctx:
    def __init__(self):
        self.nc = bass.Bass("TRN2", target_bir_lowering=False)
        self.S = Sched(self.nc)
        self.later = []
        self.scope = None
        self.uid = 0

    def un(self, name):
        self.uid += 1
        return f"{name}_u{self.uid}"

    def din(self, name, shape, dt=F32):
        return Buf(self.nc.dram_tensor(name, list(shape), dt, kind="ExternalInput").ap(), name)

    def dout(self, name, shape, dt=F32):
        return Buf(self.nc.dram_tensor(name, list(shape), dt, kind="ExternalOutput").ap(), name)

    def dscr(self, name, shape, dt=BF16):
        return Buf(self.nc.dram_tensor(name, list(shape), dt, kind="Internal").ap(), name)

    def sb(self, name, shape, dt=F32):
        name = self.un(name)
        if self.scope is not None:
            return Buf(self.scope.enter_context(self.nc.sbuf_tensor(name, list(shape), dt)), name)
        return Buf(self.nc.alloc_sbuf_tensor(name, list(shape), dt), name)

    def act(self, out, in_, func, r, w=(), wa=(), **kw):
        nc = self.nc
        return self.S.op("act", lambda: nc.scalar.activation(out=out, in_=in_, func=func, **kw), r=r, w=w, wa=wa)

    def _ve(self, e):
        return self.nc.vector if e == "dve" else self.nc.gpsimd

    def tt(self, e, out, in0, in1, op, r, w=(), wa=()):
        eng = self._ve(e)
        return self.S.op(e, lambda: eng.tensor_tensor(out=out, in0=in0, in1=in1, op=op), r=r, w=w, wa=wa)

    def ts(self, e, out, in0, s1, s2, op0, op1, r, w=(), wa=()):
        eng = self._ve(e)
        if s2 is None:
            return self.S.op(e, lambda: eng.tensor_scalar(out=out, in0=in0, scalar1=s1, scalar2=None, op0=op0), r=r, w=w, wa=wa)
        return self.S.op(e, lambda: eng.tensor_scalar(out=out, in0=in0, scalar1=s1, scalar2=s2, op0=op0, op1=op1), r=r, w=w, wa=wa)

    def stt(self, e, out, in0, scalar, in1, op0, op1, r, w=(), wa=()):
        eng = self._ve(e)
        return self.S.op(e, lambda: eng.scalar_tensor_tensor(out=out, in0=in0, scalar=scalar, in1=in1, op0=op0, op1=op1), r=r, w=w, wa=wa)

    def cp(self, e, out, in_, r, w=(), wa=()):
        if e == "act":
            nc = self.nc
            return self.S.op("act", lambda: nc.scalar.copy(out=out, in_=in_), r=r, w=w, wa=wa)
        eng = self._ve(e)
        return self.S.op(e, lambda: eng.tensor_copy(out=out, in_=in_), r=r, w=w, wa=wa)

    def recip(self, out, in_, r, w=(), wa=()):
        nc = self.nc
        return self.S.op("dve", lambda: nc.vector.reciprocal(out=out, in_=in_), r=r, w=w, wa=wa)

    def memset(self, e, ap, val, w=(), wa=()):
        eng = self._ve(e)
        return self.S.op(e, lambda: eng.memset(ap, val), w=w, wa=wa)

    def mm(self, out, lhsT, rhs, start, stop, r, w=(), wa=(), sig=True):
        nc = self.nc
        return self.S.op("pe", lambda: nc.tensor.matmul(out, lhsT=lhsT, rhs=rhs, start=start, stop=stop), r=r, w=w, wa=wa, sig=sig)

    def tr(self, out, in_, ident, r, w=(), wa=(), sig=True):
        nc = self.nc
        return self.S.op("pe", lambda: nc.tensor.transpose(out, in_, ident), r=r, w=w, wa=wa, sig=sig)

    def dma(self, e, out, in_, r, w=(), wa=()):
        return self.S.dma(e, out, in_, r=r, w=w, wa=wa)

    def defer(self, fn):
        self.later.append(fn)

    def flush(self):
        l, self.later = self.later, []
        for fn in l:
            fn()


class Scope:
    def __init__(self, cx):
        self.cx = cx

    def __enter__(self):
        self.es = ExitStack()
        self.es.__enter__()
        self.cx.scope = self.es
        return self

    def __exit__(self, *a):
        self.cx.flush()
        self.cx.S.barrier()
        self.cx.scope = None
        return self.es.__exit__(*a)


class Phase:
    def __init__(self, cx):
        self.cx = cx
        self.es = ExitStack()

    def __enter__(self):
        self.es.__enter__()
        return self

    def __exit__(self, *a):
        self.cx.flush()
        self.cx.S.barrier()
        return self.es.__exit__(*a)

    def sb(self, name, shape, dt=F32, n=1):
        name = self.cx.un(name)
        t = self.es.enter_context(self.cx.nc.sbuf_tensor(name, list(shape), dt))
        return Buf(t, name)

    def sbs(self, name, shape, dt, n):
        return [self.sb(f"{name}{i}", shape, dt) for i in range(n)]

    def ps(self, name, shape, dt=F32):
        nb = int(np.prod(shape[1:])) * (4 if dt == F32 else 2)
        assert nb % 2048 == 0, (name, shape)
        name = self.cx.un(name)
        t = self.es.enter_context(self.cx.nc.psum_tensor(name, list(shape), dt))
        return Buf(t, name, psum=True)

    def pss(self, name, shape, dt, n):
        return [self.ps(f"{name}{i}", shape, dt) for i in range(n)]


def common_consts(cx, ident_in):
    c = {}
    c["identf"] = cx.sb("identf", [128, 128], F32)
    c["identb"] = cx.sb("identb", [128, 128], BF16)
    c["onesf"] = cx.sb("onesf", [128, 128], F32)
    c["eps"] = cx.sb("epsT", [128, 1], F32)
    cx.dma("sp", c["identf"][:, :], ident_in[:, :], r=[ident_in], w=[c["identf"]])
    cx.cp("dve", c["identb"][:, :], c["identf"][:, :], r=[c["identf"]], w=[c["identb"]])
    cx.memset("pool", c["onesf"][:, :], 1.0, w=[c["onesf"]])
    cx.memset("pool", c["eps"][:, :], EPS, w=[c["eps"]])
    return c


def modulation(cx, c, ccT_in, w_ada, b_adaT_in, norm_gT_in, want_gate_bc=(0, 1), sfx=""):
    nc = cx.nc
    modT = cx.sb("modT" + sfx, [128, 48, 2], F32)
    gs = cx.sb("gsT" + sfx, [128, 16, 2], F32)
    gate_bc = cx.sb("gate_bc" + sfx, [128, 2, D], F32) if want_gate_bc else None
    with Phase(cx) as ph:
        scT = ph.sb("scT", [128, 16, 2], F32)
        badaT = ph.sb("badaT", [128, 48], F32)
        ngT = ph.sb("ngT", [128, 16], F32)
        wst = ph.sbs("wada_st", [128, 16, 512], F32, 2)
        pm = ph.ps("pm", [128, 512], F32)
        cx.dma("sp", scT[:, :, :], ccT_in[:, :, :], r=[ccT_in], w=[scT])
        cx.dma("sp", badaT[:, :], b_adaT_in[:, :], r=[b_adaT_in], w=[badaT])
        cx.dma("sp", ngT[:, :], norm_gT_in[:, :], r=[norm_gT_in], w=[ngT])
        cx.act(scT[:, :, :], scT[:, :, :], AF.Silu, r=[scT], w=[scT])
        wv = w_ada.t.rearrange("(k p) c -> p k c", p=128)
        for cb in range(12):
            st = wst[cb % 2]
            for hh in range(2):
                cx.dma("sp", st[:, hh * 8:(hh + 1) * 8, :], wv[:, hh * 8:(hh + 1) * 8, cb * 512:(cb + 1) * 512],
                       r=[w_ada], wa=[st])
            for fc in range(4):
                cc = cb * 4 + fc
                for k in range(16):
                    cx.mm(pm[:, cc * 2:cc * 2 + 2], st[:, k, fc * 128:(fc + 1) * 128], scT[:, k, :],
                          start=(k == 0), stop=(k == 15), r=[st, scT], wa=[pm], sig=(k == 15))
        cx.tt("dve", modT[:, :, :], pm[:, 0:96].rearrange("p (c j) -> p c j", j=2),
              badaT[:, :].unsqueeze(2).to_broadcast([128, 48, 2]), ALU.add, r=[pm, badaT], w=[modT])
        cx.stt("dve", gs[:, :, :], modT[:, 16:32, :], 1.0, ngT[:, :].unsqueeze(2).to_broadcast([128, 16, 2]),
               ALU.add, ALU.mult, r=[modT, ngT], w=[gs])
    if want_gate_bc:
        gate_rows(cx, c, modT, gate_bc, want_gate_bc)
    return modT, gs, gate_bc


def gate_rows(cx, c, modT, gate_bc, js):
    with Phase(cx) as ph:
        dgs = ph.sbs("dgate", [128, 128], F32, 2)
        pgs = ph.pss("pgate", [128, 512], F32, 2)
        i = 0
        for j in js:
            for k in range(16):
                dg = dgs[i % 2]
                pg = pgs[i % 2]
                cx.ts("dve", dg[:, :], c["identf"][:, :], modT[:, 32 + k, j:j + 1], None, ALU.mult, None,
                      r=[c["identf"], modT], w=[dg])
                cx.mm(pg[:, 0:128], c["onesf"][:, :], dg[:, :], True, True, r=[c["onesf"], dg], w=[pg])
                cx.cp("act", gate_bc[:, j, k * 128:(k + 1) * 128], pg[:, 0:128], r=[pg], wa=[gate_bc])
                i += 1


def build_hT(cx, ph_bufs, c, modT, gs, hT, hTb, tiles):
    xts, xns, sqj, sss, pTs = ph_bufs
    for i, (src, r0, j) in enumerate(tiles):
        xt = xts[i % 2]
        xn = xns[i % 2]
        ss = sss[i % 2]
        cx.dma("sp", xt[:, :], src[r0:r0 + 128, :], r=[src], w=[xt])
        if HT_DBG == 1:
            continue
        cx.memset("pool", ss[:, :], 0.0, w=[ss])
        cx.act(sqj[:, :], xt[:, :], AF.Square, r=[xt, ss], w=[sqj, ss], accum_out=ss[:, 0:1])
        cx.act(ss[:, 1:2], ss[:, 0:1], AF.Sqrt, r=[ss, c["eps"]], w=[ss], bias=c["eps"][:, 0:1], scale=1.0 / D)
        cx.recip(ss[:, 2:3], ss[:, 1:2], r=[ss], w=[ss])
        if HT_DBG == 2:
            continue
        cx.ts("dve", xn[:, :], xt[:, :], ss[:, 2:3], None, ALU.mult, None, r=[xt, ss], w=[xn])
        if HT_DBG == 3:
            continue
        for half in range(2):
            pT = pTs[half]
            for kk in range(8):
                k = half * 8 + kk
                cx.tr(pT[:, kk, :], xn[:, k * 128:(k + 1) * 128], c["identb"][:, :], r=[xn, c["identb"]],
                      wa=[pT], sig=(kk == 7))
            if HT_DBG == 4:
                continue
            for kk in range(8):
                k = half * 8 + kk
                dst = hT[:, k, i * 128:(i + 1) * 128]
                if half == 0:
                    cx.act(dst, pT[:, kk, :], AF.Identity, r=[pT, gs, modT], wa=[hTb[i]],
                           scale=gs[:, k, j:j + 1], bias=modT[:, k, j:j + 1])
                else:
                    cx.ts("dve", dst, pT[:, kk, :], gs[:, k, j:j + 1], modT[:, k, j:j + 1], ALU.mult, ALU.add,
                          r=[pT, gs, modT], wa=[hTb[i]])


class WStream:
    def __init__(self, cx, ph, name="w"):
        self.cx = cx
        self.st = ph.sbs(name + "_st", [128, 4, 512], F32, 2)
        self.wb = ph.sbs(name + "_bf", [128, 16, 512], BF16, 2)
        self.n = 0
        self.si = 0

    def fetch(self, w, c0):
        cx = self.cx
        wb = self.wb[self.n % 2]
        self.n += 1
        wv = w.t.rearrange("(k p) c -> p k c", p=128)
        for q in range(4):
            st = self.st[self.si % 2]
            self.si += 1
            cx.dma("sp", st[:, :, :], wv[:, q * 4:(q + 1) * 4, c0:c0 + 512], r=[w], w=[st])
            cx.cp("pool", wb[:, q * 4:(q + 1) * 4, :], st[:, :, :], r=[st], wa=[wb])
        return wb


LAM_INIT0 = 0.8 - 0.6 * math.exp(-0.3 * 0)


HT_DBG = 0
STOPF = {1.3: ("q",), 1.4: ("v",), 1.5: ("za",), 1.6: ("gg", "gv")}


class _Stop(Exception):
    pass


def build_l0(stop=99):
    cx = Ctx()
    try:
        _build_l0(cx, stop)
    except _Stop:
        cx.S.barrier()
    return cx


def _build_l0(cx, stop=99, fused=False, c=None):
    nc = cx.nc
    x_own = cx.din("x_own", [HALF, D])
    x_oth = cx.din("x_oth", [HALF, D])
    ctx_in = cx.din("ctx", [CTX, D])
    ccT_in = cx.din("ccT", [128, 16, 2])
    w_ada = cx.din("w_ada", [D, 3 * D])
    b_adaT_in = cx.din("b_adaT", [128, 48])
    norm_gT_in = cx.din("norm_gT", [128, 16])
    w_in = cx.din("w_in", [D, 7168])
    w_out = cx.din("w_out", [D, D])
    qkg_in = cx.din("qkg", [2, 64])
    lamv_in = cx.din("lamv", [4, 64])
    subg_in = cx.din("subg", [128, 1])
    cwT_in = cx.din("cwT", [128, 8, 31])
    cvp_in = cx.din("cvp", [128, 3, 8])
    rq_own = cx.din("rq_own", [HALF, 128])
    rk_own = cx.din("rk_own", [HALF, 128])
    rk_oth = cx.din("rk_oth", [HALF, 128])
    hmask_in = cx.din("hmask", [128, 2])
    if fused:
        x1_out = cx.dscr("x1_own_s", [HALF, D], F32)
        ctx1_out = cx.dscr("ctx1_s", [CTX, D], F32)
    else:
        ident_in = cx.din("ident", [128, 128])
        x1_out = cx.dout("x1", [HALF, D])
        ctx1_out = cx.dout("ctx1", [CTX, D])
    qT_s = cx.dscr("qT_s", [8, 128, HALF + CTX])
    kT_s = cx.dscr("kT_s", [8, 128, NKEY])
    v_s = cx.dscr("v_s", [NKEY, 1024])
    zaT_s = cx.dscr("zaT_s", [1024, HALF + CTX])
    zbT_s = cx.dscr("zbT_s", [1024, HALF + CTX])
    yT_s = cx.dscr("yT_s", [1024, HALF + 30])
    yTc_s = cx.dscr("yTc_s", [1024, CTX + 30])
    catT_s = cx.dscr("catT_s", [D, HALF + CTX])

    if c is None:
        c = common_consts(cx, ident_in)
    modT, gs, gate_bc = modulation(cx, c, ccT_in, w_ada, b_adaT_in, norm_gT_in)

    qg = cx.sb("qg", [128, 64]); qgs = cx.sb("qgs", [128, 64]); kg = cx.sb("kg", [128, 64])
    lamt = cx.sb("lamt", [128, 4, 64]); lam4 = cx.sb("lam4", [128, 8])
    subg = cx.sb("subg_t", [128, 1])
    hmask = cx.sb("hmask_t", [128, 2])
    cvp = cx.sb("cvp_t", [128, 3, 8])
    cwT = cx.sb("cwT_t", [128, 8, 31])
    cx.dma("sp", qg[:, :], qkg_in[0, :].partition_broadcast(128), r=[qkg_in], w=[qg])
    cx.dma("sp", kg[:, :], qkg_in[1, :].partition_broadcast(128), r=[qkg_in], w=[kg])
    cx.dma("sp", lamt[:, :, :].rearrange("p a d -> p (a d)"),
           lamv_in.t.rearrange("a d -> (a d)").partition_broadcast(128), r=[lamv_in], w=[lamt])
    cx.dma("sp", subg[:, :], subg_in[:, :], r=[subg_in], w=[subg])
    cx.dma("sp", hmask[:, :], hmask_in[:, :], r=[hmask_in], w=[hmask])
    cx.dma("sp", cvp[:, :, :], cvp_in[:, :, :], r=[cvp_in], w=[cvp])
    cx.dma("sp", cwT[:, :, :], cwT_in[:, :, :], r=[cwT_in], w=[cwT])
    cx.ts("dve", qgs[:, :], qg[:, :], 0.125, None, ALU.mult, None, r=[qg], w=[qgs])
    cx.tt("dve", lamt[:, 0, :], lamt[:, 0, :], lamt[:, 1, :], ALU.mult, r=[lamt], w=[lamt])
    cx.tt("dve", lamt[:, 2, :], lamt[:, 2, :], lamt[:, 3, :], ALU.mult, r=[lamt], w=[lamt])
    cx.S.op("dve", lambda: nc.vector.reduce_sum(out=lam4[:, 0:1], in_=lamt[:, 0, :], axis=AX.X), r=[lamt], w=[lam4])
    cx.S.op("dve", lambda: nc.vector.reduce_sum(out=lam4[:, 1:2], in_=lamt[:, 2, :], axis=AX.X), r=[lamt, lam4], w=[lam4])
    cx.act(lam4[:, 2:4], lam4[:, 0:2], AF.Exp, r=[lam4], w=[lam4])
    cx.stt("dve", lam4[:, 4:5], lam4[:, 3:4], -LAM_INIT0, lam4[:, 2:3], ALU.add, ALU.subtract, r=[lam4], w=[lam4])
    neglam = lam4[:, 4:5]
    cx.ts("dve", subg[:, :], subg[:, :], 1.0 - LAM_INIT0, None, ALU.mult, None, r=[subg], w=[subg])

    if stop < 1:
        cx.S.barrier()
        return cx
    with Phase(cx) as ph:
        hT_t = ph.sb("hT", [128, 16, 1280], BF16)
        hTb = [Buf(hT_t.t, f"hT{i}") for i in range(10)]
        hT = hT_t
        hbufs = (ph.sbs("xt", [128, D], F32, 2), ph.sbs("xn", [128, D], BF16, 2), ph.sb("sqj", [128, D], BF16),
                 ph.sbs("ss", [128, 4], F32, 2), ph.pss("pT", [128, 8, 128], BF16, 2))
        ws = WStream(cx, ph)
        pacc = ph.pss("pacc", [128, 512], F32, 3)
        pq = ph.pss("pq", [128, 8, 128], BF16, 2)
        sqs = ph.sbs("sq", [128, 512], F32, 2)
        st8 = ph.sbs("st8", [128, 16], F32, 2)
        xnq = ph.sbs("xnq", [128, 512], F32, 2)
        t1s = ph.sbs("t1", [128, 512], F32, 2)
        t2s = ph.sbs("t2", [128, 512], F32, 2)
        qbs = ph.sbs("qb", [128, 512], BF16, 2)
        qTs = ph.sbs("qTs", [128, 4, 128], BF16, 2)
        rts = ph.sbs("rt", [128, 128], F32, 2)
        vbs = ph.sbs("vb", [128, 512], BF16, 2)
        fos = ph.sbs("fo", [128, 512], BF16, 2)
        sig_t = ph.sb("sig", [128, 4, 1280], BF16)
        sigb = [Buf(sig_t.t, f"sig{i}") for i in range(4)]
        zero_t = ph.sb("zero", [128, 15], BF16)
        cnt = {"acc": 0, "qk": 0, "v": 0, "fo": 0}

        cx.memset("pool", zero_t[:, :], 0.0, w=[zero_t])
        yTc_v = yTc_s.t.rearrange("(j p) c -> p j c", p=128)
        for j in range(8):
            cx.dma("sp", yTc_v[:, j, 0:15], zero_t[:, :], r=[zero_t], wa=[yTc_s])
            cx.dma("sp", yTc_v[:, j, CTX + 15:CTX + 30], zero_t[:, :], r=[zero_t], wa=[yTc_s])
        if stop == 1.1:
            raise _Stop

        def qk_epi(ps, tile, fam, h0):
            kind, idx = tile
            i = cnt["qk"]; cnt["qk"] += 1
            sq, s8, xq, t1, t2, qb, qTt, rt, pqt = sqs[i % 2], st8[i % 2], xnq[i % 2], t1s[i % 2], t2s[i % 2], qbs[i % 2], qTs[i % 2], rts[i % 2], pq[i % 2]
            cx.act(sq[:, :], ps[:, :], AF.Square, r=[ps], w=[sq])
            cx.S.op("dve", lambda: nc.vector.reduce_sum(out=s8[:, 0:8], in_=sq[:, :].rearrange("p (g d) -> p g d", d=64), axis=AX.X), r=[sq], w=[s8])
            cx.act(s8[:, 8:16], s8[:, 0:8], AF.Sqrt, r=[s8, c["eps"]], w=[s8], bias=c["eps"][:, 0:1], scale=1.0 / 64)
            cx.recip(s8[:, 0:8], s8[:, 8:16], r=[s8], w=[s8])
            v3 = lambda a: a.rearrange("p (g d) -> p g d", d=64)
            cx.tt("dve", v3(xq[:, :]), v3(ps[:, :]), s8[:, 0:8].unsqueeze(2).to_broadcast([128, 8, 64]), ALU.mult, r=[ps, s8], w=[xq])
            g = kg if fam == "k" else (qgs if kind == "ctx" else qg)
            cx.tt("pool", v3(xq[:, :]), v3(xq[:, :]), g[:, :].unsqueeze(1).to_broadcast([128, 8, 64]), ALU.mult, r=[xq, g], w=[xq])
            if kind == "ctx":
                cx.cp("act", qb[:, :], xq[:, :], r=[xq], w=[qb])
            else:
                rsrc = (rq_own if fam == "q" else rk_own) if kind == "own" else rk_oth
                cx.dma("sp", rt[:, :], rsrc[idx * 128:(idx + 1) * 128, :], r=[rsrc], w=[rt])
                cx.tt("pool", v3(t1[:, :]), v3(xq[:, :]), rt[:, 0:64].unsqueeze(1).to_broadcast([128, 8, 64]), ALU.mult, r=[xq, rt], w=[t1])
                for a in range(2):
                    lo, hi = a * 32, a * 32 + 16
                    e = "dve" if a == 0 else "pool"
                    cx.tt(e, v3(t2[:, :])[:, :, lo:lo + 16], v3(xq[:, :])[:, :, hi:hi + 16],
                          rt[:, 64 + lo:64 + lo + 16].unsqueeze(1).to_broadcast([128, 8, 16]), ALU.mult, r=[xq, rt], wa=[t2])
                    cx.tt(e, v3(t2[:, :])[:, :, hi:hi + 16], v3(xq[:, :])[:, :, lo:lo + 16],
                          rt[:, 64 + hi:64 + hi + 16].unsqueeze(1).to_broadcast([128, 8, 16]), ALU.mult, r=[xq, rt], wa=[t2])
                cx.tt("dve", qb[:, :], t1[:, :], t2[:, :], ALU.add, r=[t1, t2], w=[qb])
            if fam == "q":
                dst_s = qT_s
                c0 = idx * 128 if kind == "own" else HALF + idx * 128
            else:
                dst_s = kT_s
                c0 = {"ctx": 0, "own": CTX, "oth": CTX + HALF}[kind] + idx * 128

            def fin():
                for hh in range(4):
                    cx.tr(pqt[:, hh, :], qb[:, hh * 128:(hh + 1) * 128], c["identb"][:, :], r=[qb, c["identb"]], wa=[pqt], sig=(hh == 3))
                cx.cp("dve", qTt[:, :, :], pqt[:, 0:4, :], r=[pqt], w=[qTt])
                cx.dma(STQ, dst_s.t[h0:h0 + 4, :, c0:c0 + 128].rearrange("h p t -> p h t"), qTt[:, :, :], r=[qTt], wa=[dst_s])
            cx.defer(fin)

        def v_epi(ps, tile, cb2):
            kind, idx = tile
            i = cnt["v"]; cnt["v"] += 1
            vb = vbs[i % 2]
            r0 = {"ctx": 0, "own": CTX, "oth": CTX + HALF}[kind] + idx * 128
            cx.cp("act", vb[:, :], ps[:, :], r=[ps], w=[vb])
            cx.defer(lambda: cx.dma(STQ, v_s[r0:r0 + 128, cb2 * 512:(cb2 + 1) * 512], vb[:, :], r=[vb], wa=[v_s]))

        own = lambda a, b: [("own", i) for i in range(a, b)]
        blocks = [
            dict(tiles=own(0, 8) + [("ctx", 0), ("ctx", 1)], full=10, fams="all",
                 chunks=[(0, 512, "own", 0), (512, 512, "own", 512), (1024, 256, "ctx", 0)]),
            dict(tiles=own(8, 16) + [("oth", 0), ("oth", 15)], full=8, fams="all",
                 chunks=[(0, 512, "own", 1024), (512, 512, "own", 1536), (1024, 256, "halo", 0)]),
            dict(tiles=[("oth", i) for i in range(1, 8)], full=0, fams="kv", chunks=[]),
            dict(tiles=[("oth", i) for i in range(8, 15)], full=0, fams="kv", chunks=[]),
        ]
        srcmap = {"own": (x_own, 0), "oth": (x_oth, 0), "ctx": (ctx_in, 1)}
        colblocks = [("q", 0, 0), ("q", 512, 1), ("k", 1024, 0), ("k", 1536, 1), ("v", 2048, 0), ("v", 2560, 1),
                     ("za", 3072, 0), ("za", 3584, 1), ("gg", 5120, 0), ("gv", 4096, 0), ("gg", 5632, 1), ("gv", 4608, 1),
                     ("zb", 6144, 0), ("zb", 6656, 1)]
        for blk in blocks:
            tiles = blk["tiles"]
            build_hT(cx, hbufs, c, modT, gs, hT, hTb, [(srcmap[k][0], i * 128, srcmap[k][1]) for (k, i) in tiles])
            if stop == 1.2:
                raise _Stop
            cbl = colblocks if blk["fams"] == "all" else [cb for cb in colblocks if cb[0] in ("k", "v")]
            if stop in STOPF:
                cbl = [cb for cb in cbl if cb[0] in STOPF[stop]]
            wnext = ws.fetch(w_in, cbl[0][1])
            for ci, (fam, c0, sub) in enumerate(cbl):
                wb = wnext
                if ci + 1 < len(cbl):
                    wnext = ws.fetch(w_in, cbl[ci + 1][1])
                if fam in ("q", "k", "v"):
                    for ti, tile in enumerate(tiles):
                        if tile[0] == "oth" and fam == "q":
                            continue
                        ps = pacc[cnt["acc"] % 3]; cnt["acc"] += 1
                        for k in range(16):
                            cx.mm(ps[:, :], hT[:, k, ti * 128:(ti + 1) * 128], wb[:, k, :], start=(k == 0), stop=(k == 15),
                                  r=[hTb[ti], wb], wa=[ps], sig=(k == 15))
                        pend, cx.later = cx.later, []
                        if fam == "v":
                            v_epi(ps, tile, sub)
                        else:
                            qk_epi(ps, tile, fam, sub * 4)
                        for fn in pend:
                            fn()
                    cx.flush()
                else:
                    for (h0c, n, ckind, d0) in blk["chunks"]:
                        if ckind == "halo" and fam not in ("gg", "gv"):
                            continue
                        t0 = h0c // 128
                        rb = [hTb[t] for t in range(t0, t0 + (n + 127) // 128)]
                        for fc in range(4):
                            ps = pacc[cnt["acc"] % 3]; cnt["acc"] += 1
                            for k in range(16):
                                cx.mm(ps[:, 0:n], wb[:, k, fc * 128:(fc + 1) * 128], hT[:, k, h0c:h0c + n], start=(k == 0), stop=(k == 15),
                                      r=rb + [wb], wa=[ps], sig=(k == 15))
                            frow = sub * 512 + fc * 128
                            dcol = d0 if ckind == "own" else HALF + d0
                            if fam in ("za", "zb"):
                                fo = fos[cnt["fo"] % 2]; cnt["fo"] += 1
                                dst = zaT_s if fam == "za" else zbT_s
                                cx.act(fo[:, 0:n], ps[:, 0:n], AF.Silu, r=[ps], w=[fo])
                                cx.dma(STQ, dst[frow:frow + 128, dcol:dcol + n], fo[:, 0:n], r=[fo], wa=[dst])
                            elif fam == "gg":
                                cx.act(sig_t[:, fc, h0c:h0c + n], ps[:, 0:n], AF.Sigmoid, r=[ps], wa=[sigb[fc]])
                            else:
                                fo = fos[cnt["fo"] % 2]; cnt["fo"] += 1
                                cx.tt("dve", fo[:, 0:n], ps[:, 0:n], sig_t[:, fc, h0c:h0c + n], ALU.mult, r=[ps, sigb[fc]], w=[fo])
                                if ckind == "own":
                                    cx.dma(STQ, yT_s[frow:frow + 128, 15 + d0:15 + d0 + n], fo[:, 0:n], r=[fo], wa=[yT_s])
                                elif ckind == "ctx":
                                    cx.dma(STQ, yTc_s[frow:frow + 128, 15 + d0:15 + d0 + n], fo[:, 0:n], r=[fo], wa=[yTc_s])
                                else:
                                    cx.ts("dve", fo[:, 0:15], fo[:, 0:15], hmask[:, 1:2], None, ALU.mult, None, r=[fo, hmask], w=[fo])
                                    cx.ts("dve", fo[:, 241:256], fo[:, 241:256], hmask[:, 0:1], None, ALU.mult, None, r=[fo, hmask], w=[fo])
                                    cx.dma(STQ, yT_s[frow:frow + 128, 15 + HALF:30 + HALF], fo[:, 0:15], r=[fo], wa=[yT_s])
                                    cx.dma(STQ, yT_s[frow:frow + 128, 0:15], fo[:, 241:256], r=[fo], wa=[yT_s])
            cx.flush()
            if stop in STOPF:
                raise _Stop

    if stop < 2:
        return cx
    with Phase(cx) as ph:
        kTs = ph.sbs("kTh", [128, NKEY], BF16, 2)
        vhs = ph.sbs("vh", [128, 34, 128], BF16, 2)
        qhs = ph.sbs("qh", [128, HALF + CTX], BF16, 2)
        pST = ph.pss("pST", [128, 2, 512], F32, 2)
        po = ph.ps("po", [128, 2, 512], F32)
        psm = ph.ps("psm", [128, 2, 512], F32)
        pts = ph.sbs("pt", [128, 2, 512], BF16, 3)
        accs = ph.sbs("accP", [128, 2, 512], F32, 2)
        accm = [[Buf(a.t, a.name + "m0"), Buf(a.t, a.name + "m1")] for a in accs]
        rs = ph.sb("rs", [128, 2, 512], F32)
        ta = ph.sb("ta", [128, 512], F32); tb = ph.sb("tb", [128, 512], F32)
        ot = ph.sb("ot", [128, 512], F32); sqo = ph.sb("sqo", [128, 512], F32)
        rstd = ph.sb("rstdo", [128, 512], F32)
        zat = ph.sbs("zat", [128, 512], BF16, 2)
        obs = ph.sbs("ob", [128, 512], BF16, 2)
        v_v = v_s.t.rearrange("(kt p) c -> p kt c", p=128)
        step = 0
        ci = 0
        for h in range(8):
            kTh, vh, qh = kTs[h % 2], vhs[h % 2], qhs[h % 2]
            cx.dma("sp", kTh[:, :], kT_s[h, :, :], r=[kT_s], w=[kTh])
            cx.dma("sp", vh[:, :, :], v_v[:, :, h * 128:(h + 1) * 128], r=[v_s], w=[vh])
            cx.dma("sp", qh[:, :], qT_s[h, :, :], r=[qT_s], w=[qh])
            for (q0, n, nkt) in [(0, 512, 34), (512, 512, 34), (1024, 512, 34), (1536, 512, 34), (HALF, 256, 2)]:
                acc = accs[ci % 2]
                za = zat[ci % 2]
                cx.dma("sp", za[:, 0:n], zaT_s[h * 128:(h + 1) * 128, q0:q0 + n], r=[zaT_s], w=[za])
                for kt in range(nkt):
                    st = pST[step % 2]
                    pt = pts[step % 3]
                    step += 1
                    for m in range(2):
                        cx.mm(st[:, m, 0:n], kTh[64 * m:64 * m + 64, kt * 128:(kt + 1) * 128], qh[64 * m:64 * m + 64, q0:q0 + n],
                              True, True, r=[kTh, qh], wa=[st], sig=(m == 1))
                    cx.act(pt[:, :, 0:n], st[:, :, 0:n], AF.Exp, r=[st], w=[pt])
                    for m in range(2):
                        cx.mm(po[:, m, 0:n], vh[:, kt, :], pt[:, m, 0:n], start=(kt == 0), stop=(kt == nkt - 1),
                              r=[vh, pt], wa=[po], sig=(m == 1))
                    for m, e in ((0, "dve"), (1, "pool")):
                        am = accm[ci % 2][m]
                        if kt == 0:
                            cx.cp(e, acc[:, m, 0:n], pt[:, m, 0:n], r=[pt], w=[am])
                        else:
                            cx.tt(e, acc[:, m, 0:n], acc[:, m, 0:n], pt[:, m, 0:n], ALU.add, r=[pt, am], w=[am])
                for m in range(2):
                    cx.mm(psm[:, m, 0:n], c["onesf"][:, :], acc[:, m, 0:n], True, True, r=[c["onesf"], accm[ci % 2][m]], wa=[psm], sig=(m == 1))
                cx.recip(rs[:, :, 0:n], psm[:, :, 0:n], r=[psm], w=[rs])
                cx.tt("dve", ta[:, 0:n], po[:, 0, 0:n], rs[:, 0, 0:n], ALU.mult, r=[po, rs], w=[ta])
                cx.tt("dve", tb[:, 0:n], po[:, 1, 0:n], rs[:, 1, 0:n], ALU.mult, r=[po, rs], w=[tb])
                cx.stt("dve", ot[:, 0:n], tb[:, 0:n], neglam, ta[:, 0:n], ALU.mult, ALU.add, r=[ta, tb, lam4], w=[ot])
                cx.act(sqo[:, 0:n], ot[:, 0:n], AF.Square, r=[ot], w=[sqo])
                cx.mm(psm[:, 0, 0:n], c["onesf"][:, :], sqo[:, 0:n], True, True, r=[c["onesf"], sqo], w=[psm])
                cx.act(rstd[:, 0:n], psm[:, 0, 0:n], AF.Sqrt, r=[psm, c["eps"]], w=[rstd], bias=c["eps"][:, 0:1], scale=1.0 / 128)
                cx.recip(rstd[:, 0:n], rstd[:, 0:n], r=[rstd], w=[rstd])
                cx.tt("dve", ot[:, 0:n], ot[:, 0:n], rstd[:, 0:n], ALU.mult, r=[ot, rstd], w=[ot])
                ob = obs[ci % 2]
                cx.stt("dve", ob[:, 0:n], ot[:, 0:n], subg[:, 0:1], za[:, 0:n], ALU.mult, ALU.mult, r=[ot, subg, za], w=[ob])
                cx.dma(STQ, catT_s[h * 128:(h + 1) * 128, q0:q0 + n], ob[:, 0:n], r=[ob], wa=[catT_s])
                ci += 1

    if stop < 3:
        return cx
    with Phase(cx) as ph:
        dg = ph.sb("dgc", [128, 8, 31, 128], BF16)
        ycs = ph.sbs("yc", [128, 8, 542], BF16, 2)
        convT = ph.sb("convT", [128, 8, 512], F32)
        sqT = ph.sb("sqT", [128, 8, 512], F32)
        onesc = ph.sb("onesc", [128, 128], F32)
        pconv = ph.pss("pconv", [128, 512], F32, 2)
        pst = ph.ps("pstat", [128, 2, 512], F32)
        mean = ph.sb("mean", [128, 512], F32); var = ph.sb("var", [128, 512], F32)
        tcs = ph.sbs("tc", [128, 512], F32, 2)
        scs = ph.sbs("sc", [128, 512], F32, 2)
        zbt = ph.sbs("zbt", [128, 512], BF16, 2)
        obs = ph.sbs("obc", [128, 512], BF16, 2)
        cx.memset("pool", onesc[:, :], 1.0 / 1024, w=[onesc])
        i = 0
        for j in range(8):
            for k in range(31):
                e = "dve" if i % 2 == 0 else "pool"
                cx.ts(e, dg[:, j, k, :], c["identf"][:, :], cwT[:, j, k:k + 1], None, ALU.mult, None, r=[c["identf"], cwT], wa=[dg])
                i += 1
        yT_v = yT_s.t.rearrange("(j p) c -> p j c", p=128)
        it = 0
        for (src, srcb, c0, n, dcol) in [(yT_v, yT_s, 0, 512, 0), (yT_v, yT_s, 512, 512, 512), (yT_v, yT_s, 1024, 512, 1024),
                                         (yT_v, yT_s, 1536, 512, 1536), (yTc_v, yTc_s, 0, 256, HALF)]:
            yc = ycs[it % 2]; it += 1
            cx.dma("sp", yc[:, :, 0:n + 30], src[:, :, c0:c0 + n + 30], r=[srcb], w=[yc])
            for j in range(8):
                pc = pconv[j % 2]
                for k in range(31):
                    cx.mm(pc[:, 0:n], dg[:, j, k, :], yc[:, j, k:k + n], start=(k == 0), stop=(k == 30), r=[dg, yc], wa=[pc], sig=(k == 30))
                cx.act(convT[:, j, 0:n], pc[:, 0:n], AF.Identity, r=[pc, cvp], wa=[convT], bias=cvp[:, 0, j:j + 1], scale=1.0)
                cx.act(sqT[:, j, 0:n], convT[:, j, 0:n], AF.Square, r=[convT], wa=[sqT])
            for j in range(8):
                cx.mm(pst[:, 0, 0:n], onesc[:, :], convT[:, j, 0:n], start=(j == 0), stop=(j == 7), r=[onesc, convT], wa=[pst], sig=(j == 7))
            for j in range(8):
                cx.mm(pst[:, 1, 0:n], onesc[:, :], sqT[:, j, 0:n], start=(j == 0), stop=(j == 7), r=[onesc, sqT], wa=[pst], sig=(j == 7))
            cx.cp("act", mean[:, 0:n], pst[:, 0, 0:n], r=[pst], w=[mean])
            cx.tt("dve", var[:, 0:n], mean[:, 0:n], mean[:, 0:n], ALU.mult, r=[mean], w=[var])
            cx.tt("dve", var[:, 0:n], pst[:, 1, 0:n], var[:, 0:n], ALU.subtract, r=[pst, var], w=[var])
            cx.ts("dve", var[:, 0:n], var[:, 0:n], 0.0, None, ALU.max, None, r=[var], w=[var])
            cx.act(var[:, 0:n], var[:, 0:n], AF.Sqrt, r=[var, c["eps"]], w=[var], bias=c["eps"][:, 0:1], scale=1.0)
            cx.recip(var[:, 0:n], var[:, 0:n], r=[var], w=[var])
            for j in range(8):
                tcb, scb, zb, ob = tcs[j % 2], scs[j % 2], zbt[j % 2], obs[j % 2]
                cx.dma("sp", zb[:, 0:n], zbT_s[j * 128:(j + 1) * 128, dcol:dcol + n], r=[zbT_s], w=[zb])
                cx.tt("dve", tcb[:, 0:n], convT[:, j, 0:n], mean[:, 0:n], ALU.subtract, r=[convT, mean], w=[tcb])
                cx.tt("pool", tcb[:, 0:n], tcb[:, 0:n], var[:, 0:n], ALU.mult, r=[tcb, var], w=[tcb])
                cx.act(scb[:, 0:n], tcb[:, 0:n], AF.Silu, r=[tcb, cvp], w=[scb], scale=cvp[:, 1, j:j + 1], bias=cvp[:, 2, j:j + 1])
                cx.tt("dve", ob[:, 0:n], scb[:, 0:n], zb[:, 0:n], ALU.mult, r=[scb, zb], w=[ob])
                cx.dma(STQ, catT_s[1024 + j * 128:1024 + (j + 1) * 128, dcol:dcol + n], ob[:, 0:n], r=[ob], wa=[catT_s])

    if stop < 4:
        return cx
    out_proj(cx, w_out, catT_s, gate_bc,
             [(x_own, x1_out, t * 128, t * 128, 0) for t in range(16)] + [(ctx_in, ctx1_out, t * 128, HALF + t * 128, 1) for t in range(2)])
    return x1_out, ctx1_out


def out_proj(cx, w_out, catT_s, gate_bc, tiles, blend=None, loader=None):
    with Phase(cx) as ph:
        wo = ph.sb("wo", [128, 16, D], BF16)
        wst = ph.sbs("wo_st", [128, 2, D], F32, 2)
        cats = ph.sbs("catt", [128, 16, 128], BF16, 2)
        catab = (ph.sbs("catA", [128, 16, 128], BF16, 2), ph.sbs("catB", [128, 16, 128], BF16, 2)) if blend is not None else None
        xts = ph.sbs("xto", [128, D], F32, 2)
        xos = ph.sbs("xoo", [128, D], F32, 2)
        tmp = ph.sbs("tmpo", [128, 512], F32, 2)
        pacc = ph.pss("pacco", [128, 512], F32, 3)
        wv = w_out.t.rearrange("(k p) c -> p k c", p=128)
        for q in range(8):
            st = wst[q % 2]
            cx.dma("sp", st[:, :, :], wv[:, q * 2:(q + 1) * 2, :], r=[w_out], w=[st])
            cx.cp("pool", wo[:, q * 2:(q + 1) * 2, :], st[:, :, :], r=[st], wa=[wo])
        cat_v = catT_s.t.rearrange("(k p) c -> p k c", p=128) if catT_s is not None else None
        n = 0
        for ti, (xsrc, xdst, r0, c0, j) in enumerate(tiles):
            cat, xt, xo = cats[ti % 2], xts[ti % 2], xos[ti % 2]
            if loader is None:
                loader = lambda dst, col: cx.dma("sp", dst[:, :, :], cat_v[:, :, col:col + 128], r=[catT_s], w=[dst])
            if blend is None:
                loader(cat, c0)
            else:
                ca, cb_ = catab[0][ti % 2], catab[1][ti % 2]
                loader(ca, c0)
                loader(cb_, HALF + c0)
                cx.ts("pool", ca[:, :, :], ca[:, :, :], blend[:, 0:1], None, ALU.mult, None, r=[ca, blend], w=[ca])
                cx.stt("dve", cat[:, :, :], cb_[:, :, :], blend[:, 1:2], ca[:, :, :], ALU.mult, ALU.add, r=[ca, cb_, blend], w=[cat])
            cx.dma("sp", xt[:, :], xsrc[r0:r0 + 128, :], r=[xsrc], w=[xt])
            for cb in range(4):
                ps = pacc[n % 3]
                tm = tmp[n % 2]
                n += 1
                for k in range(16):
                    cx.mm(ps[:, :], cat[:, k, :], wo[:, k, cb * 512:(cb + 1) * 512], start=(k == 0), stop=(k == 15), r=[cat, wo], wa=[ps], sig=(k == 15))
                cx.tt("dve", tm[:, :], ps[:, :], gate_bc[:, j, cb * 512:(cb + 1) * 512], ALU.mult, r=[ps, gate_bc], w=[tm])
                cx.tt("pool", xo[:, cb * 512:(cb + 1) * 512], tm[:, :], xt[:, cb * 512:(cb + 1) * 512], ALU.add, r=[tm, xt], wa=[xo])
            cx.dma(STQ, xdst[r0:r0 + 128, :], xo[:, :], r=[xo], wa=[xdst])


T1 = CTX + SEQ
NT1 = T1 // 128
NCH = T1 // 64


def build_l1a(stop=99):
    cx = Ctx()
    _build_l1a(cx, stop)
    return cx


def _build_l1a(cx, stop=99, fused=False, c=None, x1_tiles=None, ctx1_in=None):
    nc = cx.nc
    sfx = "1" if fused else ""
    if not fused:
        x1_in = cx.din("x1f", [SEQ, D])
        ctx1_in = cx.din("ctx1", [CTX, D])
        x1_tiles = [(x1_in, t * 128) for t in range(32)]
    ccT_in = cx.din("ccT" + sfx, [128, 16, 2])
    w_ada = cx.din("w_ada" + sfx, [D, 3 * D])
    b_adaT_in = cx.din("b_adaT" + sfx, [128, 48])
    norm_gT_in = cx.din("norm_gT" + sfx, [128, 16])
    w_c = cx.din("w_c", [D, 5120])
    lbg_in = cx.din("lbg", [128, 2, 2, 8])
    ong_in = cx.din("ong", [128, 1])
    maskF_in = cx.din("maskF", [128, 128])
    maskB_in = cx.din("maskB", [128, 128])
    if fused:
        catT1_out = cx.dscr("catT1_s", [1024, SEQ], BF16)
    else:
        ident_in = cx.din("ident", [128, 128])
        catT1_out = cx.dout("catT1", [1024, SEQ], BF16)
        modT_out = cx.dout("modT1", [128, 48, 2], F32)
    qS = cx.dscr("qS", [8, 128, T1], F32)
    sgS = cx.dscr("sgS", [2, 8, 128, T1], F32)
    vS = cx.dscr("vS", [8, 128, NT1, 128], BF16)
    zS = cx.dscr("zS", [8, 128, T1], BF16)

    if c is None:
        c = common_consts(cx, ident_in)
    modT, gs, _ = modulation(cx, c, ccT_in, w_ada, b_adaT_in, norm_gT_in, want_gate_bc=(), sfx="1")
    if not fused:
        cx.dma("sp", modT_out[:, :, :], modT[:, :, :], r=[modT], w=[modT_out])
    lbt = cx.sb("lbt", [128, 2, 2, 8]); lb = cx.sb("lb", [128, 2, 8]); oml = cx.sb("oml", [128, 2, 8]); noml = cx.sb("noml", [128, 2, 8])
    ong = cx.sb("ong_t", [128, 1]); maskF = cx.sb("maskF_t", [128, 128]); maskB = cx.sb("maskB_t", [128, 128])
    cx.dma("sp", lbt[:, :, :, :], lbg_in[:, :, :, :], r=[lbg_in], w=[lbt])
    cx.dma("sp", ong[:, :], ong_in[:, :], r=[ong_in], w=[ong])
    cx.dma("sp", maskF[:, :], maskF_in[:, :], r=[maskF_in], w=[maskF])
    cx.dma("sp", maskB[:, :], maskB_in[:, :], r=[maskB_in], w=[maskB])
    cx.tt("dve", lb[:, :, :], lbt[:, :, 1, :], lbt[:, :, 0, :], ALU.subtract, r=[lbt], w=[lb])
    cx.act(lb[:, :, :], lb[:, :, :], AF.Sigmoid, r=[lb], w=[lb])
    cx.ts("dve", oml[:, :, :], lb[:, :, :], -1.0, 1.0, ALU.mult, ALU.add, r=[lb], w=[oml])
    cx.ts("dve", noml[:, :, :], lb[:, :, :], -1.0, None, ALU.add, None, r=[lb], w=[noml])
    if stop < 1:
        cx.S.barrier()
        return cx

    with Phase(cx) as ph:
        hT = ph.sb("hT", [128, 16, 1280], BF16)
        hTb = [Buf(hT.t, f"hT{i}") for i in range(10)]
        hbufs = (ph.sbs("xt", [128, D], F32, 2), ph.sbs("xn", [128, D], BF16, 2), ph.sb("sqj", [128, D], BF16),
                 ph.sbs("ss", [128, 4], F32, 2), ph.pss("pT", [128, 8, 128], BF16, 2))
        ws = WStream(cx, ph)
        pacc = ph.pss("pacc", [128, 512], F32, 3)
        vbs = ph.sbs("vb", [128, 512], BF16, 2)
        fof = ph.sbs("fof", [128, 512], F32, 3)
        fob = ph.sbs("fob", [128, 512], BF16, 2)
        cnt = {"acc": 0, "v": 0, "f": 0, "b": 0}
        colblocks = [("q", 0, 0), ("q", 512, 1), ("i", 1024, 0), ("i", 1536, 1), ("uf", 2048, 0), ("uf", 2560, 1),
                     ("ub", 3072, 0), ("ub", 3584, 1), ("z", 4096, 0), ("z", 4608, 1)]
        for t0 in range(0, NT1, 10):
            tl = list(range(t0, min(t0 + 10, NT1)))
            tiles = [(ctx1_in, t * 128, 1) if t < 2 else (x1_tiles[t - 2][0], x1_tiles[t - 2][1], 0) for t in tl]
            build_hT(cx, hbufs, c, modT, gs, hT, hTb, tiles)
            ntok = len(tl) * 128
            chunks = [(a, min(512, ntok - a)) for a in range(0, ntok, 512)]
            wnext = ws.fetch(w_c, colblocks[0][1])
            for ci, (fam, c0, sub) in enumerate(colblocks):
                wb = wnext
                if ci + 1 < len(colblocks):
                    wnext = ws.fetch(w_c, colblocks[ci + 1][1])
                if fam == "i":
                    for ti, t in enumerate(tl):
                        ps = pacc[cnt["acc"] % 3]; cnt["acc"] += 1
                        for k in range(16):
                            cx.mm(ps[:, :], hT[:, k, ti * 128:(ti + 1) * 128], wb[:, k, :], start=(k == 0), stop=(k == 15),
                                  r=[hTb[ti], wb], wa=[ps], sig=(k == 15))
                        vb = vbs[cnt["v"] % 2]; cnt["v"] += 1
                        cx.cp("act", vb[:, :], ps[:, :], r=[ps], w=[vb])
                        cx.dma(STQ, vS.t[sub * 4:sub * 4 + 4, :, t, :].rearrange("h p c -> p h c"),
                               vb[:, :].rearrange("p (h c) -> p h c", c=128), r=[vb], wa=[vS])
                else:
                    for (h0c, n) in chunks:
                        tt0 = h0c // 128
                        rb = [hTb[x] for x in range(tt0, tt0 + (n + 127) // 128)]
                        g0 = t0 * 128 + h0c
                        for fc in range(4):
                            ps = pacc[cnt["acc"] % 3]; cnt["acc"] += 1
                            for k in range(16):
                                cx.mm(ps[:, 0:n], wb[:, k, fc * 128:(fc + 1) * 128], hT[:, k, h0c:h0c + n], start=(k == 0), stop=(k == 15),
                                      r=rb + [wb], wa=[ps], sig=(k == 15))
                            h = sub * 4 + fc
                            if fam == "z":
                                fo = fob[cnt["b"] % 2]; cnt["b"] += 1
                                cx.act(fo[:, 0:n], ps[:, 0:n], AF.Silu, r=[ps], w=[fo])
                                cx.dma(STQ, zS[h, :, g0:g0 + n], fo[:, 0:n], r=[fo], wa=[zS])
                            else:
                                fo = fof[cnt["f"] % 3]; cnt["f"] += 1
                                cx.act(fo[:, 0:n], ps[:, 0:n], AF.Silu if fam == "q" else AF.Sigmoid, r=[ps], w=[fo])
                                if fam == "q":
                                    cx.dma(STQ, qS[h, :, g0:g0 + n], fo[:, 0:n], r=[fo], wa=[qS])
                                else:
                                    cx.dma(STQ, sgS[0 if fam == "uf" else 1, h, :, g0:g0 + n], fo[:, 0:n], r=[fo], wa=[sgS])
    if stop < 2:
        return cx

    with Phase(cx) as ph:
        msk = ph.sb("msk", [128, T1], F32)
        qf = ph.sb("qf", [128, T1], F32)
        vh = ph.sb("vh1", [128, NT1, 128], BF16)
        oacc = ph.sb("oacc", [128, SEQ], F32)
        lf = ph.sb("lf", [128, T1], F32)
        kf = ph.sb("kf", [128, T1], F32)
        bc = ph.sb("bcum", [128, T1], F32)
        ef = ph.sb("ef", [128, T1], F32)
        qdec = ph.sb("qdec", [128, T1], BF16)
        kinv = ph.sb("kinv", [128, T1], BF16)
        kendT = ph.sb("kendT", [128, T1], BF16)
        kend = ph.sb("kend", [128, NT1, 128], BF16)
        dec = ph.sb("dec", [128, NCH], F32)
        bend = ph.sb("bend", [128, NCH], F32)
        Sf = ph.sb("Sf", [128, 128], F32)
        Sbs = ph.sbs("Sb", [128, 128], BF16, 2)
        Ams = ph.sbs("Am", [128, 128], BF16, 2)
        sqr = ph.sb("sqr", [128, 512], F32); rst = ph.sb("rst", [128, 512], F32); onb = ph.sb("onb", [128, 512], F32)
        zcs = ph.sbs("zc", [128, 512], BF16, 2); obs = ph.sbs("ob1", [128, 512], BF16, 2)
        pA = ph.pss("pA", [128, 512], F32, 2)
        po = ph.pss("po1", [128, 512], F32, 2)
        pS = ph.pss("pS", [128, 512], F32, 2)
        pK = ph.pss("pK", [128, 8, 128], BF16, 2)
        cx.memset("pool", msk[:, :], 1.0, w=[msk])
        cx.memset("pool", msk[:, :].rearrange("p (c d) -> p c d", d=64)[:, :, 0:1], 0.0, w=[msk])
        v3 = lambda a: a.rearrange("p (c d) -> p c d", d=64)
        HT = T1 // 2
        ia = 0
        isb = 0
        ipo = 0
        for h in range(8):
            cx.dma("sp", qf[:, :], qS[h, :, :], r=[qS], w=[qf])
            cx.dma("sp", vh[:, :, :], vS[h, :, :, :], r=[vS], w=[vh])
            for d_ in range(2):
                cx.dma("sp", lf[:, :], sgS[d_, h, :, :], r=[sgS], w=[lf])
                cx.ts("dve", kf[:, :], lf[:, :], noml[:, d_, h:h + 1], oml[:, d_, h:h + 1], ALU.mult, ALU.add, r=[lf, noml, oml], w=[kf])
                cx.act(lf[:, :], lf[:, :], AF.Ln, r=[lf, oml, lb], w=[lf], scale=oml[:, d_, h:h + 1], bias=lb[:, d_, h:h + 1])
                for hh in range(2):
                    sl = slice(hh * HT, (hh + 1) * HT)
                    cx.S.op("dve", lambda: nc.vector.tensor_tensor_scan(out=bc[:, sl], data0=msk[:, sl], data1=lf[:, sl], initial=0.0,
                                                                        op0=ALU.mult, op1=ALU.add), r=[msk, lf], wa=[bc])
                cx.cp("dve", bend[:, :], v3(bc[:, :])[:, :, 63], r=[bc], w=[bend])
                bend_bc = bend[:, :].unsqueeze(2).to_broadcast([128, NCH, 64])
                if d_ == 1:
                    cx.tt("pool", v3(ef[:, :]), v3(lf[:, :]), bend_bc, ALU.add, r=[lf, bend], w=[ef])
                    cx.tt("dve", bc[:, :], ef[:, :], bc[:, :], ALU.subtract, r=[ef, bc], w=[bc])
                cx.act(dec[:, :], bend[:, :], AF.Exp, r=[bend], w=[dec])
                cx.act(ef[:, :], bc[:, :], AF.Exp, r=[bc], w=[ef])
                cx.tt("dve", qdec[:, :], qf[:, :], ef[:, :], ALU.mult, r=[qf, ef], w=[qdec])
                cx.act(ef[:, :], bc[:, :], AF.Exp, r=[bc], w=[ef], scale=-1.0)
                cx.tt("dve", kinv[:, :], kf[:, :], ef[:, :], ALU.mult, r=[kf, ef], w=[kinv])
                cx.tt("pool", v3(ef[:, :]), bend_bc, v3(bc[:, :]), ALU.subtract, r=[bend, bc], w=[ef])
                cx.act(ef[:, :], ef[:, :], AF.Exp, r=[ef], w=[ef])
                cx.tt("dve", kendT[:, :], kf[:, :], ef[:, :], ALU.mult, r=[kf, ef], w=[kendT])
                for g in range(0, NT1, 8):
                    pk = pK[(g // 8) % 2]
                    ng = min(8, NT1 - g)
                    for t in range(ng):
                        cx.tr(pk[:, t, :], kendT[:, (g + t) * 128:(g + t + 1) * 128], c["identb"][:, :], r=[kendT, c["identb"]], wa=[pk], sig=(t == ng - 1))
                    cx.cp("pool" if False else "act", kend[:, g:g + ng, :], pk[:, 0:ng, :], r=[pk], wa=[kend])
                cx.memset("pool", Sf[:, :], 0.0, w=[Sf])
                Sb = Sbs[isb % 2]; isb += 1
                cx.memset("pool", Sb[:, :], 0.0, w=[Sb])
                mask = maskF if d_ == 0 else maskB
                pairs = list(range(NT1)) if d_ == 0 else [1, 0] + list(range(NT1 - 1, 1, -1))
                for p in pairs:
                    lat = p >= 2
                    t0_ = p * 128
                    order = (0, 1) if d_ == 0 else (1, 0)
                    if lat:
                        a_ps = pA[ia % 2]; Am = Ams[ia % 2]; ia += 1
                        o_ps = po[ipo % 2]; ipo += 1
                        cx.mm(a_ps[:, 0:128], kinv[:, t0_:t0_ + 128], qdec[:, t0_:t0_ + 128], True, True, r=[kinv, qdec], w=[a_ps])
                        cx.tt("dve", Am[:, :], a_ps[:, 0:128], mask[:, :], ALU.mult, r=[a_ps, mask], w=[Am])
                        cx.mm(o_ps[:, 0:128], vh[:, p, :], Am[:, :], True, False, r=[vh, Am], wa=[o_ps], sig=False)
                    for oi, hc in enumerate(order):
                        c64 = slice(t0_ + hc * 64, t0_ + hc * 64 + 64)
                        pr = slice(hc * 64, hc * 64 + 64)
                        if lat:
                            cx.mm(o_ps[:, hc * 64:hc * 64 + 64], Sb[:, :], qdec[:, c64], False, (oi == 1), r=[Sb, qdec], wa=[o_ps], sig=True)
                        s_ps = pS[isb % 2]
                        cx.mm(s_ps[:, 0:128], kend[pr, p, :], vh[pr, p, :], True, True, r=[kend, vh], w=[s_ps])
                        cx.stt("dve", Sf[:, :], Sf[:, :], dec[:, 2 * p + hc:2 * p + hc + 1], s_ps[:, 0:128], ALU.mult, ALU.add, r=[Sf, dec, s_ps], w=[Sf])
                        Sb = Sbs[isb % 2]; isb += 1
                        cx.cp("act", Sb[:, :], Sf[:, :], r=[Sf], w=[Sb])
                    if lat:
                        oc = slice((p - 2) * 128, (p - 1) * 128)
                        if d_ == 0:
                            cx.cp("pool" if False else "act", oacc[:, oc], o_ps[:, 0:128], r=[o_ps], wa=[oacc])
                        else:
                            cx.tt("dve", oacc[:, oc], oacc[:, oc], o_ps[:, 0:128], ALU.add, r=[o_ps, oacc], wa=[oacc])
            for q8 in range(8):
                cs = slice(q8 * 512, (q8 + 1) * 512)
                a_ps = pA[ia % 2]; ia += 1
                zc = zcs[q8 % 2]; ob = obs[q8 % 2]
                cx.dma("sp", zc[:, :], zS[h, :, CTX + q8 * 512:CTX + (q8 + 1) * 512], r=[zS], w=[zc])
                cx.act(sqr[:, :], oacc[:, cs], AF.Square, r=[oacc], w=[sqr])
                cx.mm(a_ps[:, :], c["onesf"][:, :], sqr[:, :], True, True, r=[c["onesf"], sqr], w=[a_ps])
                cx.act(rst[:, :], a_ps[:, :], AF.Sqrt, r=[a_ps, c["eps"]], w=[rst], bias=c["eps"][:, 0:1], scale=1.0 / 128)
                cx.recip(rst[:, :], rst[:, :], r=[rst], w=[rst])
                cx.tt("dve", onb[:, :], oacc[:, cs], rst[:, :], ALU.mult, r=[oacc, rst], w=[onb])
                cx.stt("dve", ob[:, :], onb[:, :], ong[:, 0:1], zc[:, :], ALU.mult, ALU.mult, r=[onb, ong, zc], w=[ob])
                cx.dma(STQ, catT1_out[h * 128:(h + 1) * 128, cs], ob[:, :], r=[ob], wa=[catT1_out])
    return catT1_out, modT


PAIRS = [[0, 1], [2, 3], [4, 5], [6, 7]]
WARMUP_COLL = True


def build_fused():
    cx = Ctx()
    ident_in = cx.din("ident", [128, 128])
    bmask_in = cx.din("bmask", [128, 2])
    w_out1 = cx.din("w_out1", [D, D])
    x2_out = cx.dout("x2", [HALF, D])
    c = common_consts(cx, ident_in)
    if WARMUP_COLL:
        wu_src = cx.dscr("wu_src", [128, 128], F32)
        wu_dst = cx.dscr("wu_dst", [256, 128], F32)
        cx.dma("sp", wu_src[:, :], ident_in[:, :], r=[ident_in], w=[wu_src])
        cx.S.coll("AllGather", wu_dst[:, :], wu_src[:, :], PAIRS, r=[wu_src], w=[wu_dst])
    with Scope(cx):
        x1_own_s, ctx1_s = _build_l0(cx, 99, fused=True, c=c)
    x1g_t = cx.nc.dram_tensor("x1g_s", [8, 512, D], F32, kind="Internal").ap()
    x1g = [Buf(x1g_t[i], f"x1g{i}") for i in range(8)]
    for i in range(8):
        cx.S.coll("AllGather", x1g[i][:, :], x1_own_s[i * 256:(i + 1) * 256, :], PAIRS, r=[x1_own_s], w=[x1g[i]])
    x1_tiles = []
    for t in range(32):
        r_, lt = t // 16, t % 16
        x1_tiles.append((x1g[lt // 2], r_ * 256 + (lt % 2) * 128))
    with Scope(cx):
        catT1_s, modT = _build_l1a(cx, 99, fused=True, c=c, x1_tiles=x1_tiles, ctx1_in=ctx1_s)
        catg_t = cx.nc.dram_tensor("catg_s", [4, 512, SEQ], BF16, kind="Internal").ap()
        catg = [Buf(catg_t[i], f"catg{i}") for i in range(4)]
        for i in range(4):
            cx.S.coll("AllGather", catg[i][:, :], catT1_s[i * 256:(i + 1) * 256, :], PAIRS, r=[catT1_s], w=[catg[i]])
        gate_bc = cx.sb("gate_bc1", [128, 2, D], F32)
        bmask = cx.sb("bmask_t", [128, 2], F32)
        cx.dma("sp", bmask[:, :], bmask_in[:, :], r=[bmask_in], w=[bmask])
        gate_rows(cx, c, modT, gate_bc, (0,))

        def loader(dst, col):
            for r_ in range(2):
                for i in range(4):
                    k0 = r_ * 8 + i * 2
                    cx.dma("sp", dst[:, k0:k0 + 2, :],
                           catg[i].t[r_ * 256:(r_ + 1) * 256, col:col + 128].rearrange("(k p) c -> p k c", p=128),
                           r=[catg[i]], wa=[dst])
        out_proj(cx, w_out1, None, gate_bc, [(x1_own_s, x2_out, t * 128, t * 128, 0) for t in range(16)], blend=bmask, loader=loader)
    return cx


def build_l1b():
    cx = Ctx()
    catT_in = cx.din("catT", [D, HALF], BF16)
    x1_own = cx.din("x1o", [HALF, D])
    w_out = cx.din("w_out", [D, D])
    modT_in = cx.din("modT1", [128, 48, 2])
    ident_in = cx.din("ident", [128, 128])
    x2_out = cx.dout("x2", [HALF, D])
    c = common_consts(cx, ident_in)
    modT = cx.sb("modT", [128, 48, 2], F32)
    gate_bc = cx.sb("gate_bc", [128, 2, D], F32)
    cx.dma("sp", modT[:, :, :], modT_in[:, :, :], r=[modT_in], w=[modT])
    gate_rows(cx, c, modT, gate_bc, (0,))
    out_proj(cx, w_out, catT_in, gate_bc, [(x1_own, x2_out, t * 128, t * 128, 0) for t in range(16)])
    return cx


def rope_tables():
    rows = SEQ // 64
    row = np.repeat(np.arange(rows), 64).astype(np.float32)
    col = np.tile(np.arange(64), rows).astype(np.float32)
    inv = (10000.0 ** (-np.arange(0, 32, 2, dtype=np.float32) / 32)).astype(np.float32)

    def axis_angles(pos):
        a = pos[:, None] * inv[None, :]
        return np.concatenate([a, a], axis=-1)
    ang = np.concatenate([axis_angles(row), axis_angles(col)], axis=-1).astype(np.float32)
    cos, sin = np.cos(ang), np.sin(ang)
    sgn = np.tile(np.concatenate([-np.ones(16), np.ones(16)]), 2).astype(np.float32)
    return np.concatenate([cos, sin * sgn], axis=-1).astype(np.float32)


def fm(v, nchunk):
    return np.ascontiguousarray(np.asarray(v, np.float32).reshape(nchunk, 128).T)


_CACHE = {}


def run_l0(inp, cores):
    if "l0" not in _CACHE:
        _CACHE["l0"] = build_l0()
    cx = _CACHE["l0"]
    rope = rope_tables()
    ident = np.eye(128, dtype=np.float32)
    in_maps = []
    for cid in cores:
        b, hf = cid // 2, cid % 2
        own = slice(hf * HALF, (hf + 1) * HALF)
        oth = slice((1 - hf) * HALF, (2 - hf) * HALF)
        cc = np.stack([inp["c"][b], inp["c_ctx"]], axis=-1)
        ccT = np.ascontiguousarray(cc.reshape(16, 128, 2).transpose(1, 0, 2))
        cw = inp["conv_w"][0]
        cwT = np.ascontiguousarray(cw.reshape(31, 8, 128).transpose(2, 1, 0))
        cvp = np.ascontiguousarray(np.stack([fm(inp["conv_b"][0], 8), fm(inp["cln_g"][0], 8), fm(inp["cln_b"][0], 8)], axis=1))
        hm = np.zeros((128, 2), np.float32)
        hm[:, 0] = 1.0 if hf == 1 else 0.0
        hm[:, 1] = 1.0 if hf == 0 else 0.0
        in_maps.append({
            "x_own": np.ascontiguousarray(inp["x"][b, own]), "x_oth": np.ascontiguousarray(inp["x"][b, oth]),
            "ctx": np.ascontiguousarray(inp["ctx"][b]), "ccT": ccT,
            "w_ada": np.ascontiguousarray(inp["w_ada"][0]), "b_adaT": fm(inp["b_ada"][0], 48), "norm_gT": fm(inp["norm_g"][0], 16),
            "w_in": np.ascontiguousarray(inp["w_in_ab"][0]), "w_out": np.ascontiguousarray(inp["w_out_ab"][0]),
            "qkg": np.ascontiguousarray(np.stack([inp["qn_g"][0], inp["kn_g"][0]])),
            "lamv": np.ascontiguousarray(np.stack([inp["lam_q1"][0], inp["lam_k1"][0], inp["lam_q2"][0], inp["lam_k2"][0]])),
            "subg": np.ascontiguousarray(inp["subln_g"][0].reshape(128, 1)),
            "cwT": cwT, "cvp": cvp,
            "rq_own": np.ascontiguousarray(rope[own] * np.float32(0.125)), "rk_own": np.ascontiguousarray(rope[own]),
            "rk_oth": np.ascontiguousarray(rope[oth]), "hmask": hm, "ident": ident,
        })
    res = run_bass_kernel_spmd(cx.nc, in_maps, core_ids=list(range(len(cores))))
    return res.results


def hgrn_masks():
    i = np.arange(128)
    same = (i[:, None] // 64) == (i[None, :] // 64)
    mF = (same & (i[:, None] <= i[None, :])).astype(np.float32)
    mB = (same & (i[:, None] >= i[None, :])).astype(np.float32)
    return mF, mB


def run_l1a(inp, x1, ctx1, cores):
    if "l1a" not in _CACHE:
        _CACHE["l1a"] = build_l1a()
    cx = _CACHE["l1a"]
    ident = np.eye(128, dtype=np.float32)
    mF, mB = hgrn_masks()
    wc = inp["w_in_c"][0]
    in_maps = []
    for cid in cores:
        b, hf = cid // 2, cid % 2
        cc = np.stack([inp["c"][b], inp["c_ctx"]], axis=-1)
        ccT = np.ascontiguousarray(cc.reshape(16, 128, 2).transpose(1, 0, 2))
        cols = np.concatenate([np.arange(f * D + hf * 1024, f * D + (hf + 1) * 1024) for f in range(5)])
        lg = inp["lb_gamma"][:, :, hf * 1024:(hf + 1) * 1024]
        lbg = np.ascontiguousarray(lg.reshape(2, 2, 8, 128).transpose(3, 0, 1, 2))
        in_maps.append({
            "x1f": np.ascontiguousarray(x1[b]), "ctx1": np.ascontiguousarray(ctx1[b]), "ccT": ccT,
            "w_ada": np.ascontiguousarray(inp["w_ada"][1]), "b_adaT": fm(inp["b_ada"][1], 48), "norm_gT": fm(inp["norm_g"][1], 16),
            "w_c": np.ascontiguousarray(wc[:, cols]), "lbg": lbg,
            "ong": np.ascontiguousarray(inp["onorm_g"][0].reshape(128, 1)),
            "maskF": mF, "maskB": mB, "ident": ident,
        })
    res = run_bass_kernel_spmd(cx.nc, in_maps, core_ids=list(range(len(cores))))
    return res.results


def run_l1b(inp, x1, cat_pairs, modTs, cores):
    if "l1b" not in _CACHE:
        _CACHE["l1b"] = build_l1b()
    cx = _CACHE["l1b"]
    ident = np.eye(128, dtype=np.float32)
    in_maps = []
    for i, cid in enumerate(cores):
        b, hf = cid // 2, cid % 2
        own = slice(hf * HALF, (hf + 1) * HALF)
        in_maps.append({
            "catT": np.ascontiguousarray(cat_pairs[b][:, own]), "x1o": np.ascontiguousarray(x1[b, own]),
            "w_out": np.ascontiguousarray(inp["w_out_c"][0]), "modT1": modTs[i], "ident": ident,
        })
    res = run_bass_kernel_spmd(cx.nc, in_maps, core_ids=list(range(len(cores))))
    return res.results


def run_l1(inp, x1, ctx1, cores):
    ra = run_l1a(inp, x1, ctx1, cores)
    cat_pairs = {}
    for i, cid in enumerate(cores):
        b, hf = cid // 2, cid % 2
        cat_pairs.setdefault(b, [None, None])[hf] = ra[i]["catT1"]
    cat_pairs = {b: np.concatenate(v, axis=0) for b, v in cat_pairs.items()}
    rb = run_l1b(inp, x1, cat_pairs, [ra[i]["modT1"] for i in range(len(cores))], cores)
    return rb


def l0_inputs(inp, cid, rope):
    b, hf = cid // 2, cid % 2
    own = slice(hf * HALF, (hf + 1) * HALF)
    oth = slice((1 - hf) * HALF, (2 - hf) * HALF)
    cc = np.stack([inp["c"][b], inp["c_ctx"]], axis=-1)
    ccT = np.ascontiguousarray(cc.reshape(16, 128, 2).transpose(1, 0, 2))
    cw = inp["conv_w"][0]
    cwT = np.ascontiguousarray(cw.reshape(31, 8, 128).transpose(2, 1, 0))
    cvp = np.ascontiguousarray(np.stack([fm(inp["conv_b"][0], 8), fm(inp["cln_g"][0], 8), fm(inp["cln_b"][0], 8)], axis=1))
    hm = np.zeros((128, 2), np.float32)
    hm[:, 0] = 1.0 if hf == 1 else 0.0
    hm[:, 1] = 1.0 if hf == 0 else 0.0
    return {
        "x_own": np.ascontiguousarray(inp["x"][b, own]), "x_oth": np.ascontiguousarray(inp["x"][b, oth]),
        "ctx": np.ascontiguousarray(inp["ctx"][b]), "ccT": ccT,
        "w_ada": np.ascontiguousarray(inp["w_ada"][0]), "b_adaT": fm(inp["b_ada"][0], 48), "norm_gT": fm(inp["norm_g"][0], 16),
        "w_in": np.ascontiguousarray(inp["w_in_ab"][0]), "w_out": np.ascontiguousarray(inp["w_out_ab"][0]),
        "qkg": np.ascontiguousarray(np.stack([inp["qn_g"][0], inp["kn_g"][0]])),
        "lamv": np.ascontiguousarray(np.stack([inp["lam_q1"][0], inp["lam_k1"][0], inp["lam_q2"][0], inp["lam_k2"][0]])),
        "subg": np.ascontiguousarray(inp["subln_g"][0].reshape(128, 1)),
        "cwT": cwT, "cvp": cvp,
        "rq_own": np.ascontiguousarray(rope[own] * np.float32(0.125)), "rk_own": np.ascontiguousarray(rope[own]),
        "rk_oth": np.ascontiguousarray(rope[oth]), "hmask": hm,
    }


def l1_inputs(inp, cid):
    b, hf = cid // 2, cid % 2
    cc = np.stack([inp["c"][b], inp["c_ctx"]], axis=-1)
    ccT = np.ascontiguousarray(cc.reshape(16, 128, 2).transpose(1, 0, 2))
    cols = np.concatenate([np.arange(f * D + hf * 1024, f * D + (hf + 1) * 1024) for f in range(5)])
    lg = inp["lb_gamma"][:, :, hf * 1024:(hf + 1) * 1024]
    lbg = np.ascontiguousarray(lg.reshape(2, 2, 8, 128).transpose(3, 0, 1, 2))
    mF, mB = hgrn_masks()
    bm = np.zeros((128, 2), np.float32)
    bm[:, hf] = 1.0
    return {
        "ccT1": ccT, "w_ada1": np.ascontiguousarray(inp["w_ada"][1]), "b_adaT1": fm(inp["b_ada"][1], 48),
        "norm_gT1": fm(inp["norm_g"][1], 16), "w_c": np.ascontiguousarray(inp["w_in_c"][0][:, cols]), "lbg": lbg,
        "ong": np.ascontiguousarray(inp["onorm_g"][0].reshape(128, 1)), "maskF": mF, "maskB": mB,
        "w_out1": np.ascontiguousarray(inp["w_out_c"][0]), "bmask": bm,
    }


def run_fused(inp, cores):
    if "fused" not in _CACHE:
        _CACHE["fused"] = build_fused()
    cx = _CACHE["fused"]
    rope = rope_tables()
    ident = np.eye(128, dtype=np.float32)
    in_maps = []
    for cid in cores:
        m = {"ident": ident}
        m.update(l0_inputs(inp, cid, rope))
        m.update(l1_inputs(inp, cid))
        in_maps.append(m)
    res = run_bass_kernel_spmd(cx.nc, in_maps, core_ids=list(range(len(cores))))
    return res.results


def kernel_unfused(**inputs):
    inp = {k: np.asarray(v) for k, v in inputs.items()}
    cores = list(range(8))
    r0 = run_l0(inp, cores)
    x1 = np.zeros_like(inp["x"])
    ctx1 = np.zeros_like(inp["ctx"])
    for cid in cores:
        b, hf = cid // 2, cid % 2
        x1[b, hf * HALF:(hf + 1) * HALF] = r0[cid]["x1"]
        ctx1[b] = r0[cid]["ctx1"]
    r1 = run_l1(inp, x1, ctx1, cores)
    out = np.zeros_like(inp["x"])
    for cid in cores:
        b, hf = cid // 2, cid % 2
        out[b, hf * HALF:(hf + 1) * HALF] = r1[cid]["x2"]
    return out


def kernel(**inputs):
    inp = {k: np.asarray(v) for k, v in inputs.items()}
    cores = list(range(8))
    r = run_fused(inp, cores)
    out = np.zeros_like(inp["x"])
    for cid in cores:
        b, hf = cid // 2, cid % 2
        out[b, hf * HALF:(hf + 1) * HALF] = r[cid]["x2"]
    return out
```

```python
from contextlib import ExitStack
import math
import numpy as np
import ml_dtypes
import concourse.bass as bass
import concourse.mybir as mybir
from concourse.bass_utils import run_bass_kernel_spmd

F32 = mybir.dt.float32
BF16 = mybir.dt.bfloat16
AF = mybir.ActivationFunctionType
ALU = mybir.AluOpType
AX = mybir.AxisListType

D = 2048
SEQ = 4096
HALF = 2048
CTX = 256
EPS = 1e-6
NKEY = CTX + SEQ
STQ = "act"


class Buf:
    def __init__(self, t, name="", psum=False):
        self.t = t
        self.name = name
        self.psum = psum
        self.w = {}
        self.r = {}
        self.pr = {}
        self.open = False

    def __getitem__(self, k):
        return self.t[k]


def _merge(d, s):
    for k, v in s.items():
        if d.get(k, 0) < v:
            d[k] = v


class Sched:
    GEN = 12000
    NSLOT = 8

    def __init__(self, nc):
        self.nc = nc
        self.eng = {"pe": nc.tensor, "dve": nc.vector, "act": nc.scalar,
                    "pool": nc.gpsimd, "sp": nc.sync}
        self.cnt = {e: 0 for e in self.eng}
        self.gen = {e: 0 for e in self.eng}
        self.sems = {}
        self.waited = {e: {} for e in self.eng}
        self.dma_i = {e: 0 for e in self.eng}
        self.nops = 0
        self.nwaits = 0

    def sem(self, key):
        if key not in self.sems:
            self.sems[key] = self.nc.alloc_semaphore("s_" + "_".join(str(k) for k in key))
        return self.sems[key]

    def _wait(self, e, deps):
        for key, val in deps.items():
            if key[0] == "E" and key[1] == e and e in ("pe", "sp"):
                continue
            if self.waited[e].get(key, 0) >= val:
                continue
            self.eng[e].wait_ge(self.sem(key), val)
            self.waited[e][key] = val
            self.nwaits += 1

    def _deps(self, e, r, w, wa):
        deps = {}
        for b in r:
            _merge(deps, b.w)
            if b.psum:
                _merge(deps, {k: v for k, v in b.r.items() if k[1] != e})
        for b in w:
            _merge(deps, b.w)
            _merge(deps, b.r)
            _merge(deps, b.pr)
        for b in wa:
            if not b.open:
                b.pr = dict(b.r)
                _merge(b.pr, b.w)
                b.r = {}
                b.w = {}
                b.open = True
            _merge(deps, b.pr)
        return deps

    def _record(self, key, val, r, w, wa):
        ev = {key: val}
        for b in r:
            _merge(b.r, ev)
            b.open = False
        for b in w:
            b.w = dict(ev)
            b.r = {}
            b.pr = {}
            b.open = False
        for b in wa:
            _merge(b.w, ev)

    def op(self, e, fn, r=(), w=(), wa=(), sig=True):
        deps = self._deps(e, r, w, wa)
        self._wait(e, deps)
        ins = fn()
        self.nops += 1
        key = ("E", e, self.gen[e])
        if sig:
            self.cnt[e] += 1
            ins.then_inc(self.sem(key), 1)
            self._record(key, self.cnt[e], r, w, wa)
            if self.cnt[e] >= self.GEN:
                self.gen[e] += 1
                self.cnt[e] = 0
        else:
            self._record(key, self.cnt[e] + 1, r, w, wa)
        return ins

    def dma(self, e, out, in_, r=(), w=(), wa=(), **kw):
        i = self.dma_i[e]
        slot = i % self.NSLOT
        key = ("D", e, slot)
        val = 16 * (i // self.NSLOT + 1)
        deps = self._deps(e, r, w, wa)
        if val > 16:
            _merge(deps, {key: val - 16})
        self._wait(e, deps)
        ins = self.eng[e].dma_start(out=out, in_=in_, **kw)
        ins.then_inc(self.sem(key), 16)
        self.dma_i[e] = i + 1
        self._record(key, val, r, w, wa)
        return ins

    def coll(self, kind, out, in_, groups, r=(), w=()):
        e = "pool"
        self.ncoll = getattr(self, "ncoll", 0) + 1
        key = ("C", e, self.ncoll)
        deps = self._deps(e, r, w, ())
        self._wait(e, deps)
        ins = self.nc.gpsimd.collective_compute(kind, ALU.bypass, replica_groups=groups, ins=[in_], outs=[out])
        ins.then_inc(self.sem(key), 1)
        self.colls = getattr(self, "colls", {})
        self.colls[key] = 1
        self._record(key, 1, r, w, ())
        return ins

    def barrier(self):
        allev = {}
        for e in self.eng:
            if self.cnt[e] > 0:
                allev[("E", e, self.gen[e])] = self.cnt[e]
            elif self.gen[e] > 0:
                allev[("E", e, self.gen[e] - 1)] = self.GEN
        for e in self.eng:
            n = self.dma_i[e]
            for slot in range(min(n, self.NSLOT)):
                last = ((n - 1 - slot) // self.NSLOT) * self.NSLOT + slot
                allev[("D", e, slot)] = 16 * (last // self.NSLOT + 1)
        allev.update(getattr(self, "colls", {}))
        for e in self.eng:
            for key, val in allev.items():
                if key[0] == "E" and key[1] == e and e in ("pe", "sp"):
                    continue
                if self.waited[e].get(key, 0) >= val:
                    continue
                self.eng[e].wait_ge(self.sem(key), val)
                self.waited[e][key] = val
                self.nwaits += 1


class Ctx:
    def __init__(self):
        self.nc = bass.Bass("TRN2", target_bir_lowering=False)
        self.S = Sched(self.nc)
        self.later = []
        self.scope = None
        self.uid = 0

    def un(self, name):
        self.uid += 1
        return f"{name}_u{self.uid}"

    def din(self, name, shape, dt=F32):
        return Buf(self.nc.dram_tensor(name, list(shape), dt, kind="ExternalInput").ap(), name)

    def dout(self, name, shape, dt=F32):
        return Buf(self.nc.dram_tensor(name, list(shape), dt, kind="ExternalOutput").ap(), name)

    def dscr(self, name, shape, dt=BF16):
        return Buf(self.nc.dram_tensor(name, list(shape), dt, kind="Internal").ap(), name)

    def sb(self, name, shape, dt=F32):
        name = self.un(name)
        if self.scope is not None:
            return Buf(self.scope.enter_context(self.nc.sbuf_tensor(name, list(shape), dt)), name)
        return Buf(self.nc.alloc_sbuf_tensor(name, list(shape), dt), name)

    def act(self, out, in_, func, r, w=(), wa=(), **kw):
        nc = self.nc
        return self.S.op("act", lambda: nc.scalar.activation(out=out, in_=in_, func=func, **kw), r=r, w=w, wa=wa)

    def _ve(self, e):
        return self.nc.vector if e == "dve" else self.nc.gpsimd

    def tt(self, e, out, in0, in1, op, r, w=(), wa=()):
        eng = self._ve(e)
        return self.S.op(e, lambda: eng.tensor_tensor(out=out, in0=in0, in1=in1, op=op), r=r, w=w, wa=wa)

    def ts(self, e, out, in0, s1, s2, op0, op1, r, w=(), wa=()):
        eng = self._ve(e)
        if s2 is None:
            return self.S.op(e, lambda: eng.tensor_scalar(out=out, in0=in0, scalar1=s1, scalar2=None, op0=op0), r=r, w=w, wa=wa)
        return self.S.op(e, lambda: eng.tensor_scalar(out=out, in0=in0, scalar1=s1, scalar2=s2, op0=op0, op1=op1), r=r, w=w, wa=wa)

    def stt(self, e, out, in0, scalar, in1, op0, op1, r, w=(), wa=()):
        eng = self._ve(e)
        return self.S.op(e, lambda: eng.scalar_tensor_tensor(out=out, in0=in0, scalar=scalar, in1=in1, op0=op0, op1=op1), r=r, w=w, wa=wa)

    def cp(self, e, out, in_, r, w=(), wa=()):
        if e == "act":
            nc = self.nc
            return self.S.op("act", lambda: nc.scalar.copy(out=out, in_=in_), r=r, w=w, wa=wa)
        eng = self._ve(e)
        return self.S.op(e, lambda: eng.tensor_copy(out=out, in_=in_), r=r, w=w, wa=wa)

    def recip(self, out, in_, r, w=(), wa=()):
        nc = self.nc
        return self.S.op("dve", lambda: nc.vector.reciprocal(out=out, in_=in_), r=r, w=w, wa=wa)

    def memset(self, e, ap, val, w=(), wa=()):
        eng = self._ve(e)
        return self.S.op(e, lambda: eng.memset(ap, val), w=w, wa=wa)

    def mm(self, out, lhsT, rhs, start, stop, r, w=(), wa=(), sig=True):
        nc = self.nc
        return self.S.op("pe", lambda: nc.tensor.matmul(out, lhsT=lhsT, rhs=rhs, start=start, stop=stop), r=r, w=w, wa=wa, sig=sig)

    def tr(self, out, in_, ident, r, w=(), wa=(), sig=True):
        nc = self.nc
        return self.S.op("pe", lambda: nc.tensor.transpose(out, in_, ident), r=r, w=w, wa=wa, sig=sig)

    def dma(self, e, out, in_, r, w=(), wa=()):
        return self.S.dma(e, out, in_, r=r, w=w, wa=wa)

    def defer(self, fn):
        self.later.append(fn)

    def flush(self):
        l, self.later = self.later, []
        for fn in l:
            fn()


class Scope:
    def __init__(self, cx):
        self.cx = cx

    def __enter__(self):
        self.es = ExitStack()
        self.es.__enter__()
        self.cx.scope = self.es
        return self

    def __exit__(self, *a):
        self.cx.flush()
        self.cx.S.barrier()
        self.cx.scope = None
        return self.es.__exit__(*a)


class Phase:
    def __init__(self, cx):
        self.cx = cx
        self.es = ExitStack()

    def __enter__(self):
        self.es.__enter__()
        return self

    def __exit__(self, *a):
        self.cx.flush()
        self.cx.S.barrier()
        return self.es.__exit__(*a)

    def sb(self, name, shape, dt=F32, n=1):
        name = self.cx.un(name)
        t = self.es.enter_context(self.cx.nc.sbuf_tensor(name, list(shape), dt))
        return Buf(t, name)

    def sbs(self, name, shape, dt, n):
        return [self.sb(f"{name}{i}", shape, dt) for i in range(n)]

    def ps(self, name, shape, dt=F32):
        nb = int(np.prod(shape[1:])) * (4 if dt == F32 else 2)
        assert nb % 2048 == 0, (name, shape)
        name = self.cx.un(name)
        t = self.es.enter_context(self.cx.nc.psum_tensor(name, list(shape), dt))
        return Buf(t, name, psum=True)

    def pss(self, name, shape, dt, n):
        return [self.ps(f"{name}{i}", shape, dt) for i in range(n)]


def common_consts(cx, ident_in):
    c = {}
    c["identf"] = cx.sb("identf", [128, 128], F32)
    c["identb"] = cx.sb("identb", [128, 128], BF16)
    c["onesf"] = cx.sb("onesf", [128, 128], F32)
    c["eps"] = cx.sb("epsT", [128, 1], F32)
    cx.dma("sp", c["identf"][:, :], ident_in[:, :], r=[ident_in], w=[c["identf"]])
    cx.cp("dve", c["identb"][:, :], c["identf"][:, :], r=[c["identf"]], w=[c["identb"]])
    cx.memset("pool", c["onesf"][:, :], 1.0, w=[c["onesf"]])
    cx.memset("pool", c["eps"][:, :], EPS, w=[c["eps"]])
    return c


def modulation(cx, c, ccT_in, w_ada, b_adaT_in, norm_gT_in, want_gate_bc=(0, 1), sfx=""):
    nc = cx.nc
    modT = cx.sb("modT" + sfx, [128, 48, 2], F32)
    gs = cx.sb("gsT" + sfx, [128, 16, 2], F32)
    gate_bc = cx.sb("gate_bc" + sfx, [128, 2, D], F32) if want_gate_bc else None
    with Phase(cx) as ph:
        scT = ph.sb("scT", [128, 16, 2], F32)
        badaT = ph.sb("badaT", [128, 48], F32)
        ngT = ph.sb("ngT", [128, 16], F32)
        wst = ph.sbs("wada_st", [128, 16, 512], F32, 2)
        pm = ph.ps("pm", [128, 512], F32)
        cx.dma("sp", scT[:, :, :], ccT_in[:, :, :], r=[ccT_in], w=[scT])
        cx.dma("sp", badaT[:, :], b_adaT_in[:, :], r=[b_adaT_in], w=[badaT])
        cx.dma("sp", ngT[:, :], norm_gT_in[:, :], r=[norm_gT_in], w=[ngT])
        cx.act(scT[:, :, :], scT[:, :, :], AF.Silu, r=[scT], w=[scT])
        wv = w_ada.t.rearrange("(k p) c -> p k c", p=128)
        for cb in range(12):
            st = wst[cb % 2]
            for hh in range(2):
                cx.dma("sp", st[:, hh * 8:(hh + 1) * 8, :], wv[:, hh * 8:(hh + 1) * 8, cb * 512:(cb + 1) * 512],
                       r=[w_ada], wa=[st])
            for fc in range(4):
                cc = cb * 4 + fc
                for k in range(16):
                    cx.mm(pm[:, cc * 2:cc * 2 + 2], st[:, k, fc * 128:(fc + 1) * 128], scT[:, k, :],
                          start=(k == 0), stop=(k == 15), r=[st, scT], wa=[pm], sig=(k == 15))
        cx.tt("dve", modT[:, :, :], pm[:, 0:96].rearrange("p (c j) -> p c j", j=2),
              badaT[:, :].unsqueeze(2).to_broadcast([128, 48, 2]), ALU.add, r=[pm, badaT], w=[modT])
        cx.stt("dve", gs[:, :, :], modT[:, 16:32, :], 1.0, ngT[:, :].unsqueeze(2).to_broadcast([128, 16, 2]),
               ALU.add, ALU.mult, r=[modT, ngT], w=[gs])
    if want_gate_bc:
        gate_rows(cx, c, modT, gate_bc, want_gate_bc)
    return modT, gs, gate_bc


def gate_rows(cx, c, modT, gate_bc, js):
    with Phase(cx) as ph:
        dgs = ph.sbs("dgate", [128, 128], F32, 2)
        pgs = ph.pss("pgate", [128, 512], F32, 2)
        i = 0
        for j in js:
            for k in range(16):
                dg = dgs[i % 2]
                pg = pgs[i % 2]
                cx.ts("dve", dg[:, :], c["identf"][:, :], modT[:, 32 + k, j:j + 1], None, ALU.mult, None,
                      r=[c["identf"], modT], w=[dg])
                cx.mm(pg[:, 0:128], c["onesf"][:, :], dg[:, :], True, True, r=[c["onesf"], dg], w=[pg])
                cx.cp("act", gate_bc[:, j, k * 128:(k + 1) * 128], pg[:, 0:128], r=[pg], wa=[gate_bc])
                i += 1


def build_hT(cx, ph_bufs, c, modT, gs, hT, hTb, tiles):
    xts, xns, sqj, sss, pTs = ph_bufs
    for i, (src, r0, j) in enumerate(tiles):
        xt = xts[i % 2]
        xn = xns[i % 2]
        ss = sss[i % 2]
        cx.dma("sp", xt[:, :], src[r0:r0 + 128, :], r=[src], w=[xt])
        if HT_DBG == 1:
            continue
        cx.memset("pool", ss[:, :], 0.0, w=[ss])
        cx.act(sqj[:, :], xt[:, :], AF.Square, r=[xt, ss], w=[sqj, ss], accum_out=ss[:, 0:1])
        cx.act(ss[:, 1:2], ss[:, 0:1], AF.Sqrt, r=[ss, c["eps"]], w=[ss], bias=c["eps"][:, 0:1], scale=1.0 / D)
        cx.recip(ss[:, 2:3], ss[:, 1:2], r=[ss], w=[ss])
        if HT_DBG == 2:
            continue
        cx.ts("dve", xn[:, :], xt[:, :], ss[:, 2:3], None, ALU.mult, None, r=[xt, ss], w=[xn])
        if HT_DBG == 3:
            continue
        for half in range(2):
            pT = pTs[half]
            for kk in range(8):
                k = half * 8 + kk
                cx.tr(pT[:, kk, :], xn[:, k * 128:(k + 1) * 128], c["identb"][:, :], r=[xn, c["identb"]],
                      wa=[pT], sig=(kk == 7))
            if HT_DBG == 4:
                continue
            for kk in range(8):
                k = half * 8 + kk
                dst = hT[:, k, i * 128:(i + 1) * 128]
                if half == 0:
                    cx.act(dst, pT[:, kk, :], AF.Identity, r=[pT, gs, modT], wa=[hTb[i]],
                           scale=gs[:, k, j:j + 1], bias=modT[:, k, j:j + 1])
                else:
                    cx.ts("dve", dst, pT[:, kk, :], gs[:, k, j:j + 1], modT[:, k, j:j + 1], ALU.mult, ALU.add,
                          r=[pT, gs, modT], wa=[hTb[i]])


class WStream:
    def __init__(self, cx, ph, name="w"):
        self.cx = cx
        self.st = ph.sbs(name + "_st", [128, 4, 512], F32, 2)
        self.wb = ph.sbs(name + "_bf", [128, 16, 512], BF16, 2)
        self.n = 0
        self.si = 0

    def fetch(self, w, c0):
        cx = self.cx
        wb = self.wb[self.n % 2]
        self.n += 1
        wv = w.t.rearrange("(k p) c -> p k c", p=128)
        for q in range(4):
            st = self.st[self.si % 2]
            self.si += 1
            cx.dma("sp", st[:, :, :], wv[:, q * 4:(q + 1) * 4, c0:c0 + 512], r=[w], w=[st])
            cx.cp("pool", wb[:, q * 4:(q + 1) * 4, :], st[:, :, :], r=[st], wa=[wb])
        return wb


LAM_INIT0 = 0.8 - 0.6 * math.exp(-0.3 * 0)


HT_DBG = 0
STOPF = {1.3: ("q",), 1.4: ("v",), 1.5: ("za",), 1.6: ("gg", "gv")}


class _Stop(Exception):
    pass


def build_l0(stop=99):
    cx = Ctx()
    try:
        _build_l0(cx, stop)
    except _Stop:
        cx.S.barrier()
    return cx


def _build_l0(cx, stop=99, fused=False, c=None):
    nc = cx.nc
    x_own = cx.din("x_own", [HALF, D])
    x_oth = cx.din("x_oth", [HALF, D])
    ctx_in = cx.din("ctx", [CTX, D])
    ccT_in = cx.din("ccT", [128, 16, 2])
    w_ada = cx.din("w_ada", [D, 3 * D])
    b_adaT_in = cx.din("b_adaT", [128, 48])
    norm_gT_in = cx.din("norm_gT", [128, 16])
    w_in = cx.din("w_in", [D, 7168])
    w_out = cx.din("w_out", [D, D])
    qkg_in = cx.din("qkg", [2, 64])
    lamv_in = cx.din("lamv", [4, 64])
    subg_in = cx.din("subg", [128, 1])
    cwT_in = cx.din("cwT", [128, 8, 31])
    cvp_in = cx.din("cvp", [128, 3, 8])
    rq_own = cx.din("rq_own", [HALF, 128])
    rk_own = cx.din("rk_own", [HALF, 128])
    rk_oth = cx.din("rk_oth", [HALF, 128])
    hmask_in = cx.din("hmask", [128, 2])
    if fused:
        x1_out = cx.dscr("x1_own_s", [HALF, D], F32)
        ctx1_out = cx.dscr("ctx1_s", [CTX, D], F32)
    else:
        ident_in = cx.din("ident", [128, 128])
        x1_out = cx.dout("x1", [HALF, D])
        ctx1_out = cx.dout("ctx1", [CTX, D])
    qT_s = cx.dscr("qT_s", [8, 128, HALF + CTX])
    kT_s = cx.dscr("kT_s", [8, 128, NKEY])
    v_s = cx.dscr("v_s", [NKEY, 1024])
    zaT_s = cx.dscr("zaT_s", [1024, HALF + CTX])
    zbT_s = cx.dscr("zbT_s", [1024, HALF + CTX])
    yT_s = cx.dscr("yT_s", [1024, HALF + 30])
    yTc_s = cx.dscr("yTc_s", [1024, CTX + 30])
    catT_s = cx.dscr("catT_s", [D, HALF + CTX])

    if c is None:
        c = common_consts(cx, ident_in)
    modT, gs, gate_bc = modulation(cx, c, ccT_in, w_ada, b_adaT_in, norm_gT_in)

    qg = cx.sb("qg", [128, 64]); qgs = cx.sb("qgs", [128, 64]); kg = cx.sb("kg", [128, 64])
    lamt = cx.sb("lamt", [128, 4, 64]); lam4 = cx.sb("lam4", [128, 8])
    subg = cx.sb("subg_t", [128, 1])
    hmask = cx.sb("hmask_t", [128, 2])
    cvp = cx.sb("cvp_t", [128, 3, 8])
    cwT = cx.sb("cwT_t", [128, 8, 31])
    cx.dma("sp", qg[:, :], qkg_in[0, :].partition_broadcast(128), r=[qkg_in], w=[qg])
    cx.dma("sp", kg[:, :], qkg_in[1, :].partition_broadcast(128), r=[qkg_in], w=[kg])
    cx.dma("sp", lamt[:, :, :].rearrange("p a d -> p (a d)"),
           lamv_in.t.rearrange("a d -> (a d)").partition_broadcast(128), r=[lamv_in], w=[lamt])
    cx.dma("sp", subg[:, :], subg_in[:, :], r=[subg_in], w=[subg])
    cx.dma("sp", hmask[:, :], hmask_in[:, :], r=[hmask_in], w=[hmask])
    cx.dma("sp", cvp[:, :, :], cvp_in[:, :, :], r=[cvp_in], w=[cvp])
    cx.dma("sp", cwT[:, :, :], cwT_in[:, :, :], r=[cwT_in], w=[cwT])
    cx.ts("dve", qgs[:, :], qg[:, :], 0.125, None, ALU.mult, None, r=[qg], w=[qgs])
    cx.tt("dve", lamt[:, 0, :], lamt[:, 0, :], lamt[:, 1, :], ALU.mult, r=[lamt], w=[lamt])
    cx.tt("dve", lamt[:, 2, :], lamt[:, 2, :], lamt[:, 3, :], ALU.mult, r=[lamt], w=[lamt])
    cx.S.op("dve", lambda: nc.vector.reduce_sum(out=lam4[:, 0:1], in_=lamt[:, 0, :], axis=AX.X), r=[lamt], w=[lam4])
    cx.S.op("dve", lambda: nc.vector.reduce_sum(out=lam4[:, 1:2], in_=lamt[:, 2, :], axis=AX.X), r=[lamt, lam4], w=[lam4])
    cx.act(lam4[:, 2:4], lam4[:, 0:2], AF.Exp, r=[lam4], w=[lam4])
    cx.stt("dve", lam4[:, 4:5], lam4[:, 3:4], -LAM_INIT0, lam4[:, 2:3], ALU.add, ALU.subtract, r=[lam4], w=[lam4])
    neglam = lam4[:, 4:5]
    cx.ts("dve", subg[:, :], subg[:, :], 1.0 - LAM_INIT0, None, ALU.mult, None, r=[subg], w=[subg])

    if stop < 1:
        cx.S.barrier()
        return cx
    with Phase(cx) as ph:
        hT_t = ph.sb("hT", [128, 16, 1280], BF16)
        hTb = [Buf(hT_t.t, f"hT{i}") for i in range(10)]
        hT = hT_t
        hbufs = (ph.sbs("xt", [128, D], F32, 2), ph.sbs("xn", [128, D], BF16, 2), ph.sb("sqj", [128, D], BF16),
                 ph.sbs("ss", [128, 4], F32, 2), ph.pss("pT", [128, 8, 128], BF16, 2))
        ws = WStream(cx, ph)
        pacc = ph.pss("pacc", [128, 512], F32, 3)
        pq = ph.pss("pq", [128, 8, 128], BF16, 2)
        sqs = ph.sbs("sq", [128, 512], F32, 2)
        st8 = ph.sbs("st8", [128, 16], F32, 2)
        xnq = ph.sbs("xnq", [128, 512], F32, 2)
        t1s = ph.sbs("t1", [128, 512], F32, 2)
        t2s = ph.sbs("t2", [128, 512], F32, 2)
        qbs = ph.sbs("qb", [128, 512], BF16, 2)
        qTs = ph.sbs("qTs", [128, 4, 128], BF16, 2)
        rts = ph.sbs("rt", [128, 128], F32, 2)
        vbs = ph.sbs("vb", [128, 512], BF16, 2)
        fos = ph.sbs("fo", [128, 512], BF16, 2)
        sig_t = ph.sb("sig", [128, 4, 1280], BF16)
        sigb = [Buf(sig_t.t, f"sig{i}") for i in range(4)]
        zero_t = ph.sb("zero", [128, 15], BF16)
        cnt = {"acc": 0, "qk": 0, "v": 0, "fo": 0}

        cx.memset("pool", zero_t[:, :], 0.0, w=[zero_t])
        yTc_v = yTc_s.t.rearrange("(j p) c -> p j c", p=128)
        for j in range(8):
            cx.dma("sp", yTc_v[:, j, 0:15], zero_t[:, :], r=[zero_t], wa=[yTc_s])
            cx.dma("sp", yTc_v[:, j, CTX + 15:CTX + 30], zero_t[:, :], r=[zero_t], wa=[yTc_s])
        if stop == 1.1:
            raise _Stop

        def qk_epi(ps, tile, fam, h0):
            kind, idx = tile
            i = cnt["qk"]; cnt["qk"] += 1
            sq, s8, xq, t1, t2, qb, qTt, rt, pqt = sqs[i % 2], st8[i % 2], xnq[i % 2], t1s[i % 2], t2s[i % 2], qbs[i % 2], qTs[i % 2], rts[i % 2], pq[i % 2]
            cx.act(sq[:, :], ps[:, :], AF.Square, r=[ps], w=[sq])
            cx.S.op("dve", lambda: nc.vector.reduce_sum(out=s8[:, 0:8], in_=sq[:, :].rearrange("p (g d) -> p g d", d=64), axis=AX.X), r=[sq], w=[s8])
            cx.act(s8[:, 8:16], s8[:, 0:8], AF.Sqrt, r=[s8, c["eps"]], w=[s8], bias=c["eps"][:, 0:1], scale=1.0 / 64)
            cx.recip(s8[:, 0:8], s8[:, 8:16], r=[s8], w=[s8])
            v3 = lambda a: a.rearrange("p (g d) -> p g d", d=64)
            cx.tt("dve", v3(xq[:, :]), v3(ps[:, :]), s8[:, 0:8].unsqueeze(2).to_broadcast([128, 8, 64]), ALU.mult, r=[ps, s8], w=[xq])
            g = kg if fam == "k" else (qgs if kind == "ctx" else qg)
            cx.tt("pool", v3(xq[:, :]), v3(xq[:, :]), g[:, :].unsqueeze(1).to_broadcast([128, 8, 64]), ALU.mult, r=[xq, g], w=[xq])
            if kind == "ctx":
                cx.cp("act", qb[:, :], xq[:, :], r=[xq], w=[qb])
            else:
                rsrc = (rq_own if fam == "q" else rk_own) if kind == "own" else rk_oth
                cx.dma("sp", rt[:, :], rsrc[idx * 128:(idx + 1) * 128, :], r=[rsrc], w=[rt])
                cx.tt("pool", v3(t1[:, :]), v3(xq[:, :]), rt[:, 0:64].unsqueeze(1).to_broadcast([128, 8, 64]), ALU.mult, r=[xq, rt], w=[t1])
                for a in range(2):
                    lo, hi = a * 32, a * 32 + 16
                    e = "dve" if a == 0 else "pool"
                    cx.tt(e, v3(t2[:, :])[:, :, lo:lo + 16], v3(xq[:, :])[:, :, hi:hi + 16],
                          rt[:, 64 + lo:64 + lo + 16].unsqueeze(1).to_broadcast([128, 8, 16]), ALU.mult, r=[xq, rt], wa=[t2])
                    cx.tt(e, v3(t2[:, :])[:, :, hi:hi + 16], v3(xq[:, :])[:, :, lo:lo + 16],
                          rt[:, 64 + hi:64 + hi + 16].unsqueeze(1).to_broadcast([128, 8, 16]), ALU.mult, r=[xq, rt], wa=[t2])
                cx.tt("dve", qb[:, :], t1[:, :], t2[:, :], ALU.add, r=[t1, t2], w=[qb])
            if fam == "q":
                dst_s = qT_s
                c0 = idx * 128 if kind == "own" else HALF + idx * 128
            else:
                dst_s = kT_s
                c0 = {"ctx": 0, "own": CTX, "oth": CTX + HALF}[kind] + idx * 128

            def fin():
                for hh in range(4):
                    cx.tr(pqt[:, hh, :], qb[:, hh * 128:(hh + 1) * 128], c["identb"][:, :], r=[qb, c["identb"]], wa=[pqt], sig=(hh == 3))
                cx.cp("dve", qTt[:, :, :], pqt[:, 0:4, :], r=[pqt], w=[qTt])
                cx.dma(STQ, dst_s.t[h0:h0 + 4, :, c0:c0 + 128].rearrange("h p t -> p h t"), qTt[:, :, :], r=[qTt], wa=[dst_s])
            cx.defer(fin)

        def v_epi(ps, tile, cb2):
            kind, idx = tile
            i = cnt["v"]; cnt["v"] += 1
            vb = vbs[i % 2]
            r0 = {"ctx": 0, "own": CTX, "oth": CTX + HALF}[kind] + idx * 128
            cx.cp("act", vb[:, :], ps[:, :], r=[ps], w=[vb])
            cx.defer(lambda: cx.dma(STQ, v_s[r0:r0 + 128, cb2 * 512:(cb2 + 1) * 512], vb[:, :], r=[vb], wa=[v_s]))

        own = lambda a, b: [("own", i) for i in range(a, b)]
        blocks = [
            dict(tiles=own(0, 8) + [("ctx", 0), ("ctx", 1)], full=10, fams="all",
                 chunks=[(0, 512, "own", 0), (512, 512, "own", 512), (1024, 256, "ctx", 0)]),
            dict(tiles=own(8, 16) + [("oth", 0), ("oth", 15)], full=8, fams="all",
                 chunks=[(0, 512, "own", 1024), (512, 512, "own", 1536), (1024, 256, "halo", 0)]),
            dict(tiles=[("oth", i) for i in range(1, 8)], full=0, fams="kv", chunks=[]),
            dict(tiles=[("oth", i) for i in range(8, 15)], full=0, fams="kv", chunks=[]),
        ]
        srcmap = {"own": (x_own, 0), "oth": (x_oth, 0), "ctx": (ctx_in, 1)}
        colblocks = [("q", 0, 0), ("q", 512, 1), ("k", 1024, 0), ("k", 1536, 1), ("v", 2048, 0), ("v", 2560, 1),
                     ("za", 3072, 0), ("za", 3584, 1), ("gg", 5120, 0), ("gv", 4096, 0), ("gg", 5632, 1), ("gv", 4608, 1),
                     ("zb", 6144, 0), ("zb", 6656, 1)]
        for blk in blocks:
            tiles = blk["tiles"]
            build_hT(cx, hbufs, c, modT, gs, hT, hTb, [(srcmap[k][0], i * 128, srcmap[k][1]) for (k, i) in tiles])
            if stop == 1.2:
                raise _Stop
            cbl = colblocks if blk["fams"] == "all" else [cb for cb in colblocks if cb[0] in ("k", "v")]
            if stop in STOPF:
                cbl = [cb for cb in cbl if cb[0] in STOPF[stop]]
            wnext = ws.fetch(w_in, cbl[0][1])
            for ci, (fam, c0, sub) in enumerate(cbl):
                wb = wnext
                if ci + 1 < len(cbl):
                    wnext = ws.fetch(w_in, cbl[ci + 1][1])
                if fam in ("q", "k", "v"):
                    for ti, tile in enumerate(tiles):
                        if tile[0] == "oth" and fam == "q":
                            continue
                        ps = pacc[cnt["acc"] % 3]; cnt["acc"] += 1
                        for k in range(16):
                            cx.mm(ps[:, :], hT[:, k, ti * 128:(ti + 1) * 128], wb[:, k, :], start=(k == 0), stop=(k == 15),
                                  r=[hTb[ti], wb], wa=[ps], sig=(k == 15))
                        pend, cx.later = cx.later, []
                        if fam == "v":
                            v_epi(ps, tile, sub)
                        else:
                            qk_epi(ps, tile, fam, sub * 4)
                        for fn in pend:
                            fn()
                    cx.flush()
                else:
                    for (h0c, n, ckind, d0) in blk["chunks"]:
                        if ckind == "halo" and fam not in ("gg", "gv"):
                            continue
                        t0 = h0c // 128
                        rb = [hTb[t] for t in range(t0, t0 + (n + 127) // 128)]
                        for fc in range(4):
                            ps = pacc[cnt["acc"] % 3]; cnt["acc"] += 1
                            for k in range(16):
                                cx.mm(ps[:, 0:n], wb[:, k, fc * 128:(fc + 1) * 128], hT[:, k, h0c:h0c + n], start=(k == 0), stop=(k == 15),
                                      r=rb + [wb], wa=[ps], sig=(k == 15))
                            frow = sub * 512 + fc * 128
                            dcol = d0 if ckind == "own" else HALF + d0
                            if fam in ("za", "zb"):
                                fo = fos[cnt["fo"] % 2]; cnt["fo"] += 1
                                dst = zaT_s if fam == "za" else zbT_s
                                cx.act(fo[:, 0:n], ps[:, 0:n], AF.Silu, r=[ps], w=[fo])
                                cx.dma(STQ, dst[frow:frow + 128, dcol:dcol + n], fo[:, 0:n], r=[fo], wa=[dst])
                            elif fam == "gg":
                                cx.act(sig_t[:, fc, h0c:h0c + n], ps[:, 0:n], AF.Sigmoid, r=[ps], wa=[sigb[fc]])
                            else:
                                fo = fos[cnt["fo"] % 2]; cnt["fo"] += 1
                                cx.tt("dve", fo[:, 0:n], ps[:, 0:n], sig_t[:, fc, h0c:h0c + n], ALU.mult, r=[ps, sigb[fc]], w=[fo])
                                if ckind == "own":
                                    cx.dma(STQ, yT_s[frow:frow + 128, 15 + d0:15 + d0 + n], fo[:, 0:n], r=[fo], wa=[yT_s])
                                elif ckind == "ctx":
                                    cx.dma(STQ, yTc_s[frow:frow + 128, 15 + d0:15 + d0 + n], fo[:, 0:n], r=[fo], wa=[yTc_s])
                                else:
                                    cx.ts("dve", fo[:, 0:15], fo[:, 0:15], hmask[:, 1:2], None, ALU.mult, None, r=[fo, hmask], w=[fo])
                                    cx.ts("dve", fo[:, 241:256], fo[:, 241:256], hmask[:, 0:1], None, ALU.mult, None, r=[fo, hmask], w=[fo])
                                    cx.dma(STQ, yT_s[frow:frow + 128, 15 + HALF:30 + HALF], fo[:, 0:15], r=[fo], wa=[yT_s])
                                    cx.dma(STQ, yT_s[frow:frow + 128, 0:15], fo[:, 241:256], r=[fo], wa=[yT_s])
            cx.flush()
            if stop in STOPF:
                raise _Stop

    if stop < 2:
        return cx
    with Phase(cx) as ph:
        kTs = ph.sbs("kTh", [128, NKEY], BF16, 2)
        vhs = ph.sbs("vh", [128, 34, 128], BF16, 2)
        qhs = ph.sbs("qh", [128, HALF + CTX], BF16, 2)
        pST = ph.pss("pST", [128, 2, 512], F32, 2)
        po = ph.ps("po", [128, 2, 512], F32)
        psm = ph.ps("psm", [128, 2, 512], F32)
        pts = ph.sbs("pt", [128, 2, 512], BF16, 3)
        accs = ph.sbs("accP", [128, 2, 512], F32, 2)
        accm = [[Buf(a.t, a.name + "m0"), Buf(a.t, a.name + "m1")] for a in accs]
        rs = ph.sb("rs", [128, 2, 512], F32)
        ta = ph.sb("ta", [128, 512], F32); tb = ph.sb("tb", [128, 512], F32)
        ot = ph.sb("ot", [128, 512], F32); sqo = ph.sb("sqo", [128, 512], F32)
        rstd = ph.sb("rstdo", [128, 512], F32)
        zat = ph.sbs("zat", [128, 512], BF16, 2)
        obs = ph.sbs("ob", [128, 512], BF16, 2)
        v_v = v_s.t.rearrange("(kt p) c -> p kt c", p=128)
        step = 0
        ci = 0
        def load_head(h):
            cx.dma("sp", kTs[h % 2][:, :], kT_s[h, :, :], r=[kT_s], w=[kTs[h % 2]])
            cx.dma("sp", vhs[h % 2][:, :, :], v_v[:, :, h * 128:(h + 1) * 128], r=[v_s], w=[vhs[h % 2]])
            cx.dma("sp", qhs[h % 2][:, :], qT_s[h, :, :], r=[qT_s], w=[qhs[h % 2]])

        load_head(0)
        for h in range(8):
            kTh, vh, qh = kTs[h % 2], vhs[h % 2], qhs[h % 2]
            if h + 1 < 8:
                load_head(h + 1)
            for (q0, n, nkt) in [(0, 512, 34), (512, 512, 34), (1024, 512, 34), (1536, 512, 34), (HALF, 256, 2)]:
                acc = accs[ci % 2]
                za = zat[ci % 2]
                cx.dma("sp", za[:, 0:n], zaT_s[h * 128:(h + 1) * 128, q0:q0 + n], r=[zaT_s], w=[za])
                for kt in range(nkt):
                    st = pST[step % 2]
                    pt = pts[step % 3]
                    step += 1
                    for m in range(2):
                        cx.mm(st[:, m, 0:n], kTh[64 * m:64 * m + 64, kt * 128:(kt + 1) * 128], qh[64 * m:64 * m + 64, q0:q0 + n],
                              True, True, r=[kTh, qh], wa=[st], sig=(m == 1))
                    cx.act(pt[:, :, 0:n], st[:, :, 0:n], AF.Exp, r=[st], w=[pt])
                    for m in range(2):
                        cx.mm(po[:, m, 0:n], vh[:, kt, :], pt[:, m, 0:n], start=(kt == 0), stop=(kt == nkt - 1),
                              r=[vh, pt], wa=[po], sig=(m == 1))
                    for m, e in ((0, "dve"), (1, "pool")):
                        am = accm[ci % 2][m]
                        if kt == 0:
                            cx.cp(e, acc[:, m, 0:n], pt[:, m, 0:n], r=[pt], w=[am])
                        else:
                            cx.tt(e, acc[:, m, 0:n], acc[:, m, 0:n], pt[:, m, 0:n], ALU.add, r=[pt, am], w=[am])
                for m in range(2):
                    cx.mm(psm[:, m, 0:n], c["onesf"][:, :], acc[:, m, 0:n], True, True, r=[c["onesf"], accm[ci % 2][m]], wa=[psm], sig=(m == 1))
                cx.recip(rs[:, :, 0:n], psm[:, :, 0:n], r=[psm], w=[rs])
                cx.tt("dve", ta[:, 0:n], po[:, 0, 0:n], rs[:, 0, 0:n], ALU.mult, r=[po, rs], w=[ta])
                cx.tt("dve", tb[:, 0:n], po[:, 1, 0:n], rs[:, 1, 0:n], ALU.mult, r=[po, rs], w=[tb])
                cx.stt("dve", ot[:, 0:n], tb[:, 0:n], neglam, ta[:, 0:n], ALU.mult, ALU.add, r=[ta, tb, lam4], w=[ot])
                cx.act(sqo[:, 0:n], ot[:, 0:n], AF.Square, r=[ot], w=[sqo])
                cx.mm(psm[:, 0, 0:n], c["onesf"][:, :], sqo[:, 0:n], True, True, r=[c["onesf"], sqo], w=[psm])
                cx.act(rstd[:, 0:n], psm[:, 0, 0:n], AF.Sqrt, r=[psm, c["eps"]], w=[rstd], bias=c["eps"][:, 0:1], scale=1.0 / 128)
                cx.recip(rstd[:, 0:n], rstd[:, 0:n], r=[rstd], w=[rstd])
                cx.tt("dve", ot[:, 0:n], ot[:, 0:n], rstd[:, 0:n], ALU.mult, r=[ot, rstd], w=[ot])
                ob = obs[ci % 2]
                cx.stt("dve", ob[:, 0:n], ot[:, 0:n], subg[:, 0:1], za[:, 0:n], ALU.mult, ALU.mult, r=[ot, subg, za], w=[ob])
                cx.dma(STQ, catT_s[h * 128:(h + 1) * 128, q0:q0 + n], ob[:, 0:n], r=[ob], wa=[catT_s])
                ci += 1

    if stop < 3:
        return cx
    with Phase(cx) as ph:
        dg = ph.sb("dgc", [128, 8, 31, 128], BF16)
        ycs = ph.sbs("yc", [128, 8, 542], BF16, 2)
        convT = ph.sb("convT", [128, 8, 512], F32)
        sqT = ph.sb("sqT", [128, 8, 512], F32)
        onesc = ph.sb("onesc", [128, 128], F32)
        pconv = ph.pss("pconv", [128, 512], F32, 2)
        pst = ph.ps("pstat", [128, 2, 512], F32)
        mean = ph.sb("mean", [128, 512], F32); var = ph.sb("var", [128, 512], F32)
        tcs = ph.sbs("tc", [128, 512], F32, 2)
        scs = ph.sbs("sc", [128, 512], F32, 2)
        zbt = ph.sbs("zbt", [128, 512], BF16, 2)
        obs = ph.sbs("obc", [128, 512], BF16, 2)
        cx.memset("pool", onesc[:, :], 1.0 / 1024, w=[onesc])
        i = 0
        for j in range(8):
            for k in range(31):
                e = "dve" if i % 2 == 0 else "pool"
                cx.ts(e, dg[:, j, k, :], c["identf"][:, :], cwT[:, j, k:k + 1], None, ALU.mult, None, r=[c["identf"], cwT], wa=[dg])
                i += 1
        yT_v = yT_s.t.rearrange("(j p) c -> p j c", p=128)
        it = 0
        for (src, srcb, c0, n, dcol) in [(yT_v, yT_s, 0, 512, 0), (yT_v, yT_s, 512, 512, 512), (yT_v, yT_s, 1024, 512, 1024),
                                         (yT_v, yT_s, 1536, 512, 1536), (yTc_v, yTc_s, 0, 256, HALF)]:
            yc = ycs[it % 2]; it += 1
            cx.dma("sp", yc[:, :, 0:n + 30], src[:, :, c0:c0 + n + 30], r=[srcb], w=[yc])
            for j in range(8):
                pc = pconv[j % 2]
                for k in range(31):
                    cx.mm(pc[:, 0:n], dg[:, j, k, :], yc[:, j, k:k + n], start=(k == 0), stop=(k == 30), r=[dg, yc], wa=[pc], sig=(k == 30))
                cx.act(convT[:, j, 0:n], pc[:, 0:n], AF.Identity, r=[pc, cvp], wa=[convT], bias=cvp[:, 0, j:j + 1], scale=1.0)
                cx.act(sqT[:, j, 0:n], convT[:, j, 0:n], AF.Square, r=[convT], wa=[sqT])
            for j in range(8):
                cx.mm(pst[:, 0, 0:n], onesc[:, :], convT[:, j, 0:n], start=(j == 0), stop=(j == 7), r=[onesc, convT], wa=[pst], sig=(j == 7))
            for j in range(8):
                cx.mm(pst[:, 1, 0:n], onesc[:, :], sqT[:, j, 0:n], start=(j == 0), stop=(j == 7), r=[onesc, sqT], wa=[pst], sig=(j == 7))
            cx.cp("act", mean[:, 0:n], pst[:, 0, 0:n], r=[pst], w=[mean])
            cx.tt("dve", var[:, 0:n], mean[:, 0:n], mean[:, 0:n], ALU.mult, r=[mean], w=[var])
            cx.tt("dve", var[:, 0:n], pst[:, 1, 0:n], var[:, 0:n], ALU.subtract, r=[pst, var], w=[var])
            cx.ts("dve", var[:, 0:n], var[:, 0:n], 0.0, None, ALU.max, None, r=[var], w=[var])
            cx.act(var[:, 0:n], var[:, 0:n], AF.Sqrt, r=[var, c["eps"]], w=[var], bias=c["eps"][:, 0:1], scale=1.0)
            cx.recip(var[:, 0:n], var[:, 0:n], r=[var], w=[var])
            for j in range(8):
                tcb, scb, zb, ob = tcs[j % 2], scs[j % 2], zbt[j % 2], obs[j % 2]
                cx.dma("sp", zb[:, 0:n], zbT_s[j * 128:(j + 1) * 128, dcol:dcol + n], r=[zbT_s], w=[zb])
                cx.tt("dve", tcb[:, 0:n], convT[:, j, 0:n], mean[:, 0:n], ALU.subtract, r=[convT, mean], w=[tcb])
                cx.tt("pool", tcb[:, 0:n], tcb[:, 0:n], var[:, 0:n], ALU.mult, r=[tcb, var], w=[tcb])
                cx.act(scb[:, 0:n], tcb[:, 0:n], AF.Silu, r=[tcb, cvp], w=[scb], scale=cvp[:, 1, j:j + 1], bias=cvp[:, 2, j:j + 1])
                cx.tt("dve", ob[:, 0:n], scb[:, 0:n], zb[:, 0:n], ALU.mult, r=[scb, zb], w=[ob])
                cx.dma(STQ, catT_s[1024 + j * 128:1024 + (j + 1) * 128, dcol:dcol + n], ob[:, 0:n], r=[ob], wa=[catT_s])

    if stop < 4:
        return cx
    out_proj(cx, w_out, catT_s, gate_bc,
             [(x_own, x1_out, t * 128, t * 128, 0) for t in range(16)] + [(ctx_in, ctx1_out, t * 128, HALF + t * 128, 1) for t in range(2)])
    return x1_out, ctx1_out


def out_proj(cx, w_out, catT_s, gate_bc, tiles, blend=None, loader=None):
    with Phase(cx) as ph:
        wo = ph.sb("wo", [128, 16, D], BF16)
        wst = ph.sbs("wo_st", [128, 2, D], F32, 2)
        cats = ph.sbs("catt", [128, 16, 128], BF16, 2)
        catab = (ph.sbs("catA", [128, 16, 128], BF16, 2), ph.sbs("catB", [128, 16, 128], BF16, 2)) if blend is not None else None
        xts = ph.sbs("xto", [128, D], F32, 2)
        xos = ph.sbs("xoo", [128, D], F32, 2)
        tmp = ph.sbs("tmpo", [128, 512], F32, 2)
        pacc = ph.pss("pacco", [128, 512], F32, 3)
        wv = w_out.t.rearrange("(k p) c -> p k c", p=128)
        for q in range(8):
            st = wst[q % 2]
            cx.dma("sp", st[:, :, :], wv[:, q * 2:(q + 1) * 2, :], r=[w_out], w=[st])
            cx.cp("pool", wo[:, q * 2:(q + 1) * 2, :], st[:, :, :], r=[st], wa=[wo])
        cat_v = catT_s.t.rearrange("(k p) c -> p k c", p=128) if catT_s is not None else None
        n = 0
        for ti, (xsrc, xdst, r0, c0, j) in enumerate(tiles):
            cat, xt, xo = cats[ti % 2], xts[ti % 2], xos[ti % 2]
            if loader is None:
                loader = lambda dst, col: cx.dma("sp", dst[:, :, :], cat_v[:, :, col:col + 128], r=[catT_s], w=[dst])
            if blend is None:
                loader(cat, c0)
            else:
                ca, cb_ = catab[0][ti % 2], catab[1][ti % 2]
                loader(ca, c0)
                loader(cb_, HALF + c0)
                cx.ts("pool", ca[:, :, :], ca[:, :, :], blend[:, 0:1], None, ALU.mult, None, r=[ca, blend], w=[ca])
                cx.stt("dve", cat[:, :, :], cb_[:, :, :], blend[:, 1:2], ca[:, :, :], ALU.mult, ALU.add, r=[ca, cb_, blend], w=[cat])
            cx.dma("sp", xt[:, :], xsrc[r0:r0 + 128, :], r=[xsrc], w=[xt])
            for cb in range(4):
                ps = pacc[n % 3]
                tm = tmp[n % 2]
                n += 1
                for k in range(16):
                    cx.mm(ps[:, :], cat[:, k, :], wo[:, k, cb * 512:(cb + 1) * 512], start=(k == 0), stop=(k == 15), r=[cat, wo], wa=[ps], sig=(k == 15))
                cx.tt("dve", tm[:, :], ps[:, :], gate_bc[:, j, cb * 512:(cb + 1) * 512], ALU.mult, r=[ps, gate_bc], w=[tm])
                cx.tt("pool", xo[:, cb * 512:(cb + 1) * 512], tm[:, :], xt[:, cb * 512:(cb + 1) * 512], ALU.add, r=[tm, xt], wa=[xo])
            cx.dma(STQ, xdst[r0:r0 + 128, :], xo[:, :], r=[xo], wa=[xdst])


T1 = CTX + SEQ
NT1 = T1 // 128
NCH = T1 // 64


def build_l1a(stop=99):
    cx = Ctx()
    _build_l1a(cx, stop)
    return cx


def _build_l1a(cx, stop=99, fused=False, c=None, x1_tiles=None, ctx1_in=None):
    nc = cx.nc
    sfx = "1" if fused else ""
    if not fused:
        x1_in = cx.din("x1f", [SEQ, D])
        ctx1_in = cx.din("ctx1", [CTX, D])
        x1_tiles = [(x1_in, t * 128) for t in range(32)]
    ccT_in = cx.din("ccT" + sfx, [128, 16, 2])
    w_ada = cx.din("w_ada" + sfx, [D, 3 * D])
    b_adaT_in = cx.din("b_adaT" + sfx, [128, 48])
    norm_gT_in = cx.din("norm_gT" + sfx, [128, 16])
    w_c = cx.din("w_c", [D, 5120])
    lbg_in = cx.din("lbg", [128, 2, 2, 8])
    ong_in = cx.din("ong", [128, 1])
    maskF_in = cx.din("maskF", [128, 128])
    maskB_in = cx.din("maskB", [128, 128])
    if fused:
        catT1_out = cx.dscr("catT1_s", [1024, SEQ], BF16)
    else:
        ident_in = cx.din("ident", [128, 128])
        catT1_out = cx.dout("catT1", [1024, SEQ], BF16)
        modT_out = cx.dout("modT1", [128, 48, 2], F32)
    qS = cx.dscr("qS", [8, 128, T1], F32)
    sgS = cx.dscr("sgS", [2, 8, 128, T1], F32)
    vS = cx.dscr("vS", [8, 128, NT1, 128], BF16)
    zS = cx.dscr("zS", [8, 128, T1], BF16)

    if c is None:
        c = common_consts(cx, ident_in)
    modT, gs, _ = modulation(cx, c, ccT_in, w_ada, b_adaT_in, norm_gT_in, want_gate_bc=(), sfx="1")
    if not fused:
        cx.dma("sp", modT_out[:, :, :], modT[:, :, :], r=[modT], w=[modT_out])
    lbt = cx.sb("lbt", [128, 2, 2, 8]); lb = cx.sb("lb", [128, 2, 8]); oml = cx.sb("oml", [128, 2, 8]); noml = cx.sb("noml", [128, 2, 8])
    ong = cx.sb("ong_t", [128, 1]); maskF = cx.sb("maskF_t", [128, 128]); maskB = cx.sb("maskB_t", [128, 128])
    cx.dma("sp", lbt[:, :, :, :], lbg_in[:, :, :, :], r=[lbg_in], w=[lbt])
    cx.dma("sp", ong[:, :], ong_in[:, :], r=[ong_in], w=[ong])
    cx.dma("sp", maskF[:, :], maskF_in[:, :], r=[maskF_in], w=[maskF])
    cx.dma("sp", maskB[:, :], maskB_in[:, :], r=[maskB_in], w=[maskB])
    cx.tt("dve", lb[:, :, :], lbt[:, :, 1, :], lbt[:, :, 0, :], ALU.subtract, r=[lbt], w=[lb])
    cx.act(lb[:, :, :], lb[:, :, :], AF.Sigmoid, r=[lb], w=[lb])
    cx.ts("dve", oml[:, :, :], lb[:, :, :], -1.0, 1.0, ALU.mult, ALU.add, r=[lb], w=[oml])
    cx.ts("dve", noml[:, :, :], lb[:, :, :], -1.0, None, ALU.add, None, r=[lb], w=[noml])
    if stop < 1:
        cx.S.barrier()
        return cx

    with Phase(cx) as ph:
        hT = ph.sb("hT", [128, 16, 1280], BF16)
        hTb = [Buf(hT.t, f"hT{i}") for i in range(10)]
        hbufs = (ph.sbs("xt", [128, D], F32, 2), ph.sbs("xn", [128, D], BF16, 2), ph.sb("sqj", [128, D], BF16),
                 ph.sbs("ss", [128, 4], F32, 2), ph.pss("pT", [128, 8, 128], BF16, 2))
        ws = WStream(cx, ph)
        pacc = ph.pss("pacc", [128, 512], F32, 3)
        vbs = ph.sbs("vb", [128, 512], BF16, 2)
        fof = ph.sbs("fof", [128, 512], F32, 3)
        fob = ph.sbs("fob", [128, 512], BF16, 2)
        cnt = {"acc": 0, "v": 0, "f": 0, "b": 0}
        colblocks = [("q", 0, 0), ("q", 512, 1), ("i", 1024, 0), ("i", 1536, 1), ("uf", 2048, 0), ("uf", 2560, 1),
                     ("ub", 3072, 0), ("ub", 3584, 1), ("z", 4096, 0), ("z", 4608, 1)]
        for t0 in range(0, NT1, 10):
            tl = list(range(t0, min(t0 + 10, NT1)))
            tiles = [(ctx1_in, t * 128, 1) if t < 2 else (x1_tiles[t - 2][0], x1_tiles[t - 2][1], 0) for t in tl]
            build_hT(cx, hbufs, c, modT, gs, hT, hTb, tiles)
            ntok = len(tl) * 128
            chunks = [(a, min(512, ntok - a)) for a in range(0, ntok, 512)]
            wnext = ws.fetch(w_c, colblocks[0][1])
            for ci, (fam, c0, sub) in enumerate(colblocks):
                wb = wnext
                if ci + 1 < len(colblocks):
                    wnext = ws.fetch(w_c, colblocks[ci + 1][1])
                if fam == "i":
                    for ti, t in enumerate(tl):
                        ps = pacc[cnt["acc"] % 3]; cnt["acc"] += 1
                        for k in range(16):
                            cx.mm(ps[:, :], hT[:, k, ti * 128:(ti + 1) * 128], wb[:, k, :], start=(k == 0), stop=(k == 15),
                                  r=[hTb[ti], wb], wa=[ps], sig=(k == 15))
                        vb = vbs[cnt["v"] % 2]; cnt["v"] += 1
                        cx.cp("act", vb[:, :], ps[:, :], r=[ps], w=[vb])
                        cx.dma(STQ, vS.t[sub * 4:sub * 4 + 4, :, t, :].rearrange("h p c -> p h c"),
                               vb[:, :].rearrange("p (h c) -> p h c", c=128), r=[vb], wa=[vS])
                else:
                    for (h0c, n) in chunks:
                        tt0 = h0c // 128
                        rb = [hTb[x] for x in range(tt0, tt0 + (n + 127) // 128)]
                        g0 = t0 * 128 + h0c
                        for fc in range(4):
                            ps = pacc[cnt["acc"] % 3]; cnt["acc"] += 1
                            for k in range(16):
                                cx.mm(ps[:, 0:n], wb[:, k, fc * 128:(fc + 1) * 128], hT[:, k, h0c:h0c + n], start=(k == 0), stop=(k == 15),
                                      r=rb + [wb], wa=[ps], sig=(k == 15))
                            h = sub * 4 + fc
                            if fam == "z":
                                fo = fob[cnt["b"] % 2]; cnt["b"] += 1
                                cx.act(fo[:, 0:n], ps[:, 0:n], AF.Silu, r=[ps], w=[fo])
                                cx.dma(STQ, zS[h, :, g0:g0 + n], fo[:, 0:n], r=[fo], wa=[zS])
                            else:
                                fo = fof[cnt["f"] % 3]; cnt["f"] += 1
                                cx.act(fo[:, 0:n], ps[:, 0:n], AF.Silu if fam == "q" else AF.Sigmoid, r=[ps], w=[fo])
                                if fam == "q":
                                    cx.dma(STQ, qS[h, :, g0:g0 + n], fo[:, 0:n], r=[fo], wa=[qS])
                                else:
                                    cx.dma(STQ, sgS[0 if fam == "uf" else 1, h, :, g0:g0 + n], fo[:, 0:n], r=[fo], wa=[sgS])
    if stop < 2:
        return cx

    with Phase(cx) as ph:
        msk = ph.sb("msk", [128, T1], F32)
        qf = ph.sb("qf", [128, T1], F32)
        vh = ph.sb("vh1", [128, NT1, 128], BF16)
        oacc = ph.sb("oacc", [128, SEQ], F32)
        lf = ph.sb("lf", [128, T1], F32)
        kf = ph.sb("kf", [128, T1], F32)
        bc = ph.sb("bcum", [128, T1], F32)
        ef = ph.sb("ef", [128, T1], F32)
        qdec = ph.sb("qdec", [128, T1], BF16)
        kinv = ph.sb("kinv", [128, T1], BF16)
        kendT = ph.sb("kendT", [128, T1], BF16)
        kend = ph.sb("kend", [128, NT1, 128], BF16)
        dec = ph.sb("dec", [128, NCH], F32)
        bend = ph.sb("bend", [128, NCH], F32)
        Sf = ph.sb("Sf", [128, 128], F32)
        Sbs = ph.sbs("Sb", [128, 128], BF16, 4)
        Ams = ph.sbs("Am", [128, 128], BF16, 3)
        sqr = ph.sb("sqr", [128, 512], F32); rst = ph.sb("rst", [128, 512], F32); onb = ph.sb("onb", [128, 512], F32)
        zcs = ph.sbs("zc", [128, 512], BF16, 2); obs = ph.sbs("ob1", [128, 512], BF16, 2)
        pA = ph.pss("pA", [128, 512], F32, 1)
        po = ph.pss("po1", [128, 512], F32, 2)
        pS = ph.pss("pS", [128, 512], F32, 4)
        pK = ph.pss("pK", [128, 8, 128], BF16, 1)
        ia_ = [0, 0]
        cx.memset("pool", msk[:, :], 1.0, w=[msk])
        cx.memset("pool", msk[:, :].rearrange("p (c d) -> p c d", d=64)[:, :, 0:1], 0.0, w=[msk])
        v3 = lambda a: a.rearrange("p (c d) -> p c d", d=64)
        HT = T1 // 2
        ia = 0
        isb = 0
        ipo = 0
        for h in range(8):
            cx.dma("sp", qf[:, :], qS[h, :, :], r=[qS], w=[qf])
            cx.dma("sp", vh[:, :, :], vS[h, :, :, :], r=[vS], w=[vh])
            for d_ in range(2):
                cx.dma("sp", lf[:, :], sgS[d_, h, :, :], r=[sgS], w=[lf])
                cx.ts("dve", kf[:, :], lf[:, :], noml[:, d_, h:h + 1], oml[:, d_, h:h + 1], ALU.mult, ALU.add, r=[lf, noml, oml], w=[kf])
                cx.act(lf[:, :], lf[:, :], AF.Ln, r=[lf, oml, lb], w=[lf], scale=oml[:, d_, h:h + 1], bias=lb[:, d_, h:h + 1])
                for hh in range(2):
                    sl = slice(hh * HT, (hh + 1) * HT)
                    cx.S.op("dve", lambda: nc.vector.tensor_tensor_scan(out=bc[:, sl], data0=msk[:, sl], data1=lf[:, sl], initial=0.0,
                                                                        op0=ALU.mult, op1=ALU.add), r=[msk, lf], wa=[bc])
                cx.cp("dve", bend[:, :], v3(bc[:, :])[:, :, 63], r=[bc], w=[bend])
                bend_bc = bend[:, :].unsqueeze(2).to_broadcast([128, NCH, 64])
                if d_ == 1:
                    cx.tt("pool", v3(ef[:, :]), v3(lf[:, :]), bend_bc, ALU.add, r=[lf, bend], w=[ef])
                    cx.tt("dve", bc[:, :], ef[:, :], bc[:, :], ALU.subtract, r=[ef, bc], w=[bc])
                cx.act(dec[:, :], bend[:, :], AF.Exp, r=[bend], w=[dec])
                cx.act(ef[:, :], bc[:, :], AF.Exp, r=[bc], w=[ef])
                cx.tt("dve", qdec[:, :], qf[:, :], ef[:, :], ALU.mult, r=[qf, ef], w=[qdec])
                cx.act(ef[:, :], bc[:, :], AF.Exp, r=[bc], w=[ef], scale=-1.0)
                cx.tt("dve", kinv[:, :], kf[:, :], ef[:, :], ALU.mult, r=[kf, ef], w=[kinv])
                cx.tt("pool", v3(ef[:, :]), bend_bc, v3(bc[:, :]), ALU.subtract, r=[bend, bc], w=[ef])
                cx.act(ef[:, :], ef[:, :], AF.Exp, r=[ef], w=[ef])
                cx.tt("dve", kendT[:, :], kf[:, :], ef[:, :], ALU.mult, r=[kf, ef], w=[kendT])
                for g in range(0, NT1, 8):
                    pk = pK[0]
                    ng = min(8, NT1 - g)
                    for t in range(ng):
                        cx.tr(pk[:, t, :], kendT[:, (g + t) * 128:(g + t + 1) * 128], c["identb"][:, :], r=[kendT, c["identb"]], wa=[pk], sig=(t == ng - 1))
                    cx.cp("pool" if False else "act", kend[:, g:g + ng, :], pk[:, 0:ng, :], r=[pk], wa=[kend])
                cx.memset("pool", Sf[:, :], 0.0, w=[Sf])
                Sb = Sbs[isb % 4]; isb += 1
                cx.memset("pool", Sb[:, :], 0.0, w=[Sb])
                mask = maskF if d_ == 0 else maskB
                pairs = list(range(NT1)) if d_ == 0 else [1, 0] + list(range(NT1 - 1, 1, -1))
                order = (0, 1) if d_ == 0 else (1, 0)
                st = {}

                def early(p):
                    t0_ = p * 128
                    e_ = {}
                    if p >= 2:
                        a_ps = pA[0]
                        Am = Ams[ia_[0] % 3]; ia_[0] += 1
                        cx.mm(a_ps[:, 0:128], kinv[:, t0_:t0_ + 128], qdec[:, t0_:t0_ + 128], True, True, r=[kinv, qdec], w=[a_ps])
                        cx.tt("dve", Am[:, :], a_ps[:, 0:128], mask[:, :], ALU.mult, r=[a_ps, mask], w=[Am])
                        e_["Am"] = Am
                    e_["s"] = []
                    for hc in order:
                        pr = slice(hc * 64, hc * 64 + 64)
                        s_ps = pS[ia_[1] % 4]; ia_[1] += 1
                        cx.mm(s_ps[:, 0:128], kend[pr, p, :], vh[pr, p, :], True, True, r=[kend, vh], w=[s_ps])
                        e_["s"].append(s_ps)
                    st[p] = e_

                early(pairs[0])
                for pi, p in enumerate(pairs):
                    if pi + 1 < len(pairs):
                        early(pairs[pi + 1])
                    e_ = st.pop(p)
                    lat = p >= 2
                    t0_ = p * 128
                    if lat:
                        o_ps = po[ipo % 2]; ipo += 1
                        cx.mm(o_ps[:, 0:128], vh[:, p, :], e_["Am"][:, :], True, False, r=[vh, e_["Am"]], wa=[o_ps], sig=False)
                    for oi, hc in enumerate(order):
                        c64 = slice(t0_ + hc * 64, t0_ + hc * 64 + 64)
                        if lat:
                            cx.mm(o_ps[:, hc * 64:hc * 64 + 64], Sb[:, :], qdec[:, c64], False, (oi == 1), r=[Sb, qdec], wa=[o_ps], sig=True)
                        s_ps = e_["s"][oi]
                        dcol = dec[:, 2 * p + hc:2 * p + hc + 1]
                        Sb = Sbs[isb % 4]; isb += 1
                        cx.stt("dve", Sb[:, :], Sf[:, :], dcol, s_ps[:, 0:128], ALU.mult, ALU.add, r=[Sf, dec, s_ps], w=[Sb])
                        cx.stt("dve", Sf[:, :], Sf[:, :], dcol, s_ps[:, 0:128], ALU.mult, ALU.add, r=[Sf, dec, s_ps], w=[Sf])
                    if lat:
                        oc = slice((p - 2) * 128, (p - 1) * 128)
                        if d_ == 0:
                            cx.cp("act", oacc[:, oc], o_ps[:, 0:128], r=[o_ps], wa=[oacc])
                        else:
                            cx.tt("dve", oacc[:, oc], oacc[:, oc], o_ps[:, 0:128], ALU.add, r=[o_ps, oacc], wa=[oacc])
            for q8 in range(8):
                cs = slice(q8 * 512, (q8 + 1) * 512)
                a_ps = pA[0]
                zc = zcs[q8 % 2]; ob = obs[q8 % 2]
                cx.dma("sp", zc[:, :], zS[h, :, CTX + q8 * 512:CTX + (q8 + 1) * 512], r=[zS], w=[zc])
                cx.act(sqr[:, :], oacc[:, cs], AF.Square, r=[oacc], w=[sqr])
                cx.mm(a_ps[:, :], c["onesf"][:, :], sqr[:, :], True, True, r=[c["onesf"], sqr], w=[a_ps])
                cx.act(rst[:, :], a_ps[:, :], AF.Sqrt, r=[a_ps, c["eps"]], w=[rst], bias=c["eps"][:, 0:1], scale=1.0 / 128)
                cx.recip(rst[:, :], rst[:, :], r=[rst], w=[rst])
                cx.tt("dve", onb[:, :], oacc[:, cs], rst[:, :], ALU.mult, r=[oacc, rst], w=[onb])
                cx.stt("dve", ob[:, :], onb[:, :], ong[:, 0:1], zc[:, :], ALU.mult, ALU.mult, r=[onb, ong, zc], w=[ob])
                cx.dma(STQ, catT1_out[h * 128:(h + 1) * 128, cs], ob[:, :], r=[ob], wa=[catT1_out])
    return catT1_out, modT


PAIRS = [[0, 1], [2, 3], [4, 5], [6, 7]]
WARMUP_COLL = True


def build_fused():
    cx = Ctx()
    ident_in = cx.din("ident", [128, 128])
    bmask_in = cx.din("bmask", [128, 2])
    w_out1 = cx.din("w_out1", [D, D])
    x2_out = cx.dout("x2", [HALF, D])
    c = common_consts(cx, ident_in)
    if WARMUP_COLL:
        wu_src = cx.dscr("wu_src", [128, 128], F32)
        wu_dst = cx.dscr("wu_dst", [256, 128], F32)
        cx.dma("sp", wu_src[:, :], ident_in[:, :], r=[ident_in], w=[wu_src])
        cx.S.coll("AllGather", wu_dst[:, :], wu_src[:, :], PAIRS, r=[wu_src], w=[wu_dst])
    with Scope(cx):
        x1_own_s, ctx1_s = _build_l0(cx, 99, fused=True, c=c)
    x1g_t = cx.nc.dram_tensor("x1g_s", [8, 512, D], F32, kind="Internal").ap()
    x1g = [Buf(x1g_t[i], f"x1g{i}") for i in range(8)]
    for i in range(8):
        cx.S.coll("AllGather", x1g[i][:, :], x1_own_s[i * 256:(i + 1) * 256, :], PAIRS, r=[x1_own_s], w=[x1g[i]])
    x1_tiles = []
    for t in range(32):
        r_, lt = t // 16, t % 16
        x1_tiles.append((x1g[lt // 2], r_ * 256 + (lt % 2) * 128))
    with Scope(cx):
        catT1_s, modT = _build_l1a(cx, 99, fused=True, c=c, x1_tiles=x1_tiles, ctx1_in=ctx1_s)
        catg_t = cx.nc.dram_tensor("catg_s", [4, 512, SEQ], BF16, kind="Internal").ap()
        catg = [Buf(catg_t[i], f"catg{i}") for i in range(4)]
        for i in range(4):
            cx.S.coll("AllGather", catg[i][:, :], catT1_s[i * 256:(i + 1) * 256, :], PAIRS, r=[catT1_s], w=[catg[i]])
        gate_bc = cx.sb("gate_bc1", [128, 2, D], F32)
        bmask = cx.sb("bmask_t", [128, 2], F32)
        cx.dma("sp", bmask[:, :], bmask_in[:, :], r=[bmask_in], w=[bmask])
        gate_rows(cx, c, modT, gate_bc, (0,))

        def loader(dst, col):
            for r_ in range(2):
                for i in range(4):
                    k0 = r_ * 8 + i * 2
                    cx.dma("sp", dst[:, k0:k0 + 2, :],
                           catg[i].t[r_ * 256:(r_ + 1) * 256, col:col + 128].rearrange("(k p) c -> p k c", p=128),
                           r=[catg[i]], wa=[dst])
        out_proj(cx, w_out1, None, gate_bc, [(x1_own_s, x2_out, t * 128, t * 128, 0) for t in range(16)], blend=bmask, loader=loader)
    return cx


def build_l1b():
    cx = Ctx()
    catT_in = cx.din("catT", [D, HALF], BF16)
    x1_own = cx.din("x1o", [HALF, D])
    w_out = cx.din("w_out", [D, D])
    modT_in = cx.din("modT1", [128, 48, 2])
    ident_in = cx.din("ident", [128, 128])
    x2_out = cx.dout("x2", [HALF, D])
    c = common_consts(cx, ident_in)
    modT = cx.sb("modT", [128, 48, 2], F32)
    gate_bc = cx.sb("gate_bc", [128, 2, D], F32)
    cx.dma("sp", modT[:, :, :], modT_in[:, :, :], r=[modT_in], w=[modT])
    gate_rows(cx, c, modT, gate_bc, (0,))
    out_proj(cx, w_out, catT_in, gate_bc, [(x1_own, x2_out, t * 128, t * 128, 0) for t in range(16)])
    return cx


def rope_tables():
    rows = SEQ // 64
    row = np.repeat(np.arange(rows), 64).astype(np.float32)
    col = np.tile(np.arange(64), rows).astype(np.float32)
    inv = (10000.0 ** (-np.arange(0, 32, 2, dtype=np.float32) / 32)).astype(np.float32)

    def axis_angles(pos):
        a = pos[:, None] * inv[None, :]
        return np.concatenate([a, a], axis=-1)
    ang = np.concatenate([axis_angles(row), axis_angles(col)], axis=-1).astype(np.float32)
    cos, sin = np.cos(ang), np.sin(ang)
    sgn = np.tile(np.concatenate([-np.ones(16), np.ones(16)]), 2).astype(np.float32)
    return np.concatenate([cos, sin * sgn], axis=-1).astype(np.float32)


def fm(v, nchunk):
    return np.ascontiguousarray(np.asarray(v, np.float32).reshape(nchunk, 128).T)


_CACHE = {}


def run_l0(inp, cores):
    if "l0" not in _CACHE:
        _CACHE["l0"] = build_l0()
    cx = _CACHE["l0"]
    rope = rope_tables()
    ident = np.eye(128, dtype=np.float32)
    in_maps = []
    for cid in cores:
        b, hf = cid // 2, cid % 2
        own = slice(hf * HALF, (hf + 1) * HALF)
        oth = slice((1 - hf) * HALF, (2 - hf) * HALF)
        cc = np.stack([inp["c"][b], inp["c_ctx"]], axis=-1)
        ccT = np.ascontiguousarray(cc.reshape(16, 128, 2).transpose(1, 0, 2))
        cw = inp["conv_w"][0]
        cwT = np.ascontiguousarray(cw.reshape(31, 8, 128).transpose(2, 1, 0))
        cvp = np.ascontiguousarray(np.stack([fm(inp["conv_b"][0], 8), fm(inp["cln_g"][0], 8), fm(inp["cln_b"][0], 8)], axis=1))
        hm = np.zeros((128, 2), np.float32)
        hm[:, 0] = 1.0 if hf == 1 else 0.0
        hm[:, 1] = 1.0 if hf == 0 else 0.0
        in_maps.append({
            "x_own": np.ascontiguousarray(inp["x"][b, own]), "x_oth": np.ascontiguousarray(inp["x"][b, oth]),
            "ctx": np.ascontiguousarray(inp["ctx"][b]), "ccT": ccT,
            "w_ada": np.ascontiguousarray(inp["w_ada"][0]), "b_adaT": fm(inp["b_ada"][0], 48), "norm_gT": fm(inp["norm_g"][0], 16),
            "w_in": np.ascontiguousarray(inp["w_in_ab"][0]), "w_out": np.ascontiguousarray(inp["w_out_ab"][0]),
            "qkg": np.ascontiguousarray(np.stack([inp["qn_g"][0], inp["kn_g"][0]])),
            "lamv": np.ascontiguousarray(np.stack([inp["lam_q1"][0], inp["lam_k1"][0], inp["lam_q2"][0], inp["lam_k2"][0]])),
            "subg": np.ascontiguousarray(inp["subln_g"][0].reshape(128, 1)),
            "cwT": cwT, "cvp": cvp,
            "rq_own": np.ascontiguousarray(rope[own] * np.float32(0.125)), "rk_own": np.ascontiguousarray(rope[own]),
            "rk_oth": np.ascontiguousarray(rope[oth]), "hmask": hm, "ident": ident,
        })
    res = run_bass_kernel_spmd(cx.nc, in_maps, core_ids=list(range(len(cores))))
    return res.results


def hgrn_masks():
    i = np.arange(128)
    same = (i[:, None] // 64) == (i[None, :] // 64)
    mF = (same & (i[:, None] <= i[None, :])).astype(np.float32)
    mB = (same & (i[:, None] >= i[None, :])).astype(np.float32)
    return mF, mB


def run_l1a(inp, x1, ctx1, cores):
    if "l1a" not in _CACHE:
        _CACHE["l1a"] = build_l1a()
    cx = _CACHE["l1a"]
    ident = np.eye(128, dtype=np.float32)
    mF, mB = hgrn_masks()
    wc = inp["w_in_c"][0]
    in_maps = []
    for cid in cores:
        b, hf = cid // 2, cid % 2
        cc = np.stack([inp["c"][b], inp["c_ctx"]], axis=-1)
        ccT = np.ascontiguousarray(cc.reshape(16, 128, 2).transpose(1, 0, 2))
        cols = np.concatenate([np.arange(f * D + hf * 1024, f * D + (hf + 1) * 1024) for f in range(5)])
        lg = inp["lb_gamma"][:, :, hf * 1024:(hf + 1) * 1024]
        lbg = np.ascontiguousarray(lg.reshape(2, 2, 8, 128).transpose(3, 0, 1, 2))
        in_maps.append({
            "x1f": np.ascontiguousarray(x1[b]), "ctx1": np.ascontiguousarray(ctx1[b]), "ccT": ccT,
            "w_ada": np.ascontiguousarray(inp["w_ada"][1]), "b_adaT": fm(inp["b_ada"][1], 48), "norm_gT": fm(inp["norm_g"][1], 16),
            "w_c": np.ascontiguousarray(wc[:, cols]), "lbg": lbg,
            "ong": np.ascontiguousarray(inp["onorm_g"][0].reshape(128, 1)),
            "maskF": mF, "maskB": mB, "ident": ident,
        })
    res = run_bass_kernel_spmd(cx.nc, in_maps, core_ids=list(range(len(cores))))
    return res.results


def run_l1b(inp, x1, cat_pairs, modTs, cores):
    if "l1b" not in _CACHE:
        _CACHE["l1b"] = build_l1b()
    cx = _CACHE["l1b"]
    ident = np.eye(128, dtype=np.float32)
    in_maps = []
    for i, cid in enumerate(cores):
        b, hf = cid // 2, cid % 2
        own = slice(hf * HALF, (hf + 1) * HALF)
        in_maps.append({
            "catT": np.ascontiguousarray(cat_pairs[b][:, own]), "x1o": np.ascontiguousarray(x1[b, own]),
            "w_out": np.ascontiguousarray(inp["w_out_c"][0]), "modT1": modTs[i], "ident": ident,
        })
    res = run_bass_kernel_spmd(cx.nc, in_maps, core_ids=list(range(len(cores))))
    return res.results


def run_l1(inp, x1, ctx1, cores):
    ra = run_l1a(inp, x1, ctx1, cores)
    cat_pairs = {}
    for i, cid in enumerate(cores):
        b, hf = cid // 2, cid % 2
        cat_pairs.setdefault(b, [None, None])[hf] = ra[i]["catT1"]
    cat_pairs = {b: np.concatenate(v, axis=0) for b, v in cat_pairs.items()}
    rb = run_l1b(inp, x1, cat_pairs, [ra[i]["modT1"] for i in range(len(cores))], cores)
    return rb


def l0_inputs(inp, cid, rope):
    b, hf = cid // 2, cid % 2
    own = slice(hf * HALF, (hf + 1) * HALF)
    oth = slice((1 - hf) * HALF, (2 - hf) * HALF)
    cc = np.stack([inp["c"][b], inp["c_ctx"]], axis=-1)
    ccT = np.ascontiguousarray(cc.reshape(16, 128, 2).transpose(1, 0, 2))
    cw = inp["conv_w"][0]
    cwT = np.ascontiguousarray(cw.reshape(31, 8, 128).transpose(2, 1, 0))
    cvp = np.ascontiguousarray(np.stack([fm(inp["conv_b"][0], 8), fm(inp["cln_g"][0], 8), fm(inp["cln_b"][0], 8)], axis=1))
    hm = np.zeros((128, 2), np.float32)
    hm[:, 0] = 1.0 if hf == 1 else 0.0
    hm[:, 1] = 1.0 if hf == 0 else 0.0
    return {
        "x_own": np.ascontiguousarray(inp["x"][b, own]), "x_oth": np.ascontiguousarray(inp["x"][b, oth]),
        "ctx": np.ascontiguousarray(inp["ctx"][b]), "ccT": ccT,
        "w_ada": np.ascontiguousarray(inp["w_ada"][0]), "b_adaT": fm(inp["b_ada"][0], 48), "norm_gT": fm(inp["norm_g"][0], 16),
        "w_in": np.ascontiguousarray(inp["w_in_ab"][0]), "w_out": np.ascontiguousarray(inp["w_out_ab"][0]),
        "qkg": np.ascontiguousarray(np.stack([inp["qn_g"][0], inp["kn_g"][0]])),
        "lamv": np.ascontiguousarray(np.stack([inp["lam_q1"][0], inp["lam_k1"][0], inp["lam_q2"][0], inp["lam_k2"][0]])),
        "subg": np.ascontiguousarray(inp["subln_g"][0].reshape(128, 1)),
        "cwT": cwT, "cvp": cvp,
        "rq_own": np.ascontiguousarray(rope[own] * np.float32(0.125)), "rk_own": np.ascontiguousarray(rope[own]),
        "rk_oth": np.ascontiguousarray(rope[oth]), "hmask": hm,
    }


def l1_inputs(inp, cid):
    b, hf = cid // 2, cid % 2
    cc = np.stack([inp["c"][b], inp["c_ctx"]], axis=-1)
    ccT = np.ascontiguousarray(cc.reshape(16, 128, 2).transpose(1, 0, 2))
    cols = np.concatenate([np.arange(f * D + hf * 1024, f * D + (hf + 1) * 1024) for f in range(5)])
    lg = inp["lb_gamma"][:, :, hf * 1024:(hf + 1) * 1024]
    lbg = np.ascontiguousarray(lg.reshape(2, 2, 8, 128).transpose(3, 0, 1, 2))
    mF, mB = hgrn_masks()
    bm = np.zeros((128, 2), np.float32)
    bm[:, hf] = 1.0
    return {
        "ccT1": ccT, "w_ada1": np.ascontiguousarray(inp["w_ada"][1]), "b_adaT1": fm(inp["b_ada"][1], 48),
        "norm_gT1": fm(inp["norm_g"][1], 16), "w_c": np.ascontiguousarray(inp["w_in_c"][0][:, cols]), "lbg": lbg,
        "ong": np.ascontiguousarray(inp["onorm_g"][0].reshape(128, 1)), "maskF": mF, "maskB": mB,
        "w_out1": np.ascontiguousarray(inp["w_out_c"][0]), "bmask": bm,
    }


def run_fused(inp, cores):
    if "fused" not in _CACHE:
        _CACHE["fused"] = build_fused()
    cx = _CACHE["fused"]
    rope = rope_tables()
    ident = np.eye(128, dtype=np.float32)
    in_maps = []
    for cid in cores:
        m = {"ident": ident}
        m.update(l0_inputs(inp, cid, rope))
        m.update(l1_inputs(inp, cid))
        in_maps.append(m)
    res = run_bass_kernel_spmd(cx.nc, in_maps, core_ids=list(range(len(cores))))
    return res.results


def kernel_unfused(**inputs):
    inp = {k: np.asarray(v) for k, v in inputs.items()}
    cores = list(range(8))
    r0 = run_l0(inp, cores)
    x1 = np.zeros_like(inp["x"])
    ctx1 = np.zeros_like(inp["ctx"])
    for cid in cores:
        b, hf = cid // 2, cid % 2
        x1[b, hf * HALF:(hf + 1) * HALF] = r0[cid]["x1"]
        ctx1[b] = r0[cid]["ctx1"]
    r1 = run_l1(inp, x1, ctx1, cores)
    out = np.zeros_like(inp["x"])
    for cid in cores:
        b, hf = cid // 2, cid % 2
        out[b, hf * HALF:(hf + 1) * HALF] = r1[cid]["x2"]
    return out


def kernel(**inputs):
    inp = {k: np.asarray(v) for k, v in inputs.items()}
    cores = list(range(8))
    r = run_fused(inp, cores)
    out = np.zeros_like(inp["x"])
    for cid in cores:
        b, hf = cid // 2, cid % 2
        out[b, hf * HALF:(hf + 1) * HALF] = r[cid]["x2"]
    return out
```

```python
from contextlib import ExitStack
import math
import numpy as np
import ml_dtypes
import concourse.bass as bass
import concourse.mybir as mybir
from concourse.bass_utils import run_bass_kernel_spmd

F32 = mybir.dt.float32
BF16 = mybir.dt.bfloat16
AF = mybir.ActivationFunctionType
ALU = mybir.AluOpType
AX = mybir.AxisListType

D = 2048
SEQ = 4096
HALF = 2048
CTX = 256
EPS = 1e-6
NKEY = CTX + SEQ
STQ = "act"


class Buf:
    def __init__(self, t, name="", psum=False):
        self.t = t
        self.name = name
        self.psum = psum
        self.w = {}
        self.r = {}
        self.pr = {}
        self.open = False

    def __getitem__(self, k):
        return self.t[k]


def _merge(d, s):
    for k, v in s.items():
        if d.get(k, 0) < v:
            d[k] = v


class Sched:
    GEN = 12000
    NSLOT = 8

    def __init__(self, nc):
        self.nc = nc
        self.eng = {"pe": nc.tensor, "dve": nc.vector, "act": nc.scalar,
                    "pool": nc.gpsimd, "sp": nc.sync}
        self.cnt = {e: 0 for e in self.eng}
        self.gen = {e: 0 for e in self.eng}
        self.sems = {}
        self.waited = {e: {} for e in self.eng}
        self.dma_i = {e: 0 for e in self.eng}
        self.nops = 0
        self.nwaits = 0

    def sem(self, key):
        if key not in self.sems:
            self.sems[key] = self.nc.alloc_semaphore("s_" + "_".join(str(k) for k in key))
        return self.sems[key]

    def _wait(self, e, deps):
        for key, val in deps.items():
            if key[0] == "E" and key[1] == e and e in ("pe", "sp"):
                continue
            if self.waited[e].get(key, 0) >= val:
                continue
            self.eng[e].wait_ge(self.sem(key), val)
            self.waited[e][key] = val
            self.nwaits += 1

    def _deps(self, e, r, w, wa):
        deps = {}
        for b in r:
            _merge(deps, b.w)
            if b.psum:
                _merge(deps, {k: v for k, v in b.r.items() if k[1] != e})
        for b in w:
            _merge(deps, b.w)
            _merge(deps, b.r)
            _merge(deps, b.pr)
        for b in wa:
            if not b.open:
                b.pr = dict(b.r)
                _merge(b.pr, b.w)
                b.r = {}
                b.w = {}
                b.open = True
            _merge(deps, b.pr)
        return deps

    def _record(self, key, val, r, w, wa):
        ev = {key: val}
        for b in r:
            _merge(b.r, ev)
            b.open = False
        for b in w:
            b.w = dict(ev)
            b.r = {}
            b.pr = {}
            b.open = False
        for b in wa:
            _merge(b.w, ev)

    def op(self, e, fn, r=(), w=(), wa=(), sig=True):
        deps = self._deps(e, r, w, wa)
        self._wait(e, deps)
        ins = fn()
        self.nops += 1
        key = ("E", e, self.gen[e])
        if sig:
            self.cnt[e] += 1
            ins.then_inc(self.sem(key), 1)
            self._record(key, self.cnt[e], r, w, wa)
            if self.cnt[e] >= self.GEN:
                self.gen[e] += 1
                self.cnt[e] = 0
        else:
            self._record(key, self.cnt[e] + 1, r, w, wa)
        return ins

    def dma(self, e, out, in_, r=(), w=(), wa=(), **kw):
        i = self.dma_i[e]
        slot = i % self.NSLOT
        key = ("D", e, slot)
        val = 16 * (i // self.NSLOT + 1)
        deps = self._deps(e, r, w, wa)
        if val > 16:
            _merge(deps, {key: val - 16})
        self._wait(e, deps)
        ins = self.eng[e].dma_start(out=out, in_=in_, **kw)
        ins.then_inc(self.sem(key), 16)
        self.dma_i[e] = i + 1
        self._record(key, val, r, w, wa)
        return ins

    def coll(self, kind, out, in_, groups, r=(), w=()):
        e = "pool"
        self.ncoll = getattr(self, "ncoll", 0) + 1
        key = ("C", e, self.ncoll)
        deps = self._deps(e, r, w, ())
        self._wait(e, deps)
        ins = self.nc.gpsimd.collective_compute(kind, ALU.bypass, replica_groups=groups, ins=[in_], outs=[out])
        ins.then_inc(self.sem(key), 1)
        self.colls = getattr(self, "colls", {})
        self.colls[key] = 1
        self._record(key, 1, r, w, ())
        return ins

    def barrier(self):
        allev = {}
        for e in self.eng:
            if self.cnt[e] > 0:
                allev[("E", e, self.gen[e])] = self.cnt[e]
            elif self.gen[e] > 0:
                allev[("E", e, self.gen[e] - 1)] = self.GEN
        for e in self.eng:
            n = self.dma_i[e]
            for slot in range(min(n, self.NSLOT)):
                last = ((n - 1 - slot) // self.NSLOT) * self.NSLOT + slot
                allev[("D", e, slot)] = 16 * (last // self.NSLOT + 1)
        allev.update(getattr(self, "colls", {}))
        for e in self.eng:
            for key, val in allev.items():
                if key[0] == "E" and key[1] == e and e in ("pe", "sp"):
                    continue
                if self.waited[e].get(key, 0) >= val:
                    continue
                self.eng[e].wait_ge(self.sem(key), val)
                self.waited[e][key] = val
                self.nwaits += 1


class Ctx:
    def __init__(self):
        self.nc = bass.Bass("TRN2", target_bir_lowering=False)
        self.S = Sched(self.nc)
        self.later = []
        self.scope = None
        self.uid = 0

    def un(self, name):
        self.uid += 1
        return f"{name}_u{self.uid}"

    def din(self, name, shape, dt=F32):
        return Buf(self.nc.dram_tensor(name, list(shape), dt, kind="ExternalInput").ap(), name)

    def dout(self, name, shape, dt=F32):
        return Buf(self.nc.dram_tensor(name, list(shape), dt, kind="ExternalOutput").ap(), name)

    def dscr(self, name, shape, dt=BF16):
        return Buf(self.nc.dram_tensor(name, list(shape), dt, kind="Internal").ap(), name)

    def sb(self, name, shape, dt=F32):
        name = self.un(name)
        if self.scope is not None:
            return Buf(self.scope.enter_context(self.nc.sbuf_tensor(name, list(shape), dt)), name)
        return Buf(self.nc.alloc_sbuf_tensor(name, list(shape), dt), name)

    def act(self, out, in_, func, r, w=(), wa=(), **kw):
        nc = self.nc
        return self.S.op("act", lambda: nc.scalar.activation(out=out, in_=in_, func=func, **kw), r=r, w=w, wa=wa)

    def _ve(self, e):
        return self.nc.vector if e == "dve" else self.nc.gpsimd

    def tt(self, e, out, in0, in1, op, r, w=(), wa=()):
        eng = self._ve(e)
        return self.S.op(e, lambda: eng.tensor_tensor(out=out, in0=in0, in1=in1, op=op), r=r, w=w, wa=wa)

    def ts(self, e, out, in0, s1, s2, op0, op1, r, w=(), wa=()):
        eng = self._ve(e)
        if s2 is None:
            return self.S.op(e, lambda: eng.tensor_scalar(out=out, in0=in0, scalar1=s1, scalar2=None, op0=op0), r=r, w=w, wa=wa)
        return self.S.op(e, lambda: eng.tensor_scalar(out=out, in0=in0, scalar1=s1, scalar2=s2, op0=op0, op1=op1), r=r, w=w, wa=wa)

    def stt(self, e, out, in0, scalar, in1, op0, op1, r, w=(), wa=()):
        eng = self._ve(e)
        return self.S.op(e, lambda: eng.scalar_tensor_tensor(out=out, in0=in0, scalar=scalar, in1=in1, op0=op0, op1=op1), r=r, w=w, wa=wa)

    def cp(self, e, out, in_, r, w=(), wa=()):
        if e == "act":
            nc = self.nc
            return self.S.op("act", lambda: nc.scalar.copy(out=out, in_=in_), r=r, w=w, wa=wa)
        eng = self._ve(e)
        return self.S.op(e, lambda: eng.tensor_copy(out=out, in_=in_), r=r, w=w, wa=wa)

    def recip(self, out, in_, r, w=(), wa=()):
        nc = self.nc
        return self.S.op("dve", lambda: nc.vector.reciprocal(out=out, in_=in_), r=r, w=w, wa=wa)

    def memset(self, e, ap, val, w=(), wa=()):
        eng = self._ve(e)
        return self.S.op(e, lambda: eng.memset(ap, val), w=w, wa=wa)

    def mm(self, out, lhsT, rhs, start, stop, r, w=(), wa=(), sig=True):
        nc = self.nc
        return self.S.op("pe", lambda: nc.tensor.matmul(out, lhsT=lhsT, rhs=rhs, start=start, stop=stop), r=r, w=w, wa=wa, sig=sig)

    def tr(self, out, in_, ident, r, w=(), wa=(), sig=True):
        nc = self.nc
        return self.S.op("pe", lambda: nc.tensor.transpose(out, in_, ident), r=r, w=w, wa=wa, sig=sig)

    def dma(self, e, out, in_, r, w=(), wa=()):
        return self.S.dma(e, out, in_, r=r, w=w, wa=wa)

    def defer(self, fn):
        self.later.append(fn)

    def flush(self):
        l, self.later = self.later, []
        for fn in l:
            fn()


class Scope:
    def __init__(self, cx):
        self.cx = cx

    def __enter__(self):
        self.es = ExitStack()
        self.es.__enter__()
        self.cx.scope = self.es
        return self

    def __exit__(self, *a):
        self.cx.flush()
        self.cx.S.barrier()
        self.cx.scope = None
        return self.es.__exit__(*a)


class Phase:
    def __init__(self, cx):
        self.cx = cx
        self.es = ExitStack()

    def __enter__(self):
        self.es.__enter__()
        return self

    def __exit__(self, *a):
        self.cx.flush()
        self.cx.S.barrier()
        return self.es.__exit__(*a)

    def sb(self, name, shape, dt=F32, n=1):
        name = self.cx.un(name)
        t = self.es.enter_context(self.cx.nc.sbuf_tensor(name, list(shape), dt))
        return Buf(t, name)

    def sbs(self, name, shape, dt, n):
        return [self.sb(f"{name}{i}", shape, dt) for i in range(n)]

    def ps(self, name, shape, dt=F32):
        nb = int(np.prod(shape[1:])) * (4 if dt == F32 else 2)
        assert nb % 2048 == 0, (name, shape)
        name = self.cx.un(name)
        t = self.es.enter_context(self.cx.nc.psum_tensor(name, list(shape), dt))
        return Buf(t, name, psum=True)

    def pss(self, name, shape, dt, n):
        return [self.ps(f"{name}{i}", shape, dt) for i in range(n)]


def common_consts(cx, ident_in):
    c = {}
    c["identf"] = cx.sb("identf", [128, 128], F32)
    c["identb"] = cx.sb("identb", [128, 128], BF16)
    c["onesf"] = cx.sb("onesf", [128, 128], F32)
    c["eps"] = cx.sb("epsT", [128, 1], F32)
    cx.dma("sp", c["identf"][:, :], ident_in[:, :], r=[ident_in], w=[c["identf"]])
    cx.cp("dve", c["identb"][:, :], c["identf"][:, :], r=[c["identf"]], w=[c["identb"]])
    cx.memset("pool", c["onesf"][:, :], 1.0, w=[c["onesf"]])
    cx.memset("pool", c["eps"][:, :], EPS, w=[c["eps"]])
    return c


def modulation(cx, c, ccT_in, w_ada, b_adaT_in, norm_gT_in, want_gate_bc=(0, 1), sfx=""):
    nc = cx.nc
    modT = cx.sb("modT" + sfx, [128, 48, 2], F32)
    gs = cx.sb("gsT" + sfx, [128, 16, 2], F32)
    gate_bc = cx.sb("gate_bc" + sfx, [128, 2, D], F32) if want_gate_bc else None
    with Phase(cx) as ph:
        scT = ph.sb("scT", [128, 16, 2], F32)
        badaT = ph.sb("badaT", [128, 48], F32)
        ngT = ph.sb("ngT", [128, 16], F32)
        wst = ph.sbs("wada_st", [128, 16, 512], F32, 2)
        pm = ph.ps("pm", [128, 512], F32)
        cx.dma("sp", scT[:, :, :], ccT_in[:, :, :], r=[ccT_in], w=[scT])
        cx.dma("sp", badaT[:, :], b_adaT_in[:, :], r=[b_adaT_in], w=[badaT])
        cx.dma("sp", ngT[:, :], norm_gT_in[:, :], r=[norm_gT_in], w=[ngT])
        cx.act(scT[:, :, :], scT[:, :, :], AF.Silu, r=[scT], w=[scT])
        wv = w_ada.t.rearrange("(k p) c -> p k c", p=128)
        for cb in range(12):
            st = wst[cb % 2]
            for hh in range(2):
                cx.dma("sp", st[:, hh * 8:(hh + 1) * 8, :], wv[:, hh * 8:(hh + 1) * 8, cb * 512:(cb + 1) * 512],
                       r=[w_ada], wa=[st])
            for fc in range(4):
                cc = cb * 4 + fc
                for k in range(16):
                    cx.mm(pm[:, cc * 2:cc * 2 + 2], st[:, k, fc * 128:(fc + 1) * 128], scT[:, k, :],
                          start=(k == 0), stop=(k == 15), r=[st, scT], wa=[pm], sig=(k == 15))
        cx.tt("dve", modT[:, :, :], pm[:, 0:96].rearrange("p (c j) -> p c j", j=2),
              badaT[:, :].unsqueeze(2).to_broadcast([128, 48, 2]), ALU.add, r=[pm, badaT], w=[modT])
        cx.stt("dve", gs[:, :, :], modT[:, 16:32, :], 1.0, ngT[:, :].unsqueeze(2).to_broadcast([128, 16, 2]),
               ALU.add, ALU.mult, r=[modT, ngT], w=[gs])
    if want_gate_bc:
        gate_rows(cx, c, modT, gate_bc, want_gate_bc)
    return modT, gs, gate_bc


def gate_rows(cx, c, modT, gate_bc, js):
    with Phase(cx) as ph:
        dgs = ph.sbs("dgate", [128, 128], F32, 2)
        pgs = ph.pss("pgate", [128, 512], F32, 2)
        i = 0
        for j in js:
            for k in range(16):
                dg = dgs[i % 2]
                pg = pgs[i % 2]
                cx.ts("dve", dg[:, :], c["identf"][:, :], modT[:, 32 + k, j:j + 1], None, ALU.mult, None,
                      r=[c["identf"], modT], w=[dg])
                cx.mm(pg[:, 0:128], c["onesf"][:, :], dg[:, :], True, True, r=[c["onesf"], dg], w=[pg])
                cx.cp("act", gate_bc[:, j, k * 128:(k + 1) * 128], pg[:, 0:128], r=[pg], wa=[gate_bc])
                i += 1


def build_hT(cx, ph_bufs, c, modT, gs, hT, hTb, tiles):
    xts, xns, sqj, sss, pTs = ph_bufs

    def stage_a(i):
        src, r0, j = tiles[i]
        xt, xn, ss = xts[i % 2], xns[i % 2], sss[i % 2]
        cx.dma("sp", xt[:, :], src[r0:r0 + 128, :], r=[src], w=[xt])
        cx.memset("pool", ss[:, :], 0.0, w=[ss])
        cx.act(sqj[:, :], xt[:, :], AF.Square, r=[xt, ss], w=[sqj, ss], accum_out=ss[:, 0:1])
        cx.act(ss[:, 1:2], ss[:, 0:1], AF.Sqrt, r=[ss, c["eps"]], w=[ss], bias=c["eps"][:, 0:1], scale=1.0 / D)
        cx.recip(ss[:, 2:3], ss[:, 1:2], r=[ss], w=[ss])
        cx.ts("dve", xn[:, :], xt[:, :], ss[:, 2:3], None, ALU.mult, None, r=[xt, ss], w=[xn])

    def stage_b(i):
        src, r0, j = tiles[i]
        xn = xns[i % 2]
        for half in range(2):
            pT = pTs[half]
            for kk in range(8):
                k = half * 8 + kk
                cx.tr(pT[:, kk, :], xn[:, k * 128:(k + 1) * 128], c["identb"][:, :], r=[xn, c["identb"]],
                      wa=[pT], sig=(kk == 7))
            for kk in range(8):
                k = half * 8 + kk
                dst = hT[:, k, i * 128:(i + 1) * 128]
                if half == 0:
                    cx.act(dst, pT[:, kk, :], AF.Identity, r=[pT, gs, modT], wa=[hTb[i]],
                           scale=gs[:, k, j:j + 1], bias=modT[:, k, j:j + 1])
                else:
                    cx.ts("dve", dst, pT[:, kk, :], gs[:, k, j:j + 1], modT[:, k, j:j + 1], ALU.mult, ALU.add,
                          r=[pT, gs, modT], wa=[hTb[i]])

    stage_a(0)
    for i in range(len(tiles)):
        if i + 1 < len(tiles):
            stage_a(i + 1)
        stage_b(i)


class WStream:
    def __init__(self, cx, ph, name="w"):
        self.cx = cx
        self.st = ph.sbs(name + "_st", [128, 4, 512], F32, 2)
        self.wb = ph.sbs(name + "_bf", [128, 16, 512], BF16, 2)
        self.n = 0
        self.si = 0

    def fetch(self, w, c0):
        cx = self.cx
        wb = self.wb[self.n % 2]
        self.n += 1
        wv = w.t.rearrange("(k p) c -> p k c", p=128)
        for q in range(4):
            st = self.st[self.si % 2]
            self.si += 1
            cx.dma("sp", st[:, :, :], wv[:, q * 4:(q + 1) * 4, c0:c0 + 512], r=[w], w=[st])
            cx.cp("pool", wb[:, q * 4:(q + 1) * 4, :], st[:, :, :], r=[st], wa=[wb])
        return wb


LAM_INIT0 = 0.8 - 0.6 * math.exp(-0.3 * 0)


HT_DBG = 0
STOPF = {1.3: ("q",), 1.4: ("v",), 1.5: ("za",), 1.6: ("gg", "gv")}


class _Stop(Exception):
    pass


def build_l0(stop=99):
    cx = Ctx()
    try:
        _build_l0(cx, stop)
    except _Stop:
        cx.S.barrier()
    return cx


def _build_l0(cx, stop=99, fused=False, c=None):
    nc = cx.nc
    x_own = cx.din("x_own", [HALF, D])
    x_oth = cx.din("x_oth", [HALF, D])
    ctx_in = cx.din("ctx", [CTX, D])
    ccT_in = cx.din("ccT", [128, 16, 2])
    w_ada = cx.din("w_ada", [D, 3 * D])
    b_adaT_in = cx.din("b_adaT", [128, 48])
    norm_gT_in = cx.din("norm_gT", [128, 16])
    w_in = cx.din("w_in", [D, 7168])
    w_out = cx.din("w_out", [D, D])
    qkg_in = cx.din("qkg", [2, 64])
    lamv_in = cx.din("lamv", [4, 64])
    subg_in = cx.din("subg", [128, 1])
    cwT_in = cx.din("cwT", [128, 8, 31])
    cvp_in = cx.din("cvp", [128, 3, 8])
    rq_own = cx.din("rq_own", [HALF, 128])
    rk_own = cx.din("rk_own", [HALF, 128])
    rk_oth = cx.din("rk_oth", [HALF, 128])
    hmask_in = cx.din("hmask", [128, 2])
    if fused:
        x1_out = cx.dscr("x1_own_s", [HALF, D], F32)
        ctx1_out = cx.dscr("ctx1_s", [CTX, D], F32)
    else:
        ident_in = cx.din("ident", [128, 128])
        x1_out = cx.dout("x1", [HALF, D])
        ctx1_out = cx.dout("ctx1", [CTX, D])
    qT_s = cx.dscr("qT_s", [8, 128, HALF + CTX])
    kT_s = cx.dscr("kT_s", [8, 128, NKEY])
    v_s = cx.dscr("v_s", [NKEY, 1024])
    zaT_s = cx.dscr("zaT_s", [1024, HALF + CTX])
    zbT_s = cx.dscr("zbT_s", [1024, HALF + CTX])
    yT_s = cx.dscr("yT_s", [1024, HALF + 30])
    yTc_s = cx.dscr("yTc_s", [1024, CTX + 30])
    catT_s = cx.dscr("catT_s", [D, HALF + CTX])

    if c is None:
        c = common_consts(cx, ident_in)
    modT, gs, gate_bc = modulation(cx, c, ccT_in, w_ada, b_adaT_in, norm_gT_in)

    qg = cx.sb("qg", [128, 64]); qgs = cx.sb("qgs", [128, 64]); kg = cx.sb("kg", [128, 64])
    lamt = cx.sb("lamt", [128, 4, 64]); lam4 = cx.sb("lam4", [128, 8])
    subg = cx.sb("subg_t", [128, 1])
    hmask = cx.sb("hmask_t", [128, 2])
    cvp = cx.sb("cvp_t", [128, 3, 8])
    cwT = cx.sb("cwT_t", [128, 8, 31])
    cx.dma("sp", qg[:, :], qkg_in[0, :].partition_broadcast(128), r=[qkg_in], w=[qg])
    cx.dma("sp", kg[:, :], qkg_in[1, :].partition_broadcast(128), r=[qkg_in], w=[kg])
    cx.dma("sp", lamt[:, :, :].rearrange("p a d -> p (a d)"),
           lamv_in.t.rearrange("a d -> (a d)").partition_broadcast(128), r=[lamv_in], w=[lamt])
    cx.dma("sp", subg[:, :], subg_in[:, :], r=[subg_in], w=[subg])
    cx.dma("sp", hmask[:, :], hmask_in[:, :], r=[hmask_in], w=[hmask])
    cx.dma("sp", cvp[:, :, :], cvp_in[:, :, :], r=[cvp_in], w=[cvp])
    cx.dma("sp", cwT[:, :, :], cwT_in[:, :, :], r=[cwT_in], w=[cwT])
    cx.ts("dve", qgs[:, :], qg[:, :], 0.125, None, ALU.mult, None, r=[qg], w=[qgs])
    cx.tt("dve", lamt[:, 0, :], lamt[:, 0, :], lamt[:, 1, :], ALU.mult, r=[lamt], w=[lamt])
    cx.tt("dve", lamt[:, 2, :], lamt[:, 2, :], lamt[:, 3, :], ALU.mult, r=[lamt], w=[lamt])
    cx.S.op("dve", lambda: nc.vector.reduce_sum(out=lam4[:, 0:1], in_=lamt[:, 0, :], axis=AX.X), r=[lamt], w=[lam4])
    cx.S.op("dve", lambda: nc.vector.reduce_sum(out=lam4[:, 1:2], in_=lamt[:, 2, :], axis=AX.X), r=[lamt, lam4], w=[lam4])
    cx.act(lam4[:, 2:4], lam4[:, 0:2], AF.Exp, r=[lam4], w=[lam4])
    cx.stt("dve", lam4[:, 4:5], lam4[:, 3:4], -LAM_INIT0, lam4[:, 2:3], ALU.add, ALU.subtract, r=[lam4], w=[lam4])
    neglam = lam4[:, 4:5]
    cx.ts("dve", subg[:, :], subg[:, :], 1.0 - LAM_INIT0, None, ALU.mult, None, r=[subg], w=[subg])

    if stop < 1:
        cx.S.barrier()
        return cx
    with Phase(cx) as ph:
        hT_t = ph.sb("hT", [128, 16, 1280], BF16)
        hTb = [Buf(hT_t.t, f"hT{i}") for i in range(10)]
        hT = hT_t
        hbufs = (ph.sbs("xt", [128, D], F32, 2), ph.sbs("xn", [128, D], BF16, 2), ph.sb("sqj", [128, D], BF16),
                 ph.sbs("ss", [128, 4], F32, 2), ph.pss("pT", [128, 8, 128], BF16, 2))
        ws = WStream(cx, ph)
        pacc = ph.pss("pacc", [128, 512], F32, 3)
        pq = ph.pss("pq", [128, 8, 128], BF16, 2)
        sqs = ph.sbs("sq", [128, 512], F32, 2)
        st8 = ph.sbs("st8", [128, 16], F32, 2)
        xnq = ph.sbs("xnq", [128, 512], F32, 2)
        t1s = ph.sbs("t1", [128, 512], F32, 2)
        t2s = ph.sbs("t2", [128, 512], F32, 2)
        qbs = ph.sbs("qb", [128, 512], BF16, 2)
        qTs = ph.sbs("qTs", [128, 4, 128], BF16, 2)
        rts = ph.sbs("rt", [128, 128], F32, 2)
        vbs = ph.sbs("vb", [128, 512], BF16, 2)
        fos = ph.sbs("fo", [128, 512], BF16, 2)
        sig_t = ph.sb("sig", [128, 4, 1280], BF16)
        sigb = [Buf(sig_t.t, f"sig{i}") for i in range(4)]
        zero_t = ph.sb("zero", [128, 15], BF16)
        cnt = {"acc": 0, "qk": 0, "v": 0, "fo": 0}

        cx.memset("pool", zero_t[:, :], 0.0, w=[zero_t])
        yTc_v = yTc_s.t.rearrange("(j p) c -> p j c", p=128)
        for j in range(8):
            cx.dma("sp", yTc_v[:, j, 0:15], zero_t[:, :], r=[zero_t], wa=[yTc_s])
            cx.dma("sp", yTc_v[:, j, CTX + 15:CTX + 30], zero_t[:, :], r=[zero_t], wa=[yTc_s])
        if stop == 1.1:
            raise _Stop

        def qk_epi(ps, tile, fam, h0):
            kind, idx = tile
            i = cnt["qk"]; cnt["qk"] += 1
            sq, s8, xq, t1, t2, qb, qTt, rt, pqt = sqs[i % 2], st8[i % 2], xnq[i % 2], t1s[i % 2], t2s[i % 2], qbs[i % 2], qTs[i % 2], rts[i % 2], pq[i % 2]
            cx.act(sq[:, :], ps[:, :], AF.Square, r=[ps], w=[sq])
            cx.S.op("dve", lambda: nc.vector.reduce_sum(out=s8[:, 0:8], in_=sq[:, :].rearrange("p (g d) -> p g d", d=64), axis=AX.X), r=[sq], w=[s8])
            cx.act(s8[:, 8:16], s8[:, 0:8], AF.Sqrt, r=[s8, c["eps"]], w=[s8], bias=c["eps"][:, 0:1], scale=1.0 / 64)
            cx.recip(s8[:, 0:8], s8[:, 8:16], r=[s8], w=[s8])
            v3 = lambda a: a.rearrange("p (g d) -> p g d", d=64)
            cx.tt("dve", v3(xq[:, :]), v3(ps[:, :]), s8[:, 0:8].unsqueeze(2).to_broadcast([128, 8, 64]), ALU.mult, r=[ps, s8], w=[xq])
            g = kg if fam == "k" else (qgs if kind == "ctx" else qg)
            cx.tt("dve", v3(xq[:, :]), v3(xq[:, :]), g[:, :].unsqueeze(1).to_broadcast([128, 8, 64]), ALU.mult, r=[xq, g], w=[xq])
            if kind == "ctx":
                cx.cp("act", qb[:, :], xq[:, :], r=[xq], w=[qb])
            else:
                rsrc = (rq_own if fam == "q" else rk_own) if kind == "own" else rk_oth
                cx.dma("sp", rt[:, :], rsrc[idx * 128:(idx + 1) * 128, :], r=[rsrc], w=[rt])
                cx.tt("dve", v3(t1[:, :]), v3(xq[:, :]), rt[:, 0:64].unsqueeze(1).to_broadcast([128, 8, 64]), ALU.mult, r=[xq, rt], w=[t1])
                for a in range(2):
                    lo, hi = a * 32, a * 32 + 16
                    e = "dve" if a == 0 else "pool"
                    cx.tt(e, v3(t2[:, :])[:, :, lo:lo + 16], v3(xq[:, :])[:, :, hi:hi + 16],
                          rt[:, 64 + lo:64 + lo + 16].unsqueeze(1).to_broadcast([128, 8, 16]), ALU.mult, r=[xq, rt], wa=[t2])
                    cx.tt(e, v3(t2[:, :])[:, :, hi:hi + 16], v3(xq[:, :])[:, :, lo:lo + 16],
                          rt[:, 64 + hi:64 + hi + 16].unsqueeze(1).to_broadcast([128, 8, 16]), ALU.mult, r=[xq, rt], wa=[t2])
                cx.tt("dve", qb[:, :], t1[:, :], t2[:, :], ALU.add, r=[t1, t2], w=[qb])
            if fam == "q":
                dst_s = qT_s
                c0 = idx * 128 if kind == "own" else HALF + idx * 128
            else:
                dst_s = kT_s
                c0 = {"ctx": 0, "own": CTX, "oth": CTX + HALF}[kind] + idx * 128

            def fin():
                for hh in range(4):
                    cx.tr(pqt[:, hh, :], qb[:, hh * 128:(hh + 1) * 128], c["identb"][:, :], r=[qb, c["identb"]], wa=[pqt], sig=(hh == 3))
                cx.cp("dve", qTt[:, :, :], pqt[:, 0:4, :], r=[pqt], w=[qTt])
                cx.dma(STQ, dst_s.t[h0:h0 + 4, :, c0:c0 + 128].rearrange("h p t -> p h t"), qTt[:, :, :], r=[qTt], wa=[dst_s])
            cx.defer(fin)

        def v_epi(ps, tile, cb2):
            kind, idx = tile
            i = cnt["v"]; cnt["v"] += 1
            vb = vbs[i % 2]
            r0 = {"ctx": 0, "own": CTX, "oth": CTX + HALF}[kind] + idx * 128
            cx.cp("act", vb[:, :], ps[:, :], r=[ps], w=[vb])
            cx.defer(lambda: cx.dma(STQ, v_s[r0:r0 + 128, cb2 * 512:(cb2 + 1) * 512], vb[:, :], r=[vb], wa=[v_s]))

        own = lambda a, b: [("own", i) for i in range(a, b)]
        blocks = [
            dict(tiles=own(0, 8) + [("ctx", 0), ("ctx", 1)], full=10, fams="all",
                 chunks=[(0, 512, "own", 0), (512, 512, "own", 512), (1024, 256, "ctx", 0)]),
            dict(tiles=own(8, 16) + [("oth", 0), ("oth", 15)], full=8, fams="all",
                 chunks=[(0, 512, "own", 1024), (512, 512, "own", 1536), (1024, 256, "halo", 0)]),
            dict(tiles=[("oth", i) for i in range(1, 8)], full=0, fams="kv", chunks=[]),
            dict(tiles=[("oth", i) for i in range(8, 15)], full=0, fams="kv", chunks=[]),
        ]
        srcmap = {"own": (x_own, 0), "oth": (x_oth, 0), "ctx": (ctx_in, 1)}
        colblocks = [("q", 0, 0), ("q", 512, 1), ("k", 1024, 0), ("k", 1536, 1), ("v", 2048, 0), ("v", 2560, 1),
                     ("za", 3072, 0), ("za", 3584, 1), ("gg", 5120, 0), ("gv", 4096, 0), ("gg", 5632, 1), ("gv", 4608, 1),
                     ("zb", 6144, 0), ("zb", 6656, 1)]
        for blk in blocks:
            tiles = blk["tiles"]
            build_hT(cx, hbufs, c, modT, gs, hT, hTb, [(srcmap[k][0], i * 128, srcmap[k][1]) for (k, i) in tiles])
            if stop == 1.2:
                raise _Stop
            cbl = colblocks if blk["fams"] == "all" else [cb for cb in colblocks if cb[0] in ("k", "v")]
            if stop in STOPF:
                cbl = [cb for cb in cbl if cb[0] in STOPF[stop]]
            wnext = ws.fetch(w_in, cbl[0][1])
            for ci, (fam, c0, sub) in enumerate(cbl):
                wb = wnext
                if ci + 1 < len(cbl):
                    wnext = ws.fetch(w_in, cbl[ci + 1][1])
                if fam in ("q", "k", "v"):
                    for ti, tile in enumerate(tiles):
                        if tile[0] == "oth" and fam == "q":
                            continue
                        ps = pacc[cnt["acc"] % 3]; cnt["acc"] += 1
                        for k in range(16):
                            cx.mm(ps[:, :], hT[:, k, ti * 128:(ti + 1) * 128], wb[:, k, :], start=(k == 0), stop=(k == 15),
                                  r=[hTb[ti], wb], wa=[ps], sig=(k == 15))
                        pend, cx.later = cx.later, []
                        if fam == "v":
                            v_epi(ps, tile, sub)
                        else:
                            qk_epi(ps, tile, fam, sub * 4)
                        for fn in pend:
                            fn()
                    cx.flush()
                else:
                    for (h0c, n, ckind, d0) in blk["chunks"]:
                        if ckind == "halo" and fam not in ("gg", "gv"):
                            continue
                        t0 = h0c // 128
                        rb = [hTb[t] for t in range(t0, t0 + (n + 127) // 128)]
                        for fc in range(4):
                            ps = pacc[cnt["acc"] % 3]; cnt["acc"] += 1
                            for k in range(16):
                                cx.mm(ps[:, 0:n], wb[:, k, fc * 128:(fc + 1) * 128], hT[:, k, h0c:h0c + n], start=(k == 0), stop=(k == 15),
                                      r=rb + [wb], wa=[ps], sig=(k == 15))
                            frow = sub * 512 + fc * 128
                            dcol = d0 if ckind == "own" else HALF + d0
                            if fam in ("za", "zb"):
                                fo = fos[cnt["fo"] % 2]; cnt["fo"] += 1
                                dst = zaT_s if fam == "za" else zbT_s
                                cx.act(fo[:, 0:n], ps[:, 0:n], AF.Silu, r=[ps], w=[fo])
                                cx.dma(STQ, dst[frow:frow + 128, dcol:dcol + n], fo[:, 0:n], r=[fo], wa=[dst])
                            elif fam == "gg":
                                cx.act(sig_t[:, fc, h0c:h0c + n], ps[:, 0:n], AF.Sigmoid, r=[ps], wa=[sigb[fc]])
                            else:
                                fo = fos[cnt["fo"] % 2]; cnt["fo"] += 1
                                cx.tt("dve", fo[:, 0:n], ps[:, 0:n], sig_t[:, fc, h0c:h0c + n], ALU.mult, r=[ps, sigb[fc]], w=[fo])
                                if ckind == "own":
                                    cx.dma(STQ, yT_s[frow:frow + 128, 15 + d0:15 + d0 + n], fo[:, 0:n], r=[fo], wa=[yT_s])
                                elif ckind == "ctx":
                                    cx.dma(STQ, yTc_s[frow:frow + 128, 15 + d0:15 + d0 + n], fo[:, 0:n], r=[fo], wa=[yTc_s])
                                else:
                                    cx.ts("dve", fo[:, 0:15], fo[:, 0:15], hmask[:, 1:2], None, ALU.mult, None, r=[fo, hmask], w=[fo])
                                    cx.ts("dve", fo[:, 241:256], fo[:, 241:256], hmask[:, 0:1], None, ALU.mult, None, r=[fo, hmask], w=[fo])
                                    cx.dma(STQ, yT_s[frow:frow + 128, 15 + HALF:30 + HALF], fo[:, 0:15], r=[fo], wa=[yT_s])
                                    cx.dma(STQ, yT_s[frow:frow + 128, 0:15], fo[:, 241:256], r=[fo], wa=[yT_s])
            cx.flush()
            if stop in STOPF:
                raise _Stop

    if stop < 2:
        return cx
    with Phase(cx) as ph:
        kTs = ph.sbs("kTh", [128, NKEY], BF16, 2)
        vhs = ph.sbs("vh", [128, 34, 128], BF16, 2)
        qhs = ph.sbs("qh", [128, HALF + CTX], BF16, 2)
        pST = ph.pss("pST", [128, 2, 512], F32, 2)
        po = ph.ps("po", [128, 2, 512], F32)
        psm = ph.ps("psm", [128, 2, 512], F32)
        pts = ph.sbs("pt", [128, 2, 512], BF16, 3)
        accs = ph.sbs("accP", [128, 2, 512], F32, 2)
        accm = [[Buf(a.t, a.name + "m0"), Buf(a.t, a.name + "m1")] for a in accs]
        rs = ph.sb("rs", [128, 2, 512], F32)
        ta = ph.sb("ta", [128, 512], F32); tb = ph.sb("tb", [128, 512], F32)
        ot = ph.sb("ot", [128, 512], F32); sqo = ph.sb("sqo", [128, 512], F32)
        rstd = ph.sb("rstdo", [128, 512], F32)
        zat = ph.sbs("zat", [128, 512], BF16, 2)
        obs = ph.sbs("ob", [128, 512], BF16, 2)
        v_v = v_s.t.rearrange("(kt p) c -> p kt c", p=128)
        step = 0
        ci = 0
        def load_head(h):
            cx.dma("sp", kTs[h % 2][:, :], kT_s[h, :, :], r=[kT_s], w=[kTs[h % 2]])
            cx.dma("sp", vhs[h % 2][:, :, :], v_v[:, :, h * 128:(h + 1) * 128], r=[v_s], w=[vhs[h % 2]])
            cx.dma("sp", qhs[h % 2][:, :], qT_s[h, :, :], r=[qT_s], w=[qhs[h % 2]])

        load_head(0)
        for h in range(8):
            kTh, vh, qh = kTs[h % 2], vhs[h % 2], qhs[h % 2]
            if h + 1 < 8:
                load_head(h + 1)
            for (q0, n, nkt) in [(0, 512, 34), (512, 512, 34), (1024, 512, 34), (1536, 512, 34), (HALF, 256, 2)]:
                acc = accs[ci % 2]
                za = zat[ci % 2]
                cx.dma("sp", za[:, 0:n], zaT_s[h * 128:(h + 1) * 128, q0:q0 + n], r=[zaT_s], w=[za])
                def qk(kt_, st_):
                    for m in range(2):
                        cx.mm(st_[:, m, 0:n], kTh[64 * m:64 * m + 64, kt_ * 128:(kt_ + 1) * 128], qh[64 * m:64 * m + 64, q0:q0 + n],
                              True, True, r=[kTh, qh], wa=[st_], sig=(m == 1))

                qk(0, pST[step % 2])
                for kt in range(nkt):
                    st = pST[step % 2]
                    pt = pts[step % 3]
                    step += 1
                    if kt + 1 < nkt:
                        qk(kt + 1, pST[step % 2])
                    cx.act(pt[:, :, 0:n], st[:, :, 0:n], AF.Exp, r=[st], w=[pt])
                    for m in range(2):
                        cx.mm(po[:, m, 0:n], vh[:, kt, :], pt[:, m, 0:n], start=(kt == 0), stop=(kt == nkt - 1),
                              r=[vh, pt], wa=[po], sig=(m == 1))
                    am = accm[ci % 2][0]
                    if kt == 0:
                        cx.cp("dve", acc[:, :, 0:n], pt[:, :, 0:n], r=[pt], w=[am])
                    else:
                        cx.tt("dve", acc[:, :, 0:n], acc[:, :, 0:n], pt[:, :, 0:n], ALU.add, r=[pt, am], w=[am])
                for m in range(2):
                    cx.mm(psm[:, m, 0:n], c["onesf"][:, :], acc[:, m, 0:n], True, True, r=[c["onesf"], accm[ci % 2][0]], wa=[psm], sig=(m == 1))
                cx.recip(rs[:, :, 0:n], psm[:, :, 0:n], r=[psm], w=[rs])
                cx.tt("dve", ta[:, 0:n], po[:, 0, 0:n], rs[:, 0, 0:n], ALU.mult, r=[po, rs], w=[ta])
                cx.tt("dve", tb[:, 0:n], po[:, 1, 0:n], rs[:, 1, 0:n], ALU.mult, r=[po, rs], w=[tb])
                cx.stt("dve", ot[:, 0:n], tb[:, 0:n], neglam, ta[:, 0:n], ALU.mult, ALU.add, r=[ta, tb, lam4], w=[ot])
                cx.act(sqo[:, 0:n], ot[:, 0:n], AF.Square, r=[ot], w=[sqo])
                cx.mm(psm[:, 0, 0:n], c["onesf"][:, :], sqo[:, 0:n], True, True, r=[c["onesf"], sqo], w=[psm])
                cx.act(rstd[:, 0:n], psm[:, 0, 0:n], AF.Sqrt, r=[psm, c["eps"]], w=[rstd], bias=c["eps"][:, 0:1], scale=1.0 / 128)
                cx.recip(rstd[:, 0:n], rstd[:, 0:n], r=[rstd], w=[rstd])
                cx.tt("dve", ot[:, 0:n], ot[:, 0:n], rstd[:, 0:n], ALU.mult, r=[ot, rstd], w=[ot])
                ob = obs[ci % 2]
                cx.stt("dve", ob[:, 0:n], ot[:, 0:n], subg[:, 0:1], za[:, 0:n], ALU.mult, ALU.mult, r=[ot, subg, za], w=[ob])
                cx.dma(STQ, catT_s[h * 128:(h + 1) * 128, q0:q0 + n], ob[:, 0:n], r=[ob], wa=[catT_s])
                ci += 1

    if stop < 3:
        return cx
    with Phase(cx) as ph:
        dg = ph.sb("dgc", [128, 8, 31, 128], BF16)
        ycs = ph.sbs("yc", [128, 8, 542], BF16, 2)
        convT = ph.sb("convT", [128, 8, 512], F32)
        sqT = ph.sb("sqT", [128, 8, 512], F32)
        onesc = ph.sb("onesc", [128, 128], F32)
        pconv = ph.pss("pconv", [128, 512], F32, 2)
        pst = ph.ps("pstat", [128, 2, 512], F32)
        mean = ph.sb("mean", [128, 512], F32); var = ph.sb("var", [128, 512], F32)
        tcs = ph.sbs("tc", [128, 512], F32, 2)
        scs = ph.sbs("sc", [128, 512], F32, 2)
        zbt = ph.sbs("zbt", [128, 512], BF16, 2)
        obs = ph.sbs("obc", [128, 512], BF16, 2)
        cx.memset("pool", onesc[:, :], 1.0 / 1024, w=[onesc])
        i = 0
        for j in range(8):
            for k in range(31):
                e = "dve" if i % 2 == 0 else "pool"
                cx.ts(e, dg[:, j, k, :], c["identf"][:, :], cwT[:, j, k:k + 1], None, ALU.mult, None, r=[c["identf"], cwT], wa=[dg])
                i += 1
        yT_v = yT_s.t.rearrange("(j p) c -> p j c", p=128)
        it = 0
        for (src, srcb, c0, n, dcol) in [(yT_v, yT_s, 0, 512, 0), (yT_v, yT_s, 512, 512, 512), (yT_v, yT_s, 1024, 512, 1024),
                                         (yT_v, yT_s, 1536, 512, 1536), (yTc_v, yTc_s, 0, 256, HALF)]:
            yc = ycs[it % 2]; it += 1
            cx.dma("sp", yc[:, :, 0:n + 30], src[:, :, c0:c0 + n + 30], r=[srcb], w=[yc])
            for j in range(8):
                pc = pconv[j % 2]
                for k in range(31):
                    cx.mm(pc[:, 0:n], dg[:, j, k, :], yc[:, j, k:k + n], start=(k == 0), stop=(k == 30), r=[dg, yc], wa=[pc], sig=(k == 30))
                cx.act(convT[:, j, 0:n], pc[:, 0:n], AF.Identity, r=[pc, cvp], wa=[convT], bias=cvp[:, 0, j:j + 1], scale=1.0)
                cx.act(sqT[:, j, 0:n], convT[:, j, 0:n], AF.Square, r=[convT], wa=[sqT])
            for j in range(8):
                cx.mm(pst[:, 0, 0:n], onesc[:, :], convT[:, j, 0:n], start=(j == 0), stop=(j == 7), r=[onesc, convT], wa=[pst], sig=(j == 7))
            for j in range(8):
                cx.mm(pst[:, 1, 0:n], onesc[:, :], sqT[:, j, 0:n], start=(j == 0), stop=(j == 7), r=[onesc, sqT], wa=[pst], sig=(j == 7))
            cx.cp("act", mean[:, 0:n], pst[:, 0, 0:n], r=[pst], w=[mean])
            cx.tt("dve", var[:, 0:n], mean[:, 0:n], mean[:, 0:n], ALU.mult, r=[mean], w=[var])
            cx.tt("dve", var[:, 0:n], pst[:, 1, 0:n], var[:, 0:n], ALU.subtract, r=[pst, var], w=[var])
            cx.ts("dve", var[:, 0:n], var[:, 0:n], 0.0, None, ALU.max, None, r=[var], w=[var])
            cx.act(var[:, 0:n], var[:, 0:n], AF.Sqrt, r=[var, c["eps"]], w=[var], bias=c["eps"][:, 0:1], scale=1.0)
            cx.recip(var[:, 0:n], var[:, 0:n], r=[var], w=[var])
            for j in range(8):
                tcb, scb, zb, ob = tcs[j % 2], scs[j % 2], zbt[j % 2], obs[j % 2]
                cx.dma("sp", zb[:, 0:n], zbT_s[j * 128:(j + 1) * 128, dcol:dcol + n], r=[zbT_s], w=[zb])
                cx.tt("dve", tcb[:, 0:n], convT[:, j, 0:n], mean[:, 0:n], ALU.subtract, r=[convT, mean], w=[tcb])
                cx.tt("pool", tcb[:, 0:n], tcb[:, 0:n], var[:, 0:n], ALU.mult, r=[tcb, var], w=[tcb])
                cx.act(scb[:, 0:n], tcb[:, 0:n], AF.Silu, r=[tcb, cvp], w=[scb], scale=cvp[:, 1, j:j + 1], bias=cvp[:, 2, j:j + 1])
                cx.tt("dve", ob[:, 0:n], scb[:, 0:n], zb[:, 0:n], ALU.mult, r=[scb, zb], w=[ob])
                cx.dma(STQ, catT_s[1024 + j * 128:1024 + (j + 1) * 128, dcol:dcol + n], ob[:, 0:n], r=[ob], wa=[catT_s])

    if stop < 4:
        return cx
    out_proj(cx, w_out, catT_s, gate_bc,
             [(x_own, x1_out, t * 128, t * 128, 0) for t in range(16)] + [(ctx_in, ctx1_out, t * 128, HALF + t * 128, 1) for t in range(2)])
    return x1_out, ctx1_out


def out_proj(cx, w_out, catT_s, gate_bc, tiles, blend=None, loader=None):
    with Phase(cx) as ph:
        wo = ph.sb("wo", [128, 16, D], BF16)
        wst = ph.sbs("wo_st", [128, 2, D], F32, 2)
        cats = ph.sbs("catt", [128, 16, 128], BF16, 2)
        catab = (ph.sbs("catA", [128, 16, 128], BF16, 2), ph.sbs("catB", [128, 16, 128], BF16, 2)) if blend is not None else None
        xts = ph.sbs("xto", [128, D], F32, 2)
        xos = ph.sbs("xoo", [128, D], F32, 2)
        tmp = ph.sbs("tmpo", [128, 512], F32, 2)
        pacc = ph.pss("pacco", [128, 512], F32, 3)
        wv = w_out.t.rearrange("(k p) c -> p k c", p=128)
        for q in range(8):
            st = wst[q % 2]
            cx.dma("sp", st[:, :, :], wv[:, q * 2:(q + 1) * 2, :], r=[w_out], w=[st])
            cx.cp("pool", wo[:, q * 2:(q + 1) * 2, :], st[:, :, :], r=[st], wa=[wo])
        cat_v = catT_s.t.rearrange("(k p) c -> p k c", p=128) if catT_s is not None else None
        n = 0
        for ti, (xsrc, xdst, r0, c0, j) in enumerate(tiles):
            cat, xt, xo = cats[ti % 2], xts[ti % 2], xos[ti % 2]
            if loader is None:
                loader = lambda dst, col: cx.dma("sp", dst[:, :, :], cat_v[:, :, col:col + 128], r=[catT_s], w=[dst])
            if blend is None:
                loader(cat, c0)
            else:
                ca, cb_ = catab[0][ti % 2], catab[1][ti % 2]
                loader(ca, c0)
                loader(cb_, HALF + c0)
                cx.ts("pool", ca[:, :, :], ca[:, :, :], blend[:, 0:1], None, ALU.mult, None, r=[ca, blend], w=[ca])
                cx.stt("dve", cat[:, :, :], cb_[:, :, :], blend[:, 1:2], ca[:, :, :], ALU.mult, ALU.add, r=[ca, cb_, blend], w=[cat])
            cx.dma("sp", xt[:, :], xsrc[r0:r0 + 128, :], r=[xsrc], w=[xt])
            for cb in range(4):
                ps = pacc[n % 3]
                tm = tmp[n % 2]
                n += 1
                for k in range(16):
                    cx.mm(ps[:, :], cat[:, k, :], wo[:, k, cb * 512:(cb + 1) * 512], start=(k == 0), stop=(k == 15), r=[cat, wo], wa=[ps], sig=(k == 15))
                cx.tt("dve", tm[:, :], ps[:, :], gate_bc[:, j, cb * 512:(cb + 1) * 512], ALU.mult, r=[ps, gate_bc], w=[tm])
                cx.tt("pool", xo[:, cb * 512:(cb + 1) * 512], tm[:, :], xt[:, cb * 512:(cb + 1) * 512], ALU.add, r=[tm, xt], wa=[xo])
            cx.dma(STQ, xdst[r0:r0 + 128, :], xo[:, :], r=[xo], wa=[xdst])


T1 = CTX + SEQ
NT1 = T1 // 128
NCH = T1 // 64


def build_l1a(stop=99):
    cx = Ctx()
    _build_l1a(cx, stop)
    return cx


def _build_l1a(cx, stop=99, fused=False, c=None, x1_tiles=None, ctx1_in=None):
    nc = cx.nc
    sfx = "1" if fused else ""
    if not fused:
        x1_in = cx.din("x1f", [SEQ, D])
        ctx1_in = cx.din("ctx1", [CTX, D])
        x1_tiles = [(x1_in, t * 128) for t in range(32)]
    ccT_in = cx.din("ccT" + sfx, [128, 16, 2])
    w_ada = cx.din("w_ada" + sfx, [D, 3 * D])
    b_adaT_in = cx.din("b_adaT" + sfx, [128, 48])
    norm_gT_in = cx.din("norm_gT" + sfx, [128, 16])
    w_c = cx.din("w_c", [D, 5120])
    lbg_in = cx.din("lbg", [128, 2, 2, 8])
    ong_in = cx.din("ong", [128, 1])
    maskF_in = cx.din("maskF", [128, 128])
    maskB_in = cx.din("maskB", [128, 128])
    if fused:
        catT1_out = cx.dscr("catT1_s", [1024, SEQ], BF16)
    else:
        ident_in = cx.din("ident", [128, 128])
        catT1_out = cx.dout("catT1", [1024, SEQ], BF16)
        modT_out = cx.dout("modT1", [128, 48, 2], F32)
    qS = cx.dscr("qS", [8, 128, T1], F32)
    sgS = cx.dscr("sgS", [2, 8, 128, T1], F32)
    vS = cx.dscr("vS", [8, 128, NT1, 128], BF16)
    zS = cx.dscr("zS", [8, 128, T1], BF16)

    if c is None:
        c = common_consts(cx, ident_in)
    modT, gs, _ = modulation(cx, c, ccT_in, w_ada, b_adaT_in, norm_gT_in, want_gate_bc=(), sfx="1")
    if not fused:
        cx.dma("sp", modT_out[:, :, :], modT[:, :, :], r=[modT], w=[modT_out])
    lbt = cx.sb("lbt", [128, 2, 2, 8]); lb = cx.sb("lb", [128, 2, 8]); oml = cx.sb("oml", [128, 2, 8]); noml = cx.sb("noml", [128, 2, 8])
    ong = cx.sb("ong_t", [128, 1]); maskF = cx.sb("maskF_t", [128, 128]); maskB = cx.sb("maskB_t", [128, 128])
    cx.dma("sp", lbt[:, :, :, :], lbg_in[:, :, :, :], r=[lbg_in], w=[lbt])
    cx.dma("sp", ong[:, :], ong_in[:, :], r=[ong_in], w=[ong])
    cx.dma("sp", maskF[:, :], maskF_in[:, :], r=[maskF_in], w=[maskF])
    cx.dma("sp", maskB[:, :], maskB_in[:, :], r=[maskB_in], w=[maskB])
    cx.tt("dve", lb[:, :, :], lbt[:, :, 1, :], lbt[:, :, 0, :], ALU.subtract, r=[lbt], w=[lb])
    cx.act(lb[:, :, :], lb[:, :, :], AF.Sigmoid, r=[lb], w=[lb])
    cx.ts("dve", oml[:, :, :], lb[:, :, :], -1.0, 1.0, ALU.mult, ALU.add, r=[lb], w=[oml])
    cx.ts("dve", noml[:, :, :], lb[:, :, :], -1.0, None, ALU.add, None, r=[lb], w=[noml])
    if stop < 1:
        cx.S.barrier()
        return cx

    with Phase(cx) as ph:
        hT = ph.sb("hT", [128, 16, 1280], BF16)
        hTb = [Buf(hT.t, f"hT{i}") for i in range(10)]
        hbufs = (ph.sbs("xt", [128, D], F32, 2), ph.sbs("xn", [128, D], BF16, 2), ph.sb("sqj", [128, D], BF16),
                 ph.sbs("ss", [128, 4], F32, 2), ph.pss("pT", [128, 8, 128], BF16, 2))
        ws = WStream(cx, ph)
        pacc = ph.pss("pacc", [128, 512], F32, 3)
        vbs = ph.sbs("vb", [128, 512], BF16, 2)
        fof = ph.sbs("fof", [128, 512], F32, 3)
        fob = ph.sbs("fob", [128, 512], BF16, 2)
        cnt = {"acc": 0, "v": 0, "f": 0, "b": 0}
        colblocks = [("q", 0, 0), ("q", 512, 1), ("i", 1024, 0), ("i", 1536, 1), ("uf", 2048, 0), ("uf", 2560, 1),
                     ("ub", 3072, 0), ("ub", 3584, 1), ("z", 4096, 0), ("z", 4608, 1)]
        for t0 in range(0, NT1, 10):
            tl = list(range(t0, min(t0 + 10, NT1)))
            tiles = [(ctx1_in, t * 128, 1) if t < 2 else (x1_tiles[t - 2][0], x1_tiles[t - 2][1], 0) for t in tl]
            build_hT(cx, hbufs, c, modT, gs, hT, hTb, tiles)
            ntok = len(tl) * 128
            chunks = [(a, min(512, ntok - a)) for a in range(0, ntok, 512)]
            wnext = ws.fetch(w_c, colblocks[0][1])
            for ci, (fam, c0, sub) in enumerate(colblocks):
                wb = wnext
                if ci + 1 < len(colblocks):
                    wnext = ws.fetch(w_c, colblocks[ci + 1][1])
                if fam == "i":
                    for ti, t in enumerate(tl):
                        ps = pacc[cnt["acc"] % 3]; cnt["acc"] += 1
                        for k in range(16):
                            cx.mm(ps[:, :], hT[:, k, ti * 128:(ti + 1) * 128], wb[:, k, :], start=(k == 0), stop=(k == 15),
                                  r=[hTb[ti], wb], wa=[ps], sig=(k == 15))
                        vb = vbs[cnt["v"] % 2]; cnt["v"] += 1
                        cx.cp("act", vb[:, :], ps[:, :], r=[ps], w=[vb])
                        cx.dma(STQ, vS.t[sub * 4:sub * 4 + 4, :, t, :].rearrange("h p c -> p h c"),
                               vb[:, :].rearrange("p (h c) -> p h c", c=128), r=[vb], wa=[vS])
                else:
                    for (h0c, n) in chunks:
                        tt0 = h0c // 128
                        rb = [hTb[x] for x in range(tt0, tt0 + (n + 127) // 128)]
                        g0 = t0 * 128 + h0c
                        for fc in range(4):
                            ps = pacc[cnt["acc"] % 3]; cnt["acc"] += 1
                            for k in range(16):
                                cx.mm(ps[:, 0:n], wb[:, k, fc * 128:(fc + 1) * 128], hT[:, k, h0c:h0c + n], start=(k == 0), stop=(k == 15),
                                      r=rb + [wb], wa=[ps], sig=(k == 15))
                            h = sub * 4 + fc
                            if fam == "z":
                                fo = fob[cnt["b"] % 2]; cnt["b"] += 1
                                cx.act(fo[:, 0:n], ps[:, 0:n], AF.Silu, r=[ps], w=[fo])
                                cx.dma(STQ, zS[h, :, g0:g0 + n], fo[:, 0:n], r=[fo], wa=[zS])
                            else:
                                fo = fof[cnt["f"] % 3]; cnt["f"] += 1
                                cx.act(fo[:, 0:n], ps[:, 0:n], AF.Silu if fam == "q" else AF.Sigmoid, r=[ps], w=[fo])
                                if fam == "q":
                                    cx.dma(STQ, qS[h, :, g0:g0 + n], fo[:, 0:n], r=[fo], wa=[qS])
                                else:
                                    cx.dma(STQ, sgS[0 if fam == "uf" else 1, h, :, g0:g0 + n], fo[:, 0:n], r=[fo], wa=[sgS])
    if stop < 2:
        return cx

    with Phase(cx) as ph:
        msk = ph.sb("msk", [128, T1], F32)
        qf = ph.sb("qf", [128, T1], F32)
        vh = ph.sb("vh1", [128, NT1, 128], BF16)
        oacc = ph.sb("oacc", [128, SEQ], F32)
        lf = ph.sb("lf", [128, T1], F32)
        kf = ph.sb("kf", [128, T1], F32)
        bc = ph.sb("bcum", [128, T1], F32)
        ef = ph.sb("ef", [128, T1], F32)
        qdec = ph.sb("qdec", [128, T1], BF16)
        kinv = ph.sb("kinv", [128, T1], BF16)
        kendT = ph.sb("kendT", [128, T1], BF16)
        kend = ph.sb("kend", [128, NT1, 128], BF16)
        dec = ph.sb("dec", [128, NCH], F32)
        bend = ph.sb("bend", [128, NCH], F32)
        Sf = ph.sb("Sf", [128, 128], F32)
        Sbs = ph.sbs("Sb", [128, 128], BF16, 4)
        Ams = ph.sbs("Am", [128, 128], BF16, 3)
        sqr = ph.sb("sqr", [128, 512], F32); rst = ph.sb("rst", [128, 512], F32); onb = ph.sb("onb", [128, 512], F32)
        zcs = ph.sbs("zc", [128, 512], BF16, 2); obs = ph.sbs("ob1", [128, 512], BF16, 2)
        pA = ph.pss("pA", [128, 512], F32, 1)
        po = ph.pss("po1", [128, 512], F32, 2)
        pS = ph.pss("pS", [128, 512], F32, 4)
        pK = ph.pss("pK", [128, 8, 128], BF16, 1)
        ia_ = [0, 0]
        cx.memset("pool", msk[:, :], 1.0, w=[msk])
        cx.memset("pool", msk[:, :].rearrange("p (c d) -> p c d", d=64)[:, :, 0:1], 0.0, w=[msk])
        v3 = lambda a: a.rearrange("p (c d) -> p c d", d=64)
        HT = T1 // 2
        ia = 0
        isb = 0
        ipo = 0
        for h in range(8):
            cx.dma("sp", qf[:, :], qS[h, :, :], r=[qS], w=[qf])
            cx.dma("sp", vh[:, :, :], vS[h, :, :, :], r=[vS], w=[vh])
            for d_ in range(2):
                cx.dma("sp", lf[:, :], sgS[d_, h, :, :], r=[sgS], w=[lf])
                cx.ts("dve", kf[:, :], lf[:, :], noml[:, d_, h:h + 1], oml[:, d_, h:h + 1], ALU.mult, ALU.add, r=[lf, noml, oml], w=[kf])
                cx.act(lf[:, :], lf[:, :], AF.Ln, r=[lf, oml, lb], w=[lf], scale=oml[:, d_, h:h + 1], bias=lb[:, d_, h:h + 1])
                for hh in range(2):
                    sl = slice(hh * HT, (hh + 1) * HT)
                    cx.S.op("dve", lambda: nc.vector.tensor_tensor_scan(out=bc[:, sl], data0=msk[:, sl], data1=lf[:, sl], initial=0.0,
                                                                        op0=ALU.mult, op1=ALU.add), r=[msk, lf], wa=[bc])
                cx.cp("dve", bend[:, :], v3(bc[:, :])[:, :, 63], r=[bc], w=[bend])
                bend_bc = bend[:, :].unsqueeze(2).to_broadcast([128, NCH, 64])
                if d_ == 1:
                    cx.tt("dve", v3(ef[:, :]), v3(lf[:, :]), bend_bc, ALU.add, r=[lf, bend], w=[ef])
                    cx.tt("dve", bc[:, :], ef[:, :], bc[:, :], ALU.subtract, r=[ef, bc], w=[bc])
                cx.act(dec[:, :], bend[:, :], AF.Exp, r=[bend], w=[dec])
                cx.act(ef[:, :], bc[:, :], AF.Exp, r=[bc], w=[ef])
                cx.tt("dve", qdec[:, :], qf[:, :], ef[:, :], ALU.mult, r=[qf, ef], w=[qdec])
                cx.act(ef[:, :], bc[:, :], AF.Exp, r=[bc], w=[ef], scale=-1.0)
                cx.tt("dve", kinv[:, :], kf[:, :], ef[:, :], ALU.mult, r=[kf, ef], w=[kinv])
                cx.tt("dve", v3(ef[:, :]), bend_bc, v3(bc[:, :]), ALU.subtract, r=[bend, bc], w=[ef])
                cx.act(ef[:, :], ef[:, :], AF.Exp, r=[ef], w=[ef])
                cx.tt("dve", kendT[:, :], kf[:, :], ef[:, :], ALU.mult, r=[kf, ef], w=[kendT])
                for g in range(0, NT1, 8):
                    pk = pK[0]
                    ng = min(8, NT1 - g)
                    for t in range(ng):
                        cx.tr(pk[:, t, :], kendT[:, (g + t) * 128:(g + t + 1) * 128], c["identb"][:, :], r=[kendT, c["identb"]], wa=[pk], sig=(t == ng - 1))
                    cx.cp("pool" if False else "act", kend[:, g:g + ng, :], pk[:, 0:ng, :], r=[pk], wa=[kend])
                cx.memset("pool", Sf[:, :], 0.0, w=[Sf])
                Sb = Sbs[isb % 4]; isb += 1
                cx.memset("pool", Sb[:, :], 0.0, w=[Sb])
                mask = maskF if d_ == 0 else maskB
                pairs = list(range(NT1)) if d_ == 0 else [1, 0] + list(range(NT1 - 1, 1, -1))
                order = (0, 1) if d_ == 0 else (1, 0)
                st = {}

                def early(p):
                    t0_ = p * 128
                    e_ = {}
                    if p >= 2:
                        a_ps = pA[0]
                        Am = Ams[ia_[0] % 3]; ia_[0] += 1
                        cx.mm(a_ps[:, 0:128], kinv[:, t0_:t0_ + 128], qdec[:, t0_:t0_ + 128], True, True, r=[kinv, qdec], w=[a_ps])
                        cx.tt("dve", Am[:, :], a_ps[:, 0:128], mask[:, :], ALU.mult, r=[a_ps, mask], w=[Am])
                        e_["Am"] = Am
                    e_["s"] = []
                    for hc in order:
                        pr = slice(hc * 64, hc * 64 + 64)
                        s_ps = pS[ia_[1] % 4]; ia_[1] += 1
                        cx.mm(s_ps[:, 0:128], kend[pr, p, :], vh[pr, p, :], True, True, r=[kend, vh], w=[s_ps])
                        e_["s"].append(s_ps)
                    st[p] = e_

                early(pairs[0])
                for pi, p in enumerate(pairs):
                    if pi + 1 < len(pairs):
                        early(pairs[pi + 1])
                    e_ = st.pop(p)
                    lat = p >= 2
                    t0_ = p * 128
                    if lat:
                        o_ps = po[ipo % 2]; ipo += 1
                        cx.mm(o_ps[:, 0:128], vh[:, p, :], e_["Am"][:, :], True, False, r=[vh, e_["Am"]], wa=[o_ps], sig=False)
                    for oi, hc in enumerate(order):
                        c64 = slice(t0_ + hc * 64, t0_ + hc * 64 + 64)
                        if lat:
                            cx.mm(o_ps[:, hc * 64:hc * 64 + 64], Sb[:, :], qdec[:, c64], False, (oi == 1), r=[Sb, qdec], wa=[o_ps], sig=True)
                        s_ps = e_["s"][oi]
                        dcol = dec[:, 2 * p + hc:2 * p + hc + 1]
                        Sb = Sbs[isb % 4]; isb += 1
                        cx.stt("dve", Sb[:, :], Sf[:, :], dcol, s_ps[:, 0:128], ALU.mult, ALU.add, r=[Sf, dec, s_ps], w=[Sb])
                        cx.stt("dve", Sf[:, :], Sf[:, :], dcol, s_ps[:, 0:128], ALU.mult, ALU.add, r=[Sf, dec, s_ps], w=[Sf])
                    if lat:
                        oc = slice((p - 2) * 128, (p - 1) * 128)
                        if d_ == 0:
                            cx.cp("act", oacc[:, oc], o_ps[:, 0:128], r=[o_ps], wa=[oacc])
                        else:
                            cx.tt("dve", oacc[:, oc], oacc[:, oc], o_ps[:, 0:128], ALU.add, r=[o_ps, oacc], wa=[oacc])
            for q8 in range(8):
                cs = slice(q8 * 512, (q8 + 1) * 512)
                a_ps = pA[0]
                zc = zcs[q8 % 2]; ob = obs[q8 % 2]
                cx.dma("sp", zc[:, :], zS[h, :, CTX + q8 * 512:CTX + (q8 + 1) * 512], r=[zS], w=[zc])
                cx.act(sqr[:, :], oacc[:, cs], AF.Square, r=[oacc], w=[sqr])
                cx.mm(a_ps[:, :], c["onesf"][:, :], sqr[:, :], True, True, r=[c["onesf"], sqr], w=[a_ps])
                cx.act(rst[:, :], a_ps[:, :], AF.Sqrt, r=[a_ps, c["eps"]], w=[rst], bias=c["eps"][:, 0:1], scale=1.0 / 128)
                cx.recip(rst[:, :], rst[:, :], r=[rst], w=[rst])
                cx.tt("dve", onb[:, :], oacc[:, cs], rst[:, :], ALU.mult, r=[oacc, rst], w=[onb])
                cx.stt("dve", ob[:, :], onb[:, :], ong[:, 0:1], zc[:, :], ALU.mult, ALU.mult, r=[onb, ong, zc], w=[ob])
                cx.dma(STQ, catT1_out[h * 128:(h + 1) * 128, cs], ob[:, :], r=[ob], wa=[catT1_out])
    return catT1_out, modT


PAIRS = [[0, 1], [2, 3], [4, 5], [6, 7]]
WARMUP_COLL = True


def build_fused():
    cx = Ctx()
    ident_in = cx.din("ident", [128, 128])
    bmask_in = cx.din("bmask", [128, 2])
    w_out1 = cx.din("w_out1", [D, D])
    x2_out = cx.dout("x2", [HALF, D])
    c = common_consts(cx, ident_in)
    if WARMUP_COLL:
        wu_src = cx.dscr("wu_src", [128, 128], F32)
        wu_dst = cx.dscr("wu_dst", [256, 128], F32)
        cx.dma("sp", wu_src[:, :], ident_in[:, :], r=[ident_in], w=[wu_src])
        cx.S.coll("AllGather", wu_dst[:, :], wu_src[:, :], PAIRS, r=[wu_src], w=[wu_dst])
    with Scope(cx):
        x1_own_s, ctx1_s = _build_l0(cx, 99, fused=True, c=c)
    x1g_t = cx.nc.dram_tensor("x1g_s", [8, 512, D], F32, kind="Internal").ap()
    x1g = [Buf(x1g_t[i], f"x1g{i}") for i in range(8)]
    for i in range(8):
        cx.S.coll("AllGather", x1g[i][:, :], x1_own_s[i * 256:(i + 1) * 256, :], PAIRS, r=[x1_own_s], w=[x1g[i]])
    x1_tiles = []
    for t in range(32):
        r_, lt = t // 16, t % 16
        x1_tiles.append((x1g[lt // 2], r_ * 256 + (lt % 2) * 128))
    with Scope(cx):
        catT1_s, modT = _build_l1a(cx, 99, fused=True, c=c, x1_tiles=x1_tiles, ctx1_in=ctx1_s)
        catg_t = cx.nc.dram_tensor("catg_s", [4, 512, SEQ], BF16, kind="Internal").ap()
        catg = [Buf(catg_t[i], f"catg{i}") for i in range(4)]
        for i in range(4):
            cx.S.coll("AllGather", catg[i][:, :], catT1_s[i * 256:(i + 1) * 256, :], PAIRS, r=[catT1_s], w=[catg[i]])
        gate_bc = cx.sb("gate_bc1", [128, 2, D], F32)
        bmask = cx.sb("bmask_t", [128, 2], F32)
        cx.dma("sp", bmask[:, :], bmask_in[:, :], r=[bmask_in], w=[bmask])
        gate_rows(cx, c, modT, gate_bc, (0,))

        def loader(dst, col):
            for r_ in range(2):
                for i in range(4):
                    k0 = r_ * 8 + i * 2
                    cx.dma("sp", dst[:, k0:k0 + 2, :],
                           catg[i].t[r_ * 256:(r_ + 1) * 256, col:col + 128].rearrange("(k p) c -> p k c", p=128),
                           r=[catg[i]], wa=[dst])
        out_proj(cx, w_out1, None, gate_bc, [(x1_own_s, x2_out, t * 128, t * 128, 0) for t in range(16)], blend=bmask, loader=loader)
    return cx


def build_l1b():
    cx = Ctx()
    catT_in = cx.din("catT", [D, HALF], BF16)
    x1_own = cx.din("x1o", [HALF, D])
    w_out = cx.din("w_out", [D, D])
    modT_in = cx.din("modT1", [128, 48, 2])
    ident_in = cx.din("ident", [128, 128])
    x2_out = cx.dout("x2", [HALF, D])
    c = common_consts(cx, ident_in)
    modT = cx.sb("modT", [128, 48, 2], F32)
    gate_bc = cx.sb("gate_bc", [128, 2, D], F32)
    cx.dma("sp", modT[:, :, :], modT_in[:, :, :], r=[modT_in], w=[modT])
    gate_rows(cx, c, modT, gate_bc, (0,))
    out_proj(cx, w_out, catT_in, gate_bc, [(x1_own, x2_out, t * 128, t * 128, 0) for t in range(16)])
    return cx


def rope_tables():
    rows = SEQ // 64
    row = np.repeat(np.arange(rows), 64).astype(np.float32)
    col = np.tile(np.arange(64), rows).astype(np.float32)
    inv = (10000.0 ** (-np.arange(0, 32, 2, dtype=np.float32) / 32)).astype(np.float32)

    def axis_angles(pos):
        a = pos[:, None] * inv[None, :]
        return np.concatenate([a, a], axis=-1)
    ang = np.concatenate([axis_angles(row), axis_angles(col)], axis=-1).astype(np.float32)
    cos, sin = np.cos(ang), np.sin(ang)
    sgn = np.tile(np.concatenate([-np.ones(16), np.ones(16)]), 2).astype(np.float32)
    return np.concatenate([cos, sin * sgn], axis=-1).astype(np.float32)


def fm(v, nchunk):
    return np.ascontiguousarray(np.asarray(v, np.float32).reshape(nchunk, 128).T)


_CACHE = {}


def run_l0(inp, cores):
    if "l0" not in _CACHE:
        _CACHE["l0"] = build_l0()
    cx = _CACHE["l0"]
    rope = rope_tables()
    ident = np.eye(128, dtype=np.float32)
    in_maps = []
    for cid in cores:
        b, hf = cid // 2, cid % 2
        own = slice(hf * HALF, (hf + 1) * HALF)
        oth = slice((1 - hf) * HALF, (2 - hf) * HALF)
        cc = np.stack([inp["c"][b], inp["c_ctx"]], axis=-1)
        ccT = np.ascontiguousarray(cc.reshape(16, 128, 2).transpose(1, 0, 2))
        cw = inp["conv_w"][0]
        cwT = np.ascontiguousarray(cw.reshape(31, 8, 128).transpose(2, 1, 0))
        cvp = np.ascontiguousarray(np.stack([fm(inp["conv_b"][0], 8), fm(inp["cln_g"][0], 8), fm(inp["cln_b"][0], 8)], axis=1))
        hm = np.zeros((128, 2), np.float32)
        hm[:, 0] = 1.0 if hf == 1 else 0.0
        hm[:, 1] = 1.0 if hf == 0 else 0.0
        in_maps.append({
            "x_own": np.ascontiguousarray(inp["x"][b, own]), "x_oth": np.ascontiguousarray(inp["x"][b, oth]),
            "ctx": np.ascontiguousarray(inp["ctx"][b]), "ccT": ccT,
            "w_ada": np.ascontiguousarray(inp["w_ada"][0]), "b_adaT": fm(inp["b_ada"][0], 48), "norm_gT": fm(inp["norm_g"][0], 16),
            "w_in": np.ascontiguousarray(inp["w_in_ab"][0]), "w_out": np.ascontiguousarray(inp["w_out_ab"][0]),
            "qkg": np.ascontiguousarray(np.stack([inp["qn_g"][0], inp["kn_g"][0]])),
            "lamv": np.ascontiguousarray(np.stack([inp["lam_q1"][0], inp["lam_k1"][0], inp["lam_q2"][0], inp["lam_k2"][0]])),
            "subg": np.ascontiguousarray(inp["subln_g"][0].reshape(128, 1)),
            "cwT": cwT, "cvp": cvp,
            "rq_own": np.ascontiguousarray(rope[own] * np.float32(0.125)), "rk_own": np.ascontiguousarray(rope[own]),
            "rk_oth": np.ascontiguousarray(rope[oth]), "hmask": hm, "ident": ident,
        })
    res = run_bass_kernel_spmd(cx.nc, in_maps, core_ids=list(range(len(cores))))
    return res.results


def hgrn_masks():
    i = np.arange(128)
    same = (i[:, None] // 64) == (i[None, :] // 64)
    mF = (same & (i[:, None] <= i[None, :])).astype(np.float32)
    mB = (same & (i[:, None] >= i[None, :])).astype(np.float32)
    return mF, mB


def run_l1a(inp, x1, ctx1, cores):
    if "l1a" not in _CACHE:
        _CACHE["l1a"] = build_l1a()
    cx = _CACHE["l1a"]
    ident = np.eye(128, dtype=np.float32)
    mF, mB = hgrn_masks()
    wc = inp["w_in_c"][0]
    in_maps = []
    for cid in cores:
        b, hf = cid // 2, cid % 2
        cc = np.stack([inp["c"][b], inp["c_ctx"]], axis=-1)
        ccT = np.ascontiguousarray(cc.reshape(16, 128, 2).transpose(1, 0, 2))
        cols = np.concatenate([np.arange(f * D + hf * 1024, f * D + (hf + 1) * 1024) for f in range(5)])
        lg = inp["lb_gamma"][:, :, hf * 1024:(hf + 1) * 1024]
        lbg = np.ascontiguousarray(lg.reshape(2, 2, 8, 128).transpose(3, 0, 1, 2))
        in_maps.append({
            "x1f": np.ascontiguousarray(x1[b]), "ctx1": np.ascontiguousarray(ctx1[b]), "ccT": ccT,
            "w_ada": np.ascontiguousarray(inp["w_ada"][1]), "b_adaT": fm(inp["b_ada"][1], 48), "norm_gT": fm(inp["norm_g"][1], 16),
            "w_c": np.ascontiguousarray(wc[:, cols]), "lbg": lbg,
            "ong": np.ascontiguousarray(inp["onorm_g"][0].reshape(128, 1)),
            "maskF": mF, "maskB": mB, "ident": ident,
        })
    res = run_bass_kernel_spmd(cx.nc, in_maps, core_ids=list(range(len(cores))))
    return res.results


def run_l1b(inp, x1, cat_pairs, modTs, cores):
    if "l1b" not in _CACHE:
        _CACHE["l1b"] = build_l1b()
    cx = _CACHE["l1b"]
    ident = np.eye(128, dtype=np.float32)
    in_maps = []
    for i, cid in enumerate(cores):
        b, hf = cid // 2, cid % 2
        own = slice(hf * HALF, (hf + 1) * HALF)
        in_maps.append({
            "catT": np.ascontiguousarray(cat_pairs[b][:, own]), "x1o": np.ascontiguousarray(x1[b, own]),
            "w_out": np.ascontiguousarray(inp["w_out_c"][0]), "modT1": modTs[i], "ident": ident,
        })
    res = run_bass_kernel_spmd(cx.nc, in_maps, core_ids=list(range(len(cores))))
    return res.results


def run_l1(inp, x1, ctx1, cores):
    ra = run_l1a(inp, x1, ctx1, cores)
    cat_pairs = {}
    for i, cid in enumerate(cores):
        b, hf = cid // 2, cid % 2
        cat_pairs.setdefault(b, [None, None])[hf] = ra[i]["catT1"]
    cat_pairs = {b: np.concatenate(v, axis=0) for b, v in cat_pairs.items()}
    rb = run_l1b(inp, x1, cat_pairs, [ra[i]["modT1"] for i in range(len(cores))], cores)
    return rb


def l0_inputs(inp, cid, rope):
    b, hf = cid // 2, cid % 2
    own = slice(hf * HALF, (hf + 1) * HALF)
    oth = slice((1 - hf) * HALF, (2 - hf) * HALF)
    cc = np.stack([inp["c"][b], inp["c_ctx"]], axis=-1)
    ccT = np.ascontiguousarray(cc.reshape(16, 128, 2).transpose(1, 0, 2))
    cw = inp["conv_w"][0]
    cwT = np.ascontiguousarray(cw.reshape(31, 8, 128).transpose(2, 1, 0))
    cvp = np.ascontiguousarray(np.stack([fm(inp["conv_b"][0], 8), fm(inp["cln_g"][0], 8), fm(inp["cln_b"][0], 8)], axis=1))
    hm = np.zeros((128, 2), np.float32)
    hm[:, 0] = 1.0 if hf == 1 else 0.0
    hm[:, 1] = 1.0 if hf == 0 else 0.0
    return {
        "x_own": np.ascontiguousarray(inp["x"][b, own]), "x_oth": np.ascontiguousarray(inp["x"][b, oth]),
        "ctx": np.ascontiguousarray(inp["ctx"][b]), "ccT": ccT,
        "w_ada": np.ascontiguousarray(inp["w_ada"][0]), "b_adaT": fm(inp["b_ada"][0], 48), "norm_gT": fm(inp["norm_g"][0], 16),
        "w_in": np.ascontiguousarray(inp["w_in_ab"][0]), "w_out": np.ascontiguousarray(inp["w_out_ab"][0]),
        "qkg": np.ascontiguousarray(np.stack([inp["qn_g"][0], inp["kn_g"][0]])),
        "lamv": np.ascontiguousarray(np.stack([inp["lam_q1"][0], inp["lam_k1"][0], inp["lam_q2"][0], inp["lam_k2"][0]])),
        "subg": np.ascontiguousarray(inp["subln_g"][0].reshape(128, 1)),
        "cwT": cwT, "cvp": cvp,
        "rq_own": np.ascontiguousarray(rope[own] * np.float32(0.125)), "rk_own": np.ascontiguousarray(rope[own]),
        "rk_oth": np.ascontiguousarray(rope[oth]), "hmask": hm,
    }


def l1_inputs(inp, cid):
    b, hf = cid // 2, cid % 2
    cc = np.stack([inp["c"][b], inp["c_ctx"]], axis=-1)
    ccT = np.ascontiguousarray(cc.reshape(16, 128, 2).transpose(1, 0, 2))
    cols = np.concatenate([np.arange(f * D + hf * 1024, f * D + (hf + 1) * 1024) for f in range(5)])
    lg = inp["lb_gamma"][:, :, hf * 1024:(hf + 1) * 1024]
    lbg = np.ascontiguousarray(lg.reshape(2, 2, 8, 128).transpose(3, 0, 1, 2))
    mF, mB = hgrn_masks()
    bm = np.zeros((128, 2), np.float32)
    bm[:, hf] = 1.0
    return {
        "ccT1": ccT, "w_ada1": np.ascontiguousarray(inp["w_ada"][1]), "b_adaT1": fm(inp["b_ada"][1], 48),
        "norm_gT1": fm(inp["norm_g"][1], 16), "w_c": np.ascontiguousarray(inp["w_in_c"][0][:, cols]), "lbg": lbg,
        "ong": np.ascontiguousarray(inp["onorm_g"][0].reshape(128, 1)), "maskF": mF, "maskB": mB,
        "w_out1": np.ascontiguousarray(inp["w_out_c"][0]), "bmask": bm,
    }


def run_fused(inp, cores):
    if "fused" not in _CACHE:
        _CACHE["fused"] = build_fused()
    cx = _CACHE["fused"]
    rope = rope_tables()
    ident = np.eye(128, dtype=np.float32)
    in_maps = []
    for cid in cores:
        m = {"ident": ident}
        m.update(l0_inputs(inp, cid, rope))
        m.update(l1_inputs(inp, cid))
        in_maps.append(m)
    res = run_bass_kernel_spmd(cx.nc, in_maps, core_ids=list(range(len(cores))))
    return res.results


def kernel_unfused(**inputs):
    inp = {k: np.asarray(v) for k, v in inputs.items()}
    cores = list(range(8))
    r0 = run_l0(inp, cores)
    x1 = np.zeros_like(inp["x"])
    ctx1 = np.zeros_like(inp["ctx"])
    for cid in cores:
        b, hf = cid // 2, cid % 2
        x1[b, hf * HALF:(hf + 1) * HALF] = r0[cid]["x1"]
        ctx1[b] = r0[cid]["ctx1"]
    r1 = run_l1(inp, x1, ctx1, cores)
    out = np.zeros_like(inp["x"])
    for cid in cores:
        b, hf = cid // 2, cid % 2
        out[b, hf * HALF:(hf + 1) * HALF] = r1[cid]["x2"]
    return out


def kernel(**inputs):
    inp = {k: np.asarray(v) for k, v in inputs.items()}
    cores = list(range(8))
    r = run_fused(inp, cores)
    out = np.zeros_like(inp["x"])
    for cid in cores:
        b, hf = cid // 2, cid % 2
        out[b, hf * HALF:(hf + 1) * HALF] = r[cid]["x2"]
    return out
```

```python
from contextlib import ExitStack
import math
import numpy as np
import ml_dtypes
import concourse.bass as bass
import concourse.mybir as mybir
from concourse.bass_utils import run_bass_kernel_spmd

F32 = mybir.dt.float32
BF16 = mybir.dt.bfloat16
AF = mybir.ActivationFunctionType
ALU = mybir.AluOpType
AX = mybir.AxisListType

D = 2048
SEQ = 4096
HALF = 2048
CTX = 256
EPS = 1e-6
NKEY = CTX + SEQ
STQ = "act"


class Buf:
    def __init__(self, t, name="", psum=False):
        self.t = t
        self.name = name
        self.psum = psum
        self.w = {}
        self.r = {}
        self.pr = {}
        self.open = False

    def __getitem__(self, k):
        return self.t[k]


def _merge(d, s):
    for k, v in s.items():
        if d.get(k, 0) < v:
            d[k] = v


class Sched:
    GEN = 12000
    NSLOT = 8

    def __init__(self, nc):
        self.nc = nc
        self.eng = {"pe": nc.tensor, "dve": nc.vector, "act": nc.scalar,
                    "pool": nc.gpsimd, "sp": nc.sync}
        self.cnt = {e: 0 for e in self.eng}
        self.gen = {e: 0 for e in self.eng}
        self.sems = {}
        self.waited = {e: {} for e in self.eng}
        self.dma_i = {e: 0 for e in self.eng}
        self.nops = 0
        self.nwaits = 0

    def sem(self, key):
        if key not in self.sems:
            self.sems[key] = self.nc.alloc_semaphore("s_" + "_".join(str(k) for k in key))
        return self.sems[key]

    def _wait(self, e, deps):
        for key, val in deps.items():
            if key[0] == "E" and key[1] == e and e in ("pe", "sp"):
                continue
            if self.waited[e].get(key, 0) >= val:
                continue
            self.eng[e].wait_ge(self.sem(key), val)
            self.waited[e][key] = val
            self.nwaits += 1

    def _deps(self, e, r, w, wa):
        deps = {}
        for b in r:
            _merge(deps, b.w)
            if b.psum:
                _merge(deps, {k: v for k, v in b.r.items() if k[1] != e})
        for b in w:
            _merge(deps, b.w)
            _merge(deps, b.r)
            _merge(deps, b.pr)
        for b in wa:
            if not b.open:
                b.pr = dict(b.r)
                _merge(b.pr, b.w)
                b.r = {}
                b.w = {}
                b.open = True
            _merge(deps, b.pr)
        return deps

    def _record(self, key, val, r, w, wa):
        ev = {key: val}
        for b in r:
            _merge(b.r, ev)
            b.open = False
        for b in w:
            b.w = dict(ev)
            b.r = {}
            b.pr = {}
            b.open = False
        for b in wa:
            _merge(b.w, ev)

    def op(self, e, fn, r=(), w=(), wa=(), sig=True):
        deps = self._deps(e, r, w, wa)
        self._wait(e, deps)
        ins = fn()
        self.nops += 1
        key = ("E", e, self.gen[e])
        if sig:
            self.cnt[e] += 1
            ins.then_inc(self.sem(key), 1)
            self._record(key, self.cnt[e], r, w, wa)
            if self.cnt[e] >= self.GEN:
                self.gen[e] += 1
                self.cnt[e] = 0
        else:
            self._record(key, self.cnt[e] + 1, r, w, wa)
        return ins

    def dma(self, e, out, in_, r=(), w=(), wa=(), **kw):
        i = self.dma_i[e]
        slot = i % self.NSLOT
        key = ("D", e, slot)
        val = 16 * (i // self.NSLOT + 1)
        deps = self._deps(e, r, w, wa)
        if val > 16:
            _merge(deps, {key: val - 16})
        self._wait(e, deps)
        ins = self.eng[e].dma_start(out=out, in_=in_, **kw)
        ins.then_inc(self.sem(key), 16)
        self.dma_i[e] = i + 1
        self._record(key, val, r, w, wa)
        return ins

    def coll(self, kind, out, in_, groups, r=(), w=()):
        e = "pool"
        self.ncoll = getattr(self, "ncoll", 0) + 1
        key = ("C", e, self.ncoll)
        deps = self._deps(e, r, w, ())
        self._wait(e, deps)
        ins = self.nc.gpsimd.collective_compute(kind, ALU.bypass, replica_groups=groups, ins=[in_], outs=[out])
        ins.then_inc(self.sem(key), 1)
        self.colls = getattr(self, "colls", {})
        self.colls[key] = 1
        self._record(key, 1, r, w, ())
        return ins

    def barrier(self):
        allev = {}
        for e in self.eng:
            if self.cnt[e] > 0:
                allev[("E", e, self.gen[e])] = self.cnt[e]
            elif self.gen[e] > 0:
                allev[("E", e, self.gen[e] - 1)] = self.GEN
        for e in self.eng:
            n = self.dma_i[e]
            for slot in range(min(n, self.NSLOT)):
                last = ((n - 1 - slot) // self.NSLOT) * self.NSLOT + slot
                allev[("D", e, slot)] = 16 * (last // self.NSLOT + 1)
        allev.update(getattr(self, "colls", {}))
        for e in self.eng:
            for key, val in allev.items():
                if key[0] == "E" and key[1] == e and e in ("pe", "sp"):
                    continue
                if self.waited[e].get(key, 0) >= val:
                    continue
                self.eng[e].wait_ge(self.sem(key), val)
                self.waited[e][key] = val
                self.nwaits += 1


class Ctx:
    def __init__(self):
        self.nc = bass.Bass("TRN2", target_bir_lowering=False)
        self.S = Sched(self.nc)
        self.later = []
        self.scope = None
        self.uid = 0

    def un(self, name):
        self.uid += 1
        return f"{name}_u{self.uid}"

    def din(self, name, shape, dt=F32):
        return Buf(self.nc.dram_tensor(name, list(shape), dt, kind="ExternalInput").ap(), name)

    def dout(self, name, shape, dt=F32):
        return Buf(self.nc.dram_tensor(name, list(shape), dt, kind="ExternalOutput").ap(), name)

    def dscr(self, name, shape, dt=BF16):
        return Buf(self.nc.dram_tensor(name, list(shape), dt, kind="Internal").ap(), name)

    def sb(self, name, shape, dt=F32):
        name = self.un(name)
        if self.scope is not None:
            return Buf(self.scope.enter_context(self.nc.sbuf_tensor(name, list(shape), dt)), name)
        return Buf(self.nc.alloc_sbuf_tensor(name, list(shape), dt), name)

    def act(self, out, in_, func, r, w=(), wa=(), **kw):
        nc = self.nc
        return self.S.op("act", lambda: nc.scalar.activation(out=out, in_=in_, func=func, **kw), r=r, w=w, wa=wa)

    def _ve(self, e):
        return self.nc.vector if e == "dve" else self.nc.gpsimd

    def tt(self, e, out, in0, in1, op, r, w=(), wa=()):
        eng = self._ve(e)
        return self.S.op(e, lambda: eng.tensor_tensor(out=out, in0=in0, in1=in1, op=op), r=r, w=w, wa=wa)

    def ts(self, e, out, in0, s1, s2, op0, op1, r, w=(), wa=()):
        eng = self._ve(e)
        if s2 is None:
            return self.S.op(e, lambda: eng.tensor_scalar(out=out, in0=in0, scalar1=s1, scalar2=None, op0=op0), r=r, w=w, wa=wa)
        return self.S.op(e, lambda: eng.tensor_scalar(out=out, in0=in0, scalar1=s1, scalar2=s2, op0=op0, op1=op1), r=r, w=w, wa=wa)

    def stt(self, e, out, in0, scalar, in1, op0, op1, r, w=(), wa=()):
        eng = self._ve(e)
        return self.S.op(e, lambda: eng.scalar_tensor_tensor(out=out, in0=in0, scalar=scalar, in1=in1, op0=op0, op1=op1), r=r, w=w, wa=wa)

    def cp(self, e, out, in_, r, w=(), wa=()):
        if e == "act":
            nc = self.nc
            return self.S.op("act", lambda: nc.scalar.copy(out=out, in_=in_), r=r, w=w, wa=wa)
        eng = self._ve(e)
        return self.S.op(e, lambda: eng.tensor_copy(out=out, in_=in_), r=r, w=w, wa=wa)

    def recip(self, out, in_, r, w=(), wa=()):
        nc = self.nc
        return self.S.op("dve", lambda: nc.vector.reciprocal(out=out, in_=in_), r=r, w=w, wa=wa)

    def memset(self, e, ap, val, w=(), wa=()):
        eng = self._ve(e)
        return self.S.op(e, lambda: eng.memset(ap, val), w=w, wa=wa)

    def mm(self, out, lhsT, rhs, start, stop, r, w=(), wa=(), sig=True):
        nc = self.nc
        return self.S.op("pe", lambda: nc.tensor.matmul(out, lhsT=lhsT, rhs=rhs, start=start, stop=stop), r=r, w=w, wa=wa, sig=sig)

    def tr(self, out, in_, ident, r, w=(), wa=(), sig=True):
        nc = self.nc
        return self.S.op("pe", lambda: nc.tensor.transpose(out, in_, ident), r=r, w=w, wa=wa, sig=sig)

    def dma(self, e, out, in_, r, w=(), wa=()):
        return self.S.dma(e, out, in_, r=r, w=w, wa=wa)

    def defer(self, fn):
        self.later.append(fn)

    def flush(self):
        l, self.later = self.later, []
        for fn in l:
            fn()


class Scope:
    def __init__(self, cx):
        self.cx = cx

    def __enter__(self):
        self.es = ExitStack()
        self.es.__enter__()
        self.cx.scope = self.es
        return self

    def __exit__(self, *a):
        self.cx.flush()
        self.cx.S.barrier()
        self.cx.scope = None
        return self.es.__exit__(*a)


class Phase:
    def __init__(self, cx):
        self.cx = cx
        self.es = ExitStack()

    def __enter__(self):
        self.es.__enter__()
        return self

    def __exit__(self, *a):
        self.cx.flush()
        self.cx.S.barrier()
        return self.es.__exit__(*a)

    def sb(self, name, shape, dt=F32, n=1):
        name = self.cx.un(name)
        t = self.es.enter_context(self.cx.nc.sbuf_tensor(name, list(shape), dt))
        return Buf(t, name)

    def sbs(self, name, shape, dt, n):
        return [self.sb(f"{name}{i}", shape, dt) for i in range(n)]

    def ps(self, name, shape, dt=F32):
        nb = int(np.prod(shape[1:])) * (4 if dt == F32 else 2)
        assert nb % 2048 == 0, (name, shape)
        name = self.cx.un(name)
        t = self.es.enter_context(self.cx.nc.psum_tensor(name, list(shape), dt))
        return Buf(t, name, psum=True)

    def pss(self, name, shape, dt, n):
        return [self.ps(f"{name}{i}", shape, dt) for i in range(n)]


def common_consts(cx, ident_in):
    c = {}
    c["identf"] = cx.sb("identf", [128, 128], F32)
    c["identb"] = cx.sb("identb", [128, 128], BF16)
    c["onesf"] = cx.sb("onesf", [128, 128], F32)
    c["eps"] = cx.sb("epsT", [128, 1], F32)
    cx.dma("sp", c["identf"][:, :], ident_in[:, :], r=[ident_in], w=[c["identf"]])
    cx.cp("dve", c["identb"][:, :], c["identf"][:, :], r=[c["identf"]], w=[c["identb"]])
    cx.memset("pool", c["onesf"][:, :], 1.0, w=[c["onesf"]])
    c["onesb"] = cx.sb("onesb", [128, 128], BF16)
    cx.memset("pool", c["onesb"][:, :], 1.0, w=[c["onesb"]])
    cx.memset("pool", c["eps"][:, :], EPS, w=[c["eps"]])
    return c


def modulation(cx, c, ccT_in, w_ada, b_adaT_in, norm_gT_in, want_gate_bc=(0, 1), sfx=""):
    nc = cx.nc
    modT = cx.sb("modT" + sfx, [128, 48, 2], F32)
    gs = cx.sb("gsT" + sfx, [128, 16, 2], F32)
    gate_bc = cx.sb("gate_bc" + sfx, [128, 2, D], F32) if want_gate_bc else None
    with Phase(cx) as ph:
        scT = ph.sb("scT", [128, 16, 2], F32)
        badaT = ph.sb("badaT", [128, 48], F32)
        ngT = ph.sb("ngT", [128, 16], F32)
        wst = ph.sbs("wada_st", [128, 16, 512], F32, 2)
        pm = ph.ps("pm", [128, 512], F32)
        cx.dma("sp", scT[:, :, :], ccT_in[:, :, :], r=[ccT_in], w=[scT])
        cx.dma("sp", badaT[:, :], b_adaT_in[:, :], r=[b_adaT_in], w=[badaT])
        cx.dma("sp", ngT[:, :], norm_gT_in[:, :], r=[norm_gT_in], w=[ngT])
        cx.act(scT[:, :, :], scT[:, :, :], AF.Silu, r=[scT], w=[scT])
        wv = w_ada.t.rearrange("(k p) c -> p k c", p=128)
        for cb in range(12):
            st = wst[cb % 2]
            for hh in range(2):
                cx.dma("sp" if hh == 0 else "act", st[:, hh * 8:(hh + 1) * 8, :], wv[:, hh * 8:(hh + 1) * 8, cb * 512:(cb + 1) * 512],
                       r=[w_ada], wa=[st])
            for fc in range(4):
                cc = cb * 4 + fc
                for k in range(16):
                    cx.mm(pm[:, cc * 2:cc * 2 + 2], st[:, k, fc * 128:(fc + 1) * 128], scT[:, k, :],
                          start=(k == 0), stop=(k == 15), r=[st, scT], wa=[pm], sig=(k == 15))
        cx.tt("dve", modT[:, :, :], pm[:, 0:96].rearrange("p (c j) -> p c j", j=2),
              badaT[:, :].unsqueeze(2).to_broadcast([128, 48, 2]), ALU.add, r=[pm, badaT], w=[modT])
        cx.stt("dve", gs[:, :, :], modT[:, 16:32, :], 1.0, ngT[:, :].unsqueeze(2).to_broadcast([128, 16, 2]),
               ALU.add, ALU.mult, r=[modT, ngT], w=[gs])
    if want_gate_bc:
        gate_rows(cx, c, modT, gate_bc, want_gate_bc)
    return modT, gs, gate_bc


def gate_rows(cx, c, modT, gate_bc, js):
    with Phase(cx) as ph:
        dgs = ph.sbs("dgate", [128, 128], F32, 2)
        pgs = ph.pss("pgate", [128, 512], F32, 2)
        i = 0
        for j in js:
            for k in range(16):
                dg = dgs[i % 2]
                pg = pgs[i % 2]
                cx.ts("dve", dg[:, :], c["identf"][:, :], modT[:, 32 + k, j:j + 1], None, ALU.mult, None,
                      r=[c["identf"], modT], w=[dg])
                cx.mm(pg[:, 0:128], c["onesf"][:, :], dg[:, :], True, True, r=[c["onesf"], dg], w=[pg])
                cx.cp("act", gate_bc[:, j, k * 128:(k + 1) * 128], pg[:, 0:128], r=[pg], wa=[gate_bc])
                i += 1


def build_hT(cx, ph_bufs, c, modT, gs, hT, hTb, tiles):
    xts, xns, sqj, sss, pTs = ph_bufs

    def stage_a(i):
        src, r0, j = tiles[i]
        xt, xn, ss = xts[i % 2], xns[i % 2], sss[i % 2]
        cx.dma("sp", xt[:, :], src[r0:r0 + 128, :], r=[src], w=[xt])
        cx.memset("pool", ss[:, :], 0.0, w=[ss])
        cx.act(sqj[:, :], xt[:, :], AF.Square, r=[xt, ss], w=[sqj, ss], accum_out=ss[:, 0:1])
        cx.act(ss[:, 1:2], ss[:, 0:1], AF.Sqrt, r=[ss, c["eps"]], w=[ss], bias=c["eps"][:, 0:1], scale=1.0 / D)
        cx.recip(ss[:, 2:3], ss[:, 1:2], r=[ss], w=[ss])
        cx.ts("dve", xn[:, :], xt[:, :], ss[:, 2:3], None, ALU.mult, None, r=[xt, ss], w=[xn])

    def stage_b(i):
        src, r0, j = tiles[i]
        xn = xns[i % 2]
        for half in range(2):
            pT = pTs[half]
            for kk in range(8):
                k = half * 8 + kk
                cx.tr(pT[:, kk, :], xn[:, k * 128:(k + 1) * 128], c["identb"][:, :], r=[xn, c["identb"]],
                      wa=[pT], sig=(kk == 7))
            for kk in range(8):
                k = half * 8 + kk
                dst = hT[:, k, i * 128:(i + 1) * 128]
                if half == 0:
                    cx.act(dst, pT[:, kk, :], AF.Identity, r=[pT, gs, modT], wa=[hTb[i]],
                           scale=gs[:, k, j:j + 1], bias=modT[:, k, j:j + 1])
                else:
                    cx.ts("dve", dst, pT[:, kk, :], gs[:, k, j:j + 1], modT[:, k, j:j + 1], ALU.mult, ALU.add,
                          r=[pT, gs, modT], wa=[hTb[i]])

    stage_a(0)
    for i in range(len(tiles)):
        if i + 1 < len(tiles):
            stage_a(i + 1)
        stage_b(i)


class WStream:
    def __init__(self, cx, ph, name="w"):
        self.cx = cx
        self.st = ph.sbs(name + "_st", [128, 4, 512], F32, 2)
        self.wb = ph.sbs(name + "_bf", [128, 16, 512], BF16, 2)
        self.n = 0
        self.si = 0

    def fetch(self, w, c0):
        cx = self.cx
        wb = self.wb[self.n % 2]
        self.n += 1
        wv = w.t.rearrange("(k p) c -> p k c", p=128)
        for q in range(4):
            st = self.st[self.si % 2]
            self.si += 1
            cx.dma("sp", st[:, :, :], wv[:, q * 4:(q + 1) * 4, c0:c0 + 512], r=[w], w=[st])
            cx.cp("pool", wb[:, q * 4:(q + 1) * 4, :], st[:, :, :], r=[st], wa=[wb])
        return wb


LAM_INIT0 = 0.8 - 0.6 * math.exp(-0.3 * 0)


HT_DBG = 0
STOPF = {1.3: ("q",), 1.4: ("v",), 1.5: ("za",), 1.6: ("gg", "gv")}


class _Stop(Exception):
    pass


def build_l0(stop=99):
    cx = Ctx()
    try:
        _build_l0(cx, stop)
    except _Stop:
        cx.S.barrier()
    return cx


def _build_l0(cx, stop=99, fused=False, c=None):
    nc = cx.nc
    x_own = cx.din("x_own", [HALF, D])
    x_oth = cx.din("x_oth", [HALF, D])
    ctx_in = cx.din("ctx", [CTX, D])
    ccT_in = cx.din("ccT", [128, 16, 2])
    w_ada = cx.din("w_ada", [D, 3 * D])
    b_adaT_in = cx.din("b_adaT", [128, 48])
    norm_gT_in = cx.din("norm_gT", [128, 16])
    w_in = cx.din("w_in", [D, 7168])
    w_out = cx.din("w_out", [D, D])
    qkg_in = cx.din("qkg", [2, 64])
    lamv_in = cx.din("lamv", [4, 64])
    subg_in = cx.din("subg", [128, 1])
    cwT_in = cx.din("cwT", [128, 8, 31])
    cvp_in = cx.din("cvp", [128, 3, 8])
    rq_own = cx.din("rq_own", [HALF, 128])
    rk_own = cx.din("rk_own", [HALF, 128])
    rk_oth = cx.din("rk_oth", [HALF, 128])
    hmask_in = cx.din("hmask", [128, 2])
    if fused:
        x1_out = cx.dscr("x1_own_s", [HALF, D], F32)
        ctx1_out = cx.dscr("ctx1_s", [CTX, D], F32)
    else:
        ident_in = cx.din("ident", [128, 128])
        x1_out = cx.dout("x1", [HALF, D])
        ctx1_out = cx.dout("ctx1", [CTX, D])
    qT_s = cx.dscr("qT_s", [8, 128, HALF + CTX])
    kT_s = cx.dscr("kT_s", [8, 128, NKEY])
    v_s = cx.dscr("v_s", [NKEY, 1024])
    zaT_s = cx.dscr("zaT_s", [1024, HALF + CTX])
    zbT_s = cx.dscr("zbT_s", [1024, HALF + CTX])
    yT_s = cx.dscr("yT_s", [1024, HALF + 30])
    yTc_s = cx.dscr("yTc_s", [1024, CTX + 30])
    catT_s = cx.dscr("catT_s", [D, HALF + CTX])

    if c is None:
        c = common_consts(cx, ident_in)
    modT, gs, gate_bc = modulation(cx, c, ccT_in, w_ada, b_adaT_in, norm_gT_in)

    qg = cx.sb("qg", [128, 64]); qgs = cx.sb("qgs", [128, 64]); kg = cx.sb("kg", [128, 64])
    lamt = cx.sb("lamt", [128, 4, 64]); lam4 = cx.sb("lam4", [128, 8])
    subg = cx.sb("subg_t", [128, 1])
    hmask = cx.sb("hmask_t", [128, 2])
    cvp = cx.sb("cvp_t", [128, 3, 8])
    cwT = cx.sb("cwT_t", [128, 8, 31])
    cx.dma("sp", qg[:, :], qkg_in[0, :].partition_broadcast(128), r=[qkg_in], w=[qg])
    cx.dma("sp", kg[:, :], qkg_in[1, :].partition_broadcast(128), r=[qkg_in], w=[kg])
    cx.dma("sp", lamt[:, :, :].rearrange("p a d -> p (a d)"),
           lamv_in.t.rearrange("a d -> (a d)").partition_broadcast(128), r=[lamv_in], w=[lamt])
    cx.dma("sp", subg[:, :], subg_in[:, :], r=[subg_in], w=[subg])
    cx.dma("sp", hmask[:, :], hmask_in[:, :], r=[hmask_in], w=[hmask])
    cx.dma("sp", cvp[:, :, :], cvp_in[:, :, :], r=[cvp_in], w=[cvp])
    cx.dma("sp", cwT[:, :, :], cwT_in[:, :, :], r=[cwT_in], w=[cwT])
    cx.ts("dve", qgs[:, :], qg[:, :], 0.125, None, ALU.mult, None, r=[qg], w=[qgs])
    cx.tt("dve", lamt[:, 0, :], lamt[:, 0, :], lamt[:, 1, :], ALU.mult, r=[lamt], w=[lamt])
    cx.tt("dve", lamt[:, 2, :], lamt[:, 2, :], lamt[:, 3, :], ALU.mult, r=[lamt], w=[lamt])
    cx.S.op("dve", lambda: nc.vector.reduce_sum(out=lam4[:, 0:1], in_=lamt[:, 0, :], axis=AX.X), r=[lamt], w=[lam4])
    cx.S.op("dve", lambda: nc.vector.reduce_sum(out=lam4[:, 1:2], in_=lamt[:, 2, :], axis=AX.X), r=[lamt, lam4], w=[lam4])
    cx.act(lam4[:, 2:4], lam4[:, 0:2], AF.Exp, r=[lam4], w=[lam4])
    cx.stt("dve", lam4[:, 4:5], lam4[:, 3:4], -LAM_INIT0, lam4[:, 2:3], ALU.add, ALU.subtract, r=[lam4], w=[lam4])
    neglam = lam4[:, 4:5]
    cx.ts("dve", subg[:, :], subg[:, :], 1.0 - LAM_INIT0, None, ALU.mult, None, r=[subg], w=[subg])

    if stop < 1:
        cx.S.barrier()
        return cx
    with Phase(cx) as ph:
        hT_t = ph.sb("hT", [128, 16, 1280], BF16)
        hTb = [Buf(hT_t.t, f"hT{i}") for i in range(10)]
        hT = hT_t
        hbufs = (ph.sbs("xt", [128, D], F32, 2), ph.sbs("xn", [128, D], BF16, 2), ph.sb("sqj", [128, D], BF16),
                 ph.sbs("ss", [128, 4], F32, 2), ph.pss("pT", [128, 8, 128], BF16, 2))
        ws = WStream(cx, ph)
        pacc = ph.pss("pacc", [128, 512], F32, 3)
        pq = ph.pss("pq", [128, 8, 128], BF16, 2)
        sqs = ph.sbs("sq", [128, 512], F32, 2)
        st8 = ph.sbs("st8", [128, 16], F32, 2)
        xnq = ph.sbs("xnq", [128, 512], F32, 2)
        t1s = ph.sbs("t1", [128, 512], F32, 2)
        t2s = ph.sbs("t2", [128, 512], F32, 2)
        qbs = ph.sbs("qb", [128, 512], BF16, 2)
        qTs = ph.sbs("qTs", [128, 4, 128], BF16, 2)
        rts = ph.sbs("rt", [128, 128], F32, 2)
        vbs = ph.sbs("vb", [128, 512], BF16, 2)
        fos = ph.sbs("fo", [128, 512], BF16, 2)
        sig_t = ph.sb("sig", [128, 4, 1280], BF16)
        sigb = [Buf(sig_t.t, f"sig{i}") for i in range(4)]
        zero_t = ph.sb("zero", [128, 15], BF16)
        cnt = {"acc": 0, "qk": 0, "v": 0, "fo": 0}

        cx.memset("pool", zero_t[:, :], 0.0, w=[zero_t])
        yTc_v = yTc_s.t.rearrange("(j p) c -> p j c", p=128)
        for j in range(8):
            cx.dma("sp", yTc_v[:, j, 0:15], zero_t[:, :], r=[zero_t], wa=[yTc_s])
            cx.dma("sp", yTc_v[:, j, CTX + 15:CTX + 30], zero_t[:, :], r=[zero_t], wa=[yTc_s])
        if stop == 1.1:
            raise _Stop

        def qk_epi(ps, tile, fam, h0):
            kind, idx = tile
            i = cnt["qk"]; cnt["qk"] += 1
            sq, s8, xq, t1, t2, qb, qTt, rt, pqt = sqs[i % 2], st8[i % 2], xnq[i % 2], t1s[i % 2], t2s[i % 2], qbs[i % 2], qTs[i % 2], rts[i % 2], pq[i % 2]
            cx.act(sq[:, :], ps[:, :], AF.Square, r=[ps], w=[sq])
            cx.S.op("dve", lambda: nc.vector.reduce_sum(out=s8[:, 0:8], in_=sq[:, :].rearrange("p (g d) -> p g d", d=64), axis=AX.X), r=[sq], w=[s8])
            cx.act(s8[:, 8:16], s8[:, 0:8], AF.Sqrt, r=[s8, c["eps"]], w=[s8], bias=c["eps"][:, 0:1], scale=1.0 / 64)
            cx.recip(s8[:, 0:8], s8[:, 8:16], r=[s8], w=[s8])
            v3 = lambda a: a.rearrange("p (g d) -> p g d", d=64)
            cx.tt("dve", v3(xq[:, :]), v3(ps[:, :]), s8[:, 0:8].unsqueeze(2).to_broadcast([128, 8, 64]), ALU.mult, r=[ps, s8], w=[xq])
            g = kg if fam == "k" else (qgs if kind == "ctx" else qg)
            cx.tt("dve", v3(xq[:, :]), v3(xq[:, :]), g[:, :].unsqueeze(1).to_broadcast([128, 8, 64]), ALU.mult, r=[xq, g], w=[xq])
            if kind == "ctx":
                cx.cp("act", qb[:, :], xq[:, :], r=[xq], w=[qb])
            else:
                rsrc = (rq_own if fam == "q" else rk_own) if kind == "own" else rk_oth
                cx.dma("sp", rt[:, :], rsrc[idx * 128:(idx + 1) * 128, :], r=[rsrc], w=[rt])
                cx.tt("dve", v3(t1[:, :]), v3(xq[:, :]), rt[:, 0:64].unsqueeze(1).to_broadcast([128, 8, 64]), ALU.mult, r=[xq, rt], w=[t1])
                for a in range(2):
                    lo, hi = a * 32, a * 32 + 16
                    e = "dve" if a == 0 else "pool"
                    cx.tt(e, v3(t2[:, :])[:, :, lo:lo + 16], v3(xq[:, :])[:, :, hi:hi + 16],
                          rt[:, 64 + lo:64 + lo + 16].unsqueeze(1).to_broadcast([128, 8, 16]), ALU.mult, r=[xq, rt], wa=[t2])
                    cx.tt(e, v3(t2[:, :])[:, :, hi:hi + 16], v3(xq[:, :])[:, :, lo:lo + 16],
                          rt[:, 64 + hi:64 + hi + 16].unsqueeze(1).to_broadcast([128, 8, 16]), ALU.mult, r=[xq, rt], wa=[t2])
                cx.tt("dve", qb[:, :], t1[:, :], t2[:, :], ALU.add, r=[t1, t2], w=[qb])
            if fam == "q":
                dst_s = qT_s
                c0 = idx * 128 if kind == "own" else HALF + idx * 128
            else:
                dst_s = kT_s
                c0 = {"ctx": 0, "own": CTX, "oth": CTX + HALF}[kind] + idx * 128

            def fin():
                for hh in range(4):
                    cx.tr(pqt[:, hh, :], qb[:, hh * 128:(hh + 1) * 128], c["identb"][:, :], r=[qb, c["identb"]], wa=[pqt], sig=(hh == 3))
                cx.cp("dve", qTt[:, :, :], pqt[:, 0:4, :], r=[pqt], w=[qTt])
                cx.dma(STQ, dst_s.t[h0:h0 + 4, :, c0:c0 + 128].rearrange("h p t -> p h t"), qTt[:, :, :], r=[qTt], wa=[dst_s])
            cx.defer(fin)

        def v_epi(ps, tile, cb2):
            kind, idx = tile
            i = cnt["v"]; cnt["v"] += 1
            vb = vbs[i % 2]
            r0 = {"ctx": 0, "own": CTX, "oth": CTX + HALF}[kind] + idx * 128
            cx.cp("act", vb[:, :], ps[:, :], r=[ps], w=[vb])
            cx.defer(lambda: cx.dma(STQ, v_s[r0:r0 + 128, cb2 * 512:(cb2 + 1) * 512], vb[:, :], r=[vb], wa=[v_s]))

        own = lambda a, b: [("own", i) for i in range(a, b)]
        blocks = [
            dict(tiles=own(0, 8) + [("ctx", 0), ("ctx", 1)], full=10, fams="all",
                 chunks=[(0, 512, "own", 0), (512, 512, "own", 512), (1024, 256, "ctx", 0)]),
            dict(tiles=own(8, 16) + [("oth", 0), ("oth", 15)], full=8, fams="all",
                 chunks=[(0, 512, "own", 1024), (512, 512, "own", 1536), (1024, 256, "halo", 0)]),
            dict(tiles=[("oth", i) for i in range(1, 8)], full=0, fams="kv", chunks=[]),
            dict(tiles=[("oth", i) for i in range(8, 15)], full=0, fams="kv", chunks=[]),
        ]
        srcmap = {"own": (x_own, 0), "oth": (x_oth, 0), "ctx": (ctx_in, 1)}
        colblocks = [("q", 0, 0), ("q", 512, 1), ("k", 1024, 0), ("k", 1536, 1), ("v", 2048, 0), ("v", 2560, 1),
                     ("za", 3072, 0), ("za", 3584, 1), ("gg", 5120, 0), ("gv", 4096, 0), ("gg", 5632, 1), ("gv", 4608, 1),
                     ("zb", 6144, 0), ("zb", 6656, 1)]
        for blk in blocks:
            tiles = blk["tiles"]
            build_hT(cx, hbufs, c, modT, gs, hT, hTb, [(srcmap[k][0], i * 128, srcmap[k][1]) for (k, i) in tiles])
            if stop == 1.2:
                raise _Stop
            cbl = colblocks if blk["fams"] == "all" else [cb for cb in colblocks if cb[0] in ("k", "v")]
            if stop in STOPF:
                cbl = [cb for cb in cbl if cb[0] in STOPF[stop]]
            wnext = ws.fetch(w_in, cbl[0][1])
            for ci, (fam, c0, sub) in enumerate(cbl):
                wb = wnext
                if ci + 1 < len(cbl):
                    wnext = ws.fetch(w_in, cbl[ci + 1][1])
                if fam in ("q", "k", "v"):
                    for ti, tile in enumerate(tiles):
                        if tile[0] == "oth" and fam == "q":
                            continue
                        ps = pacc[cnt["acc"] % 3]; cnt["acc"] += 1
                        for k in range(16):
                            cx.mm(ps[:, :], hT[:, k, ti * 128:(ti + 1) * 128], wb[:, k, :], start=(k == 0), stop=(k == 15),
                                  r=[hTb[ti], wb], wa=[ps], sig=(k == 15))
                        pend, cx.later = cx.later, []
                        if fam == "v":
                            v_epi(ps, tile, sub)
                        else:
                            qk_epi(ps, tile, fam, sub * 4)
                        for fn in pend:
                            fn()
                    cx.flush()
                else:
                    for (h0c, n, ckind, d0) in blk["chunks"]:
                        if ckind == "halo" and fam not in ("gg", "gv"):
                            continue
                        t0 = h0c // 128
                        rb = [hTb[t] for t in range(t0, t0 + (n + 127) // 128)]
                        for fc in range(4):
                            ps = pacc[cnt["acc"] % 3]; cnt["acc"] += 1
                            for k in range(16):
                                cx.mm(ps[:, 0:n], wb[:, k, fc * 128:(fc + 1) * 128], hT[:, k, h0c:h0c + n], start=(k == 0), stop=(k == 15),
                                      r=rb + [wb], wa=[ps], sig=(k == 15))
                            frow = sub * 512 + fc * 128
                            dcol = d0 if ckind == "own" else HALF + d0
                            if fam in ("za", "zb"):
                                fo = fos[cnt["fo"] % 2]; cnt["fo"] += 1
                                dst = zaT_s if fam == "za" else zbT_s
                                cx.act(fo[:, 0:n], ps[:, 0:n], AF.Silu, r=[ps], w=[fo])
                                cx.dma(STQ, dst[frow:frow + 128, dcol:dcol + n], fo[:, 0:n], r=[fo], wa=[dst])
                            elif fam == "gg":
                                cx.act(sig_t[:, fc, h0c:h0c + n], ps[:, 0:n], AF.Sigmoid, r=[ps], wa=[sigb[fc]])
                            else:
                                fo = fos[cnt["fo"] % 2]; cnt["fo"] += 1
                                cx.tt("dve", fo[:, 0:n], ps[:, 0:n], sig_t[:, fc, h0c:h0c + n], ALU.mult, r=[ps, sigb[fc]], w=[fo])
                                if ckind == "own":
                                    cx.dma(STQ, yT_s[frow:frow + 128, 15 + d0:15 + d0 + n], fo[:, 0:n], r=[fo], wa=[yT_s])
                                elif ckind == "ctx":
                                    cx.dma(STQ, yTc_s[frow:frow + 128, 15 + d0:15 + d0 + n], fo[:, 0:n], r=[fo], wa=[yTc_s])
                                else:
                                    cx.ts("dve", fo[:, 0:15], fo[:, 0:15], hmask[:, 1:2], None, ALU.mult, None, r=[fo, hmask], w=[fo])
                                    cx.ts("dve", fo[:, 241:256], fo[:, 241:256], hmask[:, 0:1], None, ALU.mult, None, r=[fo, hmask], w=[fo])
                                    cx.dma(STQ, yT_s[frow:frow + 128, 15 + HALF:30 + HALF], fo[:, 0:15], r=[fo], wa=[yT_s])
                                    cx.dma(STQ, yT_s[frow:frow + 128, 0:15], fo[:, 241:256], r=[fo], wa=[yT_s])
            cx.flush()
            if stop in STOPF:
                raise _Stop

    if stop < 2:
        return cx
    with Phase(cx) as ph:
        kTs = ph.sbs("kTh", [128, NKEY], BF16, 2)
        vhs = ph.sbs("vh", [128, 34, 128], BF16, 2)
        qhs = ph.sbs("qh", [128, HALF + CTX], BF16, 2)
        pST = ph.pss("pST", [128, 2, 512], F32, 2)
        po = ph.ps("po", [128, 2, 512], F32)
        psm = ph.ps("psm", [128, 2, 512], F32)
        pts = ph.sbs("pt", [128, 2, 512], BF16, 3)
        accs = ph.sbs("accP", [128, 2, 512], F32, 2)
        accm = [[Buf(a.t, a.name + "m0"), Buf(a.t, a.name + "m1")] for a in accs]
        rs = ph.sb("rs", [128, 2, 512], F32)
        ta = ph.sb("ta", [128, 512], F32); tb = ph.sb("tb", [128, 512], F32)
        ot = ph.sb("ot", [128, 512], F32); sqo = ph.sb("sqo", [128, 512], F32)
        rstd = ph.sb("rstdo", [128, 512], F32)
        zat = ph.sbs("zat", [128, 512], BF16, 2)
        obs = ph.sbs("ob", [128, 512], BF16, 2)
        v_v = v_s.t.rearrange("(kt p) c -> p kt c", p=128)
        step = 0
        ci = 0
        def load_head(h):
            cx.dma("sp", kTs[h % 2][:, :], kT_s[h, :, :], r=[kT_s], w=[kTs[h % 2]])
            cx.dma("sp", vhs[h % 2][:, :, :], v_v[:, :, h * 128:(h + 1) * 128], r=[v_s], w=[vhs[h % 2]])
            cx.dma("sp", qhs[h % 2][:, :], qT_s[h, :, :], r=[qT_s], w=[qhs[h % 2]])

        load_head(0)
        for h in range(8):
            kTh, vh, qh = kTs[h % 2], vhs[h % 2], qhs[h % 2]
            if h + 1 < 8:
                load_head(h + 1)
            for (q0, n, nkt) in [(0, 512, 34), (512, 512, 34), (1024, 512, 34), (1536, 512, 34), (HALF, 256, 2)]:
                acc = accs[ci % 2]
                za = zat[ci % 2]
                cx.dma("sp", za[:, 0:n], zaT_s[h * 128:(h + 1) * 128, q0:q0 + n], r=[zaT_s], w=[za])
                def qk(kt_, st_):
                    for m in range(2):
                        cx.mm(st_[:, m, 0:n], kTh[64 * m:64 * m + 64, kt_ * 128:(kt_ + 1) * 128], qh[64 * m:64 * m + 64, q0:q0 + n],
                              True, True, r=[kTh, qh], wa=[st_], sig=(m == 1))

                qk(0, pST[step % 2])
                for kt in range(nkt):
                    st = pST[step % 2]
                    pt = pts[step % 3]
                    step += 1
                    if kt + 1 < nkt:
                        qk(kt + 1, pST[step % 2])
                    cx.act(pt[:, :, 0:n], st[:, :, 0:n], AF.Exp, r=[st], w=[pt])
                    for m in range(2):
                        cx.mm(po[:, m, 0:n], vh[:, kt, :], pt[:, m, 0:n], start=(kt == 0), stop=(kt == nkt - 1),
                              r=[vh, pt], wa=[po], sig=(m == 1))
                    cx.mm(psm[:, 0, 0:n], c["onesb"][:, :], pt[:, 0, 0:n], start=(kt == 0), stop=(kt == nkt - 1),
                          r=[c["onesb"], pt], wa=[psm], sig=True)
                    am = accm[ci % 2][0]
                    if kt == 0:
                        cx.cp("dve", acc[:, 1, 0:n], pt[:, 1, 0:n], r=[pt], w=[am])
                    else:
                        cx.tt("dve", acc[:, 1, 0:n], acc[:, 1, 0:n], pt[:, 1, 0:n], ALU.add, r=[pt, am], w=[am])
                cx.mm(psm[:, 1, 0:n], c["onesf"][:, :], acc[:, 1, 0:n], True, True, r=[c["onesf"], accm[ci % 2][0]], wa=[psm], sig=True)
                cx.recip(rs[:, :, 0:n], psm[:, :, 0:n], r=[psm], w=[rs])
                cx.tt("dve", ta[:, 0:n], po[:, 0, 0:n], rs[:, 0, 0:n], ALU.mult, r=[po, rs], w=[ta])
                cx.tt("dve", tb[:, 0:n], po[:, 1, 0:n], rs[:, 1, 0:n], ALU.mult, r=[po, rs], w=[tb])
                cx.stt("dve", ot[:, 0:n], tb[:, 0:n], neglam, ta[:, 0:n], ALU.mult, ALU.add, r=[ta, tb, lam4], w=[ot])
                cx.act(sqo[:, 0:n], ot[:, 0:n], AF.Square, r=[ot], w=[sqo])
                cx.mm(psm[:, 0, 0:n], c["onesf"][:, :], sqo[:, 0:n], True, True, r=[c["onesf"], sqo], w=[psm])
                cx.act(rstd[:, 0:n], psm[:, 0, 0:n], AF.Sqrt, r=[psm, c["eps"]], w=[rstd], bias=c["eps"][:, 0:1], scale=1.0 / 128)
                cx.recip(rstd[:, 0:n], rstd[:, 0:n], r=[rstd], w=[rstd])
                cx.tt("dve", ot[:, 0:n], ot[:, 0:n], rstd[:, 0:n], ALU.mult, r=[ot, rstd], w=[ot])
                ob = obs[ci % 2]
                cx.stt("dve", ob[:, 0:n], ot[:, 0:n], subg[:, 0:1], za[:, 0:n], ALU.mult, ALU.mult, r=[ot, subg, za], w=[ob])
                cx.dma(STQ, catT_s[h * 128:(h + 1) * 128, q0:q0 + n], ob[:, 0:n], r=[ob], wa=[catT_s])
                ci += 1

    if stop < 3:
        return cx
    with Phase(cx) as ph:
        dg = ph.sb("dgc", [128, 8, 31, 128], BF16)
        ycs = ph.sbs("yc", [128, 8, 542], BF16, 2)
        convT = ph.sb("convT", [128, 8, 512], F32)
        sqT = ph.sb("sqT", [128, 8, 512], F32)
        onesc = ph.sb("onesc", [128, 128], F32)
        pconv = ph.pss("pconv", [128, 512], F32, 2)
        pst = ph.ps("pstat", [128, 2, 512], F32)
        mean = ph.sb("mean", [128, 512], F32); var = ph.sb("var", [128, 512], F32)
        tcs = ph.sbs("tc", [128, 512], F32, 2)
        scs = ph.sbs("sc", [128, 512], F32, 2)
        zbt = ph.sbs("zbt", [128, 512], BF16, 2)
        obs = ph.sbs("obc", [128, 512], BF16, 2)
        cx.memset("pool", onesc[:, :], 1.0 / 1024, w=[onesc])
        i = 0
        for j in range(8):
            for k in range(31):
                e = "dve" if i % 2 == 0 else "pool"
                cx.ts(e, dg[:, j, k, :], c["identf"][:, :], cwT[:, j, k:k + 1], None, ALU.mult, None, r=[c["identf"], cwT], wa=[dg])
                i += 1
        yT_v = yT_s.t.rearrange("(j p) c -> p j c", p=128)
        it = 0
        for (src, srcb, c0, n, dcol) in [(yT_v, yT_s, 0, 512, 0), (yT_v, yT_s, 512, 512, 512), (yT_v, yT_s, 1024, 512, 1024),
                                         (yT_v, yT_s, 1536, 512, 1536), (yTc_v, yTc_s, 0, 256, HALF)]:
            yc = ycs[it % 2]; it += 1
            cx.dma("sp", yc[:, :, 0:n + 30], src[:, :, c0:c0 + n + 30], r=[srcb], w=[yc])
            for j in range(8):
                pc = pconv[j % 2]
                for k in range(31):
                    cx.mm(pc[:, 0:n], dg[:, j, k, :], yc[:, j, k:k + n], start=(k == 0), stop=(k == 30), r=[dg, yc], wa=[pc], sig=(k == 30))
                cx.act(convT[:, j, 0:n], pc[:, 0:n], AF.Identity, r=[pc, cvp], wa=[convT], bias=cvp[:, 0, j:j + 1], scale=1.0)
                cx.act(sqT[:, j, 0:n], convT[:, j, 0:n], AF.Square, r=[convT], wa=[sqT])
            for j in range(8):
                cx.mm(pst[:, 0, 0:n], onesc[:, :], convT[:, j, 0:n], start=(j == 0), stop=(j == 7), r=[onesc, convT], wa=[pst], sig=(j == 7))
            for j in range(8):
                cx.mm(pst[:, 1, 0:n], onesc[:, :], sqT[:, j, 0:n], start=(j == 0), stop=(j == 7), r=[onesc, sqT], wa=[pst], sig=(j == 7))
            cx.cp("act", mean[:, 0:n], pst[:, 0, 0:n], r=[pst], w=[mean])
            cx.tt("dve", var[:, 0:n], mean[:, 0:n], mean[:, 0:n], ALU.mult, r=[mean], w=[var])
            cx.tt("dve", var[:, 0:n], pst[:, 1, 0:n], var[:, 0:n], ALU.subtract, r=[pst, var], w=[var])
            cx.ts("dve", var[:, 0:n], var[:, 0:n], 0.0, None, ALU.max, None, r=[var], w=[var])
            cx.act(var[:, 0:n], var[:, 0:n], AF.Sqrt, r=[var, c["eps"]], w=[var], bias=c["eps"][:, 0:1], scale=1.0)
            cx.recip(var[:, 0:n], var[:, 0:n], r=[var], w=[var])
            for j in range(8):
                tcb, scb, zb, ob = tcs[j % 2], scs[j % 2], zbt[j % 2], obs[j % 2]
                cx.dma("sp", zb[:, 0:n], zbT_s[j * 128:(j + 1) * 128, dcol:dcol + n], r=[zbT_s], w=[zb])
                cx.tt("dve", tcb[:, 0:n], convT[:, j, 0:n], mean[:, 0:n], ALU.subtract, r=[convT, mean], w=[tcb])
                cx.tt("pool", tcb[:, 0:n], tcb[:, 0:n], var[:, 0:n], ALU.mult, r=[tcb, var], w=[tcb])
                cx.act(scb[:, 0:n], tcb[:, 0:n], AF.Silu, r=[tcb, cvp], w=[scb], scale=cvp[:, 1, j:j + 1], bias=cvp[:, 2, j:j + 1])
                cx.tt("dve", ob[:, 0:n], scb[:, 0:n], zb[:, 0:n], ALU.mult, r=[scb, zb], w=[ob])
                cx.dma(STQ, catT_s[1024 + j * 128:1024 + (j + 1) * 128, dcol:dcol + n], ob[:, 0:n], r=[ob], wa=[catT_s])

    if stop < 4:
        return cx
    out_proj(cx, w_out, catT_s, gate_bc,
             [(x_own, x1_out, t * 128, t * 128, 0) for t in range(16)] + [(ctx_in, ctx1_out, t * 128, HALF + t * 128, 1) for t in range(2)])
    return x1_out, ctx1_out


def out_proj(cx, w_out, catT_s, gate_bc, tiles, blend=None, loader=None):
    with Phase(cx) as ph:
        wo = ph.sb("wo", [128, 16, D], BF16)
        wst = ph.sbs("wo_st", [128, 2, D], F32, 2)
        cats = ph.sbs("catt", [128, 16, 128], BF16, 2)
        catab = (ph.sbs("catA", [128, 16, 128], BF16, 2), ph.sbs("catB", [128, 16, 128], BF16, 2)) if blend is not None else None
        xts = ph.sbs("xto", [128, D], F32, 2)
        xos = ph.sbs("xoo", [128, D], F32, 2)
        tmp = ph.sbs("tmpo", [128, 512], F32, 2)
        pacc = ph.pss("pacco", [128, 512], F32, 3)
        wv = w_out.t.rearrange("(k p) c -> p k c", p=128)
        for q in range(8):
            st = wst[q % 2]
            cx.dma("sp", st[:, :, :], wv[:, q * 2:(q + 1) * 2, :], r=[w_out], w=[st])
            cx.cp("pool", wo[:, q * 2:(q + 1) * 2, :], st[:, :, :], r=[st], wa=[wo])
        cat_v = catT_s.t.rearrange("(k p) c -> p k c", p=128) if catT_s is not None else None
        n = 0
        for ti, (xsrc, xdst, r0, c0, j) in enumerate(tiles):
            cat, xt, xo = cats[ti % 2], xts[ti % 2], xos[ti % 2]
            if loader is None:
                loader = lambda dst, col: cx.dma("sp", dst[:, :, :], cat_v[:, :, col:col + 128], r=[catT_s], w=[dst])
            if blend is None:
                loader(cat, c0)
            else:
                ca, cb_ = catab[0][ti % 2], catab[1][ti % 2]
                loader(ca, c0)
                loader(cb_, HALF + c0)
                cx.ts("pool", ca[:, :, :], ca[:, :, :], blend[:, 0:1], None, ALU.mult, None, r=[ca, blend], w=[ca])
                cx.stt("dve", cat[:, :, :], cb_[:, :, :], blend[:, 1:2], ca[:, :, :], ALU.mult, ALU.add, r=[ca, cb_, blend], w=[cat])
            cx.dma("sp", xt[:, :], xsrc[r0:r0 + 128, :], r=[xsrc], w=[xt])
            for cb in range(4):
                ps = pacc[n % 3]
                tm = tmp[n % 2]
                n += 1
                for k in range(16):
                    cx.mm(ps[:, :], cat[:, k, :], wo[:, k, cb * 512:(cb + 1) * 512], start=(k == 0), stop=(k == 15), r=[cat, wo], wa=[ps], sig=(k == 15))
                cx.tt("dve", tm[:, :], ps[:, :], gate_bc[:, j, cb * 512:(cb + 1) * 512], ALU.mult, r=[ps, gate_bc], w=[tm])
                cx.tt("pool", xo[:, cb * 512:(cb + 1) * 512], tm[:, :], xt[:, cb * 512:(cb + 1) * 512], ALU.add, r=[tm, xt], wa=[xo])
            cx.dma(STQ, xdst[r0:r0 + 128, :], xo[:, :], r=[xo], wa=[xdst])


T1 = CTX + SEQ
NT1 = T1 // 128
NCH = T1 // 64


def build_l1a(stop=99):
    cx = Ctx()
    _build_l1a(cx, stop)
    return cx


def _build_l1a(cx, stop=99, fused=False, c=None, x1_tiles=None, ctx1_in=None):
    nc = cx.nc
    sfx = "1" if fused else ""
    if not fused:
        x1_in = cx.din("x1f", [SEQ, D])
        ctx1_in = cx.din("ctx1", [CTX, D])
        x1_tiles = [(x1_in, t * 128) for t in range(32)]
    ccT_in = cx.din("ccT" + sfx, [128, 16, 2])
    w_ada = cx.din("w_ada" + sfx, [D, 3 * D])
    b_adaT_in = cx.din("b_adaT" + sfx, [128, 48])
    norm_gT_in = cx.din("norm_gT" + sfx, [128, 16])
    w_c = cx.din("w_c", [D, 5120])
    lbg_in = cx.din("lbg", [128, 2, 2, 8])
    ong_in = cx.din("ong", [128, 1])
    maskF_in = cx.din("maskF", [128, 128])
    maskB_in = cx.din("maskB", [128, 128])
    if fused:
        catT1_out = cx.dscr("catT1_s", [1024, SEQ], BF16)
    else:
        ident_in = cx.din("ident", [128, 128])
        catT1_out = cx.dout("catT1", [1024, SEQ], BF16)
        modT_out = cx.dout("modT1", [128, 48, 2], F32)
    qS = cx.dscr("qS", [8, 128, T1], F32)
    sgS = cx.dscr("sgS", [2, 8, 128, T1], F32)
    vS = cx.dscr("vS", [8, 128, NT1, 128], BF16)
    zS = cx.dscr("zS", [8, 128, T1], BF16)

    if c is None:
        c = common_consts(cx, ident_in)
    modT, gs, _ = modulation(cx, c, ccT_in, w_ada, b_adaT_in, norm_gT_in, want_gate_bc=(), sfx="1")
    if not fused:
        cx.dma("sp", modT_out[:, :, :], modT[:, :, :], r=[modT], w=[modT_out])
    lbt = cx.sb("lbt", [128, 2, 2, 8]); lb = cx.sb("lb", [128, 2, 8]); oml = cx.sb("oml", [128, 2, 8]); noml = cx.sb("noml", [128, 2, 8])
    ong = cx.sb("ong_t", [128, 1]); maskF = cx.sb("maskF_t", [128, 128]); maskB = cx.sb("maskB_t", [128, 128])
    cx.dma("sp", lbt[:, :, :, :], lbg_in[:, :, :, :], r=[lbg_in], w=[lbt])
    cx.dma("sp", ong[:, :], ong_in[:, :], r=[ong_in], w=[ong])
    cx.dma("sp", maskF[:, :], maskF_in[:, :], r=[maskF_in], w=[maskF])
    cx.dma("sp", maskB[:, :], maskB_in[:, :], r=[maskB_in], w=[maskB])
    cx.tt("dve", lb[:, :, :], lbt[:, :, 1, :], lbt[:, :, 0, :], ALU.subtract, r=[lbt], w=[lb])
    cx.act(lb[:, :, :], lb[:, :, :], AF.Sigmoid, r=[lb], w=[lb])
    cx.ts("dve", oml[:, :, :], lb[:, :, :], -1.0, 1.0, ALU.mult, ALU.add, r=[lb], w=[oml])
    cx.ts("dve", noml[:, :, :], lb[:, :, :], -1.0, None, ALU.add, None, r=[lb], w=[noml])
    if stop < 1:
        cx.S.barrier()
        return cx

    with Phase(cx) as ph:
        hT = ph.sb("hT", [128, 16, 1280], BF16)
        hTb = [Buf(hT.t, f"hT{i}") for i in range(10)]
        hbufs = (ph.sbs("xt", [128, D], F32, 2), ph.sbs("xn", [128, D], BF16, 2), ph.sb("sqj", [128, D], BF16),
                 ph.sbs("ss", [128, 4], F32, 2), ph.pss("pT", [128, 8, 128], BF16, 2))
        ws = WStream(cx, ph)
        pacc = ph.pss("pacc", [128, 512], F32, 3)
        vbs = ph.sbs("vb", [128, 512], BF16, 2)
        fof = ph.sbs("fof", [128, 512], F32, 3)
        fob = ph.sbs("fob", [128, 512], BF16, 2)
        cnt = {"acc": 0, "v": 0, "f": 0, "b": 0}
        colblocks = [("q", 0, 0), ("q", 512, 1), ("i", 1024, 0), ("i", 1536, 1), ("uf", 2048, 0), ("uf", 2560, 1),
                     ("ub", 3072, 0), ("ub", 3584, 1), ("z", 4096, 0), ("z", 4608, 1)]
        for t0 in range(0, NT1, 10):
            tl = list(range(t0, min(t0 + 10, NT1)))
            tiles = [(ctx1_in, t * 128, 1) if t < 2 else (x1_tiles[t - 2][0], x1_tiles[t - 2][1], 0) for t in tl]
            build_hT(cx, hbufs, c, modT, gs, hT, hTb, tiles)
            ntok = len(tl) * 128
            chunks = [(a, min(512, ntok - a)) for a in range(0, ntok, 512)]
            wnext = ws.fetch(w_c, colblocks[0][1])
            for ci, (fam, c0, sub) in enumerate(colblocks):
                wb = wnext
                if ci + 1 < len(colblocks):
                    wnext = ws.fetch(w_c, colblocks[ci + 1][1])
                if fam == "i":
                    for ti, t in enumerate(tl):
                        ps = pacc[cnt["acc"] % 3]; cnt["acc"] += 1
                        for k in range(16):
                            cx.mm(ps[:, :], hT[:, k, ti * 128:(ti + 1) * 128], wb[:, k, :], start=(k == 0), stop=(k == 15),
                                  r=[hTb[ti], wb], wa=[ps], sig=(k == 15))
                        vb = vbs[cnt["v"] % 2]; cnt["v"] += 1
                        cx.cp("act", vb[:, :], ps[:, :], r=[ps], w=[vb])
                        cx.dma(STQ, vS.t[sub * 4:sub * 4 + 4, :, t, :].rearrange("h p c -> p h c"),
                               vb[:, :].rearrange("p (h c) -> p h c", c=128), r=[vb], wa=[vS])
                else:
                    for (h0c, n) in chunks:
                        tt0 = h0c // 128
                        rb = [hTb[x] for x in range(tt0, tt0 + (n + 127) // 128)]
                        g0 = t0 * 128 + h0c
                        for fc in range(4):
                            ps = pacc[cnt["acc"] % 3]; cnt["acc"] += 1
                            for k in range(16):
                                cx.mm(ps[:, 0:n], wb[:, k, fc * 128:(fc + 1) * 128], hT[:, k, h0c:h0c + n], start=(k == 0), stop=(k == 15),
                                      r=rb + [wb], wa=[ps], sig=(k == 15))
                            h = sub * 4 + fc
                            if fam == "z":
                                fo = fob[cnt["b"] % 2]; cnt["b"] += 1
                                cx.act(fo[:, 0:n], ps[:, 0:n], AF.Silu, r=[ps], w=[fo])
                                cx.dma(STQ, zS[h, :, g0:g0 + n], fo[:, 0:n], r=[fo], wa=[zS])
                            else:
                                fo = fof[cnt["f"] % 3]; cnt["f"] += 1
                                cx.act(fo[:, 0:n], ps[:, 0:n], AF.Silu if fam == "q" else AF.Sigmoid, r=[ps], w=[fo])
                                if fam == "q":
                                    cx.dma(STQ, qS[h, :, g0:g0 + n], fo[:, 0:n], r=[fo], wa=[qS])
                                else:
                                    cx.dma(STQ, sgS[0 if fam == "uf" else 1, h, :, g0:g0 + n], fo[:, 0:n], r=[fo], wa=[sgS])
    if stop < 2:
        return cx

    with Phase(cx) as ph:
        msk = ph.sb("msk", [128, T1], F32)
        qf = ph.sb("qf", [128, T1], F32)
        vh = ph.sb("vh1", [128, NT1, 128], BF16)
        oacc = ph.sb("oacc", [128, SEQ], F32)
        lf = ph.sb("lf", [128, T1], F32)
        kf = ph.sb("kf", [128, T1], F32)
        bc = ph.sb("bcum", [128, T1], F32)
        ef = ph.sb("ef", [128, T1], F32)
        qdec = ph.sb("qdec", [128, T1], BF16)
        kinv = ph.sb("kinv", [128, T1], BF16)
        kendT = ph.sb("kendT", [128, T1], BF16)
        kend = ph.sb("kend", [128, NT1, 128], BF16)
        dec = ph.sb("dec", [128, NCH], F32)
        bend = ph.sb("bend", [128, NCH], F32)
        Sf = ph.sb("Sf", [128, 128], F32)
        Sbs = ph.sbs("Sb", [128, 128], BF16, 4)
        Ams = ph.sbs("Am", [128, 128], BF16, 3)
        sqr = ph.sb("sqr", [128, 512], F32); rst = ph.sb("rst", [128, 512], F32); onb = ph.sb("onb", [128, 512], F32)
        zcs = ph.sbs("zc", [128, 512], BF16, 2); obs = ph.sbs("ob1", [128, 512], BF16, 2)
        pA = ph.pss("pA", [128, 512], F32, 1)
        po = ph.pss("po1", [128, 512], F32, 2)
        pS = ph.pss("pS", [128, 512], F32, 4)
        pK = ph.pss("pK", [128, 8, 128], BF16, 1)
        ia_ = [0, 0]
        cx.memset("pool", msk[:, :], 1.0, w=[msk])
        cx.memset("pool", msk[:, :].rearrange("p (c d) -> p c d", d=64)[:, :, 0:1], 0.0, w=[msk])
        v3 = lambda a: a.rearrange("p (c d) -> p c d", d=64)
        HT = T1 // 2
        ia = 0
        isb = 0
        ipo = 0
        for h in range(8):
            cx.dma("sp", qf[:, :], qS[h, :, :], r=[qS], w=[qf])
            cx.dma("sp", vh[:, :, :], vS[h, :, :, :], r=[vS], w=[vh])
            for d_ in range(2):
                cx.dma("sp", lf[:, :], sgS[d_, h, :, :], r=[sgS], w=[lf])
                cx.ts("dve", kf[:, :], lf[:, :], noml[:, d_, h:h + 1], oml[:, d_, h:h + 1], ALU.mult, ALU.add, r=[lf, noml, oml], w=[kf])
                cx.act(lf[:, :], lf[:, :], AF.Ln, r=[lf, oml, lb], w=[lf], scale=oml[:, d_, h:h + 1], bias=lb[:, d_, h:h + 1])
                for hh in range(2):
                    sl = slice(hh * HT, (hh + 1) * HT)
                    cx.S.op("dve", lambda: nc.vector.tensor_tensor_scan(out=bc[:, sl], data0=msk[:, sl], data1=lf[:, sl], initial=0.0,
                                                                        op0=ALU.mult, op1=ALU.add), r=[msk, lf], wa=[bc])
                cx.cp("dve", bend[:, :], v3(bc[:, :])[:, :, 63], r=[bc], w=[bend])
                bend_bc = bend[:, :].unsqueeze(2).to_broadcast([128, NCH, 64])
                if d_ == 1:
                    cx.tt("dve", v3(ef[:, :]), v3(lf[:, :]), bend_bc, ALU.add, r=[lf, bend], w=[ef])
                    cx.tt("dve", bc[:, :], ef[:, :], bc[:, :], ALU.subtract, r=[ef, bc], w=[bc])
                cx.act(dec[:, :], bend[:, :], AF.Exp, r=[bend], w=[dec])
                cx.act(ef[:, :], bc[:, :], AF.Exp, r=[bc], w=[ef])
                cx.tt("dve", qdec[:, :], qf[:, :], ef[:, :], ALU.mult, r=[qf, ef], w=[qdec])
                cx.act(ef[:, :], bc[:, :], AF.Exp, r=[bc], w=[ef], scale=-1.0)
                cx.tt("dve", kinv[:, :], kf[:, :], ef[:, :], ALU.mult, r=[kf, ef], w=[kinv])
                cx.tt("dve", v3(ef[:, :]), bend_bc, v3(bc[:, :]), ALU.subtract, r=[bend, bc], w=[ef])
                cx.act(ef[:, :], ef[:, :], AF.Exp, r=[ef], w=[ef])
                cx.tt("dve", kendT[:, :], kf[:, :], ef[:, :], ALU.mult, r=[kf, ef], w=[kendT])
                for g in range(0, NT1, 8):
                    pk = pK[0]
                    ng = min(8, NT1 - g)
                    for t in range(ng):
                        cx.tr(pk[:, t, :], kendT[:, (g + t) * 128:(g + t + 1) * 128], c["identb"][:, :], r=[kendT, c["identb"]], wa=[pk], sig=(t == ng - 1))
                    cx.cp("pool" if False else "act", kend[:, g:g + ng, :], pk[:, 0:ng, :], r=[pk], wa=[kend])
                cx.memset("pool", Sf[:, :], 0.0, w=[Sf])
                Sb = Sbs[isb % 4]; isb += 1
                cx.memset("pool", Sb[:, :], 0.0, w=[Sb])
                mask = maskF if d_ == 0 else maskB
                pairs = list(range(NT1)) if d_ == 0 else [1, 0] + list(range(NT1 - 1, 1, -1))
                order = (0, 1) if d_ == 0 else (1, 0)
                st = {}

                def early(p):
                    t0_ = p * 128
                    e_ = {}
                    if p >= 2:
                        a_ps = pA[0]
                        Am = Ams[ia_[0] % 3]; ia_[0] += 1
                        cx.mm(a_ps[:, 0:128], kinv[:, t0_:t0_ + 128], qdec[:, t0_:t0_ + 128], True, True, r=[kinv, qdec], w=[a_ps])
                        cx.tt("dve", Am[:, :], a_ps[:, 0:128], mask[:, :], ALU.mult, r=[a_ps, mask], w=[Am])
                        e_["Am"] = Am
                    e_["s"] = []
                    for hc in order:
                        pr = slice(hc * 64, hc * 64 + 64)
                        s_ps = pS[ia_[1] % 4]; ia_[1] += 1
                        cx.mm(s_ps[:, 0:128], kend[pr, p, :], vh[pr, p, :], True, True, r=[kend, vh], w=[s_ps])
                        e_["s"].append(s_ps)
                    st[p] = e_

                early(pairs[0])
                for pi, p in enumerate(pairs):
                    if pi + 1 < len(pairs):
                        early(pairs[pi + 1])
                    e_ = st.pop(p)
                    lat = p >= 2
                    t0_ = p * 128
                    if lat:
                        o_ps = po[ipo % 2]; ipo += 1
                        cx.mm(o_ps[:, 0:128], vh[:, p, :], e_["Am"][:, :], True, False, r=[vh, e_["Am"]], wa=[o_ps], sig=False)
                    for oi, hc in enumerate(order):
                        c64 = slice(t0_ + hc * 64, t0_ + hc * 64 + 64)
                        if lat:
                            cx.mm(o_ps[:, hc * 64:hc * 64 + 64], Sb[:, :], qdec[:, c64], False, (oi == 1), r=[Sb, qdec], wa=[o_ps], sig=True)
                        s_ps = e_["s"][oi]
                        dcol = dec[:, 2 * p + hc:2 * p + hc + 1]
                        Sb = Sbs[isb % 4]; isb += 1
                        cx.stt("dve", Sb[:, :], Sf[:, :], dcol, s_ps[:, 0:128], ALU.mult, ALU.add, r=[Sf, dec, s_ps], w=[Sb])
                        cx.stt("dve", Sf[:, :], Sf[:, :], dcol, s_ps[:, 0:128], ALU.mult, ALU.add, r=[Sf, dec, s_ps], w=[Sf])
                    if lat:
                        oc = slice((p - 2) * 128, (p - 1) * 128)
                        if d_ == 0:
                            cx.cp("act", oacc[:, oc], o_ps[:, 0:128], r=[o_ps], wa=[oacc])
                        else:
                            cx.tt("dve", oacc[:, oc], oacc[:, oc], o_ps[:, 0:128], ALU.add, r=[o_ps, oacc], wa=[oacc])
            for q8 in range(8):
                cs = slice(q8 * 512, (q8 + 1) * 512)
                a_ps = pA[0]
                zc = zcs[q8 % 2]; ob = obs[q8 % 2]
                cx.dma("sp", zc[:, :], zS[h, :, CTX + q8 * 512:CTX + (q8 + 1) * 512], r=[zS], w=[zc])
                cx.act(sqr[:, :], oacc[:, cs], AF.Square, r=[oacc], w=[sqr])
                cx.mm(a_ps[:, :], c["onesf"][:, :], sqr[:, :], True, True, r=[c["onesf"], sqr], w=[a_ps])
                cx.act(rst[:, :], a_ps[:, :], AF.Sqrt, r=[a_ps, c["eps"]], w=[rst], bias=c["eps"][:, 0:1], scale=1.0 / 128)
                cx.recip(rst[:, :], rst[:, :], r=[rst], w=[rst])
                cx.tt("dve", onb[:, :], oacc[:, cs], rst[:, :], ALU.mult, r=[oacc, rst], w=[onb])
                cx.stt("dve", ob[:, :], onb[:, :], ong[:, 0:1], zc[:, :], ALU.mult, ALU.mult, r=[onb, ong, zc], w=[ob])
                cx.dma(STQ, catT1_out[h * 128:(h + 1) * 128, cs], ob[:, :], r=[ob], wa=[catT1_out])
    return catT1_out, modT


PAIRS = [[0, 1], [2, 3], [4, 5], [6, 7]]
WARMUP_COLL = True


def build_fused():
    cx = Ctx()
    ident_in = cx.din("ident", [128, 128])
    bmask_in = cx.din("bmask", [128, 2])
    w_out1 = cx.din("w_out1", [D, D])
    x2_out = cx.dout("x2", [HALF, D])
    c = common_consts(cx, ident_in)
    if WARMUP_COLL:
        wu_src = cx.dscr("wu_src", [128, 128], F32)
        wu_dst = cx.dscr("wu_dst", [256, 128], F32)
        cx.dma("sp", wu_src[:, :], ident_in[:, :], r=[ident_in], w=[wu_src])
        cx.S.coll("AllGather", wu_dst[:, :], wu_src[:, :], PAIRS, r=[wu_src], w=[wu_dst])
    with Scope(cx):
        x1_own_s, ctx1_s = _build_l0(cx, 99, fused=True, c=c)
    x1g_t = cx.nc.dram_tensor("x1g_s", [8, 512, D], F32, kind="Internal").ap()
    x1g = [Buf(x1g_t[i], f"x1g{i}") for i in range(8)]
    for i in range(8):
        cx.S.coll("AllGather", x1g[i][:, :], x1_own_s[i * 256:(i + 1) * 256, :], PAIRS, r=[x1_own_s], w=[x1g[i]])
    x1_tiles = []
    for t in range(32):
        r_, lt = t // 16, t % 16
        x1_tiles.append((x1g[lt // 2], r_ * 256 + (lt % 2) * 128))
    with Scope(cx):
        catT1_s, modT = _build_l1a(cx, 99, fused=True, c=c, x1_tiles=x1_tiles, ctx1_in=ctx1_s)
        catg_t = cx.nc.dram_tensor("catg_s", [4, 512, SEQ], BF16, kind="Internal").ap()
        catg = [Buf(catg_t[i], f"catg{i}") for i in range(4)]
        for i in range(4):
            cx.S.coll("AllGather", catg[i][:, :], catT1_s[i * 256:(i + 1) * 256, :], PAIRS, r=[catT1_s], w=[catg[i]])
        gate_bc = cx.sb("gate_bc1", [128, 2, D], F32)
        bmask = cx.sb("bmask_t", [128, 2], F32)
        cx.dma("sp", bmask[:, :], bmask_in[:, :], r=[bmask_in], w=[bmask])
        gate_rows(cx, c, modT, gate_bc, (0,))

        def loader(dst, col):
            for r_ in range(2):
                for i in range(4):
                    k0 = r_ * 8 + i * 2
                    cx.dma("sp", dst[:, k0:k0 + 2, :],
                           catg[i].t[r_ * 256:(r_ + 1) * 256, col:col + 128].rearrange("(k p) c -> p k c", p=128),
                           r=[catg[i]], wa=[dst])
        out_proj(cx, w_out1, None, gate_bc, [(x1_own_s, x2_out, t * 128, t * 128, 0) for t in range(16)], blend=bmask, loader=loader)
    return cx


def build_l1b():
    cx = Ctx()
    catT_in = cx.din("catT", [D, HALF], BF16)
    x1_own = cx.din("x1o", [HALF, D])
    w_out = cx.din("w_out", [D, D])
    modT_in = cx.din("modT1", [128, 48, 2])
    ident_in = cx.din("ident", [128, 128])
    x2_out = cx.dout("x2", [HALF, D])
    c = common_consts(cx, ident_in)
    modT = cx.sb("modT", [128, 48, 2], F32)
    gate_bc = cx.sb("gate_bc", [128, 2, D], F32)
    cx.dma("sp", modT[:, :, :], modT_in[:, :, :], r=[modT_in], w=[modT])
    gate_rows(cx, c, modT, gate_bc, (0,))
    out_proj(cx, w_out, catT_in, gate_bc, [(x1_own, x2_out, t * 128, t * 128, 0) for t in range(16)])
    return cx


def rope_tables():
    rows = SEQ // 64
    row = np.repeat(np.arange(rows), 64).astype(np.float32)
    col = np.tile(np.arange(64), rows).astype(np.float32)
    inv = (10000.0 ** (-np.arange(0, 32, 2, dtype=np.float32) / 32)).astype(np.float32)

    def axis_angles(pos):
        a = pos[:, None] * inv[None, :]
        return np.concatenate([a, a], axis=-1)
    ang = np.concatenate([axis_angles(row), axis_angles(col)], axis=-1).astype(np.float32)
    cos, sin = np.cos(ang), np.sin(ang)
    sgn = np.tile(np.concatenate([-np.ones(16), np.ones(16)]), 2).astype(np.float32)
    return np.concatenate([cos, sin * sgn], axis=-1).astype(np.float32)


def fm(v, nchunk):
    return np.ascontiguousarray(np.asarray(v, np.float32).reshape(nchunk, 128).T)


_CACHE = {}


def run_l0(inp, cores):
    if "l0" not in _CACHE:
        _CACHE["l0"] = build_l0()
    cx = _CACHE["l0"]
    rope = rope_tables()
    ident = np.eye(128, dtype=np.float32)
    in_maps = []
    for cid in cores:
        b, hf = cid // 2, cid % 2
        own = slice(hf * HALF, (hf + 1) * HALF)
        oth = slice((1 - hf) * HALF, (2 - hf) * HALF)
        cc = np.stack([inp["c"][b], inp["c_ctx"]], axis=-1)
        ccT = np.ascontiguousarray(cc.reshape(16, 128, 2).transpose(1, 0, 2))
        cw = inp["conv_w"][0]
        cwT = np.ascontiguousarray(cw.reshape(31, 8, 128).transpose(2, 1, 0))
        cvp = np.ascontiguousarray(np.stack([fm(inp["conv_b"][0], 8), fm(inp["cln_g"][0], 8), fm(inp["cln_b"][0], 8)], axis=1))
        hm = np.zeros((128, 2), np.float32)
        hm[:, 0] = 1.0 if hf == 1 else 0.0
        hm[:, 1] = 1.0 if hf == 0 else 0.0
        in_maps.append({
            "x_own": np.ascontiguousarray(inp["x"][b, own]), "x_oth": np.ascontiguousarray(inp["x"][b, oth]),
            "ctx": np.ascontiguousarray(inp["ctx"][b]), "ccT": ccT,
            "w_ada": np.ascontiguousarray(inp["w_ada"][0]), "b_adaT": fm(inp["b_ada"][0], 48), "norm_gT": fm(inp["norm_g"][0], 16),
            "w_in": np.ascontiguousarray(inp["w_in_ab"][0]), "w_out": np.ascontiguousarray(inp["w_out_ab"][0]),
            "qkg": np.ascontiguousarray(np.stack([inp["qn_g"][0], inp["kn_g"][0]])),
            "lamv": np.ascontiguousarray(np.stack([inp["lam_q1"][0], inp["lam_k1"][0], inp["lam_q2"][0], inp["lam_k2"][0]])),
            "subg": np.ascontiguousarray(inp["subln_g"][0].reshape(128, 1)),
            "cwT": cwT, "cvp": cvp,
            "rq_own": np.ascontiguousarray(rope[own] * np.float32(0.125)), "rk_own": np.ascontiguousarray(rope[own]),
            "rk_oth": np.ascontiguousarray(rope[oth]), "hmask": hm, "ident": ident,
        })
    res = run_bass_kernel_spmd(cx.nc, in_maps, core_ids=list(range(len(cores))))
    return res.results


def hgrn_masks():
    i = np.arange(128)
    same = (i[:, None] // 64) == (i[None, :] // 64)
    mF = (same & (i[:, None] <= i[None, :])).astype(np.float32)
    mB = (same & (i[:, None] >= i[None, :])).astype(np.float32)
    return mF, mB


def run_l1a(inp, x1, ctx1, cores):
    if "l1a" not in _CACHE:
        _CACHE["l1a"] = build_l1a()
    cx = _CACHE["l1a"]
    ident = np.eye(128, dtype=np.float32)
    mF, mB = hgrn_masks()
    wc = inp["w_in_c"][0]
    in_maps = []
    for cid in cores:
        b, hf = cid // 2, cid % 2
        cc = np.stack([inp["c"][b], inp["c_ctx"]], axis=-1)
        ccT = np.ascontiguousarray(cc.reshape(16, 128, 2).transpose(1, 0, 2))
        cols = np.concatenate([np.arange(f * D + hf * 1024, f * D + (hf + 1) * 1024) for f in range(5)])
        lg = inp["lb_gamma"][:, :, hf * 1024:(hf + 1) * 1024]
        lbg = np.ascontiguousarray(lg.reshape(2, 2, 8, 128).transpose(3, 0, 1, 2))
        in_maps.append({
            "x1f": np.ascontiguousarray(x1[b]), "ctx1": np.ascontiguousarray(ctx1[b]), "ccT": ccT,
            "w_ada": np.ascontiguousarray(inp["w_ada"][1]), "b_adaT": fm(inp["b_ada"][1], 48), "norm_gT": fm(inp["norm_g"][1], 16),
            "w_c": np.ascontiguousarray(wc[:, cols]), "lbg": lbg,
            "ong": np.ascontiguousarray(inp["onorm_g"][0].reshape(128, 1)),
            "maskF": mF, "maskB": mB, "ident": ident,
        })
    res = run_bass_kernel_spmd(cx.nc, in_maps, core_ids=list(range(len(cores))))
    return res.results


def run_l1b(inp, x1, cat_pairs, modTs, cores):
    if "l1b" not in _CACHE:
        _CACHE["l1b"] = build_l1b()
    cx = _CACHE["l1b"]
    ident = np.eye(128, dtype=np.float32)
    in_maps = []
    for i, cid in enumerate(cores):
        b, hf = cid // 2, cid % 2
        own = slice(hf * HALF, (hf + 1) * HALF)
        in_maps.append({
            "catT": np.ascontiguousarray(cat_pairs[b][:, own]), "x1o": np.ascontiguousarray(x1[b, own]),
            "w_out": np.ascontiguousarray(inp["w_out_c"][0]), "modT1": modTs[i], "ident": ident,
        })
    res = run_bass_kernel_spmd(cx.nc, in_maps, core_ids=list(range(len(cores))))
    return res.results


def run_l1(inp, x1, ctx1, cores):
    ra = run_l1a(inp, x1, ctx1, cores)
    cat_pairs = {}
    for i, cid in enumerate(cores):
        b, hf = cid // 2, cid % 2
        cat_pairs.setdefault(b, [None, None])[hf] = ra[i]["catT1"]
    cat_pairs = {b: np.concatenate(v, axis=0) for b, v in cat_pairs.items()}
    rb = run_l1b(inp, x1, cat_pairs, [ra[i]["modT1"] for i in range(len(cores))], cores)
    return rb


def l0_inputs(inp, cid, rope):
    b, hf = cid // 2, cid % 2
    own = slice(hf * HALF, (hf + 1) * HALF)
    oth = slice((1 - hf) * HALF, (2 - hf) * HALF)
    cc = np.stack([inp["c"][b], inp["c_ctx"]], axis=-1)
    ccT = np.ascontiguousarray(cc.reshape(16, 128, 2).transpose(1, 0, 2))
    cw = inp["conv_w"][0]
    cwT = np.ascontiguousarray(cw.reshape(31, 8, 128).transpose(2, 1, 0))
    cvp = np.ascontiguousarray(np.stack([fm(inp["conv_b"][0], 8), fm(inp["cln_g"][0], 8), fm(inp["cln_b"][0], 8)], axis=1))
    hm = np.zeros((128, 2), np.float32)
    hm[:, 0] = 1.0 if hf == 1 else 0.0
    hm[:, 1] = 1.0 if hf == 0 else 0.0
    return {
        "x_own": np.ascontiguousarray(inp["x"][b, own]), "x_oth": np.ascontiguousarray(inp["x"][b, oth]),
        "ctx": np.ascontiguousarray(inp["ctx"][b]), "ccT": ccT,
        "w_ada": np.ascontiguousarray(inp["w_ada"][0]), "b_adaT": fm(inp["b_ada"][0], 48), "norm_gT": fm(inp["norm_g"][0], 16),
        "w_in": np.ascontiguousarray(inp["w_in_ab"][0]), "w_out": np.ascontiguousarray(inp["w_out_ab"][0]),
        "qkg": np.ascontiguousarray(np.stack([inp["qn_g"][0], inp["kn_g"][0]])),
        "lamv": np.ascontiguousarray(np.stack([inp["lam_q1"][0], inp["lam_k1"][0], inp["lam_q2"][0], inp["lam_k2"][0]])),
        "subg": np.ascontiguousarray(inp["subln_g"][0].reshape(128, 1)),
        "cwT": cwT, "cvp": cvp,
        "rq_own": np.ascontiguousarray(rope[own] * np.float32(0.125)), "rk_own": np.ascontiguousarray(rope[own]),
        "rk_oth": np.ascontiguousarray(rope[oth]), "hmask": hm,
    }


def l1_inputs(inp, cid):
    b, hf = cid // 2, cid % 2
    cc = np.stack([inp["c"][b], inp["c_ctx"]], axis=-1)
    ccT = np.ascontiguousarray(cc.reshape(16, 128, 2).transpose(1, 0, 2))
    cols = np.concatenate([np.arange(f * D + hf * 1024, f * D + (hf + 1) * 1024) for f in range(5)])
    lg = inp["lb_gamma"][:, :, hf * 1024:(hf + 1) * 1024]
    lbg = np.ascontiguousarray(lg.reshape(2, 2, 8, 128).transpose(3, 0, 1, 2))
    mF, mB = hgrn_masks()
    bm = np.zeros((128, 2), np.float32)
    bm[:, hf] = 1.0
    return {
        "ccT1": ccT, "w_ada1": np.ascontiguousarray(inp["w_ada"][1]), "b_adaT1": fm(inp["b_ada"][1], 48),
        "norm_gT1": fm(inp["norm_g"][1], 16), "w_c": np.ascontiguousarray(inp["w_in_c"][0][:, cols]), "lbg": lbg,
        "ong": np.ascontiguousarray(inp["onorm_g"][0].reshape(128, 1)), "maskF": mF, "maskB": mB,
        "w_out1": np.ascontiguousarray(inp["w_out_c"][0]), "bmask": bm,
    }


def run_fused(inp, cores):
    if "fused" not in _CACHE:
        _CACHE["fused"] = build_fused()
    cx = _CACHE["fused"]
    rope = rope_tables()
    ident = np.eye(128, dtype=np.float32)
    in_maps = []
    for cid in cores:
        m = {"ident": ident}
        m.update(l0_inputs(inp, cid, rope))
        m.update(l1_inputs(inp, cid))
        in_maps.append(m)
    res = run_bass_kernel_spmd(cx.nc, in_maps, core_ids=list(range(len(cores))))
    return res.results


def kernel_unfused(**inputs):
    inp = {k: np.asarray(v) for k, v in inputs.items()}
    cores = list(range(8))
    r0 = run_l0(inp, cores)
    x1 = np.zeros_like(inp["x"])
    ctx1 = np.zeros_like(inp["ctx"])
    for cid in cores:
        b, hf = cid // 2, cid % 2
        x1[b, hf * HALF:(hf + 1) * HALF] = r0[cid]["x1"]
        ctx1[b] = r0[cid]["ctx1"]
    r1 = run_l1(inp, x1, ctx1, cores)
    out = np.zeros_like(inp["x"])
    for cid in cores:
        b, hf = cid // 2, cid % 2
        out[b, hf * HALF:(hf + 1) * HALF] = r1[cid]["x2"]
    return out


def kernel(**inputs):
    inp = {k: np.asarray(v) for k, v in inputs.items()}
    cores = list(range(8))
    r = run_fused(inp, cores)
    out = np.zeros_like(inp["x"])
    for cid in cores:
        b, hf = cid // 2, cid % 2
        out[b, hf * HALF:(hf + 1) * HALF] = r[cid]["x2"]
    return out
```

```python
from contextlib import ExitStack
import math
import numpy as np
import ml_dtypes
import concourse.bass as bass
import concourse.mybir as mybir
from concourse.bass_utils import run_bass_kernel_spmd

F32 = mybir.dt.float32
BF16 = mybir.dt.bfloat16
AF = mybir.ActivationFunctionType
ALU = mybir.AluOpType
AX = mybir.AxisListType

D = 2048
SEQ = 4096
HALF = 2048
CTX = 256
EPS = 1e-6
NKEY = CTX + SEQ
STQ = "act"


class Buf:
    def __init__(self, t, name="", psum=False):
        self.t = t
        self.name = name
        self.psum = psum
        self.w = {}
        self.r = {}
        self.pr = {}
        self.open = False

    def __getitem__(self, k):
        return self.t[k]


def _merge(d, s):
    for k, v in s.items():
        if d.get(k, 0) < v:
            d[k] = v


class Sched:
    GEN = 12000
    NSLOT = 8

    def __init__(self, nc):
        self.nc = nc
        self.eng = {"pe": nc.tensor, "dve": nc.vector, "act": nc.scalar,
                    "pool": nc.gpsimd, "sp": nc.sync}
        self.cnt = {e: 0 for e in self.eng}
        self.gen = {e: 0 for e in self.eng}
        self.sems = {}
        self.waited = {e: {} for e in self.eng}
        self.dma_i = {e: 0 for e in self.eng}
        self.nops = 0
        self.nwaits = 0

    def sem(self, key):
        if key not in self.sems:
            self.sems[key] = self.nc.alloc_semaphore("s_" + "_".join(str(k) for k in key))
        return self.sems[key]

    def _wait(self, e, deps):
        for key, val in deps.items():
            if key[0] == "E" and key[1] == e and e in ("pe", "sp"):
                continue
            if self.waited[e].get(key, 0) >= val:
                continue
            self.eng[e].wait_ge(self.sem(key), val)
            self.waited[e][key] = val
            self.nwaits += 1

    def _deps(self, e, r, w, wa):
        deps = {}
        for b in r:
            _merge(deps, b.w)
            if b.psum:
                _merge(deps, {k: v for k, v in b.r.items() if k[1] != e})
        for b in w:
            _merge(deps, b.w)
            _merge(deps, b.r)
            _merge(deps, b.pr)
        for b in wa:
            if not b.open:
                b.pr = dict(b.r)
                _merge(b.pr, b.w)
                b.r = {}
                b.w = {}
                b.open = True
            _merge(deps, b.pr)
        return deps

    def _record(self, key, val, r, w, wa):
        ev = {key: val}
        for b in r:
            _merge(b.r, ev)
            b.open = False
        for b in w:
            b.w = dict(ev)
            b.r = {}
            b.pr = {}
            b.open = False
        for b in wa:
            _merge(b.w, ev)

    def op(self, e, fn, r=(), w=(), wa=(), sig=True):
        deps = self._deps(e, r, w, wa)
        self._wait(e, deps)
        ins = fn()
        self.nops += 1
        key = ("E", e, self.gen[e])
        if sig:
            self.cnt[e] += 1
            ins.then_inc(self.sem(key), 1)
            self._record(key, self.cnt[e], r, w, wa)
            if self.cnt[e] >= self.GEN:
                self.gen[e] += 1
                self.cnt[e] = 0
        else:
            self._record(key, self.cnt[e] + 1, r, w, wa)
        return ins

    def dma(self, e, out, in_, r=(), w=(), wa=(), **kw):
        i = self.dma_i[e]
        slot = i % self.NSLOT
        key = ("D", e, slot)
        val = 16 * (i // self.NSLOT + 1)
        deps = self._deps(e, r, w, wa)
        if val > 16:
            _merge(deps, {key: val - 16})
        self._wait(e, deps)
        ins = self.eng[e].dma_start(out=out, in_=in_, **kw)
        ins.then_inc(self.sem(key), 16)
        self.dma_i[e] = i + 1
        self._record(key, val, r, w, wa)
        return ins

    def coll(self, kind, out, in_, groups, r=(), w=()):
        e = "pool"
        self.ncoll = getattr(self, "ncoll", 0) + 1
        key = ("C", e, self.ncoll)
        deps = self._deps(e, r, w, ())
        self._wait(e, deps)
        ins = self.nc.gpsimd.collective_compute(kind, ALU.bypass, replica_groups=groups, ins=[in_], outs=[out])
        ins.then_inc(self.sem(key), 1)
        self.colls = getattr(self, "colls", {})
        self.colls[key] = 1
        self._record(key, 1, r, w, ())
        return ins

    def barrier(self):
        allev = {}
        for e in self.eng:
            if self.cnt[e] > 0:
                allev[("E", e, self.gen[e])] = self.cnt[e]
            elif self.gen[e] > 0:
                allev[("E", e, self.gen[e] - 1)] = self.GEN
        for e in self.eng:
            n = self.dma_i[e]
            for slot in range(min(n, self.NSLOT)):
                last = ((n - 1 - slot) // self.NSLOT) * self.NSLOT + slot
                allev[("D", e, slot)] = 16 * (last // self.NSLOT + 1)
        allev.update(getattr(self, "colls", {}))
        for e in self.eng:
            for key, val in allev.items():
                if key[0] == "E" and key[1] == e and e in ("pe", "sp"):
                    continue
                if self.waited[e].get(key, 0) >= val:
                    continue
                self.eng[e].wait_ge(self.sem(key), val)
                self.waited[e][key] = val
                self.nwaits += 1


class Ctx:
    def __init__(self):
        self.nc = bass.Bass("TRN2", target_bir_lowering=False)
        self.S = Sched(self.nc)
        self.later = []
        self.scope = None
        self.uid = 0

    def un(self, name):
        self.uid += 1
        return f"{name}_u{self.uid}"

    def din(self, name, shape, dt=F32):
        return Buf(self.nc.dram_tensor(name, list(shape), dt, kind="ExternalInput").ap(), name)

    def dout(self, name, shape, dt=F32):
        return Buf(self.nc.dram_tensor(name, list(shape), dt, kind="ExternalOutput").ap(), name)

    def dscr(self, name, shape, dt=BF16):
        return Buf(self.nc.dram_tensor(name, list(shape), dt, kind="Internal").ap(), name)

    def sb(self, name, shape, dt=F32):
        name = self.un(name)
        if self.scope is not None:
            return Buf(self.scope.enter_context(self.nc.sbuf_tensor(name, list(shape), dt)), name)
        return Buf(self.nc.alloc_sbuf_tensor(name, list(shape), dt), name)

    def act(self, out, in_, func, r, w=(), wa=(), **kw):
        nc = self.nc
        return self.S.op("act", lambda: nc.scalar.activation(out=out, in_=in_, func=func, **kw), r=r, w=w, wa=wa)

    def _ve(self, e):
        return self.nc.vector if e == "dve" else self.nc.gpsimd

    def tt(self, e, out, in0, in1, op, r, w=(), wa=()):
        eng = self._ve(e)
        return self.S.op(e, lambda: eng.tensor_tensor(out=out, in0=in0, in1=in1, op=op), r=r, w=w, wa=wa)

    def ts(self, e, out, in0, s1, s2, op0, op1, r, w=(), wa=()):
        eng = self._ve(e)
        if s2 is None:
            return self.S.op(e, lambda: eng.tensor_scalar(out=out, in0=in0, scalar1=s1, scalar2=None, op0=op0), r=r, w=w, wa=wa)
        return self.S.op(e, lambda: eng.tensor_scalar(out=out, in0=in0, scalar1=s1, scalar2=s2, op0=op0, op1=op1), r=r, w=w, wa=wa)

    def stt(self, e, out, in0, scalar, in1, op0, op1, r, w=(), wa=()):
        eng = self._ve(e)
        return self.S.op(e, lambda: eng.scalar_tensor_tensor(out=out, in0=in0, scalar=scalar, in1=in1, op0=op0, op1=op1), r=r, w=w, wa=wa)

    def cp(self, e, out, in_, r, w=(), wa=()):
        if e == "act":
            nc = self.nc
            return self.S.op("act", lambda: nc.scalar.copy(out=out, in_=in_), r=r, w=w, wa=wa)
        eng = self._ve(e)
        return self.S.op(e, lambda: eng.tensor_copy(out=out, in_=in_), r=r, w=w, wa=wa)

    def recip(self, out, in_, r, w=(), wa=()):
        nc = self.nc
        return self.S.op("dve", lambda: nc.vector.reciprocal(out=out, in_=in_), r=r, w=w, wa=wa)

    def memset(self, e, ap, val, w=(), wa=()):
        eng = self._ve(e)
        return self.S.op(e, lambda: eng.memset(ap, val), w=w, wa=wa)

    def mm(self, out, lhsT, rhs, start, stop, r, w=(), wa=(), sig=True):
        nc = self.nc
        return self.S.op("pe", lambda: nc.tensor.matmul(out, lhsT=lhsT, rhs=rhs, start=start, stop=stop), r=r, w=w, wa=wa, sig=sig)

    def tr(self, out, in_, ident, r, w=(), wa=(), sig=True):
        nc = self.nc
        return self.S.op("pe", lambda: nc.tensor.transpose(out, in_, ident), r=r, w=w, wa=wa, sig=sig)

    def dma(self, e, out, in_, r, w=(), wa=()):
        return self.S.dma(e, out, in_, r=r, w=w, wa=wa)

    def defer(self, fn):
        self.later.append(fn)

    def flush(self):
        l, self.later = self.later, []
        for fn in l:
            fn()


class Scope:
    def __init__(self, cx):
        self.cx = cx

    def __enter__(self):
        self.es = ExitStack()
        self.es.__enter__()
        self.cx.scope = self.es
        return self

    def __exit__(self, *a):
        self.cx.flush()
        self.cx.S.barrier()
        self.cx.scope = None
        return self.es.__exit__(*a)


class Phase:
    def __init__(self, cx):
        self.cx = cx
        self.es = ExitStack()

    def __enter__(self):
        self.es.__enter__()
        return self

    def __exit__(self, *a):
        self.cx.flush()
        self.cx.S.barrier()
        return self.es.__exit__(*a)

    def sb(self, name, shape, dt=F32, n=1):
        name = self.cx.un(name)
        t = self.es.enter_context(self.cx.nc.sbuf_tensor(name, list(shape), dt))
        return Buf(t, name)

    def sbs(self, name, shape, dt, n):
        return [self.sb(f"{name}{i}", shape, dt) for i in range(n)]

    def ps(self, name, shape, dt=F32):
        nb = int(np.prod(shape[1:])) * (4 if dt == F32 else 2)
        assert nb % 2048 == 0, (name, shape)
        name = self.cx.un(name)
        t = self.es.enter_context(self.cx.nc.psum_tensor(name, list(shape), dt))
        return Buf(t, name, psum=True)

    def pss(self, name, shape, dt, n):
        return [self.ps(f"{name}{i}", shape, dt) for i in range(n)]


def common_consts(cx, ident_in):
    c = {}
    c["identf"] = cx.sb("identf", [128, 128], F32)
    c["identb"] = cx.sb("identb", [128, 128], BF16)
    c["onesf"] = cx.sb("onesf", [128, 128], F32)
    c["eps"] = cx.sb("epsT", [128, 1], F32)
    cx.dma("sp", c["identf"][:, :], ident_in[:, :], r=[ident_in], w=[c["identf"]])
    cx.cp("dve", c["identb"][:, :], c["identf"][:, :], r=[c["identf"]], w=[c["identb"]])
    cx.memset("pool", c["onesf"][:, :], 1.0, w=[c["onesf"]])
    c["onesb"] = cx.sb("onesb", [128, 128], BF16)
    cx.memset("pool", c["onesb"][:, :], 1.0, w=[c["onesb"]])
    cx.memset("pool", c["eps"][:, :], EPS, w=[c["eps"]])
    return c


def modulation(cx, c, ccT_in, w_ada, b_adaT_in, norm_gT_in, want_gate_bc=(0, 1), sfx=""):
    nc = cx.nc
    modT = cx.sb("modT" + sfx, [128, 48, 2], F32)
    gs = cx.sb("gsT" + sfx, [128, 16, 2], F32)
    gate_bc = cx.sb("gate_bc" + sfx, [128, 2, D], F32) if want_gate_bc else None
    with Phase(cx) as ph:
        scT = ph.sb("scT", [128, 16, 2], F32)
        badaT = ph.sb("badaT", [128, 48], F32)
        ngT = ph.sb("ngT", [128, 16], F32)
        wst = ph.sbs("wada_st", [128, 16, 512], F32, 2)
        pm = ph.ps("pm", [128, 512], F32)
        cx.dma("sp", scT[:, :, :], ccT_in[:, :, :], r=[ccT_in], w=[scT])
        cx.dma("sp", badaT[:, :], b_adaT_in[:, :], r=[b_adaT_in], w=[badaT])
        cx.dma("sp", ngT[:, :], norm_gT_in[:, :], r=[norm_gT_in], w=[ngT])
        cx.act(scT[:, :, :], scT[:, :, :], AF.Silu, r=[scT], w=[scT])
        wv = w_ada.t.rearrange("(k p) c -> p k c", p=128)
        for cb in range(12):
            st = wst[cb % 2]
            for hh in range(2):
                cx.dma("sp" if hh == 0 else "act", st[:, hh * 8:(hh + 1) * 8, :], wv[:, hh * 8:(hh + 1) * 8, cb * 512:(cb + 1) * 512],
                       r=[w_ada], wa=[st])
            for fc in range(4):
                cc = cb * 4 + fc
                for k in range(16):
                    cx.mm(pm[:, cc * 2:cc * 2 + 2], st[:, k, fc * 128:(fc + 1) * 128], scT[:, k, :],
                          start=(k == 0), stop=(k == 15), r=[st, scT], wa=[pm], sig=(k == 15))
        cx.tt("dve", modT[:, :, :], pm[:, 0:96].rearrange("p (c j) -> p c j", j=2),
              badaT[:, :].unsqueeze(2).to_broadcast([128, 48, 2]), ALU.add, r=[pm, badaT], w=[modT])
        cx.stt("dve", gs[:, :, :], modT[:, 16:32, :], 1.0, ngT[:, :].unsqueeze(2).to_broadcast([128, 16, 2]),
               ALU.add, ALU.mult, r=[modT, ngT], w=[gs])
    if want_gate_bc:
        gate_rows(cx, c, modT, gate_bc, want_gate_bc)
    return modT, gs, gate_bc


def gate_rows(cx, c, modT, gate_bc, js):
    with Phase(cx) as ph:
        dgs = ph.sbs("dgate", [128, 128], F32, 2)
        pgs = ph.pss("pgate", [128, 512], F32, 2)
        i = 0
        for j in js:
            for k in range(16):
                dg = dgs[i % 2]
                pg = pgs[i % 2]
                cx.ts("dve", dg[:, :], c["identf"][:, :], modT[:, 32 + k, j:j + 1], None, ALU.mult, None,
                      r=[c["identf"], modT], w=[dg])
                cx.mm(pg[:, 0:128], c["onesf"][:, :], dg[:, :], True, True, r=[c["onesf"], dg], w=[pg])
                cx.cp("act", gate_bc[:, j, k * 128:(k + 1) * 128], pg[:, 0:128], r=[pg], wa=[gate_bc])
                i += 1


def build_hT(cx, ph_bufs, c, modT, gs, hT, hTb, tiles):
    xts, xns, sqj, sss, pTs = ph_bufs

    def stage_a(i):
        src, r0, j = tiles[i]
        xt, xn, ss = xts[i % 2], xns[i % 2], sss[i % 2]
        cx.dma("sp", xt[:, :], src[r0:r0 + 128, :], r=[src], w=[xt])
        cx.memset("pool", ss[:, :], 0.0, w=[ss])
        cx.act(sqj[:, :], xt[:, :], AF.Square, r=[xt, ss], w=[sqj, ss], accum_out=ss[:, 0:1])
        cx.act(ss[:, 1:2], ss[:, 0:1], AF.Sqrt, r=[ss, c["eps"]], w=[ss], bias=c["eps"][:, 0:1], scale=1.0 / D)
        cx.recip(ss[:, 2:3], ss[:, 1:2], r=[ss], w=[ss])
        cx.ts("dve", xn[:, :], xt[:, :], ss[:, 2:3], None, ALU.mult, None, r=[xt, ss], w=[xn])

    def stage_b(i):
        src, r0, j = tiles[i]
        xn = xns[i % 2]
        for half in range(2):
            pT = pTs[half]
            for kk in range(8):
                k = half * 8 + kk
                cx.tr(pT[:, kk, :], xn[:, k * 128:(k + 1) * 128], c["identb"][:, :], r=[xn, c["identb"]],
                      wa=[pT], sig=(kk == 7))
            for kk in range(8):
                k = half * 8 + kk
                dst = hT[:, k, i * 128:(i + 1) * 128]
                if half == 0:
                    cx.act(dst, pT[:, kk, :], AF.Identity, r=[pT, gs, modT], wa=[hTb[i]],
                           scale=gs[:, k, j:j + 1], bias=modT[:, k, j:j + 1])
                else:
                    cx.ts("dve", dst, pT[:, kk, :], gs[:, k, j:j + 1], modT[:, k, j:j + 1], ALU.mult, ALU.add,
                          r=[pT, gs, modT], wa=[hTb[i]])

    stage_a(0)
    for i in range(len(tiles)):
        if i + 1 < len(tiles):
            stage_a(i + 1)
        stage_b(i)


class WStream:
    def __init__(self, cx, ph, name="w"):
        self.cx = cx
        self.st = ph.sbs(name + "_st", [128, 4, 512], F32, 2)
        self.wb = ph.sbs(name + "_bf", [128, 16, 512], BF16, 2)
        self.n = 0
        self.si = 0

    def fetch(self, w, c0):
        cx = self.cx
        wb = self.wb[self.n % 2]
        self.n += 1
        wv = w.t.rearrange("(k p) c -> p k c", p=128)
        for q in range(4):
            st = self.st[self.si % 2]
            self.si += 1
            cx.dma("sp", st[:, :, :], wv[:, q * 4:(q + 1) * 4, c0:c0 + 512], r=[w], w=[st])
            cx.cp("pool" if q % 2 == 0 else "act", wb[:, q * 4:(q + 1) * 4, :], st[:, :, :], r=[st], wa=[wb])
        return wb


LAM_INIT0 = 0.8 - 0.6 * math.exp(-0.3 * 0)


HT_DBG = 0
STOPF = {1.3: ("q",), 1.4: ("v",), 1.5: ("za",), 1.6: ("gg", "gv")}


class _Stop(Exception):
    pass


def build_l0(stop=99):
    cx = Ctx()
    try:
        _build_l0(cx, stop)
    except _Stop:
        cx.S.barrier()
    return cx


def _build_l0(cx, stop=99, fused=False, c=None, x1_hook=None):
    nc = cx.nc
    x_own = cx.din("x_own", [HALF, D])
    x_oth = cx.din("x_oth", [HALF, D])
    ctx_in = cx.din("ctx", [CTX, D])
    ccT_in = cx.din("ccT", [128, 16, 2])
    w_ada = cx.din("w_ada", [D, 3 * D])
    b_adaT_in = cx.din("b_adaT", [128, 48])
    norm_gT_in = cx.din("norm_gT", [128, 16])
    w_in = cx.din("w_in", [D, 7168])
    w_out = cx.din("w_out", [D, D])
    qkg_in = cx.din("qkg", [2, 64])
    lamv_in = cx.din("lamv", [4, 64])
    subg_in = cx.din("subg", [128, 1])
    cwT_in = cx.din("cwT", [128, 8, 31])
    cvp_in = cx.din("cvp", [128, 3, 8])
    rq_own = cx.din("rq_own", [HALF, 128])
    rk_own = cx.din("rk_own", [HALF, 128])
    rk_oth = cx.din("rk_oth", [HALF, 128])
    hmask_in = cx.din("hmask", [128, 2])
    if fused:
        x1_out = cx.dscr("x1_own_s", [HALF, D], F32)
        ctx1_out = cx.dscr("ctx1_s", [CTX, D], F32)
    else:
        ident_in = cx.din("ident", [128, 128])
        x1_out = cx.dout("x1", [HALF, D])
        ctx1_out = cx.dout("ctx1", [CTX, D])
    qT_s = cx.dscr("qT_s", [8, 128, HALF + CTX])
    kT_s = cx.dscr("kT_s", [8, 128, NKEY])
    v_s = cx.dscr("v_s", [NKEY, 1024])
    zaT_s = cx.dscr("zaT_s", [1024, HALF + CTX])
    zbT_s = cx.dscr("zbT_s", [1024, HALF + CTX])
    yT_s = cx.dscr("yT_s", [1024, HALF + 30])
    yTc_s = cx.dscr("yTc_s", [1024, CTX + 30])
    catT_s = cx.dscr("catT_s", [D, HALF + CTX])

    if c is None:
        c = common_consts(cx, ident_in)
    modT, gs, gate_bc = modulation(cx, c, ccT_in, w_ada, b_adaT_in, norm_gT_in)

    qg = cx.sb("qg", [128, 64]); qgs = cx.sb("qgs", [128, 64]); kg = cx.sb("kg", [128, 64])
    lamt = cx.sb("lamt", [128, 4, 64]); lam4 = cx.sb("lam4", [128, 8])
    subg = cx.sb("subg_t", [128, 1])
    hmask = cx.sb("hmask_t", [128, 2])
    cvp = cx.sb("cvp_t", [128, 3, 8])
    cwT = cx.sb("cwT_t", [128, 8, 31])
    cx.dma("sp", qg[:, :], qkg_in[0, :].partition_broadcast(128), r=[qkg_in], w=[qg])
    cx.dma("sp", kg[:, :], qkg_in[1, :].partition_broadcast(128), r=[qkg_in], w=[kg])
    cx.dma("sp", lamt[:, :, :].rearrange("p a d -> p (a d)"),
           lamv_in.t.rearrange("a d -> (a d)").partition_broadcast(128), r=[lamv_in], w=[lamt])
    cx.dma("sp", subg[:, :], subg_in[:, :], r=[subg_in], w=[subg])
    cx.dma("sp", hmask[:, :], hmask_in[:, :], r=[hmask_in], w=[hmask])
    cx.dma("sp", cvp[:, :, :], cvp_in[:, :, :], r=[cvp_in], w=[cvp])
    cx.dma("sp", cwT[:, :, :], cwT_in[:, :, :], r=[cwT_in], w=[cwT])
    cx.ts("dve", qgs[:, :], qg[:, :], 0.125, None, ALU.mult, None, r=[qg], w=[qgs])
    cx.tt("dve", lamt[:, 0, :], lamt[:, 0, :], lamt[:, 1, :], ALU.mult, r=[lamt], w=[lamt])
    cx.tt("dve", lamt[:, 2, :], lamt[:, 2, :], lamt[:, 3, :], ALU.mult, r=[lamt], w=[lamt])
    cx.S.op("dve", lambda: nc.vector.reduce_sum(out=lam4[:, 0:1], in_=lamt[:, 0, :], axis=AX.X), r=[lamt], w=[lam4])
    cx.S.op("dve", lambda: nc.vector.reduce_sum(out=lam4[:, 1:2], in_=lamt[:, 2, :], axis=AX.X), r=[lamt, lam4], w=[lam4])
    cx.act(lam4[:, 2:4], lam4[:, 0:2], AF.Exp, r=[lam4], w=[lam4])
    cx.stt("dve", lam4[:, 4:5], lam4[:, 3:4], -LAM_INIT0, lam4[:, 2:3], ALU.add, ALU.subtract, r=[lam4], w=[lam4])
    neglam = lam4[:, 4:5]
    cx.ts("dve", subg[:, :], subg[:, :], 1.0 - LAM_INIT0, None, ALU.mult, None, r=[subg], w=[subg])

    if stop < 1:
        cx.S.barrier()
        return cx
    with Phase(cx) as ph:
        hT_t = ph.sb("hT", [128, 16, 1280], BF16)
        hTb = [Buf(hT_t.t, f"hT{i}") for i in range(10)]
        hT = hT_t
        hbufs = (ph.sbs("xt", [128, D], F32, 2), ph.sbs("xn", [128, D], BF16, 2), ph.sb("sqj", [128, D], BF16),
                 ph.sbs("ss", [128, 4], F32, 2), ph.pss("pT", [128, 8, 128], BF16, 2))
        ws = WStream(cx, ph)
        pacc = ph.pss("pacc", [128, 512], F32, 3)
        pq = ph.pss("pq", [128, 8, 128], BF16, 2)
        sqs = ph.sbs("sq", [128, 512], F32, 2)
        st8 = ph.sbs("st8", [128, 16], F32, 2)
        xnq = ph.sbs("xnq", [128, 512], F32, 2)
        t1s = ph.sbs("t1", [128, 512], F32, 2)
        t2s = ph.sbs("t2", [128, 512], F32, 2)
        qbs = ph.sbs("qb", [128, 512], BF16, 2)
        qTs = ph.sbs("qTs", [128, 4, 128], BF16, 2)
        rts = ph.sbs("rt", [128, 128], F32, 2)
        vbs = ph.sbs("vb", [128, 512], BF16, 2)
        fos = ph.sbs("fo", [128, 512], BF16, 2)
        sig_t = ph.sb("sig", [128, 4, 1280], BF16)
        sigb = [Buf(sig_t.t, f"sig{i}") for i in range(4)]
        zero_t = ph.sb("zero", [128, 15], BF16)
        cnt = {"acc": 0, "qk": 0, "v": 0, "fo": 0}

        cx.memset("pool", zero_t[:, :], 0.0, w=[zero_t])
        yTc_v = yTc_s.t.rearrange("(j p) c -> p j c", p=128)
        for j in range(8):
            cx.dma("sp", yTc_v[:, j, 0:15], zero_t[:, :], r=[zero_t], wa=[yTc_s])
            cx.dma("sp", yTc_v[:, j, CTX + 15:CTX + 30], zero_t[:, :], r=[zero_t], wa=[yTc_s])
        if stop == 1.1:
            raise _Stop

        def qk_epi(ps, tile, fam, h0):
            kind, idx = tile
            i = cnt["qk"]; cnt["qk"] += 1
            sq, s8, xq, t1, t2, qb, qTt, rt, pqt = sqs[i % 2], st8[i % 2], xnq[i % 2], t1s[i % 2], t2s[i % 2], qbs[i % 2], qTs[i % 2], rts[i % 2], pq[i % 2]
            cx.act(sq[:, :], ps[:, :], AF.Square, r=[ps], w=[sq])
            cx.S.op("dve", lambda: nc.vector.reduce_sum(out=s8[:, 0:8], in_=sq[:, :].rearrange("p (g d) -> p g d", d=64), axis=AX.X), r=[sq], w=[s8])
            cx.act(s8[:, 8:16], s8[:, 0:8], AF.Sqrt, r=[s8, c["eps"]], w=[s8], bias=c["eps"][:, 0:1], scale=1.0 / 64)
            cx.recip(s8[:, 0:8], s8[:, 8:16], r=[s8], w=[s8])
            v3 = lambda a: a.rearrange("p (g d) -> p g d", d=64)
            cx.tt("dve", v3(xq[:, :]), v3(ps[:, :]), s8[:, 0:8].unsqueeze(2).to_broadcast([128, 8, 64]), ALU.mult, r=[ps, s8], w=[xq])
            g = kg if fam == "k" else (qgs if kind == "ctx" else qg)
            cx.tt("dve", v3(xq[:, :]), v3(xq[:, :]), g[:, :].unsqueeze(1).to_broadcast([128, 8, 64]), ALU.mult, r=[xq, g], w=[xq])
            if kind == "ctx":
                cx.cp("act", qb[:, :], xq[:, :], r=[xq], w=[qb])
            else:
                rsrc = (rq_own if fam == "q" else rk_own) if kind == "own" else rk_oth
                cx.dma("sp", rt[:, :], rsrc[idx * 128:(idx + 1) * 128, :], r=[rsrc], w=[rt])
                cx.tt("dve", v3(t1[:, :]), v3(xq[:, :]), rt[:, 0:64].unsqueeze(1).to_broadcast([128, 8, 64]), ALU.mult, r=[xq, rt], w=[t1])
                for a in range(2):
                    lo, hi = a * 32, a * 32 + 16
                    e = "dve" if a == 0 else "pool"
                    cx.tt(e, v3(t2[:, :])[:, :, lo:lo + 16], v3(xq[:, :])[:, :, hi:hi + 16],
                          rt[:, 64 + lo:64 + lo + 16].unsqueeze(1).to_broadcast([128, 8, 16]), ALU.mult, r=[xq, rt], wa=[t2])
                    cx.tt(e, v3(t2[:, :])[:, :, hi:hi + 16], v3(xq[:, :])[:, :, lo:lo + 16],
                          rt[:, 64 + hi:64 + hi + 16].unsqueeze(1).to_broadcast([128, 8, 16]), ALU.mult, r=[xq, rt], wa=[t2])
                cx.tt("dve", qb[:, :], t1[:, :], t2[:, :], ALU.add, r=[t1, t2], w=[qb])
            if fam == "q":
                dst_s = qT_s
                c0 = idx * 128 if kind == "own" else HALF + idx * 128
            else:
                dst_s = kT_s
                c0 = {"ctx": 0, "own": CTX, "oth": CTX + HALF}[kind] + idx * 128

            def fin():
                for hh in range(4):
                    cx.tr(pqt[:, hh, :], qb[:, hh * 128:(hh + 1) * 128], c["identb"][:, :], r=[qb, c["identb"]], wa=[pqt], sig=(hh == 3))
                cx.cp("dve", qTt[:, :, :], pqt[:, 0:4, :], r=[pqt], w=[qTt])
                cx.dma(STQ, dst_s.t[h0:h0 + 4, :, c0:c0 + 128].rearrange("h p t -> p h t"), qTt[:, :, :], r=[qTt], wa=[dst_s])
            cx.defer(fin)

        def v_epi(ps, tile, cb2):
            kind, idx = tile
            i = cnt["v"]; cnt["v"] += 1
            vb = vbs[i % 2]
            r0 = {"ctx": 0, "own": CTX, "oth": CTX + HALF}[kind] + idx * 128
            cx.cp("act", vb[:, :], ps[:, :], r=[ps], w=[vb])
            cx.defer(lambda: cx.dma(STQ, v_s[r0:r0 + 128, cb2 * 512:(cb2 + 1) * 512], vb[:, :], r=[vb], wa=[v_s]))

        own = lambda a, b: [("own", i) for i in range(a, b)]
        blocks = [
            dict(tiles=own(0, 8) + [("ctx", 0), ("ctx", 1)], full=10, fams="all",
                 chunks=[(0, 512, "own", 0), (512, 512, "own", 512), (1024, 256, "ctx", 0)]),
            dict(tiles=own(8, 16) + [("oth", 0), ("oth", 15)], full=8, fams="all",
                 chunks=[(0, 512, "own", 1024), (512, 512, "own", 1536), (1024, 256, "halo", 0)]),
            dict(tiles=[("oth", i) for i in range(1, 8)], full=0, fams="kv", chunks=[]),
            dict(tiles=[("oth", i) for i in range(8, 15)], full=0, fams="kv", chunks=[]),
        ]
        srcmap = {"own": (x_own, 0), "oth": (x_oth, 0), "ctx": (ctx_in, 1)}
        colblocks = [("q", 0, 0), ("q", 512, 1), ("k", 1024, 0), ("k", 1536, 1), ("v", 2048, 0), ("v", 2560, 1),
                     ("za", 3072, 0), ("za", 3584, 1), ("gg", 5120, 0), ("gv", 4096, 0), ("gg", 5632, 1), ("gv", 4608, 1),
                     ("zb", 6144, 0), ("zb", 6656, 1)]
        for blk in blocks:
            tiles = blk["tiles"]
            build_hT(cx, hbufs, c, modT, gs, hT, hTb, [(srcmap[k][0], i * 128, srcmap[k][1]) for (k, i) in tiles])
            if stop == 1.2:
                raise _Stop
            cbl = colblocks if blk["fams"] == "all" else [cb for cb in colblocks if cb[0] in ("k", "v")]
            if stop in STOPF:
                cbl = [cb for cb in cbl if cb[0] in STOPF[stop]]
            wnext = ws.fetch(w_in, cbl[0][1])
            for ci, (fam, c0, sub) in enumerate(cbl):
                wb = wnext
                if ci + 1 < len(cbl):
                    wnext = ws.fetch(w_in, cbl[ci + 1][1])
                if fam in ("q", "k", "v"):
                    for ti, tile in enumerate(tiles):
                        if tile[0] == "oth" and fam == "q":
                            continue
                        ps = pacc[cnt["acc"] % 3]; cnt["acc"] += 1
                        for k in range(16):
                            cx.mm(ps[:, :], hT[:, k, ti * 128:(ti + 1) * 128], wb[:, k, :], start=(k == 0), stop=(k == 15),
                                  r=[hTb[ti], wb], wa=[ps], sig=(k == 15))
                        pend, cx.later = cx.later, []
                        if fam == "v":
                            v_epi(ps, tile, sub)
                        else:
                            qk_epi(ps, tile, fam, sub * 4)
                        for fn in pend:
                            fn()
                    cx.flush()
                else:
                    for (h0c, n, ckind, d0) in blk["chunks"]:
                        if ckind == "halo" and fam not in ("gg", "gv"):
                            continue
                        t0 = h0c // 128
                        rb = [hTb[t] for t in range(t0, t0 + (n + 127) // 128)]
                        for fc in range(4):
                            ps = pacc[cnt["acc"] % 3]; cnt["acc"] += 1
                            for k in range(16):
                                cx.mm(ps[:, 0:n], wb[:, k, fc * 128:(fc + 1) * 128], hT[:, k, h0c:h0c + n], start=(k == 0), stop=(k == 15),
                                      r=rb + [wb], wa=[ps], sig=(k == 15))
                            frow = sub * 512 + fc * 128
                            dcol = d0 if ckind == "own" else HALF + d0
                            if fam in ("za", "zb"):
                                fo = fos[cnt["fo"] % 2]; cnt["fo"] += 1
                                dst = zaT_s if fam == "za" else zbT_s
                                cx.act(fo[:, 0:n], ps[:, 0:n], AF.Silu, r=[ps], w=[fo])
                                cx.dma(STQ, dst[frow:frow + 128, dcol:dcol + n], fo[:, 0:n], r=[fo], wa=[dst])
                            elif fam == "gg":
                                cx.act(sig_t[:, fc, h0c:h0c + n], ps[:, 0:n], AF.Sigmoid, r=[ps], wa=[sigb[fc]])
                            else:
                                fo = fos[cnt["fo"] % 2]; cnt["fo"] += 1
                                cx.tt("dve", fo[:, 0:n], ps[:, 0:n], sig_t[:, fc, h0c:h0c + n], ALU.mult, r=[ps, sigb[fc]], w=[fo])
                                if ckind == "own":
                                    cx.dma(STQ, yT_s[frow:frow + 128, 15 + d0:15 + d0 + n], fo[:, 0:n], r=[fo], wa=[yT_s])
                                elif ckind == "ctx":
                                    cx.dma(STQ, yTc_s[frow:frow + 128, 15 + d0:15 + d0 + n], fo[:, 0:n], r=[fo], wa=[yTc_s])
                                else:
                                    cx.ts("dve", fo[:, 0:15], fo[:, 0:15], hmask[:, 1:2], None, ALU.mult, None, r=[fo, hmask], w=[fo])
                                    cx.ts("dve", fo[:, 241:256], fo[:, 241:256], hmask[:, 0:1], None, ALU.mult, None, r=[fo, hmask], w=[fo])
                                    cx.dma(STQ, yT_s[frow:frow + 128, 15 + HALF:30 + HALF], fo[:, 0:15], r=[fo], wa=[yT_s])
                                    cx.dma(STQ, yT_s[frow:frow + 128, 0:15], fo[:, 241:256], r=[fo], wa=[yT_s])
            cx.flush()
            if stop in STOPF:
                raise _Stop

    if stop < 2:
        return cx
    with Phase(cx) as ph:
        kTs = ph.sbs("kTh", [128, NKEY], BF16, 2)
        vhs = ph.sbs("vh", [128, 34, 128], BF16, 2)
        qhs = ph.sbs("qh", [128, HALF + CTX], BF16, 2)
        pST = ph.pss("pST", [128, 2, 512], F32, 2)
        po = ph.ps("po", [128, 2, 512], F32)
        psm = ph.ps("psm", [128, 2, 512], F32)
        pts = ph.sbs("pt", [128, 2, 512], BF16, 3)
        accs = ph.sbs("accP", [128, 2, 512], F32, 2)
        accm = [[Buf(a.t, a.name + "m0"), Buf(a.t, a.name + "m1")] for a in accs]
        rs = ph.sb("rs", [128, 2, 512], F32)
        ta = ph.sb("ta", [128, 512], F32); tb = ph.sb("tb", [128, 512], F32)
        ot = ph.sb("ot", [128, 512], F32); sqo = ph.sb("sqo", [128, 512], F32)
        rstd = ph.sb("rstdo", [128, 512], F32)
        zat = ph.sbs("zat", [128, 512], BF16, 2)
        obs = ph.sbs("ob", [128, 512], BF16, 2)
        v_v = v_s.t.rearrange("(kt p) c -> p kt c", p=128)
        step = 0
        ci = 0
        def load_head(h):
            cx.dma("sp", kTs[h % 2][:, :], kT_s[h, :, :], r=[kT_s], w=[kTs[h % 2]])
            cx.dma("sp", vhs[h % 2][:, :, :], v_v[:, :, h * 128:(h + 1) * 128], r=[v_s], w=[vhs[h % 2]])
            cx.dma("sp", qhs[h % 2][:, :], qT_s[h, :, :], r=[qT_s], w=[qhs[h % 2]])

        load_head(0)
        for h in range(8):
            kTh, vh, qh = kTs[h % 2], vhs[h % 2], qhs[h % 2]
            if h + 1 < 8:
                load_head(h + 1)
            for (q0, n, nkt) in [(0, 512, 34), (512, 512, 34), (1024, 512, 34), (1536, 512, 34), (HALF, 256, 2)]:
                acc = accs[ci % 2]
                za = zat[ci % 2]
                cx.dma("sp", za[:, 0:n], zaT_s[h * 128:(h + 1) * 128, q0:q0 + n], r=[zaT_s], w=[za])
                def qk(kt_, st_):
                    for m in range(2):
                        cx.mm(st_[:, m, 0:n], kTh[64 * m:64 * m + 64, kt_ * 128:(kt_ + 1) * 128], qh[64 * m:64 * m + 64, q0:q0 + n],
                              True, True, r=[kTh, qh], wa=[st_], sig=(m == 1))

                qk(0, pST[step % 2])
                for kt in range(nkt):
                    st = pST[step % 2]
                    pt = pts[step % 3]
                    step += 1
                    if kt + 1 < nkt:
                        qk(kt + 1, pST[step % 2])
                    cx.act(pt[:, :, 0:n], st[:, :, 0:n], AF.Exp, r=[st], w=[pt])
                    for m in range(2):
                        cx.mm(po[:, m, 0:n], vh[:, kt, :], pt[:, m, 0:n], start=(kt == 0), stop=(kt == nkt - 1),
                              r=[vh, pt], wa=[po], sig=(m == 1))
                    cx.mm(psm[:, 0, 0:n], c["onesb"][:, :], pt[:, 0, 0:n], start=(kt == 0), stop=(kt == nkt - 1),
                          r=[c["onesb"], pt], wa=[psm], sig=True)
                    am = accm[ci % 2][0]
                    if kt == 0:
                        cx.cp("dve", acc[:, 1, 0:n], pt[:, 1, 0:n], r=[pt], w=[am])
                    else:
                        cx.tt("dve", acc[:, 1, 0:n], acc[:, 1, 0:n], pt[:, 1, 0:n], ALU.add, r=[pt, am], w=[am])
                cx.mm(psm[:, 1, 0:n], c["onesf"][:, :], acc[:, 1, 0:n], True, True, r=[c["onesf"], accm[ci % 2][0]], wa=[psm], sig=True)
                cx.recip(rs[:, :, 0:n], psm[:, :, 0:n], r=[psm], w=[rs])
                cx.tt("dve", ta[:, 0:n], po[:, 0, 0:n], rs[:, 0, 0:n], ALU.mult, r=[po, rs], w=[ta])
                cx.tt("dve", tb[:, 0:n], po[:, 1, 0:n], rs[:, 1, 0:n], ALU.mult, r=[po, rs], w=[tb])
                cx.stt("dve", ot[:, 0:n], tb[:, 0:n], neglam, ta[:, 0:n], ALU.mult, ALU.add, r=[ta, tb, lam4], w=[ot])
                cx.act(sqo[:, 0:n], ot[:, 0:n], AF.Square, r=[ot], w=[sqo])
                cx.mm(psm[:, 0, 0:n], c["onesf"][:, :], sqo[:, 0:n], True, True, r=[c["onesf"], sqo], w=[psm])
                cx.act(rstd[:, 0:n], psm[:, 0, 0:n], AF.Sqrt, r=[psm, c["eps"]], w=[rstd], bias=c["eps"][:, 0:1], scale=1.0 / 128)
                cx.recip(rstd[:, 0:n], rstd[:, 0:n], r=[rstd], w=[rstd])
                cx.tt("dve", ot[:, 0:n], ot[:, 0:n], rstd[:, 0:n], ALU.mult, r=[ot, rstd], w=[ot])
                ob = obs[ci % 2]
                cx.stt("dve", ob[:, 0:n], ot[:, 0:n], subg[:, 0:1], za[:, 0:n], ALU.mult, ALU.mult, r=[ot, subg, za], w=[ob])
                cx.dma(STQ, catT_s[h * 128:(h + 1) * 128, q0:q0 + n], ob[:, 0:n], r=[ob], wa=[catT_s])
                ci += 1

    if stop < 3:
        return cx
    with Phase(cx) as ph:
        dg = ph.sb("dgc", [128, 8, 31, 128], BF16)
        ycs = ph.sbs("yc", [128, 8, 542], BF16, 2)
        convT = ph.sb("convT", [128, 8, 512], F32)
        sqT = ph.sb("sqT", [128, 8, 512], F32)
        onesc = ph.sb("onesc", [128, 128], F32)
        pconv = ph.pss("pconv", [128, 512], F32, 2)
        pst = ph.ps("pstat", [128, 2, 512], F32)
        mean = ph.sb("mean", [128, 512], F32); var = ph.sb("var", [128, 512], F32)
        tcs = ph.sbs("tc", [128, 512], F32, 2)
        scs = ph.sbs("sc", [128, 512], F32, 2)
        zbt = ph.sbs("zbt", [128, 512], BF16, 2)
        obs = ph.sbs("obc", [128, 512], BF16, 2)
        cx.memset("pool", onesc[:, :], 1.0 / 1024, w=[onesc])
        i = 0
        for j in range(8):
            for k in range(31):
                e = "dve" if i % 2 == 0 else "pool"
                cx.ts(e, dg[:, j, k, :], c["identf"][:, :], cwT[:, j, k:k + 1], None, ALU.mult, None, r=[c["identf"], cwT], wa=[dg])
                i += 1
        yT_v = yT_s.t.rearrange("(j p) c -> p j c", p=128)
        it = 0
        for (src, srcb, c0, n, dcol) in [(yT_v, yT_s, 0, 512, 0), (yT_v, yT_s, 512, 512, 512), (yT_v, yT_s, 1024, 512, 1024),
                                         (yT_v, yT_s, 1536, 512, 1536), (yTc_v, yTc_s, 0, 256, HALF)]:
            yc = ycs[it % 2]; it += 1
            cx.dma("sp", yc[:, :, 0:n + 30], src[:, :, c0:c0 + n + 30], r=[srcb], w=[yc])
            for j in range(8):
                pc = pconv[j % 2]
                for k in range(31):
                    cx.mm(pc[:, 0:n], dg[:, j, k, :], yc[:, j, k:k + n], start=(k == 0), stop=(k == 30), r=[dg, yc], wa=[pc], sig=(k == 30))
                cx.act(convT[:, j, 0:n], pc[:, 0:n], AF.Identity, r=[pc, cvp], wa=[convT], bias=cvp[:, 0, j:j + 1], scale=1.0)
                cx.act(sqT[:, j, 0:n], convT[:, j, 0:n], AF.Square, r=[convT], wa=[sqT])
            for j in range(8):
                cx.mm(pst[:, 0, 0:n], onesc[:, :], convT[:, j, 0:n], start=(j == 0), stop=(j == 7), r=[onesc, convT], wa=[pst], sig=(j == 7))
            for j in range(8):
                cx.mm(pst[:, 1, 0:n], onesc[:, :], sqT[:, j, 0:n], start=(j == 0), stop=(j == 7), r=[onesc, sqT], wa=[pst], sig=(j == 7))
            cx.cp("act", mean[:, 0:n], pst[:, 0, 0:n], r=[pst], w=[mean])
            cx.tt("dve", var[:, 0:n], mean[:, 0:n], mean[:, 0:n], ALU.mult, r=[mean], w=[var])
            cx.tt("dve", var[:, 0:n], pst[:, 1, 0:n], var[:, 0:n], ALU.subtract, r=[pst, var], w=[var])
            cx.ts("dve", var[:, 0:n], var[:, 0:n], 0.0, None, ALU.max, None, r=[var], w=[var])
            cx.act(var[:, 0:n], var[:, 0:n], AF.Sqrt, r=[var, c["eps"]], w=[var], bias=c["eps"][:, 0:1], scale=1.0)
            cx.recip(var[:, 0:n], var[:, 0:n], r=[var], w=[var])
            for j in range(8):
                tcb, scb, zb, ob = tcs[j % 2], scs[j % 2], zbt[j % 2], obs[j % 2]
                cx.dma("sp", zb[:, 0:n], zbT_s[j * 128:(j + 1) * 128, dcol:dcol + n], r=[zbT_s], w=[zb])
                cx.tt("dve", tcb[:, 0:n], convT[:, j, 0:n], mean[:, 0:n], ALU.subtract, r=[convT, mean], w=[tcb])
                cx.tt("dve", tcb[:, 0:n], tcb[:, 0:n], var[:, 0:n], ALU.mult, r=[tcb, var], w=[tcb])
                cx.act(scb[:, 0:n], tcb[:, 0:n], AF.Silu, r=[tcb, cvp], w=[scb], scale=cvp[:, 1, j:j + 1], bias=cvp[:, 2, j:j + 1])
                cx.tt("dve", ob[:, 0:n], scb[:, 0:n], zb[:, 0:n], ALU.mult, r=[scb, zb], w=[ob])
                cx.dma(STQ, catT_s[1024 + j * 128:1024 + (j + 1) * 128, dcol:dcol + n], ob[:, 0:n], r=[ob], wa=[catT_s])

    if stop < 4:
        return cx
    out_proj(cx, w_out, catT_s, gate_bc,
             [(x_own, x1_out, t * 128, t * 128, 0) for t in range(16)] + [(ctx_in, ctx1_out, t * 128, HALF + t * 128, 1) for t in range(2)],
             after_tile=(lambda ti: x1_hook(ti, x1_out)) if x1_hook is not None else None)
    return x1_out, ctx1_out


def out_proj(cx, w_out, catT_s, gate_bc, tiles, blend=None, loader=None, after_tile=None):
    with Phase(cx) as ph:
        wo = ph.sb("wo", [128, 16, D], BF16)
        wst = ph.sbs("wo_st", [128, 2, D], F32, 2)
        cats = ph.sbs("catt", [128, 16, 128], BF16, 2)
        catab = (ph.sbs("catA", [128, 16, 128], BF16, 2), ph.sbs("catB", [128, 16, 128], BF16, 2)) if blend is not None else None
        xts = ph.sbs("xto", [128, D], F32, 2)
        xos = ph.sbs("xoo", [128, D], F32, 2)
        tmp = ph.sbs("tmpo", [128, 512], F32, 2)
        pacc = ph.pss("pacco", [128, 512], F32, 3)
        wv = w_out.t.rearrange("(k p) c -> p k c", p=128)
        for q in range(8):
            st = wst[q % 2]
            cx.dma("sp", st[:, :, :], wv[:, q * 2:(q + 1) * 2, :], r=[w_out], w=[st])
            cx.cp("pool", wo[:, q * 2:(q + 1) * 2, :], st[:, :, :], r=[st], wa=[wo])
        cat_v = catT_s.t.rearrange("(k p) c -> p k c", p=128) if catT_s is not None else None
        n = 0
        for ti, (xsrc, xdst, r0, c0, j) in enumerate(tiles):
            cat, xt, xo = cats[ti % 2], xts[ti % 2], xos[ti % 2]
            if loader is None:
                loader = lambda dst, col: cx.dma("sp", dst[:, :, :], cat_v[:, :, col:col + 128], r=[catT_s], w=[dst])
            if blend is None:
                loader(cat, c0)
            else:
                ca, cb_ = catab[0][ti % 2], catab[1][ti % 2]
                loader(ca, c0)
                loader(cb_, HALF + c0)
                cx.ts("pool", ca[:, :, :], ca[:, :, :], blend[:, 0:1], None, ALU.mult, None, r=[ca, blend], w=[ca])
                cx.stt("dve", cat[:, :, :], cb_[:, :, :], blend[:, 1:2], ca[:, :, :], ALU.mult, ALU.add, r=[ca, cb_, blend], w=[cat])
            cx.dma("sp", xt[:, :], xsrc[r0:r0 + 128, :], r=[xsrc], w=[xt])
            for cb in range(4):
                ps = pacc[n % 3]
                tm = tmp[n % 2]
                n += 1
                for k in range(16):
                    cx.mm(ps[:, :], cat[:, k, :], wo[:, k, cb * 512:(cb + 1) * 512], start=(k == 0), stop=(k == 15), r=[cat, wo], wa=[ps], sig=(k == 15))
                cx.tt("dve", tm[:, :], ps[:, :], gate_bc[:, j, cb * 512:(cb + 1) * 512], ALU.mult, r=[ps, gate_bc], w=[tm])
                cx.tt("dve", xo[:, cb * 512:(cb + 1) * 512], tm[:, :], xt[:, cb * 512:(cb + 1) * 512], ALU.add, r=[tm, xt], wa=[xo])
            cx.dma(STQ, xdst[r0:r0 + 128, :], xo[:, :], r=[xo], wa=[xdst])
            if after_tile is not None:
                after_tile(ti)


T1 = CTX + SEQ
NT1 = T1 // 128
NCH = T1 // 64


def build_l1a(stop=99):
    cx = Ctx()
    _build_l1a(cx, stop)
    return cx


def _build_l1a(cx, stop=99, fused=False, c=None, x1_tiles=None, ctx1_in=None, cat_hook=None):
    nc = cx.nc
    sfx = "1" if fused else ""
    if not fused:
        x1_in = cx.din("x1f", [SEQ, D])
        ctx1_in = cx.din("ctx1", [CTX, D])
        x1_tiles = [(x1_in, t * 128) for t in range(32)]
    ccT_in = cx.din("ccT" + sfx, [128, 16, 2])
    w_ada = cx.din("w_ada" + sfx, [D, 3 * D])
    b_adaT_in = cx.din("b_adaT" + sfx, [128, 48])
    norm_gT_in = cx.din("norm_gT" + sfx, [128, 16])
    w_c = cx.din("w_c", [D, 5120])
    lbg_in = cx.din("lbg", [128, 2, 2, 8])
    ong_in = cx.din("ong", [128, 1])
    maskF_in = cx.din("maskF", [128, 128])
    maskB_in = cx.din("maskB", [128, 128])
    if fused:
        catT1_out = cx.dscr("catT1_s", [1024, SEQ], BF16)
    else:
        ident_in = cx.din("ident", [128, 128])
        catT1_out = cx.dout("catT1", [1024, SEQ], BF16)
        modT_out = cx.dout("modT1", [128, 48, 2], F32)
    qS = cx.dscr("qS", [8, 128, T1], F32)
    sgS = cx.dscr("sgS", [2, 8, 128, T1], F32)
    vS = cx.dscr("vS", [8, 128, NT1, 128], BF16)
    zS = cx.dscr("zS", [8, 128, T1], BF16)

    if c is None:
        c = common_consts(cx, ident_in)
    modT, gs, _ = modulation(cx, c, ccT_in, w_ada, b_adaT_in, norm_gT_in, want_gate_bc=(), sfx="1")
    if not fused:
        cx.dma("sp", modT_out[:, :, :], modT[:, :, :], r=[modT], w=[modT_out])
    lbt = cx.sb("lbt", [128, 2, 2, 8]); lb = cx.sb("lb", [128, 2, 8]); oml = cx.sb("oml", [128, 2, 8]); noml = cx.sb("noml", [128, 2, 8])
    ong = cx.sb("ong_t", [128, 1]); maskF = cx.sb("maskF_t", [128, 128]); maskB = cx.sb("maskB_t", [128, 128])
    cx.dma("sp", lbt[:, :, :, :], lbg_in[:, :, :, :], r=[lbg_in], w=[lbt])
    cx.dma("sp", ong[:, :], ong_in[:, :], r=[ong_in], w=[ong])
    cx.dma("sp", maskF[:, :], maskF_in[:, :], r=[maskF_in], w=[maskF])
    cx.dma("sp", maskB[:, :], maskB_in[:, :], r=[maskB_in], w=[maskB])
    cx.tt("dve", lb[:, :, :], lbt[:, :, 1, :], lbt[:, :, 0, :], ALU.subtract, r=[lbt], w=[lb])
    cx.act(lb[:, :, :], lb[:, :, :], AF.Sigmoid, r=[lb], w=[lb])
    cx.ts("dve", oml[:, :, :], lb[:, :, :], -1.0, 1.0, ALU.mult, ALU.add, r=[lb], w=[oml])
    cx.ts("dve", noml[:, :, :], lb[:, :, :], -1.0, None, ALU.add, None, r=[lb], w=[noml])
    if stop < 1:
        cx.S.barrier()
        return cx

    with Phase(cx) as ph:
        hT = ph.sb("hT", [128, 16, 1280], BF16)
        hTb = [Buf(hT.t, f"hT{i}") for i in range(10)]
        hbufs = (ph.sbs("xt", [128, D], F32, 2), ph.sbs("xn", [128, D], BF16, 2), ph.sb("sqj", [128, D], BF16),
                 ph.sbs("ss", [128, 4], F32, 2), ph.pss("pT", [128, 8, 128], BF16, 2))
        ws = WStream(cx, ph)
        pacc = ph.pss("pacc", [128, 512], F32, 3)
        vbs = ph.sbs("vb", [128, 512], BF16, 2)
        fof = ph.sbs("fof", [128, 512], F32, 3)
        fob = ph.sbs("fob", [128, 512], BF16, 2)
        cnt = {"acc": 0, "v": 0, "f": 0, "b": 0}
        colblocks = [("q", 0, 0), ("q", 512, 1), ("i", 1024, 0), ("i", 1536, 1), ("uf", 2048, 0), ("uf", 2560, 1),
                     ("ub", 3072, 0), ("ub", 3584, 1), ("z", 4096, 0), ("z", 4608, 1)]
        for t0 in range(0, NT1, 10):
            tl = list(range(t0, min(t0 + 10, NT1)))
            tiles = [(ctx1_in, t * 128, 1) if t < 2 else (x1_tiles[t - 2][0], x1_tiles[t - 2][1], 0) for t in tl]
            build_hT(cx, hbufs, c, modT, gs, hT, hTb, tiles)
            ntok = len(tl) * 128
            chunks = [(a, min(512, ntok - a)) for a in range(0, ntok, 512)]
            wnext = ws.fetch(w_c, colblocks[0][1])
            for ci, (fam, c0, sub) in enumerate(colblocks):
                wb = wnext
                if ci + 1 < len(colblocks):
                    wnext = ws.fetch(w_c, colblocks[ci + 1][1])
                if fam == "i":
                    for ti, t in enumerate(tl):
                        ps = pacc[cnt["acc"] % 3]; cnt["acc"] += 1
                        for k in range(16):
                            cx.mm(ps[:, :], hT[:, k, ti * 128:(ti + 1) * 128], wb[:, k, :], start=(k == 0), stop=(k == 15),
                                  r=[hTb[ti], wb], wa=[ps], sig=(k == 15))
                        vb = vbs[cnt["v"] % 2]; cnt["v"] += 1
                        cx.cp("act", vb[:, :], ps[:, :], r=[ps], w=[vb])
                        cx.dma(STQ, vS.t[sub * 4:sub * 4 + 4, :, t, :].rearrange("h p c -> p h c"),
                               vb[:, :].rearrange("p (h c) -> p h c", c=128), r=[vb], wa=[vS])
                else:
                    for (h0c, n) in chunks:
                        tt0 = h0c // 128
                        rb = [hTb[x] for x in range(tt0, tt0 + (n + 127) // 128)]
                        g0 = t0 * 128 + h0c
                        for fc in range(4):
                            ps = pacc[cnt["acc"] % 3]; cnt["acc"] += 1
                            for k in range(16):
                                cx.mm(ps[:, 0:n], wb[:, k, fc * 128:(fc + 1) * 128], hT[:, k, h0c:h0c + n], start=(k == 0), stop=(k == 15),
                                      r=rb + [wb], wa=[ps], sig=(k == 15))
                            h = sub * 4 + fc
                            if fam == "z":
                                fo = fob[cnt["b"] % 2]; cnt["b"] += 1
                                cx.act(fo[:, 0:n], ps[:, 0:n], AF.Silu, r=[ps], w=[fo])
                                cx.dma(STQ, zS[h, :, g0:g0 + n], fo[:, 0:n], r=[fo], wa=[zS])
                            else:
                                fo = fof[cnt["f"] % 3]; cnt["f"] += 1
                                cx.act(fo[:, 0:n], ps[:, 0:n], AF.Silu if fam == "q" else AF.Sigmoid, r=[ps], w=[fo])
                                if fam == "q":
                                    cx.dma(STQ, qS[h, :, g0:g0 + n], fo[:, 0:n], r=[fo], wa=[qS])
                                else:
                                    cx.dma(STQ, sgS[0 if fam == "uf" else 1, h, :, g0:g0 + n], fo[:, 0:n], r=[fo], wa=[sgS])
    if stop < 2:
        return cx

    with Phase(cx) as ph:
        msk = ph.sb("msk", [128, T1], F32)
        qf = ph.sb("qf", [128, T1], F32)
        vh = ph.sb("vh1", [128, NT1, 128], BF16)
        oacc = ph.sb("oacc", [128, SEQ], F32)
        lf = ph.sb("lf", [128, T1], F32)
        kf = ph.sb("kf", [128, T1], F32)
        bc = ph.sb("bcum", [128, T1], F32)
        ef = ph.sb("ef", [128, T1], F32)
        qdec = ph.sb("qdec", [128, T1], BF16)
        kinv = ph.sb("kinv", [128, T1], BF16)
        kendT = ph.sb("kendT", [128, T1], BF16)
        kend = ph.sb("kend", [128, NT1, 128], BF16)
        dec = ph.sb("dec", [128, NCH], F32)
        bend = ph.sb("bend", [128, NCH], F32)
        Sf = ph.sb("Sf", [128, 128], F32)
        Sbs = ph.sbs("Sb", [128, 128], BF16, 4)
        Ams = ph.sbs("Am", [128, 128], BF16, 3)
        sqr = ph.sb("sqr", [128, 512], F32); rst = ph.sb("rst", [128, 512], F32); onb = ph.sb("onb", [128, 512], F32)
        zcs = ph.sbs("zc", [128, 512], BF16, 2); obs = ph.sbs("ob1", [128, 512], BF16, 2)
        pA = ph.pss("pA", [128, 512], F32, 1)
        po = ph.pss("po1", [128, 512], F32, 2)
        pS = ph.pss("pS", [128, 512], F32, 4)
        pK = ph.pss("pK", [128, 8, 128], BF16, 1)
        ia_ = [0, 0]
        cx.memset("pool", msk[:, :], 1.0, w=[msk])
        cx.memset("pool", msk[:, :].rearrange("p (c d) -> p c d", d=64)[:, :, 0:1], 0.0, w=[msk])
        v3 = lambda a: a.rearrange("p (c d) -> p c d", d=64)
        HT = T1 // 2
        ia = 0
        isb = 0
        ipo = 0
        for h in range(8):
            cx.dma("sp", qf[:, :], qS[h, :, :], r=[qS], w=[qf])
            cx.dma("sp", vh[:, :, :], vS[h, :, :, :], r=[vS], w=[vh])
            for d_ in range(2):
                cx.dma("sp", lf[:, :], sgS[d_, h, :, :], r=[sgS], w=[lf])
                cx.ts("dve", kf[:, :], lf[:, :], noml[:, d_, h:h + 1], oml[:, d_, h:h + 1], ALU.mult, ALU.add, r=[lf, noml, oml], w=[kf])
                cx.act(lf[:, :], lf[:, :], AF.Ln, r=[lf, oml, lb], w=[lf], scale=oml[:, d_, h:h + 1], bias=lb[:, d_, h:h + 1])
                for hh in range(2):
                    sl = slice(hh * HT, (hh + 1) * HT)
                    cx.S.op("dve", lambda: nc.vector.tensor_tensor_scan(out=bc[:, sl], data0=msk[:, sl], data1=lf[:, sl], initial=0.0,
                                                                        op0=ALU.mult, op1=ALU.add), r=[msk, lf], wa=[bc])
                cx.cp("dve", bend[:, :], v3(bc[:, :])[:, :, 63], r=[bc], w=[bend])
                bend_bc = bend[:, :].unsqueeze(2).to_broadcast([128, NCH, 64])
                if d_ == 1:
                    cx.tt("dve", v3(ef[:, :]), v3(lf[:, :]), bend_bc, ALU.add, r=[lf, bend], w=[ef])
                    cx.tt("dve", bc[:, :], ef[:, :], bc[:, :], ALU.subtract, r=[ef, bc], w=[bc])
                cx.act(dec[:, :], bend[:, :], AF.Exp, r=[bend], w=[dec])
                cx.act(ef[:, :], bc[:, :], AF.Exp, r=[bc], w=[ef])
                cx.tt("dve", qdec[:, :], qf[:, :], ef[:, :], ALU.mult, r=[qf, ef], w=[qdec])
                cx.act(ef[:, :], bc[:, :], AF.Exp, r=[bc], w=[ef], scale=-1.0)
                cx.tt("dve", kinv[:, :], kf[:, :], ef[:, :], ALU.mult, r=[kf, ef], w=[kinv])
                cx.tt("dve", v3(ef[:, :]), bend_bc, v3(bc[:, :]), ALU.subtract, r=[bend, bc], w=[ef])
                cx.act(ef[:, :], ef[:, :], AF.Exp, r=[ef], w=[ef])
                cx.tt("dve", kendT[:, :], kf[:, :], ef[:, :], ALU.mult, r=[kf, ef], w=[kendT])
                for g in range(0, NT1, 8):
                    pk = pK[0]
                    ng = min(8, NT1 - g)
                    for t in range(ng):
                        cx.tr(pk[:, t, :], kendT[:, (g + t) * 128:(g + t + 1) * 128], c["identb"][:, :], r=[kendT, c["identb"]], wa=[pk], sig=(t == ng - 1))
                    cx.cp("pool" if False else "act", kend[:, g:g + ng, :], pk[:, 0:ng, :], r=[pk], wa=[kend])
                cx.memset("pool", Sf[:, :], 0.0, w=[Sf])
                Sb = Sbs[isb % 4]; isb += 1
                cx.memset("pool", Sb[:, :], 0.0, w=[Sb])
                mask = maskF if d_ == 0 else maskB
                pairs = list(range(NT1)) if d_ == 0 else [1, 0] + list(range(NT1 - 1, 1, -1))
                order = (0, 1) if d_ == 0 else (1, 0)
                st = {}

                def early(p):
                    t0_ = p * 128
                    e_ = {}
                    if p >= 2:
                        a_ps = pA[0]
                        Am = Ams[ia_[0] % 3]; ia_[0] += 1
                        cx.mm(a_ps[:, 0:128], kinv[:, t0_:t0_ + 128], qdec[:, t0_:t0_ + 128], True, True, r=[kinv, qdec], w=[a_ps])
                        cx.tt("dve", Am[:, :], a_ps[:, 0:128], mask[:, :], ALU.mult, r=[a_ps, mask], w=[Am])
                        e_["Am"] = Am
                    e_["s"] = []
                    for hc in order:
                        pr = slice(hc * 64, hc * 64 + 64)
                        s_ps = pS[ia_[1] % 4]; ia_[1] += 1
                        cx.mm(s_ps[:, 0:128], kend[pr, p, :], vh[pr, p, :], True, True, r=[kend, vh], w=[s_ps])
                        e_["s"].append(s_ps)
                    st[p] = e_

                early(pairs[0])
                for pi, p in enumerate(pairs):
                    if pi + 1 < len(pairs):
                        early(pairs[pi + 1])
                    e_ = st.pop(p)
                    lat = p >= 2
                    t0_ = p * 128
                    if lat:
                        o_ps = po[ipo % 2]; ipo += 1
                        cx.mm(o_ps[:, 0:128], vh[:, p, :], e_["Am"][:, :], True, False, r=[vh, e_["Am"]], wa=[o_ps], sig=False)
                    for oi, hc in enumerate(order):
                        c64 = slice(t0_ + hc * 64, t0_ + hc * 64 + 64)
                        if lat:
                            cx.mm(o_ps[:, hc * 64:hc * 64 + 64], Sb[:, :], qdec[:, c64], False, (oi == 1), r=[Sb, qdec], wa=[o_ps], sig=True)
                        s_ps = e_["s"][oi]
                        dcol = dec[:, 2 * p + hc:2 * p + hc + 1]
                        Sb = Sbs[isb % 4]; isb += 1
                        cx.stt("dve", Sb[:, :], Sf[:, :], dcol, s_ps[:, 0:128], ALU.mult, ALU.add, r=[Sf, dec, s_ps], w=[Sb])
                        cx.stt("dve", Sf[:, :], Sf[:, :], dcol, s_ps[:, 0:128], ALU.mult, ALU.add, r=[Sf, dec, s_ps], w=[Sf])
                    if lat:
                        oc = slice((p - 2) * 128, (p - 1) * 128)
                        if d_ == 0:
                            cx.cp("act", oacc[:, oc], o_ps[:, 0:128], r=[o_ps], wa=[oacc])
                        else:
                            cx.tt("dve", oacc[:, oc], oacc[:, oc], o_ps[:, 0:128], ALU.add, r=[o_ps, oacc], wa=[oacc])
            for q8 in range(8):
                cs = slice(q8 * 512, (q8 + 1) * 512)
                a_ps = pA[0]
                zc = zcs[q8 % 2]; ob = obs[q8 % 2]
                cx.dma("sp", zc[:, :], zS[h, :, CTX + q8 * 512:CTX + (q8 + 1) * 512], r=[zS], w=[zc])
                cx.act(sqr[:, :], oacc[:, cs], AF.Square, r=[oacc], w=[sqr])
                cx.mm(a_ps[:, :], c["onesf"][:, :], sqr[:, :], True, True, r=[c["onesf"], sqr], w=[a_ps])
                cx.act(rst[:, :], a_ps[:, :], AF.Sqrt, r=[a_ps, c["eps"]], w=[rst], bias=c["eps"][:, 0:1], scale=1.0 / 128)
                cx.recip(rst[:, :], rst[:, :], r=[rst], w=[rst])
                cx.tt("dve", onb[:, :], oacc[:, cs], rst[:, :], ALU.mult, r=[oacc, rst], w=[onb])
                cx.stt("dve", ob[:, :], onb[:, :], ong[:, 0:1], zc[:, :], ALU.mult, ALU.mult, r=[onb, ong, zc], w=[ob])
                cx.dma(STQ, catT1_out[h * 128:(h + 1) * 128, cs], ob[:, :], r=[ob], wa=[catT1_out])
            if cat_hook is not None:
                cat_hook(h, catT1_out)
    return catT1_out, modT


PAIRS = [[0, 1], [2, 3], [4, 5], [6, 7]]
WARMUP_COLL = True


def build_fused():
    cx = Ctx()
    ident_in = cx.din("ident", [128, 128])
    bmask_in = cx.din("bmask", [128, 2])
    w_out1 = cx.din("w_out1", [D, D])
    x2_out = cx.dout("x2", [HALF, D])
    c = common_consts(cx, ident_in)
    if WARMUP_COLL:
        wu_src = cx.dscr("wu_src", [128, 128], F32)
        wu_dst = cx.dscr("wu_dst", [256, 128], F32)
        cx.dma("sp", wu_src[:, :], ident_in[:, :], r=[ident_in], w=[wu_src])
        cx.S.coll("AllGather", wu_dst[:, :], wu_src[:, :], PAIRS, r=[wu_src], w=[wu_dst])
    x1g_t = cx.nc.dram_tensor("x1g_s", [8, 512, D], F32, kind="Internal").ap()
    x1g = [Buf(x1g_t[i], f"x1g{i}") for i in range(8)]

    def x1_hook(ti, x1_own):
        if ti < 16 and ti % 2 == 1:
            i = ti // 2
            cx.S.coll("AllGather", x1g[i][:, :], x1_own[i * 256:(i + 1) * 256, :], PAIRS, r=[x1_own], w=[x1g[i]])

    with Scope(cx):
        x1_own_s, ctx1_s = _build_l0(cx, 99, fused=True, c=c, x1_hook=x1_hook)
    x1_tiles = []
    for t in range(32):
        r_, lt = t // 16, t % 16
        x1_tiles.append((x1g[lt // 2], r_ * 256 + (lt % 2) * 128))
    with Scope(cx):
        catg_t = cx.nc.dram_tensor("catg_s", [4, 512, SEQ], BF16, kind="Internal").ap()
        catg = [Buf(catg_t[i], f"catg{i}") for i in range(4)]

        def cat_hook(h, cat_own):
            if h % 2 == 1:
                i = h // 2
                cx.S.coll("AllGather", catg[i][:, :], cat_own[i * 256:(i + 1) * 256, :], PAIRS, r=[cat_own], w=[catg[i]])

        catT1_s, modT = _build_l1a(cx, 99, fused=True, c=c, x1_tiles=x1_tiles, ctx1_in=ctx1_s, cat_hook=cat_hook)
        gate_bc = cx.sb("gate_bc1", [128, 2, D], F32)
        bmask = cx.sb("bmask_t", [128, 2], F32)
        cx.dma("sp", bmask[:, :], bmask_in[:, :], r=[bmask_in], w=[bmask])
        gate_rows(cx, c, modT, gate_bc, (0,))

        def loader(dst, col):
            for r_ in range(2):
                for i in range(4):
                    k0 = r_ * 8 + i * 2
                    cx.dma("sp", dst[:, k0:k0 + 2, :],
                           catg[i].t[r_ * 256:(r_ + 1) * 256, col:col + 128].rearrange("(k p) c -> p k c", p=128),
                           r=[catg[i]], wa=[dst])
        out_proj(cx, w_out1, None, gate_bc, [(x1_own_s, x2_out, t * 128, t * 128, 0) for t in range(16)], blend=bmask, loader=loader)
    return cx


def build_l1b():
    cx = Ctx()
    catT_in = cx.din("catT", [D, HALF], BF16)
    x1_own = cx.din("x1o", [HALF, D])
    w_out = cx.din("w_out", [D, D])
    modT_in = cx.din("modT1", [128, 48, 2])
    ident_in = cx.din("ident", [128, 128])
    x2_out = cx.dout("x2", [HALF, D])
    c = common_consts(cx, ident_in)
    modT = cx.sb("modT", [128, 48, 2], F32)
    gate_bc = cx.sb("gate_bc", [128, 2, D], F32)
    cx.dma("sp", modT[:, :, :], modT_in[:, :, :], r=[modT_in], w=[modT])
    gate_rows(cx, c, modT, gate_bc, (0,))
    out_proj(cx, w_out, catT_in, gate_bc, [(x1_own, x2_out, t * 128, t * 128, 0) for t in range(16)])
    return cx


def rope_tables():
    rows = SEQ // 64
    row = np.repeat(np.arange(rows), 64).astype(np.float32)
    col = np.tile(np.arange(64), rows).astype(np.float32)
    inv = (10000.0 ** (-np.arange(0, 32, 2, dtype=np.float32) / 32)).astype(np.float32)

    def axis_angles(pos):
        a = pos[:, None] * inv[None, :]
        return np.concatenate([a, a], axis=-1)
    ang = np.concatenate([axis_angles(row), axis_angles(col)], axis=-1).astype(np.float32)
    cos, sin = np.cos(ang), np.sin(ang)
    sgn = np.tile(np.concatenate([-np.ones(16), np.ones(16)]), 2).astype(np.float32)
    return np.concatenate([cos, sin * sgn], axis=-1).astype(np.float32)


def fm(v, nchunk):
    return np.ascontiguousarray(np.asarray(v, np.float32).reshape(nchunk, 128).T)


_CACHE = {}


def run_l0(inp, cores):
    if "l0" not in _CACHE:
        _CACHE["l0"] = build_l0()
    cx = _CACHE["l0"]
    rope = rope_tables()
    ident = np.eye(128, dtype=np.float32)
    in_maps = []
    for cid in cores:
        b, hf = cid // 2, cid % 2
        own = slice(hf * HALF, (hf + 1) * HALF)
        oth = slice((1 - hf) * HALF, (2 - hf) * HALF)
        cc = np.stack([inp["c"][b], inp["c_ctx"]], axis=-1)
        ccT = np.ascontiguousarray(cc.reshape(16, 128, 2).transpose(1, 0, 2))
        cw = inp["conv_w"][0]
        cwT = np.ascontiguousarray(cw.reshape(31, 8, 128).transpose(2, 1, 0))
        cvp = np.ascontiguousarray(np.stack([fm(inp["conv_b"][0], 8), fm(inp["cln_g"][0], 8), fm(inp["cln_b"][0], 8)], axis=1))
        hm = np.zeros((128, 2), np.float32)
        hm[:, 0] = 1.0 if hf == 1 else 0.0
        hm[:, 1] = 1.0 if hf == 0 else 0.0
        in_maps.append({
            "x_own": np.ascontiguousarray(inp["x"][b, own]), "x_oth": np.ascontiguousarray(inp["x"][b, oth]),
            "ctx": np.ascontiguousarray(inp["ctx"][b]), "ccT": ccT,
            "w_ada": np.ascontiguousarray(inp["w_ada"][0]), "b_adaT": fm(inp["b_ada"][0], 48), "norm_gT": fm(inp["norm_g"][0], 16),
            "w_in": np.ascontiguousarray(inp["w_in_ab"][0]), "w_out": np.ascontiguousarray(inp["w_out_ab"][0]),
            "qkg": np.ascontiguousarray(np.stack([inp["qn_g"][0], inp["kn_g"][0]])),
            "lamv": np.ascontiguousarray(np.stack([inp["lam_q1"][0], inp["lam_k1"][0], inp["lam_q2"][0], inp["lam_k2"][0]])),
            "subg": np.ascontiguousarray(inp["subln_g"][0].reshape(128, 1)),
            "cwT": cwT, "cvp": cvp,
            "rq_own": np.ascontiguousarray(rope[own] * np.float32(0.125)), "rk_own": np.ascontiguousarray(rope[own]),
            "rk_oth": np.ascontiguousarray(rope[oth]), "hmask": hm, "ident": ident,
        })
    res = run_bass_kernel_spmd(cx.nc, in_maps, core_ids=list(range(len(cores))))
    return res.results


def hgrn_masks():
    i = np.arange(128)
    same = (i[:, None] // 64) == (i[None, :] // 64)
    mF = (same & (i[:, None] <= i[None, :])).astype(np.float32)
    mB = (same & (i[:, None] >= i[None, :])).astype(np.float32)
    return mF, mB


def run_l1a(inp, x1, ctx1, cores):
    if "l1a" not in _CACHE:
        _CACHE["l1a"] = build_l1a()
    cx = _CACHE["l1a"]
    ident = np.eye(128, dtype=np.float32)
    mF, mB = hgrn_masks()
    wc = inp["w_in_c"][0]
    in_maps = []
    for cid in cores:
        b, hf = cid // 2, cid % 2
        cc = np.stack([inp["c"][b], inp["c_ctx"]], axis=-1)
        ccT = np.ascontiguousarray(cc.reshape(16, 128, 2).transpose(1, 0, 2))
        cols = np.concatenate([np.arange(f * D + hf * 1024, f * D + (hf + 1) * 1024) for f in range(5)])
        lg = inp["lb_gamma"][:, :, hf * 1024:(hf + 1) * 1024]
        lbg = np.ascontiguousarray(lg.reshape(2, 2, 8, 128).transpose(3, 0, 1, 2))
        in_maps.append({
            "x1f": np.ascontiguousarray(x1[b]), "ctx1": np.ascontiguousarray(ctx1[b]), "ccT": ccT,
            "w_ada": np.ascontiguousarray(inp["w_ada"][1]), "b_adaT": fm(inp["b_ada"][1], 48), "norm_gT": fm(inp["norm_g"][1], 16),
            "w_c": np.ascontiguousarray(wc[:, cols]), "lbg": lbg,
            "ong": np.ascontiguousarray(inp["onorm_g"][0].reshape(128, 1)),
            "maskF": mF, "maskB": mB, "ident": ident,
        })
    res = run_bass_kernel_spmd(cx.nc, in_maps, core_ids=list(range(len(cores))))
    return res.results


def run_l1b(inp, x1, cat_pairs, modTs, cores):
    if "l1b" not in _CACHE:
        _CACHE["l1b"] = build_l1b()
    cx = _CACHE["l1b"]
    ident = np.eye(128, dtype=np.float32)
    in_maps = []
    for i, cid in enumerate(cores):
        b, hf = cid // 2, cid % 2
        own = slice(hf * HALF, (hf + 1) * HALF)
        in_maps.append({
            "catT": np.ascontiguousarray(cat_pairs[b][:, own]), "x1o": np.ascontiguousarray(x1[b, own]),
            "w_out": np.ascontiguousarray(inp["w_out_c"][0]), "modT1": modTs[i], "ident": ident,
        })
    res = run_bass_kernel_spmd(cx.nc, in_maps, core_ids=list(range(len(cores))))
    return res.results


def run_l1(inp, x1, ctx1, cores):
    ra = run_l1a(inp, x1, ctx1, cores)
    cat_pairs = {}
    for i, cid in enumerate(cores):
        b, hf = cid // 2, cid % 2
        cat_pairs.setdefault(b, [None, None])[hf] = ra[i]["catT1"]
    cat_pairs = {b: np.concatenate(v, axis=0) for b, v in cat_pairs.items()}
    rb = run_l1b(inp, x1, cat_pairs, [ra[i]["modT1"] for i in range(len(cores))], cores)
    return rb


def l0_inputs(inp, cid, rope):
    b, hf = cid // 2, cid % 2
    own = slice(hf * HALF, (hf + 1) * HALF)
    oth = slice((1 - hf) * HALF, (2 - hf) * HALF)
    cc = np.stack([inp["c"][b], inp["c_ctx"]], axis=-1)
    ccT = np.ascontiguousarray(cc.reshape(16, 128, 2).transpose(1, 0, 2))
    cw = inp["conv_w"][0]
    cwT = np.ascontiguousarray(cw.reshape(31, 8, 128).transpose(2, 1, 0))
    cvp = np.ascontiguousarray(np.stack([fm(inp["conv_b"][0], 8), fm(inp["cln_g"][0], 8), fm(inp["cln_b"][0], 8)], axis=1))
    hm = np.zeros((128, 2), np.float32)
    hm[:, 0] = 1.0 if hf == 1 else 0.0
    hm[:, 1] = 1.0 if hf == 0 else 0.0
    return {
        "x_own": np.ascontiguousarray(inp["x"][b, own]), "x_oth": np.ascontiguousarray(inp["x"][b, oth]),
        "ctx": np.ascontiguousarray(inp["ctx"][b]), "ccT": ccT,
        "w_ada": np.ascontiguousarray(inp["w_ada"][0]), "b_adaT": fm(inp["b_ada"][0], 48), "norm_gT": fm(inp["norm_g"][0], 16),
        "w_in": np.ascontiguousarray(inp["w_in_ab"][0]), "w_out": np.ascontiguousarray(inp["w_out_ab"][0]),
        "qkg": np.ascontiguousarray(np.stack([inp["qn_g"][0], inp["kn_g"][0]])),
        "lamv": np.ascontiguousarray(np.stack([inp["lam_q1"][0], inp["lam_k1"][0], inp["lam_q2"][0], inp["lam_k2"][0]])),
        "subg": np.ascontiguousarray(inp["subln_g"][0].reshape(128, 1)),
        "cwT": cwT, "cvp": cvp,
        "rq_own": np.ascontiguousarray(rope[own] * np.float32(0.125)), "rk_own": np.ascontiguousarray(rope[own]),
        "rk_oth": np.ascontiguousarray(rope[oth]), "hmask": hm,
    }


def l1_inputs(inp, cid):
    b, hf = cid // 2, cid % 2
    cc = np.stack([inp["c"][b], inp["c_ctx"]], axis=-1)
    ccT = np.ascontiguousarray(cc.reshape(16, 128, 2).transpose(1, 0, 2))
    cols = np.concatenate([np.arange(f * D + hf * 1024, f * D + (hf + 1) * 1024) for f in range(5)])
    lg = inp["lb_gamma"][:, :, hf * 1024:(hf + 1) * 1024]
    lbg = np.ascontiguousarray(lg.reshape(2, 2, 8, 128).transpose(3, 0, 1, 2))
    mF, mB = hgrn_masks()
    bm = np.zeros((128, 2), np.float32)
    bm[:, hf] = 1.0
    return {
        "ccT1": ccT, "w_ada1": np.ascontiguousarray(inp["w_ada"][1]), "b_adaT1": fm(inp["b_ada"][1], 48),
        "norm_gT1": fm(inp["norm_g"][1], 16), "w_c": np.ascontiguousarray(inp["w_in_c"][0][:, cols]), "lbg": lbg,
        "ong": np.ascontiguousarray(inp["onorm_g"][0].reshape(128, 1)), "maskF": mF, "maskB": mB,
        "w_out1": np.ascontiguousarray(inp["w_out_c"][0]), "bmask": bm,
    }


def run_fused(inp, cores):
    if "fused" not in _CACHE:
        _CACHE["fused"] = build_fused()
    cx = _CACHE["fused"]
    rope = rope_tables()
    ident = np.eye(128, dtype=np.float32)
    in_maps = []
    for cid in cores:
        m = {"ident": ident}
        m.update(l0_inputs(inp, cid, rope))
        m.update(l1_inputs(inp, cid))
        in_maps.append(m)
    res = run_bass_kernel_spmd(cx.nc, in_maps, core_ids=list(range(len(cores))))
    return res.results


def kernel_unfused(**inputs):
    inp = {k: np.asarray(v) for k, v in inputs.items()}
    cores = list(range(8))
    r0 = run_l0(inp, cores)
    x1 = np.zeros_like(inp["x"])
    ctx1 = np.zeros_like(inp["ctx"])
    for cid in cores:
        b, hf = cid // 2, cid % 2
        x1[b, hf * HALF:(hf + 1) * HALF] = r0[cid]["x1"]
        ctx1[b] = r0[cid]["ctx1"]
    r1 = run_l1(inp, x1, ctx1, cores)
    out = np.zeros_like(inp["x"])
    for cid in cores:
        b, hf = cid // 2, cid % 2
        out[b, hf * HALF:(hf + 1) * HALF] = r1[cid]["x2"]
    return out


def kernel(**inputs):
    inp = {k: np.asarray(v) for k, v in inputs.items()}
    cores = list(range(8))
    r = run_fused(inp, cores)
    out = np.zeros_like(inp["x"])
    for cid in cores:
        b, hf = cid // 2, cid % 2
        out[b, hf * HALF:(hf + 1) * HALF] = r[cid]["x2"]
    return out
```

```python
from contextlib import ExitStack
import math
import numpy as np
import ml_dtypes
import concourse.bass as bass
import concourse.mybir as mybir
from concourse.bass_utils import run_bass_kernel_spmd

F32 = mybir.dt.float32
BF16 = mybir.dt.bfloat16
AF = mybir.ActivationFunctionType
ALU = mybir.AluOpType
AX = mybir.AxisListType

D = 2048
SEQ = 4096
HALF = 2048
CTX = 256
EPS = 1e-6
NKEY = CTX + SEQ
STQ = "act"


class Buf:
    def __init__(self, t, name="", psum=False):
        self.t = t
        self.name = name
        self.psum = psum
        self.w = {}
        self.r = {}
        self.pr = {}
        self.open = False

    def __getitem__(self, k):
        return self.t[k]


def _merge(d, s):
    for k, v in s.items():
        if d.get(k, 0) < v:
            d[k] = v


class Sched:
    GEN = 12000
    NSLOT = 8

    def __init__(self, nc):
        self.nc = nc
        self.eng = {"pe": nc.tensor, "dve": nc.vector, "act": nc.scalar,
                    "pool": nc.gpsimd, "sp": nc.sync}
        self.cnt = {e: 0 for e in self.eng}
        self.gen = {e: 0 for e in self.eng}
        self.sems = {}
        self.waited = {e: {} for e in self.eng}
        self.dma_i = {e: 0 for e in self.eng}
        self.nops = 0
        self.nwaits = 0

    def sem(self, key):
        if key not in self.sems:
            self.sems[key] = self.nc.alloc_semaphore("s_" + "_".join(str(k) for k in key))
        return self.sems[key]

    def _wait(self, e, deps):
        for key, val in deps.items():
            if key[0] == "E" and key[1] == e and e in ("pe", "sp"):
                continue
            if self.waited[e].get(key, 0) >= val:
                continue
            self.eng[e].wait_ge(self.sem(key), val)
            self.waited[e][key] = val
            self.nwaits += 1

    def _deps(self, e, r, w, wa):
        deps = {}
        for b in r:
            _merge(deps, b.w)
            if b.psum:
                _merge(deps, {k: v for k, v in b.r.items() if k[1] != e})
        for b in w:
            _merge(deps, b.w)
            _merge(deps, b.r)
            _merge(deps, b.pr)
        for b in wa:
            if not b.open:
                b.pr = dict(b.r)
                _merge(b.pr, b.w)
                b.r = {}
                b.w = {}
                b.open = True
            _merge(deps, b.pr)
        return deps

    def _record(self, key, val, r, w, wa):
        ev = {key: val}
        for b in r:
            _merge(b.r, ev)
            b.open = False
        for b in w:
            b.w = dict(ev)
            b.r = {}
            b.pr = {}
            b.open = False
        for b in wa:
            _merge(b.w, ev)

    def op(self, e, fn, r=(), w=(), wa=(), sig=True):
        deps = self._deps(e, r, w, wa)
        self._wait(e, deps)
        ins = fn()
        self.nops += 1
        key = ("E", e, self.gen[e])
        if sig:
            self.cnt[e] += 1
            ins.then_inc(self.sem(key), 1)
            self._record(key, self.cnt[e], r, w, wa)
            if self.cnt[e] >= self.GEN:
                self.gen[e] += 1
                self.cnt[e] = 0
        else:
            self._record(key, self.cnt[e] + 1, r, w, wa)
        return ins

    def dma(self, e, out, in_, r=(), w=(), wa=(), **kw):
        i = self.dma_i[e]
        slot = i % self.NSLOT
        key = ("D", e, slot)
        val = 16 * (i // self.NSLOT + 1)
        deps = self._deps(e, r, w, wa)
        if val > 16:
            _merge(deps, {key: val - 16})
        self._wait(e, deps)
        ins = self.eng[e].dma_start(out=out, in_=in_, **kw)
        ins.then_inc(self.sem(key), 16)
        self.dma_i[e] = i + 1
        self._record(key, val, r, w, wa)
        return ins

    def coll(self, kind, out, in_, groups, r=(), w=()):
        e = "pool"
        self.ncoll = getattr(self, "ncoll", 0) + 1
        key = ("C", e, self.ncoll)
        deps = self._deps(e, r, w, ())
        self._wait(e, deps)
        ins = self.nc.gpsimd.collective_compute(kind, ALU.bypass, replica_groups=groups, ins=[in_], outs=[out])
        ins.then_inc(self.sem(key), 1)
        self.colls = getattr(self, "colls", {})
        self.colls[key] = 1
        self._record(key, 1, r, w, ())
        return ins

    def barrier(self):
        allev = {}
        for e in self.eng:
            if self.cnt[e] > 0:
                allev[("E", e, self.gen[e])] = self.cnt[e]
            elif self.gen[e] > 0:
                allev[("E", e, self.gen[e] - 1)] = self.GEN
        for e in self.eng:
            n = self.dma_i[e]
            for slot in range(min(n, self.NSLOT)):
                last = ((n - 1 - slot) // self.NSLOT) * self.NSLOT + slot
                allev[("D", e, slot)] = 16 * (last // self.NSLOT + 1)
        allev.update(getattr(self, "colls", {}))
        for e in self.eng:
            for key, val in allev.items():
                if key[0] == "E" and key[1] == e and e in ("pe", "sp"):
                    continue
                if self.waited[e].get(key, 0) >= val:
                    continue
                self.eng[e].wait_ge(self.sem(key), val)
                self.waited[e][key] = val
                self.nwaits += 1


class Ctx:
    def __init__(self):
        self.nc = bass.Bass("TRN2", target_bir_lowering=False)
        self.S = Sched(self.nc)
        self.later = []
        self.scope = None
        self.uid = 0

    def un(self, name):
        self.uid += 1
        return f"{name}_u{self.uid}"

    def din(self, name, shape, dt=F32):
        return Buf(self.nc.dram_tensor(name, list(shape), dt, kind="ExternalInput").ap(), name)

    def dout(self, name, shape, dt=F32):
        return Buf(self.nc.dram_tensor(name, list(shape), dt, kind="ExternalOutput").ap(), name)

    def dscr(self, name, shape, dt=BF16):
        return Buf(self.nc.dram_tensor(name, list(shape), dt, kind="Internal").ap(), name)

    def sb(self, name, shape, dt=F32):
        name = self.un(name)
        if self.scope is not None:
            return Buf(self.scope.enter_context(self.nc.sbuf_tensor(name, list(shape), dt)), name)
        return Buf(self.nc.alloc_sbuf_tensor(name, list(shape), dt), name)

    def act(self, out, in_, func, r, w=(), wa=(), **kw):
        nc = self.nc
        return self.S.op("act", lambda: nc.scalar.activation(out=out, in_=in_, func=func, **kw), r=r, w=w, wa=wa)

    def _ve(self, e):
        return self.nc.vector if e == "dve" else self.nc.gpsimd

    def tt(self, e, out, in0, in1, op, r, w=(), wa=()):
        eng = self._ve(e)
        return self.S.op(e, lambda: eng.tensor_tensor(out=out, in0=in0, in1=in1, op=op), r=r, w=w, wa=wa)

    def ts(self, e, out, in0, s1, s2, op0, op1, r, w=(), wa=()):
        eng = self._ve(e)
        if s2 is None:
            return self.S.op(e, lambda: eng.tensor_scalar(out=out, in0=in0, scalar1=s1, scalar2=None, op0=op0), r=r, w=w, wa=wa)
        return self.S.op(e, lambda: eng.tensor_scalar(out=out, in0=in0, scalar1=s1, scalar2=s2, op0=op0, op1=op1), r=r, w=w, wa=wa)

    def stt(self, e, out, in0, scalar, in1, op0, op1, r, w=(), wa=()):
        eng = self._ve(e)
        return self.S.op(e, lambda: eng.scalar_tensor_tensor(out=out, in0=in0, scalar=scalar, in1=in1, op0=op0, op1=op1), r=r, w=w, wa=wa)

    def cp(self, e, out, in_, r, w=(), wa=()):
        if e == "act":
            nc = self.nc
            return self.S.op("act", lambda: nc.scalar.copy(out=out, in_=in_), r=r, w=w, wa=wa)
        eng = self._ve(e)
        return self.S.op(e, lambda: eng.tensor_copy(out=out, in_=in_), r=r, w=w, wa=wa)

    def recip(self, out, in_, r, w=(), wa=()):
        nc = self.nc
        return self.S.op("dve", lambda: nc.vector.reciprocal(out=out, in_=in_), r=r, w=w, wa=wa)

    def memset(self, e, ap, val, w=(), wa=()):
        eng = self._ve(e)
        return self.S.op(e, lambda: eng.memset(ap, val), w=w, wa=wa)

    def mm(self, out, lhsT, rhs, start, stop, r, w=(), wa=(), sig=True):
        nc = self.nc
        return self.S.op("pe", lambda: nc.tensor.matmul(out, lhsT=lhsT, rhs=rhs, start=start, stop=stop), r=r, w=w, wa=wa, sig=sig)

    def tr(self, out, in_, ident, r, w=(), wa=(), sig=True):
        nc = self.nc
        return self.S.op("pe", lambda: nc.tensor.transpose(out, in_, ident), r=r, w=w, wa=wa, sig=sig)

    def dma(self, e, out, in_, r, w=(), wa=()):
        return self.S.dma(e, out, in_, r=r, w=w, wa=wa)

    def defer(self, fn):
        self.later.append(fn)

    def flush(self):
        l, self.later = self.later, []
        for fn in l:
            fn()


class Scope:
    def __init__(self, cx):
        self.cx = cx

    def __enter__(self):
        self.es = ExitStack()
        self.es.__enter__()
        self.cx.scope = self.es
        return self

    def __exit__(self, *a):
        self.cx.flush()
        self.cx.S.barrier()
        self.cx.scope = None
        return self.es.__exit__(*a)


class Phase:
    def __init__(self, cx):
        self.cx = cx
        self.es = ExitStack()

    def __enter__(self):
        self.es.__enter__()
        return self

    def __exit__(self, *a):
        self.cx.flush()
        self.cx.S.barrier()
        return self.es.__exit__(*a)

    def sb(self, name, shape, dt=F32, n=1):
        name = self.cx.un(name)
        t = self.es.enter_context(self.cx.nc.sbuf_tensor(name, list(shape), dt))
        return Buf(t, name)

    def sbs(self, name, shape, dt, n):
        return [self.sb(f"{name}{i}", shape, dt) for i in range(n)]

    def ps(self, name, shape, dt=F32):
        nb = int(np.prod(shape[1:])) * (4 if dt == F32 else 2)
        assert nb % 2048 == 0, (name, shape)
        name = self.cx.un(name)
        t = self.es.enter_context(self.cx.nc.psum_tensor(name, list(shape), dt))
        return Buf(t, name, psum=True)

    def pss(self, name, shape, dt, n):
        return [self.ps(f"{name}{i}", shape, dt) for i in range(n)]


def common_consts(cx, ident_in):
    c = {}
    c["identf"] = cx.sb("identf", [128, 128], F32)
    c["identb"] = cx.sb("identb", [128, 128], BF16)
    c["onesf"] = cx.sb("onesf", [128, 128], F32)
    c["eps"] = cx.sb("epsT", [128, 1], F32)
    cx.dma("sp", c["identf"][:, :], ident_in[:, :], r=[ident_in], w=[c["identf"]])
    cx.cp("dve", c["identb"][:, :], c["identf"][:, :], r=[c["identf"]], w=[c["identb"]])
    cx.memset("pool", c["onesf"][:, :], 1.0, w=[c["onesf"]])
    c["onesb"] = cx.sb("onesb", [128, 128], BF16)
    cx.memset("pool", c["onesb"][:, :], 1.0, w=[c["onesb"]])
    cx.memset("pool", c["eps"][:, :], EPS, w=[c["eps"]])
    return c


def modulation(cx, c, ccT_in, w_ada, b_adaT_in, norm_gT_in, want_gate_bc=(0, 1), sfx=""):
    nc = cx.nc
    modT = cx.sb("modT" + sfx, [128, 48, 2], F32)
    gs = cx.sb("gsT" + sfx, [128, 16, 2], F32)
    gate_bc = cx.sb("gate_bc" + sfx, [128, 2, D], F32) if want_gate_bc else None
    with Phase(cx) as ph:
        scT = ph.sb("scT", [128, 16, 2], F32)
        badaT = ph.sb("badaT", [128, 48], F32)
        ngT = ph.sb("ngT", [128, 16], F32)
        wst = ph.sbs("wada_st", [128, 16, 512], F32, 2)
        pm = ph.ps("pm", [128, 512], F32)
        cx.dma("sp", scT[:, :, :], ccT_in[:, :, :], r=[ccT_in], w=[scT])
        cx.dma("sp", badaT[:, :], b_adaT_in[:, :], r=[b_adaT_in], w=[badaT])
        cx.dma("sp", ngT[:, :], norm_gT_in[:, :], r=[norm_gT_in], w=[ngT])
        cx.act(scT[:, :, :], scT[:, :, :], AF.Silu, r=[scT], w=[scT])
        wv = w_ada.t.rearrange("(k p) c -> p k c", p=128)
        for cb in range(12):
            st = wst[cb % 2]
            for hh in range(2):
                cx.dma("sp" if hh == 0 else "act", st[:, hh * 8:(hh + 1) * 8, :], wv[:, hh * 8:(hh + 1) * 8, cb * 512:(cb + 1) * 512],
                       r=[w_ada], wa=[st])
            for fc in range(4):
                cc = cb * 4 + fc
                for k in range(16):
                    cx.mm(pm[:, cc * 2:cc * 2 + 2], st[:, k, fc * 128:(fc + 1) * 128], scT[:, k, :],
                          start=(k == 0), stop=(k == 15), r=[st, scT], wa=[pm], sig=(k == 15))
        cx.tt("dve", modT[:, :, :], pm[:, 0:96].rearrange("p (c j) -> p c j", j=2),
              badaT[:, :].unsqueeze(2).to_broadcast([128, 48, 2]), ALU.add, r=[pm, badaT], w=[modT])
        cx.stt("dve", gs[:, :, :], modT[:, 16:32, :], 1.0, ngT[:, :].unsqueeze(2).to_broadcast([128, 16, 2]),
               ALU.add, ALU.mult, r=[modT, ngT], w=[gs])
    if want_gate_bc:
        gate_rows(cx, c, modT, gate_bc, want_gate_bc)
    return modT, gs, gate_bc


def gate_rows(cx, c, modT, gate_bc, js):
    with Phase(cx) as ph:
        dgs = ph.sbs("dgate", [128, 128], F32, 2)
        pgs = ph.pss("pgate", [128, 512], F32, 2)
        i = 0
        for j in js:
            for k in range(16):
                dg = dgs[i % 2]
                pg = pgs[i % 2]
                cx.ts("dve", dg[:, :], c["identf"][:, :], modT[:, 32 + k, j:j + 1], None, ALU.mult, None,
                      r=[c["identf"], modT], w=[dg])
                cx.mm(pg[:, 0:128], c["onesf"][:, :], dg[:, :], True, True, r=[c["onesf"], dg], w=[pg])
                cx.cp("act", gate_bc[:, j, k * 128:(k + 1) * 128], pg[:, 0:128], r=[pg], wa=[gate_bc])
                i += 1


def build_hT(cx, ph_bufs, c, modT, gs, hT, hTb, tiles):
    xts, xns, sqj, sss, pTs = ph_bufs

    def stage_a(i):
        src, r0, j = tiles[i]
        xt, xn, ss = xts[i % 2], xns[i % 2], sss[i % 2]
        cx.dma("sp", xt[:, :], src[r0:r0 + 128, :], r=[src], w=[xt])
        cx.memset("pool", ss[:, :], 0.0, w=[ss])
        cx.act(sqj[:, :], xt[:, :], AF.Square, r=[xt, ss], w=[sqj, ss], accum_out=ss[:, 0:1])
        cx.act(ss[:, 1:2], ss[:, 0:1], AF.Sqrt, r=[ss, c["eps"]], w=[ss], bias=c["eps"][:, 0:1], scale=1.0 / D)
        cx.recip(ss[:, 2:3], ss[:, 1:2], r=[ss], w=[ss])
        cx.ts("dve", xn[:, :], xt[:, :], ss[:, 2:3], None, ALU.mult, None, r=[xt, ss], w=[xn])

    def stage_b(i):
        src, r0, j = tiles[i]
        xn = xns[i % 2]
        for half in range(2):
            pT = pTs[half]
            for kk in range(8):
                k = half * 8 + kk
                cx.tr(pT[:, kk, :], xn[:, k * 128:(k + 1) * 128], c["identb"][:, :], r=[xn, c["identb"]],
                      wa=[pT], sig=(kk == 7))
            for kk in range(8):
                k = half * 8 + kk
                dst = hT[:, k, i * 128:(i + 1) * 128]
                if half == 0:
                    cx.act(dst, pT[:, kk, :], AF.Identity, r=[pT, gs, modT], wa=[hTb[i]],
                           scale=gs[:, k, j:j + 1], bias=modT[:, k, j:j + 1])
                else:
                    cx.ts("dve", dst, pT[:, kk, :], gs[:, k, j:j + 1], modT[:, k, j:j + 1], ALU.mult, ALU.add,
                          r=[pT, gs, modT], wa=[hTb[i]])

    stage_a(0)
    for i in range(len(tiles)):
        if i + 1 < len(tiles):
            stage_a(i + 1)
        stage_b(i)


class WStream:
    def __init__(self, cx, ph, name="w"):
        self.cx = cx
        self.st = ph.sbs(name + "_st", [128, 4, 512], F32, 2)
        self.wb = ph.sbs(name + "_bf", [128, 16, 512], BF16, 2)
        self.n = 0
        self.si = 0

    def fetch(self, w, c0):
        cx = self.cx
        wb = self.wb[self.n % 2]
        self.n += 1
        wv = w.t.rearrange("(k p) c -> p k c", p=128)
        for q in range(4):
            st = self.st[self.si % 2]
            self.si += 1
            cx.dma("sp", st[:, :, :], wv[:, q * 4:(q + 1) * 4, c0:c0 + 512], r=[w], w=[st])
            cx.cp("pool" if q % 2 == 0 else "act", wb[:, q * 4:(q + 1) * 4, :], st[:, :, :], r=[st], wa=[wb])
        return wb


LAM_INIT0 = 0.8 - 0.6 * math.exp(-0.3 * 0)


HT_DBG = 0
STOPF = {1.3: ("q",), 1.4: ("v",), 1.5: ("za",), 1.6: ("gg", "gv")}


class _Stop(Exception):
    pass


def build_l0(stop=99):
    cx = Ctx()
    try:
        _build_l0(cx, stop)
    except _Stop:
        cx.S.barrier()
    return cx


def _build_l0(cx, stop=99, fused=False, c=None, x1_hook=None):
    nc = cx.nc
    x_own = cx.din("x_own", [HALF, D])
    x_oth = cx.din("x_oth", [HALF, D])
    ctx_in = cx.din("ctx", [CTX, D])
    ccT_in = cx.din("ccT", [128, 16, 2])
    w_ada = cx.din("w_ada", [D, 3 * D])
    b_adaT_in = cx.din("b_adaT", [128, 48])
    norm_gT_in = cx.din("norm_gT", [128, 16])
    w_in = cx.din("w_in", [D, 7168])
    w_out = cx.din("w_out", [D, D])
    qkg_in = cx.din("qkg", [2, 64])
    lamv_in = cx.din("lamv", [4, 64])
    subg_in = cx.din("subg", [128, 1])
    cwT_in = cx.din("cwT", [128, 8, 31])
    cvp_in = cx.din("cvp", [128, 3, 8])
    rq_own = cx.din("rq_own", [HALF, 128])
    rk_own = cx.din("rk_own", [HALF, 128])
    rk_oth = cx.din("rk_oth", [HALF, 128])
    hmask_in = cx.din("hmask", [128, 2])
    if fused:
        x1_out = cx.dscr("x1_own_s", [HALF, D], F32)
        ctx1_out = cx.dscr("ctx1_s", [CTX, D], F32)
    else:
        ident_in = cx.din("ident", [128, 128])
        x1_out = cx.dout("x1", [HALF, D])
        ctx1_out = cx.dout("ctx1", [CTX, D])
    qT_s = cx.dscr("qT_s", [8, 128, HALF + CTX])
    kT_s = cx.dscr("kT_s", [8, 128, NKEY])
    v_s = cx.dscr("v_s", [NKEY, 1024])
    zaT_s = cx.dscr("zaT_s", [1024, HALF + CTX])
    zbT_s = cx.dscr("zbT_s", [1024, HALF + CTX])
    yT_s = cx.dscr("yT_s", [1024, HALF + 30])
    yTc_s = cx.dscr("yTc_s", [1024, CTX + 30])
    catT_s = cx.dscr("catT_s", [D, HALF + CTX])

    if c is None:
        c = common_consts(cx, ident_in)
    modT, gs, gate_bc = modulation(cx, c, ccT_in, w_ada, b_adaT_in, norm_gT_in)

    qg = cx.sb("qg", [128, 64]); qgs = cx.sb("qgs", [128, 64]); kg = cx.sb("kg", [128, 64])
    lamt = cx.sb("lamt", [128, 4, 64]); lam4 = cx.sb("lam4", [128, 8])
    subg = cx.sb("subg_t", [128, 1])
    hmask = cx.sb("hmask_t", [128, 2])
    cvp = cx.sb("cvp_t", [128, 3, 8])
    cwT = cx.sb("cwT_t", [128, 8, 31])
    cx.dma("sp", qg[:, :], qkg_in[0, :].partition_broadcast(128), r=[qkg_in], w=[qg])
    cx.dma("sp", kg[:, :], qkg_in[1, :].partition_broadcast(128), r=[qkg_in], w=[kg])
    cx.dma("sp", lamt[:, :, :].rearrange("p a d -> p (a d)"),
           lamv_in.t.rearrange("a d -> (a d)").partition_broadcast(128), r=[lamv_in], w=[lamt])
    cx.dma("sp", subg[:, :], subg_in[:, :], r=[subg_in], w=[subg])
    cx.dma("sp", hmask[:, :], hmask_in[:, :], r=[hmask_in], w=[hmask])
    cx.dma("sp", cvp[:, :, :], cvp_in[:, :, :], r=[cvp_in], w=[cvp])
    cx.dma("sp", cwT[:, :, :], cwT_in[:, :, :], r=[cwT_in], w=[cwT])
    cx.ts("dve", qgs[:, :], qg[:, :], 0.125, None, ALU.mult, None, r=[qg], w=[qgs])
    cx.tt("dve", lamt[:, 0, :], lamt[:, 0, :], lamt[:, 1, :], ALU.mult, r=[lamt], w=[lamt])
    cx.tt("dve", lamt[:, 2, :], lamt[:, 2, :], lamt[:, 3, :], ALU.mult, r=[lamt], w=[lamt])
    cx.S.op("dve", lambda: nc.vector.reduce_sum(out=lam4[:, 0:1], in_=lamt[:, 0, :], axis=AX.X), r=[lamt], w=[lam4])
    cx.S.op("dve", lambda: nc.vector.reduce_sum(out=lam4[:, 1:2], in_=lamt[:, 2, :], axis=AX.X), r=[lamt, lam4], w=[lam4])
    cx.act(lam4[:, 2:4], lam4[:, 0:2], AF.Exp, r=[lam4], w=[lam4])
    cx.stt("dve", lam4[:, 4:5], lam4[:, 3:4], -LAM_INIT0, lam4[:, 2:3], ALU.add, ALU.subtract, r=[lam4], w=[lam4])
    neglam = lam4[:, 4:5]
    cx.ts("dve", subg[:, :], subg[:, :], 1.0 - LAM_INIT0, None, ALU.mult, None, r=[subg], w=[subg])

    if stop < 1:
        cx.S.barrier()
        return cx
    with Phase(cx) as ph:
        hT_t = ph.sb("hT", [128, 16, 1280], BF16)
        hTb = [Buf(hT_t.t, f"hT{i}") for i in range(10)]
        hT = hT_t
        hbufs = (ph.sbs("xt", [128, D], F32, 2), ph.sbs("xn", [128, D], BF16, 2), ph.sb("sqj", [128, D], BF16),
                 ph.sbs("ss", [128, 4], F32, 2), ph.pss("pT", [128, 8, 128], BF16, 2))
        ws = WStream(cx, ph)
        pacc = ph.pss("pacc", [128, 512], F32, 3)
        pq = ph.pss("pq", [128, 8, 128], BF16, 2)
        sqs = ph.sbs("sq", [128, 512], F32, 2)
        st8 = ph.sbs("st8", [128, 16], F32, 2)
        xnq = ph.sbs("xnq", [128, 512], F32, 2)
        t1s = ph.sbs("t1", [128, 512], F32, 2)
        t2s = ph.sbs("t2", [128, 512], F32, 2)
        qbs = ph.sbs("qb", [128, 512], BF16, 2)
        qTs = ph.sbs("qTs", [128, 4, 128], BF16, 2)
        rts = ph.sbs("rt", [128, 128], F32, 2)
        vbs = ph.sbs("vb", [128, 512], BF16, 2)
        fos = ph.sbs("fo", [128, 512], BF16, 2)
        sig_t = ph.sb("sig", [128, 4, 1280], BF16)
        sigb = [Buf(sig_t.t, f"sig{i}") for i in range(4)]
        zero_t = ph.sb("zero", [128, 15], BF16)
        cnt = {"acc": 0, "qk": 0, "v": 0, "fo": 0}

        cx.memset("pool", zero_t[:, :], 0.0, w=[zero_t])
        yTc_v = yTc_s.t.rearrange("(j p) c -> p j c", p=128)
        for j in range(8):
            cx.dma("sp", yTc_v[:, j, 0:15], zero_t[:, :], r=[zero_t], wa=[yTc_s])
            cx.dma("sp", yTc_v[:, j, CTX + 15:CTX + 30], zero_t[:, :], r=[zero_t], wa=[yTc_s])
        if stop == 1.1:
            raise _Stop

        def qk_epi(ps, tile, fam, h0):
            kind, idx = tile
            i = cnt["qk"]; cnt["qk"] += 1
            sq, s8, xq, t1, t2, qb, qTt, rt, pqt = sqs[i % 2], st8[i % 2], xnq[i % 2], t1s[i % 2], t2s[i % 2], qbs[i % 2], qTs[i % 2], rts[i % 2], pq[i % 2]
            cx.act(sq[:, :], ps[:, :], AF.Square, r=[ps], w=[sq])
            cx.S.op("dve", lambda: nc.vector.reduce_sum(out=s8[:, 0:8], in_=sq[:, :].rearrange("p (g d) -> p g d", d=64), axis=AX.X), r=[sq], w=[s8])
            cx.act(s8[:, 8:16], s8[:, 0:8], AF.Sqrt, r=[s8, c["eps"]], w=[s8], bias=c["eps"][:, 0:1], scale=1.0 / 64)
            cx.recip(s8[:, 0:8], s8[:, 8:16], r=[s8], w=[s8])
            v3 = lambda a: a.rearrange("p (g d) -> p g d", d=64)
            cx.tt("dve", v3(xq[:, :]), v3(ps[:, :]), s8[:, 0:8].unsqueeze(2).to_broadcast([128, 8, 64]), ALU.mult, r=[ps, s8], w=[xq])
            g = kg if fam == "k" else (qgs if kind == "ctx" else qg)
            cx.tt("dve", v3(xq[:, :]), v3(xq[:, :]), g[:, :].unsqueeze(1).to_broadcast([128, 8, 64]), ALU.mult, r=[xq, g], w=[xq])
            if kind == "ctx":
                cx.cp("act", qb[:, :], xq[:, :], r=[xq], w=[qb])
            else:
                rsrc = (rq_own if fam == "q" else rk_own) if kind == "own" else rk_oth
                cx.dma("sp", rt[:, :], rsrc[idx * 128:(idx + 1) * 128, :], r=[rsrc], w=[rt])
                cx.tt("dve", v3(t1[:, :]), v3(xq[:, :]), rt[:, 0:64].unsqueeze(1).to_broadcast([128, 8, 64]), ALU.mult, r=[xq, rt], w=[t1])
                for a in range(2):
                    lo, hi = a * 32, a * 32 + 16
                    e = "dve" if a == 0 else "pool"
                    cx.tt(e, v3(t2[:, :])[:, :, lo:lo + 16], v3(xq[:, :])[:, :, hi:hi + 16],
                          rt[:, 64 + lo:64 + lo + 16].unsqueeze(1).to_broadcast([128, 8, 16]), ALU.mult, r=[xq, rt], wa=[t2])
                    cx.tt(e, v3(t2[:, :])[:, :, hi:hi + 16], v3(xq[:, :])[:, :, lo:lo + 16],
                          rt[:, 64 + hi:64 + hi + 16].unsqueeze(1).to_broadcast([128, 8, 16]), ALU.mult, r=[xq, rt], wa=[t2])
                cx.tt("dve", qb[:, :], t1[:, :], t2[:, :], ALU.add, r=[t1, t2], w=[qb])
            if fam == "q":
                dst_s = qT_s
                c0 = idx * 128 if kind == "own" else HALF + idx * 128
            else:
                dst_s = kT_s
                c0 = {"ctx": 0, "own": CTX, "oth": CTX + HALF}[kind] + idx * 128

            def fin():
                for hh in range(4):
                    cx.tr(pqt[:, hh, :], qb[:, hh * 128:(hh + 1) * 128], c["identb"][:, :], r=[qb, c["identb"]], wa=[pqt], sig=(hh == 3))
                cx.cp("dve", qTt[:, :, :], pqt[:, 0:4, :], r=[pqt], w=[qTt])
                cx.dma(STQ, dst_s.t[h0:h0 + 4, :, c0:c0 + 128].rearrange("h p t -> p h t"), qTt[:, :, :], r=[qTt], wa=[dst_s])
            cx.defer(fin)

        def v_epi(ps, tile, cb2):
            kind, idx = tile
            i = cnt["v"]; cnt["v"] += 1
            vb = vbs[i % 2]
            r0 = {"ctx": 0, "own": CTX, "oth": CTX + HALF}[kind] + idx * 128
            cx.cp("act", vb[:, :], ps[:, :], r=[ps], w=[vb])
            cx.defer(lambda: cx.dma(STQ, v_s[r0:r0 + 128, cb2 * 512:(cb2 + 1) * 512], vb[:, :], r=[vb], wa=[v_s]))

        own = lambda a, b: [("own", i) for i in range(a, b)]
        blocks = [
            dict(tiles=own(0, 8) + [("ctx", 0), ("ctx", 1)], full=10, fams="all",
                 chunks=[(0, 512, "own", 0), (512, 512, "own", 512), (1024, 256, "ctx", 0)]),
            dict(tiles=own(8, 16) + [("oth", 0), ("oth", 15)], full=8, fams="all",
                 chunks=[(0, 512, "own", 1024), (512, 512, "own", 1536), (1024, 256, "halo", 0)]),
            dict(tiles=[("oth", i) for i in range(1, 8)], full=0, fams="kv", chunks=[]),
            dict(tiles=[("oth", i) for i in range(8, 15)], full=0, fams="kv", chunks=[]),
        ]
        srcmap = {"own": (x_own, 0), "oth": (x_oth, 0), "ctx": (ctx_in, 1)}
        colblocks = [("q", 0, 0), ("q", 512, 1), ("k", 1024, 0), ("k", 1536, 1), ("v", 2048, 0), ("v", 2560, 1),
                     ("za", 3072, 0), ("za", 3584, 1), ("gg", 5120, 0), ("gv", 4096, 0), ("gg", 5632, 1), ("gv", 4608, 1),
                     ("zb", 6144, 0), ("zb", 6656, 1)]
        for blk in blocks:
            tiles = blk["tiles"]
            build_hT(cx, hbufs, c, modT, gs, hT, hTb, [(srcmap[k][0], i * 128, srcmap[k][1]) for (k, i) in tiles])
            if stop == 1.2:
                raise _Stop
            cbl = colblocks if blk["fams"] == "all" else [cb for cb in colblocks if cb[0] in ("k", "v")]
            if stop in STOPF:
                cbl = [cb for cb in cbl if cb[0] in STOPF[stop]]
            wnext = ws.fetch(w_in, cbl[0][1])
            for ci, (fam, c0, sub) in enumerate(cbl):
                wb = wnext
                if ci + 1 < len(cbl):
                    wnext = ws.fetch(w_in, cbl[ci + 1][1])
                if fam in ("q", "k", "v"):
                    for ti, tile in enumerate(tiles):
                        if tile[0] == "oth" and fam == "q":
                            continue
                        ps = pacc[cnt["acc"] % 3]; cnt["acc"] += 1
                        for k in range(16):
                            cx.mm(ps[:, :], hT[:, k, ti * 128:(ti + 1) * 128], wb[:, k, :], start=(k == 0), stop=(k == 15),
                                  r=[hTb[ti], wb], wa=[ps], sig=(k == 15))
                        pend, cx.later = cx.later, []
                        if fam == "v":
                            v_epi(ps, tile, sub)
                        else:
                            qk_epi(ps, tile, fam, sub * 4)
                        for fn in pend:
                            fn()
                    cx.flush()
                else:
                    for (h0c, n, ckind, d0) in blk["chunks"]:
                        if ckind == "halo" and fam not in ("gg", "gv"):
                            continue
                        t0 = h0c // 128
                        rb = [hTb[t] for t in range(t0, t0 + (n + 127) // 128)]
                        for fc in range(4):
                            ps = pacc[cnt["acc"] % 3]; cnt["acc"] += 1
                            for k in range(16):
                                cx.mm(ps[:, 0:n], wb[:, k, fc * 128:(fc + 1) * 128], hT[:, k, h0c:h0c + n], start=(k == 0), stop=(k == 15),
                                      r=rb + [wb], wa=[ps], sig=(k == 15))
                            frow = sub * 512 + fc * 128
                            dcol = d0 if ckind == "own" else HALF + d0
                            if fam in ("za", "zb"):
                                fo = fos[cnt["fo"] % 2]; cnt["fo"] += 1
                                dst = zaT_s if fam == "za" else zbT_s
                                cx.act(fo[:, 0:n], ps[:, 0:n], AF.Silu, r=[ps], w=[fo])
                                cx.dma(STQ, dst[frow:frow + 128, dcol:dcol + n], fo[:, 0:n], r=[fo], wa=[dst])
                            elif fam == "gg":
                                cx.act(sig_t[:, fc, h0c:h0c + n], ps[:, 0:n], AF.Sigmoid, r=[ps], wa=[sigb[fc]])
                            else:
                                fo = fos[cnt["fo"] % 2]; cnt["fo"] += 1
                                cx.tt("dve", fo[:, 0:n], ps[:, 0:n], sig_t[:, fc, h0c:h0c + n], ALU.mult, r=[ps, sigb[fc]], w=[fo])
                                if ckind == "own":
                                    cx.dma(STQ, yT_s[frow:frow + 128, 15 + d0:15 + d0 + n], fo[:, 0:n], r=[fo], wa=[yT_s])
                                elif ckind == "ctx":
                                    cx.dma(STQ, yTc_s[frow:frow + 128, 15 + d0:15 + d0 + n], fo[:, 0:n], r=[fo], wa=[yTc_s])
                                else:
                                    cx.ts("dve", fo[:, 0:15], fo[:, 0:15], hmask[:, 1:2], None, ALU.mult, None, r=[fo, hmask], w=[fo])
                                    cx.ts("dve", fo[:, 241:256], fo[:, 241:256], hmask[:, 0:1], None, ALU.mult, None, r=[fo, hmask], w=[fo])
                                    cx.dma(STQ, yT_s[frow:frow + 128, 15 + HALF:30 + HALF], fo[:, 0:15], r=[fo], wa=[yT_s])
                                    cx.dma(STQ, yT_s[frow:frow + 128, 0:15], fo[:, 241:256], r=[fo], wa=[yT_s])
            cx.flush()
            if stop in STOPF:
                raise _Stop

    if stop < 2:
        return cx
    with Phase(cx) as ph:
        kTs = ph.sbs("kTh", [128, NKEY], BF16, 2)
        vhs = ph.sbs("vh", [128, 34, 128], BF16, 2)
        qhs = ph.sbs("qh", [128, HALF + CTX], BF16, 2)
        pST = ph.pss("pST", [128, 2, 512], F32, 2)
        po = ph.ps("po", [128, 2, 512], F32)
        psm = ph.ps("psm", [128, 2, 512], F32)
        pts = ph.sbs("pt", [128, 2, 512], BF16, 3)
        accs = ph.sbs("accP", [128, 2, 512], F32, 2)
        accm = [[Buf(a.t, a.name + "m0"), Buf(a.t, a.name + "m1")] for a in accs]
        rs = ph.sb("rs", [128, 2, 512], F32)
        ta = ph.sb("ta", [128, 512], F32); tb = ph.sb("tb", [128, 512], F32)
        ot = ph.sb("ot", [128, 512], F32); sqo = ph.sb("sqo", [128, 512], F32)
        rstd = ph.sb("rstdo", [128, 512], F32)
        zat = ph.sbs("zat", [128, 512], BF16, 2)
        obs = ph.sbs("ob", [128, 512], BF16, 2)
        v_v = v_s.t.rearrange("(kt p) c -> p kt c", p=128)
        step = 0
        ci = 0
        def load_head(h):
            cx.dma("sp", kTs[h % 2][:, :], kT_s[h, :, :], r=[kT_s], w=[kTs[h % 2]])
            cx.dma("sp", vhs[h % 2][:, :, :], v_v[:, :, h * 128:(h + 1) * 128], r=[v_s], w=[vhs[h % 2]])
            cx.dma("sp", qhs[h % 2][:, :], qT_s[h, :, :], r=[qT_s], w=[qhs[h % 2]])

        load_head(0)
        for h in range(8):
            kTh, vh, qh = kTs[h % 2], vhs[h % 2], qhs[h % 2]
            if h + 1 < 8:
                load_head(h + 1)
            for (q0, n, nkt) in [(0, 512, 34), (512, 512, 34), (1024, 512, 34), (1536, 512, 34), (HALF, 256, 2)]:
                acc = accs[ci % 2]
                za = zat[ci % 2]
                cx.dma("sp", za[:, 0:n], zaT_s[h * 128:(h + 1) * 128, q0:q0 + n], r=[zaT_s], w=[za])
                def qk(kt_, st_):
                    for m in range(2):
                        cx.mm(st_[:, m, 0:n], kTh[64 * m:64 * m + 64, kt_ * 128:(kt_ + 1) * 128], qh[64 * m:64 * m + 64, q0:q0 + n],
                              True, True, r=[kTh, qh], wa=[st_], sig=(m == 1))

                qk(0, pST[step % 2])
                for kt in range(nkt):
                    st = pST[step % 2]
                    pt = pts[step % 3]
                    step += 1
                    if kt + 1 < nkt:
                        qk(kt + 1, pST[step % 2])
                    cx.act(pt[:, :, 0:n], st[:, :, 0:n], AF.Exp, r=[st], w=[pt])
                    for m in range(2):
                        cx.mm(po[:, m, 0:n], vh[:, kt, :], pt[:, m, 0:n], start=(kt == 0), stop=(kt == nkt - 1),
                              r=[vh, pt], wa=[po], sig=(m == 1))
                    cx.mm(psm[:, 0, 0:n], c["onesb"][:, :], pt[:, 0, 0:n], start=(kt == 0), stop=(kt == nkt - 1),
                          r=[c["onesb"], pt], wa=[psm], sig=True)
                    am = accm[ci % 2][0]
                    if kt == 0:
                        cx.cp("dve", acc[:, 1, 0:n], pt[:, 1, 0:n], r=[pt], w=[am])
                    else:
                        cx.tt("dve", acc[:, 1, 0:n], acc[:, 1, 0:n], pt[:, 1, 0:n], ALU.add, r=[pt, am], w=[am])
                cx.mm(psm[:, 1, 0:n], c["onesf"][:, :], acc[:, 1, 0:n], True, True, r=[c["onesf"], accm[ci % 2][0]], wa=[psm], sig=True)
                cx.recip(rs[:, :, 0:n], psm[:, :, 0:n], r=[psm], w=[rs])
                cx.tt("dve", ta[:, 0:n], po[:, 0, 0:n], rs[:, 0, 0:n], ALU.mult, r=[po, rs], w=[ta])
                cx.tt("dve", tb[:, 0:n], po[:, 1, 0:n], rs[:, 1, 0:n], ALU.mult, r=[po, rs], w=[tb])
                cx.stt("dve", ot[:, 0:n], tb[:, 0:n], neglam, ta[:, 0:n], ALU.mult, ALU.add, r=[ta, tb, lam4], w=[ot])
                cx.act(sqo[:, 0:n], ot[:, 0:n], AF.Square, r=[ot], w=[sqo])
                cx.mm(psm[:, 0, 0:n], c["onesf"][:, :], sqo[:, 0:n], True, True, r=[c["onesf"], sqo], w=[psm])
                cx.act(rstd[:, 0:n], psm[:, 0, 0:n], AF.Sqrt, r=[psm, c["eps"]], w=[rstd], bias=c["eps"][:, 0:1], scale=1.0 / 128)
                cx.recip(rstd[:, 0:n], rstd[:, 0:n], r=[rstd], w=[rstd])
                cx.tt("dve", ot[:, 0:n], ot[:, 0:n], rstd[:, 0:n], ALU.mult, r=[ot, rstd], w=[ot])
                ob = obs[ci % 2]
                cx.stt("dve", ob[:, 0:n], ot[:, 0:n], subg[:, 0:1], za[:, 0:n], ALU.mult, ALU.mult, r=[ot, subg, za], w=[ob])
                cx.dma(STQ, catT_s[h * 128:(h + 1) * 128, q0:q0 + n], ob[:, 0:n], r=[ob], wa=[catT_s])
                ci += 1

    if stop < 3:
        return cx
    with Phase(cx) as ph:
        dg = ph.sb("dgc", [128, 8, 31, 128], BF16)
        ycs = ph.sbs("yc", [128, 8, 542], BF16, 2)
        convT = ph.sb("convT", [128, 8, 512], F32)
        sqT = ph.sb("sqT", [128, 8, 512], F32)
        onesc = ph.sb("onesc", [128, 128], F32)
        pconv = ph.pss("pconv", [128, 512], F32, 2)
        pst = ph.ps("pstat", [128, 2, 512], F32)
        mean = ph.sb("mean", [128, 512], F32); var = ph.sb("var", [128, 512], F32)
        tcs = ph.sbs("tc", [128, 512], F32, 2)
        scs = ph.sbs("sc", [128, 512], F32, 2)
        zbt = ph.sbs("zbt", [128, 512], BF16, 2)
        obs = ph.sbs("obc", [128, 512], BF16, 2)
        cx.memset("pool", onesc[:, :], 1.0 / 1024, w=[onesc])
        i = 0
        for j in range(8):
            for k in range(31):
                cx.ts("dve", dg[:, j, k, :], c["identf"][:, :], cwT[:, j, k:k + 1], None, ALU.mult, None, r=[c["identf"], cwT], wa=[dg])
                i += 1
        yT_v = yT_s.t.rearrange("(j p) c -> p j c", p=128)
        it = 0
        for (src, srcb, c0, n, dcol) in [(yT_v, yT_s, 0, 512, 0), (yT_v, yT_s, 512, 512, 512), (yT_v, yT_s, 1024, 512, 1024),
                                         (yT_v, yT_s, 1536, 512, 1536), (yTc_v, yTc_s, 0, 256, HALF)]:
            yc = ycs[it % 2]; it += 1
            cx.dma("sp", yc[:, :, 0:n + 30], src[:, :, c0:c0 + n + 30], r=[srcb], w=[yc])
            for j in range(8):
                pc = pconv[j % 2]
                for k in range(31):
                    cx.mm(pc[:, 0:n], dg[:, j, k, :], yc[:, j, k:k + n], start=(k == 0), stop=(k == 30), r=[dg, yc], wa=[pc], sig=(k == 30))
                cx.act(convT[:, j, 0:n], pc[:, 0:n], AF.Identity, r=[pc, cvp], wa=[convT], bias=cvp[:, 0, j:j + 1], scale=1.0)
                cx.act(sqT[:, j, 0:n], convT[:, j, 0:n], AF.Square, r=[convT], wa=[sqT])
            for j in range(8):
                cx.mm(pst[:, 0, 0:n], onesc[:, :], convT[:, j, 0:n], start=(j == 0), stop=(j == 7), r=[onesc, convT], wa=[pst], sig=(j == 7))
            for j in range(8):
                cx.mm(pst[:, 1, 0:n], onesc[:, :], sqT[:, j, 0:n], start=(j == 0), stop=(j == 7), r=[onesc, sqT], wa=[pst], sig=(j == 7))
            cx.cp("act", mean[:, 0:n], pst[:, 0, 0:n], r=[pst], w=[mean])
            cx.tt("dve", var[:, 0:n], mean[:, 0:n], mean[:, 0:n], ALU.mult, r=[mean], w=[var])
            cx.tt("dve", var[:, 0:n], pst[:, 1, 0:n], var[:, 0:n], ALU.subtract, r=[pst, var], w=[var])
            cx.ts("dve", var[:, 0:n], var[:, 0:n], 0.0, None, ALU.max, None, r=[var], w=[var])
            cx.act(var[:, 0:n], var[:, 0:n], AF.Sqrt, r=[var, c["eps"]], w=[var], bias=c["eps"][:, 0:1], scale=1.0)
            cx.recip(var[:, 0:n], var[:, 0:n], r=[var], w=[var])
            for j in range(8):
                tcb, scb, zb, ob = tcs[j % 2], scs[j % 2], zbt[j % 2], obs[j % 2]
                cx.dma("sp", zb[:, 0:n], zbT_s[j * 128:(j + 1) * 128, dcol:dcol + n], r=[zbT_s], w=[zb])
                cx.tt("dve", tcb[:, 0:n], convT[:, j, 0:n], mean[:, 0:n], ALU.subtract, r=[convT, mean], w=[tcb])
                cx.tt("dve", tcb[:, 0:n], tcb[:, 0:n], var[:, 0:n], ALU.mult, r=[tcb, var], w=[tcb])
                cx.act(scb[:, 0:n], tcb[:, 0:n], AF.Silu, r=[tcb, cvp], w=[scb], scale=cvp[:, 1, j:j + 1], bias=cvp[:, 2, j:j + 1])
                cx.tt("dve", ob[:, 0:n], scb[:, 0:n], zb[:, 0:n], ALU.mult, r=[scb, zb], w=[ob])
                cx.dma(STQ, catT_s[1024 + j * 128:1024 + (j + 1) * 128, dcol:dcol + n], ob[:, 0:n], r=[ob], wa=[catT_s])

    if stop < 4:
        return cx
    out_proj(cx, w_out, catT_s, gate_bc,
             [(x_own, x1_out, t * 128, t * 128, 0) for t in range(16)] + [(ctx_in, ctx1_out, t * 128, HALF + t * 128, 1) for t in range(2)],
             after_tile=(lambda ti: x1_hook(ti, x1_out)) if x1_hook is not None else None)
    return x1_out, ctx1_out


def out_proj(cx, w_out, catT_s, gate_bc, tiles, blend=None, loader=None, after_tile=None):
    with Phase(cx) as ph:
        wo = ph.sb("wo", [128, 16, D], BF16)
        wst = ph.sbs("wo_st", [128, 2, D], F32, 2)
        cats = ph.sbs("catt", [128, 16, 128], BF16, 2)
        catab = (ph.sbs("catA", [128, 16, 128], BF16, 2), ph.sbs("catB", [128, 16, 128], BF16, 2)) if blend is not None else None
        xts = ph.sbs("xto", [128, D], F32, 2)
        xos = ph.sbs("xoo", [128, D], F32, 2)
        tmp = ph.sbs("tmpo", [128, 512], F32, 2)
        pacc = ph.pss("pacco", [128, 512], F32, 3)
        wv = w_out.t.rearrange("(k p) c -> p k c", p=128)
        for q in range(8):
            st = wst[q % 2]
            cx.dma("sp", st[:, :, :], wv[:, q * 2:(q + 1) * 2, :], r=[w_out], w=[st])
            cx.cp("pool", wo[:, q * 2:(q + 1) * 2, :], st[:, :, :], r=[st], wa=[wo])
        cat_v = catT_s.t.rearrange("(k p) c -> p k c", p=128) if catT_s is not None else None
        n = 0
        for ti, (xsrc, xdst, r0, c0, j) in enumerate(tiles):
            cat, xt, xo = cats[ti % 2], xts[ti % 2], xos[ti % 2]
            if loader is None:
                loader = lambda dst, col: cx.dma("sp", dst[:, :, :], cat_v[:, :, col:col + 128], r=[catT_s], w=[dst])
            if blend is None:
                loader(cat, c0)
            else:
                ca, cb_ = catab[0][ti % 2], catab[1][ti % 2]
                loader(ca, c0)
                loader(cb_, HALF + c0)
                cx.ts("pool", ca[:, :, :], ca[:, :, :], blend[:, 0:1], None, ALU.mult, None, r=[ca, blend], w=[ca])
                cx.stt("dve", cat[:, :, :], cb_[:, :, :], blend[:, 1:2], ca[:, :, :], ALU.mult, ALU.add, r=[ca, cb_, blend], w=[cat])
            cx.dma("sp", xt[:, :], xsrc[r0:r0 + 128, :], r=[xsrc], w=[xt])
            for cb in range(4):
                ps = pacc[n % 3]
                tm = tmp[n % 2]
                n += 1
                for k in range(16):
                    cx.mm(ps[:, :], cat[:, k, :], wo[:, k, cb * 512:(cb + 1) * 512], start=(k == 0), stop=(k == 15), r=[cat, wo], wa=[ps], sig=(k == 15))
                cx.tt("dve", tm[:, :], ps[:, :], gate_bc[:, j, cb * 512:(cb + 1) * 512], ALU.mult, r=[ps, gate_bc], w=[tm])
                cx.tt("dve", xo[:, cb * 512:(cb + 1) * 512], tm[:, :], xt[:, cb * 512:(cb + 1) * 512], ALU.add, r=[tm, xt], wa=[xo])
            cx.dma(STQ, xdst[r0:r0 + 128, :], xo[:, :], r=[xo], wa=[xdst])
            if after_tile is not None:
                after_tile(ti)


T1 = CTX + SEQ
NT1 = T1 // 128
NCH = T1 // 64


def build_l1a(stop=99):
    cx = Ctx()
    _build_l1a(cx, stop)
    return cx


def _build_l1a(cx, stop=99, fused=False, c=None, x1_tiles=None, ctx1_in=None, cat_hook=None):
    nc = cx.nc
    sfx = "1" if fused else ""
    if not fused:
        x1_in = cx.din("x1f", [SEQ, D])
        ctx1_in = cx.din("ctx1", [CTX, D])
        x1_tiles = [(x1_in, t * 128) for t in range(32)]
    ccT_in = cx.din("ccT" + sfx, [128, 16, 2])
    w_ada = cx.din("w_ada" + sfx, [D, 3 * D])
    b_adaT_in = cx.din("b_adaT" + sfx, [128, 48])
    norm_gT_in = cx.din("norm_gT" + sfx, [128, 16])
    w_c = cx.din("w_c", [D, 5120])
    lbg_in = cx.din("lbg", [128, 2, 2, 8])
    ong_in = cx.din("ong", [128, 1])
    maskF_in = cx.din("maskF", [128, 128])
    maskB_in = cx.din("maskB", [128, 128])
    if fused:
        catT1_out = cx.dscr("catT1_s", [1024, SEQ], BF16)
    else:
        ident_in = cx.din("ident", [128, 128])
        catT1_out = cx.dout("catT1", [1024, SEQ], BF16)
        modT_out = cx.dout("modT1", [128, 48, 2], F32)
    qS = cx.dscr("qS", [8, 128, T1], F32)
    sgS = cx.dscr("sgS", [2, 8, 128, T1], F32)
    vS = cx.dscr("vS", [8, 128, NT1, 128], BF16)
    zS = cx.dscr("zS", [8, 128, T1], BF16)

    if c is None:
        c = common_consts(cx, ident_in)
    modT, gs, _ = modulation(cx, c, ccT_in, w_ada, b_adaT_in, norm_gT_in, want_gate_bc=(), sfx="1")
    if not fused:
        cx.dma("sp", modT_out[:, :, :], modT[:, :, :], r=[modT], w=[modT_out])
    lbt = cx.sb("lbt", [128, 2, 2, 8]); lb = cx.sb("lb", [128, 2, 8]); oml = cx.sb("oml", [128, 2, 8]); noml = cx.sb("noml", [128, 2, 8])
    ong = cx.sb("ong_t", [128, 1]); maskF = cx.sb("maskF_t", [128, 128]); maskB = cx.sb("maskB_t", [128, 128])
    cx.dma("sp", lbt[:, :, :, :], lbg_in[:, :, :, :], r=[lbg_in], w=[lbt])
    cx.dma("sp", ong[:, :], ong_in[:, :], r=[ong_in], w=[ong])
    cx.dma("sp", maskF[:, :], maskF_in[:, :], r=[maskF_in], w=[maskF])
    cx.dma("sp", maskB[:, :], maskB_in[:, :], r=[maskB_in], w=[maskB])
    cx.tt("dve", lb[:, :, :], lbt[:, :, 1, :], lbt[:, :, 0, :], ALU.subtract, r=[lbt], w=[lb])
    cx.act(lb[:, :, :], lb[:, :, :], AF.Sigmoid, r=[lb], w=[lb])
    cx.ts("dve", oml[:, :, :], lb[:, :, :], -1.0, 1.0, ALU.mult, ALU.add, r=[lb], w=[oml])
    cx.ts("dve", noml[:, :, :], lb[:, :, :], -1.0, None, ALU.add, None, r=[lb], w=[noml])
    if stop < 1:
        cx.S.barrier()
        return cx

    with Phase(cx) as ph:
        hT = ph.sb("hT", [128, 16, 1280], BF16)
        hTb = [Buf(hT.t, f"hT{i}") for i in range(10)]
        hbufs = (ph.sbs("xt", [128, D], F32, 2), ph.sbs("xn", [128, D], BF16, 2), ph.sb("sqj", [128, D], BF16),
                 ph.sbs("ss", [128, 4], F32, 2), ph.pss("pT", [128, 8, 128], BF16, 2))
        ws = WStream(cx, ph)
        pacc = ph.pss("pacc", [128, 512], F32, 3)
        vbs = ph.sbs("vb", [128, 512], BF16, 2)
        fof = ph.sbs("fof", [128, 512], F32, 3)
        fob = ph.sbs("fob", [128, 512], BF16, 2)
        cnt = {"acc": 0, "v": 0, "f": 0, "b": 0}
        colblocks = [("q", 0, 0), ("q", 512, 1), ("i", 1024, 0), ("i", 1536, 1), ("uf", 2048, 0), ("uf", 2560, 1),
                     ("ub", 3072, 0), ("ub", 3584, 1), ("z", 4096, 0), ("z", 4608, 1)]
        for t0 in range(0, NT1, 10):
            tl = list(range(t0, min(t0 + 10, NT1)))
            tiles = [(ctx1_in, t * 128, 1) if t < 2 else (x1_tiles[t - 2][0], x1_tiles[t - 2][1], 0) for t in tl]
            build_hT(cx, hbufs, c, modT, gs, hT, hTb, tiles)
            ntok = len(tl) * 128
            chunks = [(a, min(512, ntok - a)) for a in range(0, ntok, 512)]
            wnext = ws.fetch(w_c, colblocks[0][1])
            for ci, (fam, c0, sub) in enumerate(colblocks):
                wb = wnext
                if ci + 1 < len(colblocks):
                    wnext = ws.fetch(w_c, colblocks[ci + 1][1])
                if fam == "i":
                    for ti, t in enumerate(tl):
                        ps = pacc[cnt["acc"] % 3]; cnt["acc"] += 1
                        for k in range(16):
                            cx.mm(ps[:, :], hT[:, k, ti * 128:(ti + 1) * 128], wb[:, k, :], start=(k == 0), stop=(k == 15),
                                  r=[hTb[ti], wb], wa=[ps], sig=(k == 15))
                        vb = vbs[cnt["v"] % 2]; cnt["v"] += 1
                        cx.cp("act", vb[:, :], ps[:, :], r=[ps], w=[vb])
                        cx.dma(STQ, vS.t[sub * 4:sub * 4 + 4, :, t, :].rearrange("h p c -> p h c"),
                               vb[:, :].rearrange("p (h c) -> p h c", c=128), r=[vb], wa=[vS])
                else:
                    for (h0c, n) in chunks:
                        tt0 = h0c // 128
                        rb = [hTb[x] for x in range(tt0, tt0 + (n + 127) // 128)]
                        g0 = t0 * 128 + h0c
                        for fc in range(4):
                            ps = pacc[cnt["acc"] % 3]; cnt["acc"] += 1
                            for k in range(16):
                                cx.mm(ps[:, 0:n], wb[:, k, fc * 128:(fc + 1) * 128], hT[:, k, h0c:h0c + n], start=(k == 0), stop=(k == 15),
                                      r=rb + [wb], wa=[ps], sig=(k == 15))
                            h = sub * 4 + fc
                            if fam == "z":
                                fo = fob[cnt["b"] % 2]; cnt["b"] += 1
                                cx.act(fo[:, 0:n], ps[:, 0:n], AF.Silu, r=[ps], w=[fo])
                                cx.dma(STQ, zS[h, :, g0:g0 + n], fo[:, 0:n], r=[fo], wa=[zS])
                            else:
                                fo = fof[cnt["f"] % 3]; cnt["f"] += 1
                                cx.act(fo[:, 0:n], ps[:, 0:n], AF.Silu if fam == "q" else AF.Sigmoid, r=[ps], w=[fo])
                                if fam == "q":
                                    cx.dma(STQ, qS[h, :, g0:g0 + n], fo[:, 0:n], r=[fo], wa=[qS])
                                else:
                                    cx.dma(STQ, sgS[0 if fam == "uf" else 1, h, :, g0:g0 + n], fo[:, 0:n], r=[fo], wa=[sgS])
    if stop < 2:
        return cx

    with Phase(cx) as ph:
        msk = ph.sb("msk", [128, T1], F32)
        qf = ph.sb("qf", [128, T1], F32)
        vh = ph.sb("vh1", [128, NT1, 128], BF16)
        oacc = ph.sb("oacc", [128, SEQ], F32)
        lf = ph.sb("lf", [128, T1], F32)
        kf = ph.sb("kf", [128, T1], F32)
        bc = ph.sb("bcum", [128, T1], F32)
        ef = ph.sb("ef", [128, T1], F32)
        qdec = ph.sb("qdec", [128, T1], BF16)
        kinv = ph.sb("kinv", [128, T1], BF16)
        kendT = ph.sb("kendT", [128, T1], BF16)
        kend = ph.sb("kend", [128, NT1, 128], BF16)
        dec = ph.sb("dec", [128, NCH], F32)
        bend = ph.sb("bend", [128, NCH], F32)
        Sf = ph.sb("Sf", [128, 128], F32)
        Sbs = ph.sbs("Sb", [128, 128], BF16, 4)
        Ams = ph.sbs("Am", [128, 128], BF16, 3)
        sqr = ph.sb("sqr", [128, 512], F32); rst = ph.sb("rst", [128, 512], F32); onb = ph.sb("onb", [128, 512], F32)
        zcs = ph.sbs("zc", [128, 512], BF16, 2); obs = ph.sbs("ob1", [128, 512], BF16, 2)
        pA = ph.pss("pA", [128, 512], F32, 1)
        po = ph.pss("po1", [128, 512], F32, 2)
        pS = ph.pss("pS", [128, 512], F32, 4)
        pK = ph.pss("pK", [128, 8, 128], BF16, 1)
        ia_ = [0, 0]
        cx.memset("pool", msk[:, :], 1.0, w=[msk])
        cx.memset("pool", msk[:, :].rearrange("p (c d) -> p c d", d=64)[:, :, 0:1], 0.0, w=[msk])
        v3 = lambda a: a.rearrange("p (c d) -> p c d", d=64)
        HT = T1 // 2
        ia = 0
        isb = 0
        ipo = 0
        for h in range(8):
            cx.dma("sp", qf[:, :], qS[h, :, :], r=[qS], w=[qf])
            cx.dma("sp", vh[:, :, :], vS[h, :, :, :], r=[vS], w=[vh])
            for d_ in range(2):
                cx.dma("sp", lf[:, :], sgS[d_, h, :, :], r=[sgS], w=[lf])
                cx.ts("dve", kf[:, :], lf[:, :], noml[:, d_, h:h + 1], oml[:, d_, h:h + 1], ALU.mult, ALU.add, r=[lf, noml, oml], w=[kf])
                cx.act(lf[:, :], lf[:, :], AF.Ln, r=[lf, oml, lb], w=[lf], scale=oml[:, d_, h:h + 1], bias=lb[:, d_, h:h + 1])
                for hh in range(2):
                    sl = slice(hh * HT, (hh + 1) * HT)
                    cx.S.op("dve", lambda: nc.vector.tensor_tensor_scan(out=bc[:, sl], data0=msk[:, sl], data1=lf[:, sl], initial=0.0,
                                                                        op0=ALU.mult, op1=ALU.add), r=[msk, lf], wa=[bc])
                cx.cp("dve", bend[:, :], v3(bc[:, :])[:, :, 63], r=[bc], w=[bend])
                bend_bc = bend[:, :].unsqueeze(2).to_broadcast([128, NCH, 64])
                if d_ == 1:
                    cx.tt("dve", v3(ef[:, :]), v3(lf[:, :]), bend_bc, ALU.add, r=[lf, bend], w=[ef])
                    cx.tt("dve", bc[:, :], ef[:, :], bc[:, :], ALU.subtract, r=[ef, bc], w=[bc])
                cx.act(dec[:, :], bend[:, :], AF.Exp, r=[bend], w=[dec])
                cx.act(ef[:, :], bc[:, :], AF.Exp, r=[bc], w=[ef])
                cx.tt("dve", qdec[:, :], qf[:, :], ef[:, :], ALU.mult, r=[qf, ef], w=[qdec])
                cx.act(ef[:, :], bc[:, :], AF.Exp, r=[bc], w=[ef], scale=-1.0)
                cx.tt("dve", kinv[:, :], kf[:, :], ef[:, :], ALU.mult, r=[kf, ef], w=[kinv])
                cx.tt("dve", v3(ef[:, :]), bend_bc, v3(bc[:, :]), ALU.subtract, r=[bend, bc], w=[ef])
                cx.act(ef[:, :], ef[:, :], AF.Exp, r=[ef], w=[ef])
                cx.tt("dve", kendT[:, :], kf[:, :], ef[:, :], ALU.mult, r=[kf, ef], w=[kendT])
                for g in range(0, NT1, 8):
                    pk = pK[0]
                    ng = min(8, NT1 - g)
                    for t in range(ng):
                        cx.tr(pk[:, t, :], kendT[:, (g + t) * 128:(g + t + 1) * 128], c["identb"][:, :], r=[kendT, c["identb"]], wa=[pk], sig=(t == ng - 1))
                    cx.cp("pool" if False else "act", kend[:, g:g + ng, :], pk[:, 0:ng, :], r=[pk], wa=[kend])
                cx.memset("pool", Sf[:, :], 0.0, w=[Sf])
                Sb = Sbs[isb % 4]; isb += 1
                cx.memset("pool", Sb[:, :], 0.0, w=[Sb])
                mask = maskF if d_ == 0 else maskB
                pairs = list(range(NT1)) if d_ == 0 else [1, 0] + list(range(NT1 - 1, 1, -1))
                order = (0, 1) if d_ == 0 else (1, 0)
                st = {}

                def early(p):
                    t0_ = p * 128
                    e_ = {}
                    if p >= 2:
                        a_ps = pA[0]
                        Am = Ams[ia_[0] % 3]; ia_[0] += 1
                        cx.mm(a_ps[:, 0:128], kinv[:, t0_:t0_ + 128], qdec[:, t0_:t0_ + 128], True, True, r=[kinv, qdec], w=[a_ps])
                        cx.tt("dve", Am[:, :], a_ps[:, 0:128], mask[:, :], ALU.mult, r=[a_ps, mask], w=[Am])
                        e_["Am"] = Am
                    e_["s"] = []
                    for hc in order:
                        pr = slice(hc * 64, hc * 64 + 64)
                        s_ps = pS[ia_[1] % 4]; ia_[1] += 1
                        cx.mm(s_ps[:, 0:128], kend[pr, p, :], vh[pr, p, :], True, True, r=[kend, vh], w=[s_ps])
                        e_["s"].append(s_ps)
                    st[p] = e_

                early(pairs[0])
                for pi, p in enumerate(pairs):
                    if pi + 1 < len(pairs):
                        early(pairs[pi + 1])
                    e_ = st.pop(p)
                    lat = p >= 2
                    t0_ = p * 128
                    if lat:
                        o_ps = po[ipo % 2]; ipo += 1
                        cx.mm(o_ps[:, 0:128], vh[:, p, :], e_["Am"][:, :], True, False, r=[vh, e_["Am"]], wa=[o_ps], sig=False)
                    for oi, hc in enumerate(order):
                        c64 = slice(t0_ + hc * 64, t0_ + hc * 64 + 64)
                        if lat:
                            cx.mm(o_ps[:, hc * 64:hc * 64 + 64], Sb[:, :], qdec[:, c64], False, (oi == 1), r=[Sb, qdec], wa=[o_ps], sig=True)
                        s_ps = e_["s"][oi]
                        dcol = dec[:, 2 * p + hc:2 * p + hc + 1]
                        Sb = Sbs[isb % 4]; isb += 1
                        cx.stt("dve", Sb[:, :], Sf[:, :], dcol, s_ps[:, 0:128], ALU.mult, ALU.add, r=[Sf, dec, s_ps], w=[Sb])
                        cx.stt("dve", Sf[:, :], Sf[:, :], dcol, s_ps[:, 0:128], ALU.mult, ALU.add, r=[Sf, dec, s_ps], w=[Sf])
                    if lat:
                        oc = slice((p - 2) * 128, (p - 1) * 128)
                        if d_ == 0:
                            cx.cp("act", oacc[:, oc], o_ps[:, 0:128], r=[o_ps], wa=[oacc])
                        else:
                            cx.tt("dve", oacc[:, oc], oacc[:, oc], o_ps[:, 0:128], ALU.add, r=[o_ps, oacc], wa=[oacc])
            for q8 in range(8):
                cs = slice(q8 * 512, (q8 + 1) * 512)
                a_ps = pA[0]
                zc = zcs[q8 % 2]; ob = obs[q8 % 2]
                cx.dma("sp", zc[:, :], zS[h, :, CTX + q8 * 512:CTX + (q8 + 1) * 512], r=[zS], w=[zc])
                cx.act(sqr[:, :], oacc[:, cs], AF.Square, r=[oacc], w=[sqr])
                cx.mm(a_ps[:, :], c["onesf"][:, :], sqr[:, :], True, True, r=[c["onesf"], sqr], w=[a_ps])
                cx.act(rst[:, :], a_ps[:, :], AF.Sqrt, r=[a_ps, c["eps"]], w=[rst], bias=c["eps"][:, 0:1], scale=1.0 / 128)
                cx.recip(rst[:, :], rst[:, :], r=[rst], w=[rst])
                cx.tt("dve", onb[:, :], oacc[:, cs], rst[:, :], ALU.mult, r=[oacc, rst], w=[onb])
                cx.stt("dve", ob[:, :], onb[:, :], ong[:, 0:1], zc[:, :], ALU.mult, ALU.mult, r=[onb, ong, zc], w=[ob])
                cx.dma(STQ, catT1_out[h * 128:(h + 1) * 128, cs], ob[:, :], r=[ob], wa=[catT1_out])
            if cat_hook is not None:
                cat_hook(h, catT1_out)
    return catT1_out, modT


PAIRS = [[0, 1], [2, 3], [4, 5], [6, 7]]
WARMUP_COLL = True


def build_fused():
    cx = Ctx()
    ident_in = cx.din("ident", [128, 128])
    bmask_in = cx.din("bmask", [128, 2])
    w_out1 = cx.din("w_out1", [D, D])
    x2_out = cx.dout("x2", [HALF, D])
    c = common_consts(cx, ident_in)
    if WARMUP_COLL:
        wu_src = cx.dscr("wu_src", [128, 128], F32)
        wu_dst = cx.dscr("wu_dst", [256, 128], F32)
        cx.dma("sp", wu_src[:, :], ident_in[:, :], r=[ident_in], w=[wu_src])
        cx.S.coll("AllGather", wu_dst[:, :], wu_src[:, :], PAIRS, r=[wu_src], w=[wu_dst])
    x1g_t = cx.nc.dram_tensor("x1g_s", [8, 512, D], F32, kind="Internal").ap()
    x1g = [Buf(x1g_t[i], f"x1g{i}") for i in range(8)]

    def x1_hook(ti, x1_own):
        if ti < 16 and ti % 2 == 1:
            i = ti // 2
            cx.S.coll("AllGather", x1g[i][:, :], x1_own[i * 256:(i + 1) * 256, :], PAIRS, r=[x1_own], w=[x1g[i]])

    with Scope(cx):
        x1_own_s, ctx1_s = _build_l0(cx, 99, fused=True, c=c, x1_hook=x1_hook)
    x1_tiles = []
    for t in range(32):
        r_, lt = t // 16, t % 16
        x1_tiles.append((x1g[lt // 2], r_ * 256 + (lt % 2) * 128))
    with Scope(cx):
        catg_t = cx.nc.dram_tensor("catg_s", [4, 512, SEQ], BF16, kind="Internal").ap()
        catg = [Buf(catg_t[i], f"catg{i}") for i in range(4)]

        def cat_hook(h, cat_own):
            if h % 2 == 1:
                i = h // 2
                cx.S.coll("AllGather", catg[i][:, :], cat_own[i * 256:(i + 1) * 256, :], PAIRS, r=[cat_own], w=[catg[i]])

        catT1_s, modT = _build_l1a(cx, 99, fused=True, c=c, x1_tiles=x1_tiles, ctx1_in=ctx1_s, cat_hook=cat_hook)
        gate_bc = cx.sb("gate_bc1", [128, 2, D], F32)
        bmask = cx.sb("bmask_t", [128, 2], F32)
        cx.dma("sp", bmask[:, :], bmask_in[:, :], r=[bmask_in], w=[bmask])
        gate_rows(cx, c, modT, gate_bc, (0,))

        def loader(dst, col):
            for r_ in range(2):
                for i in range(4):
                    k0 = r_ * 8 + i * 2
                    cx.dma("sp", dst[:, k0:k0 + 2, :],
                           catg[i].t[r_ * 256:(r_ + 1) * 256, col:col + 128].rearrange("(k p) c -> p k c", p=128),
                           r=[catg[i]], wa=[dst])
        out_proj(cx, w_out1, None, gate_bc, [(x1_own_s, x2_out, t * 128, t * 128, 0) for t in range(16)], blend=bmask, loader=loader)
    return cx


def build_l1b():
    cx = Ctx()
    catT_in = cx.din("catT", [D, HALF], BF16)
    x1_own = cx.din("x1o", [HALF, D])
    w_out = cx.din("w_out", [D, D])
    modT_in = cx.din("modT1", [128, 48, 2])
    ident_in = cx.din("ident", [128, 128])
    x2_out = cx.dout("x2", [HALF, D])
    c = common_consts(cx, ident_in)
    modT = cx.sb("modT", [128, 48, 2], F32)
    gate_bc = cx.sb("gate_bc", [128, 2, D], F32)
    cx.dma("sp", modT[:, :, :], modT_in[:, :, :], r=[modT_in], w=[modT])
    gate_rows(cx, c, modT, gate_bc, (0,))
    out_proj(cx, w_out, catT_in, gate_bc, [(x1_own, x2_out, t * 128, t * 128, 0) for t in range(16)])
    return cx


def rope_tables():
    rows = SEQ // 64
    row = np.repeat(np.arange(rows), 64).astype(np.float32)
    col = np.tile(np.arange(64), rows).astype(np.float32)
    inv = (10000.0 ** (-np.arange(0, 32, 2, dtype=np.float32) / 32)).astype(np.float32)

    def axis_angles(pos):
        a = pos[:, None] * inv[None, :]
        return np.concatenate([a, a], axis=-1)
    ang = np.concatenate([axis_angles(row), axis_angles(col)], axis=-1).astype(np.float32)
    cos, sin = np.cos(ang), np.sin(ang)
    sgn = np.tile(np.concatenate([-np.ones(16), np.ones(16)]), 2).astype(np.float32)
    return np.concatenate([cos, sin * sgn], axis=-1).astype(np.float32)


def fm(v, nchunk):
    return np.ascontiguousarray(np.asarray(v, np.float32).reshape(nchunk, 128).T)


_CACHE = {}


def run_l0(inp, cores):
    if "l0" not in _CACHE:
        _CACHE["l0"] = build_l0()
    cx = _CACHE["l0"]
    rope = rope_tables()
    ident = np.eye(128, dtype=np.float32)
    in_maps = []
    for cid in cores:
        b, hf = cid // 2, cid % 2
        own = slice(hf * HALF, (hf + 1) * HALF)
        oth = slice((1 - hf) * HALF, (2 - hf) * HALF)
        cc = np.stack([inp["c"][b], inp["c_ctx"]], axis=-1)
        ccT = np.ascontiguousarray(cc.reshape(16, 128, 2).transpose(1, 0, 2))
        cw = inp["conv_w"][0]
        cwT = np.ascontiguousarray(cw.reshape(31, 8, 128).transpose(2, 1, 0))
        cvp = np.ascontiguousarray(np.stack([fm(inp["conv_b"][0], 8), fm(inp["cln_g"][0], 8), fm(inp["cln_b"][0], 8)], axis=1))
        hm = np.zeros((128, 2), np.float32)
        hm[:, 0] = 1.0 if hf == 1 else 0.0
        hm[:, 1] = 1.0 if hf == 0 else 0.0
        in_maps.append({
            "x_own": np.ascontiguousarray(inp["x"][b, own]), "x_oth": np.ascontiguousarray(inp["x"][b, oth]),
            "ctx": np.ascontiguousarray(inp["ctx"][b]), "ccT": ccT,
            "w_ada": np.ascontiguousarray(inp["w_ada"][0]), "b_adaT": fm(inp["b_ada"][0], 48), "norm_gT": fm(inp["norm_g"][0], 16),
            "w_in": np.ascontiguousarray(inp["w_in_ab"][0]), "w_out": np.ascontiguousarray(inp["w_out_ab"][0]),
            "qkg": np.ascontiguousarray(np.stack([inp["qn_g"][0], inp["kn_g"][0]])),
            "lamv": np.ascontiguousarray(np.stack([inp["lam_q1"][0], inp["lam_k1"][0], inp["lam_q2"][0], inp["lam_k2"][0]])),
            "subg": np.ascontiguousarray(inp["subln_g"][0].reshape(128, 1)),
            "cwT": cwT, "cvp": cvp,
            "rq_own": np.ascontiguousarray(rope[own] * np.float32(0.125)), "rk_own": np.ascontiguousarray(rope[own]),
            "rk_oth": np.ascontiguousarray(rope[oth]), "hmask": hm, "ident": ident,
        })
    res = run_bass_kernel_spmd(cx.nc, in_maps, core_ids=list(range(len(cores))))
    return res.results


def hgrn_masks():
    i = np.arange(128)
    same = (i[:, None] // 64) == (i[None, :] // 64)
    mF = (same & (i[:, None] <= i[None, :])).astype(np.float32)
    mB = (same & (i[:, None] >= i[None, :])).astype(np.float32)
    return mF, mB


def run_l1a(inp, x1, ctx1, cores):
    if "l1a" not in _CACHE:
        _CACHE["l1a"] = build_l1a()
    cx = _CACHE["l1a"]
    ident = np.eye(128, dtype=np.float32)
    mF, mB = hgrn_masks()
    wc = inp["w_in_c"][0]
    in_maps = []
    for cid in cores:
        b, hf = cid // 2, cid % 2
        cc = np.stack([inp["c"][b], inp["c_ctx"]], axis=-1)
        ccT = np.ascontiguousarray(cc.reshape(16, 128, 2).transpose(1, 0, 2))
        cols = np.concatenate([np.arange(f * D + hf * 1024, f * D + (hf + 1) * 1024) for f in range(5)])
        lg = inp["lb_gamma"][:, :, hf * 1024:(hf + 1) * 1024]
        lbg = np.ascontiguousarray(lg.reshape(2, 2, 8, 128).transpose(3, 0, 1, 2))
        in_maps.append({
            "x1f": np.ascontiguousarray(x1[b]), "ctx1": np.ascontiguousarray(ctx1[b]), "ccT": ccT,
            "w_ada": np.ascontiguousarray(inp["w_ada"][1]), "b_adaT": fm(inp["b_ada"][1], 48), "norm_gT": fm(inp["norm_g"][1], 16),
            "w_c": np.ascontiguousarray(wc[:, cols]), "lbg": lbg,
            "ong": np.ascontiguousarray(inp["onorm_g"][0].reshape(128, 1)),
            "maskF": mF, "maskB": mB, "ident": ident,
        })
    res = run_bass_kernel_spmd(cx.nc, in_maps, core_ids=list(range(len(cores))))
    return res.results


def run_l1b(inp, x1, cat_pairs, modTs, cores):
    if "l1b" not in _CACHE:
        _CACHE["l1b"] = build_l1b()
    cx = _CACHE["l1b"]
    ident = np.eye(128, dtype=np.float32)
    in_maps = []
    for i, cid in enumerate(cores):
        b, hf = cid // 2, cid % 2
        own = slice(hf * HALF, (hf + 1) * HALF)
        in_maps.append({
            "catT": np.ascontiguousarray(cat_pairs[b][:, own]), "x1o": np.ascontiguousarray(x1[b, own]),
            "w_out": np.ascontiguousarray(inp["w_out_c"][0]), "modT1": modTs[i], "ident": ident,
        })
    res = run_bass_kernel_spmd(cx.nc, in_maps, core_ids=list(range(len(cores))))
    return res.results


def run_l1(inp, x1, ctx1, cores):
    ra = run_l1a(inp, x1, ctx1, cores)
    cat_pairs = {}
    for i, cid in enumerate(cores):
        b, hf = cid // 2, cid % 2
        cat_pairs.setdefault(b, [None, None])[hf] = ra[i]["catT1"]
    cat_pairs = {b: np.concatenate(v, axis=0) for b, v in cat_pairs.items()}
    rb = run_l1b(inp, x1, cat_pairs, [ra[i]["modT1"] for i in range(len(cores))], cores)
    return rb


def l0_inputs(inp, cid, rope):
    b, hf = cid // 2, cid % 2
    own = slice(hf * HALF, (hf + 1) * HALF)
    oth = slice((1 - hf) * HALF, (2 - hf) * HALF)
    cc = np.stack([inp["c"][b], inp["c_ctx"]], axis=-1)
    ccT = np.ascontiguousarray(cc.reshape(16, 128, 2).transpose(1, 0, 2))
    cw = inp["conv_w"][0]
    cwT = np.ascontiguousarray(cw.reshape(31, 8, 128).transpose(2, 1, 0))
    cvp = np.ascontiguousarray(np.stack([fm(inp["conv_b"][0], 8), fm(inp["cln_g"][0], 8), fm(inp["cln_b"][0], 8)], axis=1))
    hm = np.zeros((128, 2), np.float32)
    hm[:, 0] = 1.0 if hf == 1 else 0.0
    hm[:, 1] = 1.0 if hf == 0 else 0.0
    return {
        "x_own": np.ascontiguousarray(inp["x"][b, own]), "x_oth": np.ascontiguousarray(inp["x"][b, oth]),
        "ctx": np.ascontiguousarray(inp["ctx"][b]), "ccT": ccT,
        "w_ada": np.ascontiguousarray(inp["w_ada"][0]), "b_adaT": fm(inp["b_ada"][0], 48), "norm_gT": fm(inp["norm_g"][0], 16),
        "w_in": np.ascontiguousarray(inp["w_in_ab"][0]), "w_out": np.ascontiguousarray(inp["w_out_ab"][0]),
        "qkg": np.ascontiguousarray(np.stack([inp["qn_g"][0], inp["kn_g"][0]])),
        "lamv": np.ascontiguousarray(np.stack([inp["lam_q1"][0], inp["lam_k1"][0], inp["lam_q2"][0], inp["lam_k2"][0]])),
        "subg": np.ascontiguousarray(inp["subln_g"][0].reshape(128, 1)),
        "cwT": cwT, "cvp": cvp,
        "rq_own": np.ascontiguousarray(rope[own] * np.float32(0.125)), "rk_own": np.ascontiguousarray(rope[own]),
        "rk_oth": np.ascontiguousarray(rope[oth]), "hmask": hm,
    }


def l1_inputs(inp, cid):
    b, hf = cid // 2, cid % 2
    cc = np.stack([inp["c"][b], inp["c_ctx"]], axis=-1)
    ccT = np.ascontiguousarray(cc.reshape(16, 128, 2).transpose(1, 0, 2))
    cols = np.concatenate([np.arange(f * D + hf * 1024, f * D + (hf + 1) * 1024) for f in range(5)])
    lg = inp["lb_gamma"][:, :, hf * 1024:(hf + 1) * 1024]
    lbg = np.ascontiguousarray(lg.reshape(2, 2, 8, 128).transpose(3, 0, 1, 2))
    mF, mB = hgrn_masks()
    bm = np.zeros((128, 2), np.float32)
    bm[:, hf] = 1.0
    return {
        "ccT1": ccT, "w_ada1": np.ascontiguousarray(inp["w_ada"][1]), "b_adaT1": fm(inp["b_ada"][1], 48),
        "norm_gT1": fm(inp["norm_g"][1], 16), "w_c": np.ascontiguousarray(inp["w_in_c"][0][:, cols]), "lbg": lbg,
        "ong": np.ascontiguousarray(inp["onorm_g"][0].reshape(128, 1)), "maskF": mF, "maskB": mB,
        "w_out1": np.ascontiguousarray(inp["w_out_c"][0]), "bmask": bm,
    }


def run_fused(inp, cores):
    if "fused" not in _CACHE:
        _CACHE["fused"] = build_fused()
    cx = _CACHE["fused"]
    rope = rope_tables()
    ident = np.eye(128, dtype=np.float32)
    in_maps = []
    for cid in cores:
        m = {"ident": ident}
        m.update(l0_inputs(inp, cid, rope))
        m.update(l1_inputs(inp, cid))
        in_maps.append(m)
    res = run_bass_kernel_spmd(cx.nc, in_maps, core_ids=list(range(len(cores))))
    return res.results


def kernel_unfused(**inputs):
    inp = {k: np.asarray(v) for k, v in inputs.items()}
    cores = list(range(8))
    r0 = run_l0(inp, cores)
    x1 = np.zeros_like(inp["x"])
    ctx1 = np.zeros_like(inp["ctx"])
    for cid in cores:
        b, hf = cid // 2, cid % 2
        x1[b, hf * HALF:(hf + 1) * HALF] = r0[cid]["x1"]
        ctx1[b] = r0[cid]["ctx1"]
    r1 = run_l1(inp, x1, ctx1, cores)
    out = np.zeros_like(inp["x"])
    for cid in cores:
        b, hf = cid // 2, cid % 2
        out[b, hf * HALF:(hf + 1) * HALF] = r1[cid]["x2"]
    return out


def kernel(**inputs):
    inp = {k: np.asarray(v) for k, v in inputs.items()}
    cores = list(range(8))
    r = run_fused(inp, cores)
    out = np.zeros_like(inp["x"])
    for cid in cores:
        b, hf = cid // 2, cid % 2
        out[b, hf * HALF:(hf + 1) * HALF] = r[cid]["x2"]
    return out
```
